# Optimizing a Trainium2 kernel written in Bass

```python
import jax, jax.numpy as jnp
from jax import lax
import numpy as np

D_MODEL = 1024
BATCH = 4
SEQ = 8192
DEPTH = 1

CONV_CH = D_MODEL
CONV_WIDTH = 31
HEAD_DIM = 64
N_Q_HEADS = 16
N_KV_HEADS = 2
GROUP = N_Q_HEADS // N_KV_HEADS
ATTN_W = N_Q_HEADS * HEAD_DIM
KV_W = N_KV_HEADS * HEAD_DIM
WINDOW = 128
BLOCK = 128
ROPE_THETA = 10000.0
PEER_HEADS = 8
N_KEYS = 128
N_EXPERTS = N_KEYS * N_KEYS
PEER_DKEY = 256
PEER_DHALF = PEER_DKEY // 2
PEER_TOPK = 16
PEER_CHUNK = 128
IN_COLS = 2 * CONV_CH + ATTN_W + 2 * KV_W + 2 * D_MODEL
EPS = 1e-6
NEG = -1e30

kernel_name = "hybrid_conformer_swa_sink_peer_block"


def rmsnorm(x, g):
    xf = x.astype(jnp.float32)
    y = xf * lax.rsqrt(jnp.mean(xf * xf, axis=-1, keepdims=True) + EPS)
    return (y * g.astype(jnp.float32)).astype(x.dtype)


def layernorm(x, g, b):
    xf = x.astype(jnp.float32)
    mu = jnp.mean(xf, axis=-1, keepdims=True)
    xc = xf - mu
    y = xc * lax.rsqrt(jnp.mean(xc * xc, axis=-1, keepdims=True) + EPS)
    return (y * g.astype(jnp.float32) + b.astype(jnp.float32)).astype(x.dtype)


def rope(t, positions):
    half = HEAD_DIM // 2
    inv = ROPE_THETA ** (-(jnp.arange(half, dtype=jnp.float32) * 2.0 / HEAD_DIM))
    ang = positions.astype(jnp.float32)[..., None] * inv
    cos = jnp.cos(ang)[:, :, None, :]
    sin = jnp.sin(ang)[:, :, None, :]
    tf = t.astype(jnp.float32)
    t1, t2 = tf[..., :half], tf[..., half:]
    out = jnp.concatenate([t1 * cos - t2 * sin, t2 * cos + t1 * sin], axis=-1)
    return out.astype(t.dtype)


def conformer_conv(val, gate, conv_w, conv_b, ln_g, ln_b, w_conv_out):
    u = val * jax.nn.sigmoid(gate)
    up = jnp.pad(u, ((0, 0), (CONV_WIDTH - 1, 0), (0, 0)))
    y = lax.conv_general_dilated(
        up, conv_w[:, None, :], window_strides=(1,), padding="VALID",
        dimension_numbers=("NWC", "WIO", "NWC"), feature_group_count=CONV_CH) + conv_b
    y = jax.nn.silu(layernorm(y, ln_g, ln_b))
    return y @ w_conv_out


def swa_sink_attention(q, k, v, sinks):
    B, S = q.shape[0], q.shape[1]
    nb = S // BLOCK
    qb = q.reshape(B, nb, BLOCK, N_KV_HEADS, GROUP, HEAD_DIM)

    def band(t):
        tp = jnp.pad(t, ((0, 0), (BLOCK, 0), (0, 0), (0, 0)))
        tb = tp.reshape(B, nb + 1, BLOCK, N_KV_HEADS, HEAD_DIM)
        return jnp.concatenate([tb[:, :-1], tb[:, 1:]], axis=2)

    kb, vb = band(k), band(v)
    s = jnp.einsum("bnqhgd,bnkhd->bnhgqk", qb, kb,
                   preferred_element_type=jnp.float32) * (HEAD_DIM ** -0.5)
    qi = jnp.arange(BLOCK)[:, None]
    kj = jnp.arange(2 * BLOCK)[None, :]
    rel = qi + BLOCK - kj
    in_window = (rel >= 0) & (rel < WINDOW)
    not_pad = (kj >= BLOCK)[None] | (jnp.arange(nb)[:, None, None] > 0)
    mask = in_window[None] & not_pad
    s = jnp.where(mask[None, :, None, None], s, NEG)
    sk = sinks.astype(jnp.float32).reshape(N_KV_HEADS, GROUP)[:, :, None, None]
    m = jnp.maximum(jnp.max(s, axis=-1, keepdims=True), sk)
    e = jnp.exp(s - m)
    p = e / (jnp.sum(e, axis=-1, keepdims=True) + jnp.exp(sk - m))
    o = jnp.einsum("bnhgqk,bnkhd->bnqhgd", p.astype(v.dtype), vb)
    return o.reshape(B, S, ATTN_W)


def peer(h, w_pq, sub_keys, u_emb, v_emb):
    B, S, D = h.shape
    q = (h @ w_pq).reshape(B, S, PEER_HEADS, 2, PEER_DHALF)
    sc = jnp.einsum("bshcd,hcnd->bshcn", q, sub_keys,
                    preferred_element_type=jnp.float32)
    s_top, i_top = lax.top_k(sc, PEER_TOPK)
    cand = s_top[..., 0, :, None] + s_top[..., 1, None, :]
    cand_idx = i_top[..., 0, :, None] * N_KEYS + i_top[..., 1, None, :]
    cand = cand.reshape(B, S, PEER_HEADS, PEER_TOPK * PEER_TOPK)
    cand_idx = cand_idx.reshape(B, S, PEER_HEADS, PEER_TOPK * PEER_TOPK)
    best, pos = lax.top_k(cand, PEER_TOPK)
    expert_idx = jnp.take_along_axis(cand_idx, pos, axis=-1)
    gate = jax.nn.softmax(best, axis=-1).astype(h.dtype)

    n_chunks = (B * S) // PEER_CHUNK
    hc = h.reshape(n_chunks, PEER_CHUNK, D)
    ic = expert_idx.reshape(n_chunks, PEER_CHUNK, PEER_HEADS * PEER_TOPK)
    gc = gate.reshape(n_chunks, PEER_CHUNK, PEER_HEADS * PEER_TOPK)

    def chunk(args):
        hx, idx, g = args
        a = jax.nn.gelu(jnp.einsum("ced,cd->ce", u_emb[idx], hx), approximate=False)
        return jnp.einsum("ce,ced->cd", g * a, v_emb[idx])

    out = lax.map(chunk, (hc, ic, gc))
    return out.reshape(B, S, D)


def setup_inputs(seed: int = 0) -> dict:
    key = jax.random.key(seed)
    ks = jax.random.split(key, 20)
    f32 = jnp.float32
    nrm = lambda k, shape, scale: jax.random.normal(k, shape, f32) * scale
    L = DEPTH
    return {
        "x": nrm(ks[0], (BATCH, SEQ, D_MODEL), 1.0),
        "positions": jnp.broadcast_to(jnp.arange(SEQ, dtype=jnp.int32)[None, :], (BATCH, SEQ)),
        "norm1_g": 1.0 + nrm(ks[1], (L, D_MODEL), 0.02),
        "w_in": nrm(ks[2], (L, D_MODEL, IN_COLS), D_MODEL ** -0.5),
        "b_in": nrm(ks[3], (L, IN_COLS), 0.01),
        "conv_w": nrm(ks[4], (L, CONV_WIDTH, CONV_CH), CONV_WIDTH ** -0.5),
        "conv_b": nrm(ks[5], (L, CONV_CH), 0.01),
        "conv_ln_g": 1.0 + nrm(ks[6], (L, CONV_CH), 0.02),
        "conv_ln_b": nrm(ks[7], (L, CONV_CH), 0.01),
        "w_conv_out": nrm(ks[8], (L, CONV_CH, D_MODEL), CONV_CH ** -0.5),
        "attn_sinks": nrm(ks[9], (L, N_Q_HEADS), 0.5),
        "w_attn_o": nrm(ks[10], (L, ATTN_W, D_MODEL), ATTN_W ** -0.5),
        "w_out": nrm(ks[11], (L, D_MODEL, D_MODEL), D_MODEL ** -0.5),
        "norm2_g": 1.0 + nrm(ks[12], (L, D_MODEL), 0.02),
        "w_peer_q": nrm(ks[13], (L, D_MODEL, PEER_HEADS * PEER_DKEY), D_MODEL ** -0.5),
        "peer_sub_keys": nrm(ks[14], (L, PEER_HEADS, 2, N_KEYS, PEER_DHALF), PEER_DHALF ** -0.5),
        "peer_u": nrm(ks[15], (L, N_EXPERTS, D_MODEL), D_MODEL ** -0.5),
        "peer_v": nrm(ks[16], (L, N_EXPERTS, D_MODEL), (PEER_HEADS * PEER_TOPK) ** -0.5),
        "final_g": 1.0 + nrm(ks[17], (D_MODEL,), 0.02),
    }


def reference(x, positions, norm1_g, w_in, b_in, conv_w, conv_b, conv_ln_g, conv_ln_b,
              w_conv_out, attn_sinks, w_attn_o, w_out, norm2_g, w_peer_q, peer_sub_keys,
              peer_u, peer_v, final_g):
    B, S, _ = x.shape
    c0 = CONV_CH
    c1 = c0 + CONV_CH
    c2 = c1 + ATTN_W
    c3 = c2 + KV_W
    c4 = c3 + KV_W
    c5 = c4 + D_MODEL
    for l in range(DEPTH):
        h = rmsnorm(x, norm1_g[l])
        z = h @ w_in[l] + b_in[l]
        glu_val, glu_gate, q, k, v, g_conv, g_attn = jnp.split(z, [c0, c1, c2, c3, c4, c5], axis=-1)

        conv_out = conformer_conv(glu_val, glu_gate, conv_w[l], conv_b[l],
                                  conv_ln_g[l], conv_ln_b[l], w_conv_out[l])

        q = rope(q.reshape(B, S, N_Q_HEADS, HEAD_DIM), positions)
        k = rope(k.reshape(B, S, N_KV_HEADS, HEAD_DIM), positions)
        v = v.reshape(B, S, N_KV_HEADS, HEAD_DIM)
        attn_out = swa_sink_attention(q, k, v, attn_sinks[l]) @ w_attn_o[l]

        merged = jax.nn.sigmoid(g_conv) * conv_out + jax.nn.sigmoid(g_attn) * attn_out
        x = x + merged @ w_out[l]

        h2 = rmsnorm(x, norm2_g[l])
        x = x + peer(h2, w_peer_q[l], peer_sub_keys[l], peer_u[l], peer_v[l])
    return rmsnorm(x, final_g)
```

```python
import numpy as np
from contextlib import ExitStack
import concourse.bass as bass
import concourse.mybir as mybir
from concourse.bass_utils import run_bass_kernel_spmd

F32 = mybir.dt.float32
BF16 = mybir.dt.bfloat16
I32 = mybir.dt.int32
U32 = mybir.dt.uint32
AF = mybir.ActivationFunctionType
ALU = mybir.AluOpType
AX = mybir.AxisListType


class Buf:
    __slots__ = ("name", "w", "r")

    def __init__(self, name=""):
        self.name = name
        self.w = None
        self.r = []


class BG(list):
    def __init__(self, name, n):
        super().__init__(Buf("%s%d" % (name, i)) for i in range(n))


def _flat(bs):
    out = []
    for b in bs:
        if isinstance(b, list):
            out.extend(_flat(b))
        else:
            out.append(b)
    return out


class _Ins:
    __slots__ = ("eng", "fn", "deps", "dma", "idx", "sig", "cnt", "semi", "final")

    def __init__(self, eng, fn, dma):
        self.eng = eng
        self.fn = fn
        self.deps = set()
        self.dma = dma
        self.sig = False
        self.cnt = 0
        self.semi = 0
        self.final = False


class Prog:
    NDMA_SEM = 12
    ENGS = ("pe", "act", "dve", "pool", "sp")

    def __init__(self, nc, es):
        self.nc = nc
        self.es = es
        self.q = {e: [] for e in self.ENGS}
        self.all = []
        self.dma_engine = "sp"
        self.halted = False

    def _add(self, ins, reads, writes):
        if self.halted:
            return ins
        reads = _flat(reads)
        writes = _flat(writes)
        for b in reads:
            if b.w is not None:
                ins.deps.add(b.w)
        for b in writes:
            if b.w is not None:
                ins.deps.add(b.w)
            for r in b.r:
                ins.deps.add(r)
        ins.deps.discard(ins)
        for b in reads:
            b.r.append(ins)
        for b in writes:
            b.w = ins
            b.r = []
        ins.idx = len(self.all)
        self.all.append(ins)
        self.q[ins.eng].append(ins)
        return ins

    def op(self, eng, fn, reads=(), writes=()):
        return self._add(_Ins(eng, fn, False), reads, writes)

    def dma(self, fn, reads=(), writes=(), final=False, eng=None):
        ins = _Ins(eng or self.dma_engine, fn, True)
        ins.final = final
        return self._add(ins, reads, writes)

    def emit(self):
        nc = self.nc
        for ins in self.all:
            for d in ins.deps:
                if d.eng == "pe" and ins.eng == "pe" and not d.dma and not ins.dma:
                    continue
                d.sig = True
            if ins.final:
                ins.sig = True
        sems = {e: self.es.enter_context(nc.semaphore("s_" + e)) for e in self.ENGS}
        dsems = [self.es.enter_context(nc.semaphore("d%d" % i)) for i in range(self.NDMA_SEM)]
        cnt = {e: 0 for e in self.ENGS}
        ndma = 0
        dma_prev = {}
        last_on_sem = [None] * self.NDMA_SEM
        for ins in self.all:
            if ins.dma:
                ins.semi = ndma % self.NDMA_SEM
                ins.cnt = 16 * (ndma // self.NDMA_SEM + 1)
                dma_prev[ins] = last_on_sem[ins.semi]
                last_on_sem[ins.semi] = ins
                ndma += 1
            elif ins.sig:
                cnt[ins.eng] += 1
                ins.cnt = cnt[ins.eng]
        finals = [i for i in self.all if i.final]
        block = self.es.enter_context(nc.Block())

        def run(engname, e):
            waited = {}

            def wait_for(d):
                if d.dma:
                    key = ("d", d.semi)
                    sem = dsems[d.semi]
                else:
                    key = ("e", d.eng)
                    sem = sems[d.eng]
                if waited.get(key, 0) >= d.cnt:
                    return
                e.wait_ge(sem, d.cnt)
                waited[key] = d.cnt

            for ins in self.q[engname]:
                for d in sorted(ins.deps, key=lambda z: z.idx):
                    if (d.eng == "pe" and engname == "pe" and not d.dma and not ins.dma):
                        continue
                    wait_for(d)
                if ins.dma:
                    p = dma_prev[ins]
                    if p is not None:
                        wait_for(p)
                h = ins.fn(e)
                if ins.dma:
                    h.then_inc(dsems[ins.semi], 16)
                elif ins.sig:
                    h.then_inc(sems[engname], 1)
            if engname == self.dma_engine:
                for f in finals:
                    wait_for(f)

        @block.sync
        def _(e):
            run("sp", e)

        @block.tensor
        def _(e):
            run("pe", e)

        @block.scalar
        def _(e):
            run("act", e)

        @block.vector
        def _(e):
            run("dve", e)

        @block.gpsimd
        def _(e):
            run("pool", e)


D = 1024
KC = 8
TT = 256
HALO = 128
EPSV = 1e-6
NMIXG = 21
NPIECE = 2 * NMIXG + 128
UV0 = 2 * NMIXG
V_G1, V_BIN, V_CB, V_LNG, V_LNB, V_G2, V_GF, V_INVF, V_SGN, V_HV, V_CW = 0, 8, 52, 60, 68, 76, 84, 92, 93, 94, 95
NV = 95 + 248
C_ID, C_PERM, C_IOTA, C_IOTA16, C_MASK0, C_MASK1 = 0, 128, 256, 384, 400, 656
NCM = 912
MAGIC = 12582912.0
CW1 = 6.28125
CW2 = 2.0 * np.pi - 6.28125


def _chunk_val(c):
    return (c // 4) * 8 + (c % 4)


def _chunk_gate(c):
    return (c // 4) * 8 + 4 + (c % 4)


class _Stop(Exception):
    pass


def build_nc(NT, dbg=False, stop=None):
    T = TT
    TOK = NT * T
    TOKH = TOK + HALO
    nc = bass.Bass("TRN2", target_bir_lowering=False)
    dx = nc.dram_tensor("xT", [8, 128, TOKH], F32, kind="ExternalInput").ap()
    dpos = nc.dram_tensor("pos", [1, TOKH], I32, kind="ExternalInput").ap()
    dwall = nc.dram_tensor("wall", [NPIECE, 128, 2048], F32, kind="ExternalInput").ap()
    dvec = nc.dram_tensor("vec", [128, NV], F32, kind="ExternalInput").ap()
    dcm = nc.dram_tensor("cm", [128, NCM], F32, kind="ExternalInput").ap()
    drow = nc.dram_tensor("rows", [1, 144], F32, kind="ExternalInput").ap()
    dsk = nc.dram_tensor("skT", [128, 2048], F32, kind="ExternalInput").ap()
    dout = nc.dram_tensor("outT", [8, 128, TOK], F32, kind="ExternalOutput").ap()
    dscr = nc.dram_tensor("wscr", [NPIECE, 128, 2048], BF16, kind="Internal").ap()
    ddiag = nc.dram_tensor("dgscr", [8, 128, 31 * 128], BF16, kind="Internal").ap()
    if dbg:
        ddbg = nc.dram_tensor("dbg", [8, 128, T], F32, kind="ExternalOutput").ap()

    es = ExitStack()
    with es:
        def sb(name, shape, dt):
            return es.enter_context(nc.sbuf_tensor("sb_" + name, shape, dt))

        P = Prog(nc, es)
        ps = es.enter_context(nc.psum_tensor("ps", [128, 8, 512], F32))
        PB = [Buf("bank%d" % i) for i in range(8)]

        vec = sb("vec", [128, NV], F32)
        cmf = sb("cmf", [128, NCM], F32)
        identb = sb("identb", [128, 128], BF16)
        permb = sb("permb", [128, 128], BF16)
        onesb = sb("onesb", [128, 128], BF16)
        onesmb = sb("onesmb", [128, 128], BF16)
        onesmf = sb("onesmf", [128, 128], F32)
        iotab = sb("iotab", [128, 128], BF16)
        maskb = sb("maskb", [128, 2, 2, 2, 128], BF16)
        esink = sb("esink", [128, 16], F32)
        bvb = sb("bvb", [128, 128], F32)
        skf = sb("skf", [128, 2048], F32)
        skb = sb("skb", [128, 16, 128], BF16)
        xt_a = sb("xt", [128, 8, T], F32)
        xt_b = sb("xt2", [128, 8, T], F32)
        xts = [xt_a, xt_b]
        sqn = sb("sqn", [128, 8, T], BF16)
        rn1 = sb("rn1", [128, T], F32)
        ubuf = sb("ubuf", [128, 2, 8, 32 + T], BF16)
        kbuf = sb("kbuf", [128, 2, 2, 128 + T], BF16)
        vdup = sb("vdup", [128, 2, 3, 2, 128], BF16)
        cosT = sb("cosT", [128, 128 + T], F32)
        sinT = sb("sinT", [128, 128 + T], F32)
        posi = sb("posi", [128, 128 + T], I32)
        r1 = sb("r1", [128, 128 + T], F32)
        r2 = sb("r2", [128, 128 + T], F32)
        r3 = sb("r3", [128, 128 + T], F32)
        st1 = sb("st1", [128, 128 + T], F32)
        st2 = sb("st2", [128, 128 + T], F32)
        st3 = sb("st3", [128, 128 + T], F32)
        sg1 = sb("sg1", [128, 128 + T], F32)
        sg2 = sb("sg2", [128, 128 + T], F32)
        m1buf = sb("m1buf", [128, 4, T], F32)
        qb = sb("qb", [128, 128 + T], BF16)
        qb2 = sb("qb2", [128, 128 + T], BF16)
        r4 = sb("r4", [128, 128 + T], F32)
        eT = sb("eT", [128, 2, 2, 4, 128], BF16)
        den = sb("den", [128, 512], F32)
        rden = sb("rden", [128, 512], F32)
        h2T = sb("h2T", [128, 8, T], BF16)
        gl = sb("gl", [128, 2, T], BF16)
        GA = sb("GA", [128, 2, T], BF16)
        Pt = sb("Pt", [128, 2, 8, 128], BF16)
        Qt = sb("Qt", [128, 2, 8, 128], BF16)
        UTs = sb("UTs", [128, 2, 8, 256], BF16)
        Vs = sb("Vs", [128, 2, 2, 1024], BF16)
        v16 = sb("v16", [128, 16, 16], F32)
        i16 = sb("i16", [128, 16, 16], U32)
        i16f = sb("i16f", [128, 16, 16], F32)
        best = sb("best", [128, 8, 16], F32)
        posu = sb("posu", [128, 8, 16], U32)
        k1u = sb("k1u", [128, 8, 16], U32)
        posf = sb("posf", [128, 8, 16], F32)
        k1f = sb("k1f", [128, 8, 16], F32)
        k2f = sb("k2f", [128, 8, 16], F32)
        ebuf = sb("ebuf", [128, 8, 16], F32)
        gate = sb("gate", [128, 8, 16], F32)
        Zs = sb("Zs", [128, 8], F32)
        av = sb("av", [128, 8, 16], F32)
        bvv = sb("bvv", [128, 8, 16], F32)
        abgT = sb("abgT", [128, 2, 3, 128], F32)
        one1 = sb("one1", [128, 4], F32)
        arena = sb("arena", [128, 32768], BF16)

        def av_(off, nbytes, dt):
            a = arena[:, off // 2:(off + nbytes) // 2]
            return a if dt == BF16 else a.bitcast(dt)

        K = 1024
        wbuf = [av_(0, 8 * K, BF16).rearrange("p (k c) -> p k c", k=8),
                av_(8 * K, 8 * K, BF16).rearrange("p (k c) -> p k c", k=8)]
        diag = [av_(16 * K, 7936, BF16).rearrange("p (j c) -> p j c", j=31),
                av_(24 * K, 7936, BF16).rearrange("p (j c) -> p j c", j=31)]
        hT = av_(32 * K, 6 * K, BF16).rearrange("p (k c) -> p k c", k=8)
        sqb = av_(38 * K, 6 * K, BF16).rearrange("p (k c) -> p k c", k=8)
        ysb = av_(44 * K, 8 * K, F32).rearrange("p (k c) -> p k c", k=8)
        sT = av_(52 * K, 4 * K, BF16).rearrange("p (k c) -> p k c", k=8)
        qrope = av_(56 * K, 4 * K, BF16).rearrange("p (k c) -> p k c", k=8)
        attnT = av_(60 * K, 4 * K, BF16).rearrange("p (k c) -> p k c", k=8)
        mergedT = sqb
        stin = [av_(i * 8 * K, 8 * K, F32) for i in range(4)]
        stout = [av_(32 * K + i * 4 * K, 4 * K, BF16) for i in range(4)]
        dgst = [av_(48 * K, 7936, BF16).rearrange("p (j c) -> p j c", j=31),
                av_(56 * K, 7936, BF16).rearrange("p (j c) -> p j c", j=31)]
        qTb = av_(16 * K, 8 * K, BF16).rearrange("p (k c) -> p k c", k=16)
        cand = av_(24 * K, 8 * K, F32).rearrange("p (h a b) -> p h a b", h=8, a=16)
        work2 = av_(32 * K, 8 * K, F32).rearrange("p (h c) -> p h c", h=8)
        E1 = av_(40 * K, 8 * K, F32).rearrange("p (h a b) -> p h a b", h=8, a=16)
        scw = av_(48 * K, 2 * K, F32).rearrange("p (l c) -> p l c", l=4)
        G = arena[:, :].rearrange("p (i t) -> p i t", i=128)
        outtmp = av_(0, 8 * K, F32).rearrange("p (k c) -> p k c", k=8)

        ATOK = Buf("atok")

        def vcol(i):
            return vec[:, i:i + 1]

        def OP(eng, fn, reads, writes, arena_use=False):
            if arena_use:
                reads = list(reads) + [ATOK]
            return P.op(eng, fn, reads, writes)

        def DMA(fn, reads, writes, arena_use=False, final=False, eng=None):
            if arena_use:
                reads = list(reads) + [ATOK]
            return P.dma(fn, reads, writes, final=final, eng=eng)

        def barrier():
            P.op("pool", lambda e: e.memset(one1[:, 0:1], 0.0), [], [ATOK])

        def MM(out, lhsT, rhs, start, stop, reads, writes, au=True, sgc=False):
            OP("pe", lambda e: e.matmul(out, lhsT=lhsT, rhs=rhs, start=start, stop=stop,
                                        skip_group_check=sgc), reads, writes, au)

        def ACT(out, in_, func, reads, writes, bias=None, scale=None, au=True):
            kw = {}
            if bias is not None:
                kw["bias"] = bias
            if scale is not None:
                kw["scale"] = scale
            OP("act", lambda e: e.activation(out=out, in_=in_, func=func, **kw), reads, writes, au)

        def TTo(out, in0, in1, op, reads, writes, eng="dve", au=True):
            OP(eng, lambda e: e.tensor_tensor(out=out, in0=in0, in1=in1, op=op), reads, writes, au)

        def TS(out, in0, s1, op0, reads, writes, s2=None, op1=None, eng="dve", au=True):
            if op1 is None:
                OP(eng, lambda e: e.tensor_scalar(out=out, in0=in0, scalar1=s1, scalar2=None, op0=op0),
                   reads, writes, au)
            else:
                OP(eng, lambda e: e.tensor_scalar(out=out, in0=in0, scalar1=s1, scalar2=s2, op0=op0, op1=op1),
                   reads, writes, au)

        def STT(out, in0, scalar, in1, op0, op1, reads, writes, au=True):
            OP("dve", lambda e: e.scalar_tensor_tensor(out=out, in0=in0, scalar=scalar, in1=in1,
                                                       op0=op0, op1=op1), reads, writes, au)

        def CP(out, in_, reads, writes, eng="dve", au=True):
            OP(eng, lambda e: e.tensor_copy(out=out, in_=in_), reads, writes, au)

        def RECIP(out, in_, reads, writes, au=True):
            OP("dve", lambda e: e.reciprocal(out=out, in_=in_), reads, writes, au)

        Bvec, Bcm, Bconst, Bsk = Buf("vec"), Buf("cm"), Buf("const"), Buf("sk")
        Bxs = [BG("xta", 8), BG("xtb", 8)]
        Bsqn, Brn1 = BG("sqn", 8), Buf("rn1")
        Bscr = [Buf("scr%d" % i) for i in range(NPIECE)]
        Bst_in = BG("sti", 4)
        Bst_out = BG("sto", 4)
        Bw = [Buf("w0"), Buf("w1")]
        Bdiag = [Buf("dg0"), Buf("dg1")]
        BhT, Bsq, Bys, BsT, Bqr, Bat = BG("hT", 8), BG("sq", 8), BG("ys", 8), BG("sT", 8), BG("qr", 8), BG("at", 8)
        Bub = [BG("ub0_", 8), BG("ub1_", 8)]
        Bkb = [Buf("kb0"), Buf("kb1")]
        Bvd = [[Buf("vd%d%d" % (p, b)) for b in range(3)] for p in range(2)]
        Bcs, Bposi, Br1, Br2, Br3 = Buf("cs"), Buf("posi"), Buf("r1"), Buf("r2"), Buf("r3")
        Bs1, Bs2, Bs3, Bg1, Bg2, Bm1, Bqb = Buf("st1"), Buf("st2"), Buf("st3"), Buf("sg1"), Buf("sg2"), Buf("m1"), Buf("qb")
        BeT = [Buf("eT0"), Buf("eT1")]
        Bden, Brden = Buf("den"), Buf("rden")
        Bh2, Bgl, BGA = Buf("h2T"), [Buf("gl0"), Buf("gl1")], [Buf("GA0"), Buf("GA1")]
        BPt, BQt = [BG("Pt0_", 8), BG("Pt1_", 8)], [BG("Qt0_", 8), BG("Qt1_", 8)]
        BUT, BVs = [Buf("UT0"), Buf("UT1")], [Buf("Vs0"), Buf("Vs1")]
        Bv16, Bi16, Bi16f, Bbest, Bpos, Btk, Babg = BG("v16_", 16), BG("i16_", 16), Buf("i16f"), BG("best", 8), BG("pos", 8), Buf("tk"), Buf("abg")
        BqT, Bcand, Bw2, BE1, Bscw, BGm, Bot = BG("qTb", 16), Buf("cand"), BG("w2_", 8), Buf("E1"), BG("scw", 4), BG("G", 64), Buf("ot")
        Bsgs = BG("sgs", 2)
        Bra, Brb, Bqbs = BG("ra", 2), BG("rb", 2), BG("qbs", 2)
        Bout = Buf("out")

        DMA(lambda e: e.dma_start(out=vec[:], in_=dvec[:, :]), [], [Bvec])
        DMA(lambda e: e.dma_start(out=cmf[:], in_=dcm[:, :]), [], [Bcm])
        DMA(lambda e: e.dma_start(out=skf[:], in_=dsk[:, :]), [], [Bsk])
        DMA(lambda e: e.dma_start(out=bvb[:], in_=drow[0:1, 0:128].partition_broadcast(128)[:, 0, :]), [], [Bconst])
        DMA(lambda e: e.dma_start(out=esink[:], in_=drow[0:1, 128:144].partition_broadcast(128)[:, 0, :]), [], [Bconst])
        CP(identb[:], cmf[:, C_ID:C_ID + 128], [Bcm], [Bconst], au=False)
        CP(permb[:], cmf[:, C_PERM:C_PERM + 128], [Bcm], [Bconst], au=False)
        CP(iotab[:], cmf[:, C_IOTA:C_IOTA + 128], [Bcm], [Bconst], au=False)
        OP("dve", lambda e: e.memset(onesb[:], 1.0), [], [Bconst])
        OP("dve", lambda e: e.memset(onesmb[:], 1.0 / 1024.0), [], [Bconst])
        OP("dve", lambda e: e.memset(onesmf[:], 1.0 / 1024.0), [], [Bconst])
        for var in range(2):
            for kb in range(2):
                for h4 in range(2):
                    c0 = (C_MASK0 if var == 0 else C_MASK1) + kb * 128
                    CP(maskb[:, var, kb, h4, :], cmf[:, c0:c0 + 128], [Bcm], [Bconst], au=False)
        ACT(esink[:], esink[:], AF.Exp, [Bconst], [Bconst], au=False)
        CP(skb[:].rearrange("p a b -> p (a b)"), skf[:], [Bsk], [Bsk], au=False)

        cast_engs = ["act", "dve", "pool"]
        NPR = NPIECE if stop != 1 else 0

        def pl_load(i):
            s = i % 4
            DMA(lambda e: e.dma_start(out=stin[s], in_=dwall[i]), [], [Bst_in[s]], True)

        def pl_cast_store(i):
            s = i % 4
            ce = cast_engs[i % 3]
            if ce == "act":
                ACT(stout[s], stin[s], AF.Copy, [Bst_in[s]], [Bst_out[s]])
            else:
                CP(stout[s], stin[s], [Bst_in[s]], [Bst_out[s]], eng=ce)
            DMA(lambda e: e.dma_start(out=dscr[i], in_=stout[s]), [Bst_out[s]], [Bscr[i]], True, eng="act")

        for i in range(min(3, NPR)):
            pl_load(i)
        for i in range(NPR):
            if i + 3 < NPR:
                pl_load(i + 3)
            pl_cast_store(i)

        Bdg = [BG("dgs0_", 31), BG("dgs1_", 31)]
        Bdgd = BG("dgd", 8)
        for c in range(8 if stop != 1 else 0):
            s = c % 2
            for jt in range(31):
                ACT(dgst[s][:, jt, :], cmf[:, C_ID:C_ID + 128], AF.Copy, [Bcm, Bvec], [Bdg[s][jt]],
                    scale=vcol(V_CW + c * 31 + jt))
            DMA(lambda e, c=c, s=s: e.dma_start(out=ddiag[c], in_=dgst[s][:, :, :].rearrange("p j c -> p (j c)")),
                [Bdg[s]], [Bdgd[c]], True)

        wslot = [0]

        def loadw(g):
            s = wslot[0]
            wslot[0] ^= 1
            DMA(lambda e: e.dma_start(out=wbuf[s].rearrange("p (r k) c -> p r (k c)", r=2),
                                      in_=dscr[2 * g:2 * g + 2].rearrange("r p f -> p r f")),
                [Bscr[2 * g], Bscr[2 * g + 1]], [Bw[s]], True)
            return s

        bankrr = [0]

        def nextbank():
            b = bankrr[0]
            bankrr[0] = (b + 1) % 4
            return b

        def proj(bank, s, j, rhs3, c0, n, rbuf):
            for k in range(8):
                MM(ps[:, bank, 0:n], wbuf[s][:, k, j * 128:(j + 1) * 128], rhs3[:, k, c0:c0 + n],
                   k == 0, k == 7, [Bw[s], rbuf], [PB[bank]])

        def colstats(src3, c0, n, srcbuf, outrr, outbuf, tmp, tmpbuf, sqv, sqbuf, bank, sq_au=True):
            for k in range(8):
                ACT(sqv[:, k, c0:c0 + n], src3[:, k, c0:c0 + n], AF.Square, [srcbuf[k]], [sqbuf[k]], au=sq_au)
            for k in range(8):
                MM(ps[:, bank, 0:n], onesmb[:], sqv[:, k, c0:c0 + n], k == 0, k == 7, [Bconst, sqbuf[k]], [PB[bank]], au=sq_au)
            TS(tmp[:, c0:c0 + n], ps[:, bank, 0:n], EPSV, ALU.add, [PB[bank]], [tmpbuf], au=False)
            ACT(tmp[:, c0:c0 + n], tmp[:, c0:c0 + n], AF.Sqrt, [tmpbuf], [tmpbuf], au=False)
            RECIP(outrr[:, c0:c0 + n], tmp[:, c0:c0 + n], [tmpbuf], [outbuf], au=False)

        def rope_tables(c0, n, colbase):
            DMA(lambda e: e.dma_start(out=posi[:, c0:c0 + n],
                                      in_=dpos[0:1, colbase:colbase + n].partition_broadcast(128)[:, 0, :]),
                [], [Bposi])
            sl = slice(c0, c0 + n)
            CP(r1[:, sl], posi[:, sl], [Bposi], [Br1], au=False)
            TS(r1[:, sl], r1[:, sl], vcol(V_INVF), ALU.mult, [Br1, Bvec], [Br1], au=False)
            for (dst, shift, useSgn) in ((sinT, 0.0, True), (cosT, float(np.pi / 2), False)):
                TS(r2[:, sl], r1[:, sl], shift, ALU.add, [Br1], [Br2], au=False)
                TS(r3[:, sl], r2[:, sl], float(1.0 / (2 * np.pi)), ALU.mult, [Br2], [Br3], s2=MAGIC, op1=ALU.add, au=False)
                TS(r3[:, sl], r3[:, sl], MAGIC, ALU.subtract, [Br3], [Br3], au=False)
                STT(r2[:, sl], r3[:, sl], -CW1, r2[:, sl], ALU.mult, ALU.add, [Br3, Br2], [Br2], au=False)
                STT(r2[:, sl], r3[:, sl], -CW2, r2[:, sl], ALU.mult, ALU.add, [Br3, Br2], [Br2], au=False)
                TS(r2[:, sl], r2[:, sl], 3.1415925, ALU.min, [Br2], [Br2], s2=-3.1415925, op1=ALU.max, au=False)
                if useSgn:
                    ACT(dst[:, sl], r2[:, sl], AF.Sin, [Br2, Bvec], [Bcs], scale=vcol(V_SGN), au=False)
                else:
                    ACT(dst[:, sl], r2[:, sl], AF.Sin, [Br2], [Bcs], au=False)

        def ckpt(i):
            if stop == i:
                P.halted = True

        if stop in (1, 2):
            P.halted = True
        for ti in range(NT):
            par = ti % 2
            c0 = 0 if ti == 0 else 128
            n = 128 + T - c0
            xcol = HALO + ti * T
            barrier()
            xt = xts[par]
            Bx = Bxs[par]

            def prefetch(tn):
                pn = tn % 2
                xc = HALO + tn * T
                cc0 = 0 if tn == 0 else 128
                DMA(lambda e: e.dma_start(out=xts[pn][:], in_=dx[:, :, xc:xc + T].rearrange("k p t -> p k t")),
                    [], [Bxs[pn]])
                rope_tables(cc0, 128 + T - cc0, xc - 128 + cc0)
                colstats(xts[pn], 0, T, Bxs[pn], rn1, Brn1, st2, Bs2, sqn, Bsqn, 6, sq_au=False)

            if ti == 0:
                prefetch(0)
            for k in range(8):
                STT(hT[:, k, 128:128 + T], xt[:, k, :], vcol(V_G1 + k), rn1[:, 0:T], ALU.mult, ALU.mult,
                    [Bx[k], Bvec, Brn1], [BhT[k]])
            if ti == 0:
                DMA(lambda e: e.dma_start(out=ysb[:, :, 0:128], in_=dx[:, :, 0:128].rearrange("k p t -> p k t")),
                    [], [Bys], True)
                colstats(ysb, 0, 128, Bys, st3, Bs3, st2, Bs2, sqb, Bsq, 4)
                for k in range(8):
                    STT(hT[:, k, 0:128], ysb[:, k, 0:128], vcol(V_G1 + k), st3[:, 0:128], ALU.mult, ALU.mult,
                        [Bys[k], Bvec, Bs3], [BhT[k]])
            else:
                for c in range(8):
                    CP(ubuf[:, par, c, 0:32], ubuf[:, 1 - par, c, T:T + 32], [Bub[1 - par][c]], [Bub[par][c]], eng="pool", au=False)
                for g in range(2):
                    CP(kbuf[:, par, g, 0:128], kbuf[:, 1 - par, g, T:T + 128], [Bkb[1 - par]], [Bkb[par]], eng="pool", au=False)
                CP(vdup[:, par, 0, :, :], vdup[:, 1 - par, 2, :, :], [Bvd[1 - par][2]], [Bvd[par][0]], eng="pool", au=False)

            ckpt(3)
            cu0 = 96 if ti == 0 else 128
            nu = 128 + T - cu0
            wsl = {}
            sgl = [sg1, sg2]

            def glu_proj(c):
                pr, j = c // 4, c % 4
                if j == 0:
                    wsl[pr] = (loadw(2 * pr), loadw(2 * pr + 1))
                sv, sgt = wsl[pr]
                bA = nextbank()
                proj(bA, sv, j, hT, cu0, nu, BhT)
                bB = nextbank()
                proj(bB, sgt, j, hT, cu0, nu, BhT)
                sg = sgl[c % 2]
                ACT(sg[:, 0:nu], ps[:, bB, 0:nu], AF.Sigmoid, [PB[bB], Bvec], [Bsgs[c % 2]],
                    bias=vcol(V_BIN + _chunk_gate(c)), au=False)
                STT(ubuf[:, par, c, cu0 - 96:cu0 - 96 + nu], ps[:, bA, 0:nu], vcol(V_BIN + _chunk_val(c)),
                    sg[:, 0:nu], ALU.add, ALU.mult, [PB[bA], Bvec, Bsgs[c % 2]], [Bub[par][c]], au=False)
                if ti == 0:
                    TS(ubuf[:, par, c, 0:32], ubuf[:, par, c, 0:32], vcol(V_HV), ALU.mult,
                       [Bub[par][c], Bvec], [Bub[par][c]], au=False)
                ds_ = c % 2
                DMA(lambda e: e.dma_start(out=diag[ds_][:, :, :].rearrange("p j c -> p (j c)"), in_=ddiag[c]),
                    [Bdgd[c]], [Bdiag[ds_]], True)

            def conv(c):
                ds_ = c % 2
                bC = nextbank()
                for jt in range(31):
                    MM(ps[:, bC, 0:T], diag[ds_][:, jt, :], ubuf[:, par, c, 2 + jt:2 + jt + T],
                       jt == 0, jt == 30, [Bdiag[ds_], Bub[par][c]], [PB[bC]])
                ACT(ysb[:, c, :], ps[:, bC, 0:T], AF.Identity, [PB[bC], Bvec], [Bys[c]], bias=vcol(V_CB + c))
                ACT(sqb[:, c, 0:T], ps[:, bC, 0:T], AF.Square, [PB[bC], Bvec], [Bsq[c]], bias=vcol(V_CB + c))

            glu_proj(0)
            for c in range(8):
                if c + 1 < 8:
                    glu_proj(c + 1)
                conv(c)
            for c in range(8):
                MM(ps[:, 4, 0:T], onesmf[:], ysb[:, c, :], c == 0, c == 7, [Bconst, Bys[c]], [PB[4]])
            for c in range(8):
                MM(ps[:, 5, 0:T], onesmb[:], sqb[:, c, 0:T], c == 0, c == 7, [Bconst, Bsq[c]], [PB[5]])
            CP(st1[:, 0:T], ps[:, 4, 0:T], [PB[4]], [Bs1], au=False)
            TTo(st2[:, 0:T], st1[:, 0:T], st1[:, 0:T], ALU.mult, [Bs1], [Bs2], au=False)
            TTo(st2[:, 0:T], ps[:, 5, 0:T], st2[:, 0:T], ALU.subtract, [PB[5], Bs2], [Bs2], au=False)
            TS(st2[:, 0:T], st2[:, 0:T], EPSV, ALU.add, [Bs2], [Bs2], au=False)
            ACT(st2[:, 0:T], st2[:, 0:T], AF.Sqrt, [Bs2], [Bs2], au=False)
            RECIP(st3[:, 0:T], st2[:, 0:T], [Bs2], [Bs3], au=False)
            STT(st1[:, 0:T], st1[:, 0:T], -1.0, st3[:, 0:T], ALU.mult, ALU.mult, [Bs1, Bs3], [Bs1], au=False)
            for c in range(8):
                ra_, rb_ = (r1, r2) if c % 2 == 0 else (r3, r4)
                Ba_, Bb_ = ([Br1, Bra[0]], [Br2, Brb[0]]) if c % 2 == 0 else ([Br3, Bra[1]], [Brb[1]])
                TTo(ra_[:, 0:T], ysb[:, c, :], st3[:, 0:T], ALU.mult, [Bys[c], Bs3], Ba_)
                TTo(rb_[:, 0:T], ra_[:, 0:T], st1[:, 0:T], ALU.add, Ba_ + [Bs1], Bb_, au=False)
                ACT(sT[:, c, :], rb_[:, 0:T], AF.Silu, Bb_ + [Bvec], [BsT[c]], bias=vcol(V_LNB + c), scale=vcol(V_LNG + c))

            ckpt(4)
            rcnt = [0]

            def rope_chunk(bank, outap, cc0, nn, bcol, outbuf):
                i_ = rcnt[0] % 2
                rcnt[0] += 1
                qb_ = (qb, qb2)[i_]
                ra_, rb_ = ((r1, r2), (r3, r4))[i_]
                Bq_ = [Bqb, Bqbs[0]] if i_ == 0 else [Bqbs[1]]
                Ba_ = [Br1, Bra[0]] if i_ == 0 else [Br3, Bra[1]]
                Bb_ = [Br2, Brb[0]] if i_ == 0 else [Brb[1]]
                ACT(qb_[:, cc0:cc0 + nn], ps[:, bank, 0:nn], AF.Identity, [PB[bank], Bvec], Bq_, bias=vcol(bcol), au=False)
                b2 = nextbank()
                MM(ps[:, b2, 0:nn], permb[:], qb_[:, cc0:cc0 + nn], True, True, [Bconst] + Bq_, [PB[b2]], au=False)
                TTo(ra_[:, cc0:cc0 + nn], qb_[:, cc0:cc0 + nn], cosT[:, cc0:cc0 + nn], ALU.mult, Bq_ + [Bcs], Ba_, au=False)
                TTo(rb_[:, cc0:cc0 + nn], ps[:, b2, 0:nn], sinT[:, cc0:cc0 + nn], ALU.mult, [PB[b2], Bcs], Bb_, au=False)
                TTo(outap, ra_[:, cc0:cc0 + nn], rb_[:, cc0:cc0 + nn], ALU.add, Ba_ + Bb_, [outbuf], au=True)

            for qg in range(2):
                s = loadw(4 + qg)
                for j in range(4):
                    cq = qg * 4 + j
                    b = nextbank()
                    proj(b, s, j, hT, 128, T, BhT)
                    rope_chunk(b, qrope[:, cq, :], 128, T, V_BIN + 16 + cq, Bqr[cq])
            s = loadw(6)
            for g in range(2):
                b = nextbank()
                proj(b, s, g, hT, c0, n, BhT)
                rope_chunk(b, kbuf[:, par, g, c0:c0 + n], c0, n, V_BIN + 24 + g, Bkb[par])
            for blk in range(3):
                if ti > 0 and blk == 0:
                    continue
                b = nextbank()
                for k in range(8):
                    MM(ps[:, b, 0:128], hT[:, k, blk * 128:(blk + 1) * 128], wbuf[s][:, k, 256:384],
                       k == 0, k == 7, [BhT, Bw[s]], [PB[b]])
                for dup in range(2):
                    TTo(vdup[:, par, blk, :, dup * 64:(dup + 1) * 64],
                        ps[:, b, 0:128].rearrange("p (g d) -> p g d", g=2),
                        bvb[:].rearrange("p (g d) -> p g d", g=2), ALU.add, [PB[b], Bconst], [Bvd[par][blk]], au=False)

            ckpt(5)
            iters = [(b, g, hg) for b in range(2) for g in range(2) for hg in range(2)]

            def att_S(i):
                b, g, hg = iters[i]
                sb0 = 6 if i % 2 == 0 else 2
                var = 0 if (ti == 0 and b == 0) else 1
                j0 = (8 * g + 4 * hg) // 2
                for half in range(2):
                    MM(ps[:, sb0 + half, :], identb[:], maskb[:, var, :, :, :].rearrange("p k a q -> p (k a q)"),
                       True, False, [Bconst], [PB[sb0 + half]], au=False, sgc=True)
                for half in range(2):
                    pa = slice(half * 64, (half + 1) * 64)
                    for kb in range(2):
                        for a in range(2):
                            MM(ps[:, sb0 + half, (kb * 2 + a) * 128:(kb * 2 + a + 1) * 128],
                               kbuf[pa, par, g, (b + kb) * 128:(b + kb + 1) * 128],
                               qrope[pa, j0 + a, b * 128:(b + 1) * 128],
                               False, True, [Bkb[par], Bqr[j0 + a]], [PB[sb0 + half]], sgc=True)

            def att_rest(i):
                b, g, hg = iters[i]
                sl_ = i % 2
                sb0 = 6 if i % 2 == 0 else 2
                ob, db = (4, 5) if i % 2 == 0 else (0, 1)
                h0 = 8 * g + 4 * hg
                j0 = h0 // 2
                ACT(eT[:, sl_, :, :, :].rearrange("p k h q -> p (k h q)"),
                    ps[:, sb0:sb0 + 2, :].rearrange("p k c -> p (k c)"), AF.Exp, [PB[sb0], PB[sb0 + 1]], [BeT[sl_]],
                    scale=0.125, au=False)
                eTv = eT[:, sl_, :, :, :].rearrange("p half (kb a) q -> p half kb a q", kb=2)
                for kb in range(2):
                    for half in range(2):
                        MM(ps[:, ob, half * 256:(half + 1) * 256], vdup[:, par, b + kb, g, :],
                           eTv[:, half, kb, :, :].rearrange("p a q -> p (a q)"),
                           (kb == 0 and half == 0), kb == 1, [Bvd[par][b + kb], BeT[sl_]], [PB[ob]], au=False, sgc=True)
                for kb in range(2):
                    for half in range(2):
                        MM(ps[:, db, half * 256:(half + 1) * 256], onesb[:],
                           eTv[:, half, kb, :, :].rearrange("p a q -> p (a q)"),
                           (kb == 0 and half == 0), kb == 1, [Bconst, BeT[sl_]], [PB[db]], au=False, sgc=True)
                TTo(den[:].rearrange("p (half a q) -> p half a q", half=2, a=2),
                    ps[:, db, :].rearrange("p (half a q) -> p half a q", half=2, a=2),
                    esink[:, h0:h0 + 4].rearrange("p (a half) -> p half a", half=2).unsqueeze(3).to_broadcast([128, 2, 2, 128]),
                    ALU.add, [PB[db], Bconst], [Bden], au=False)
                RECIP(rden[:], den[:], [Bden], [Brden], au=False)
                for half in range(2):
                    pa = slice(half * 64, (half + 1) * 64)
                    TTo(attnT[pa, j0:j0 + 2, b * 128:(b + 1) * 128],
                        ps[pa, ob, half * 256:(half + 1) * 256].rearrange("p (a q) -> p a q", a=2),
                        rden[pa, half * 256:(half + 1) * 256].rearrange("p (a q) -> p a q", a=2),
                        ALU.mult, [PB[ob], Brden], [Bat[j0], Bat[j0 + 1]])

            att_S(0)
            for i in range(8):
                if i + 1 < 8:
                    att_S(i + 1)
                att_rest(i)

            ckpt(6)
            Bm1s = BG("m1s", 4)
            for jg in range(2):
                sa = loadw(11 + jg)
                sb_ = loadw(7 + jg)
                for j in range(4):
                    c = jg * 4 + j
                    bA = nextbank()
                    proj(bA, sa, j, sT, 0, T, BsT)
                    bB = nextbank()
                    proj(bB, sb_, j, hT, 128, T, BhT)
                    sg = sgl[j % 2]
                    ACT(sg[:, 0:T], ps[:, bB, 0:T], AF.Sigmoid, [PB[bB], Bvec], [Bsgs[j % 2]], bias=vcol(V_BIN + 28 + c), au=False)
                    TTo(m1buf[:, j, :], ps[:, bA, 0:T], sg[:, 0:T], ALU.mult, [PB[bA], Bsgs[j % 2]], [Bm1s[j]], au=False)
                sa = loadw(13 + jg)
                sb_ = loadw(9 + jg)
                for j in range(4):
                    c = jg * 4 + j
                    bA = nextbank()
                    proj(bA, sa, j, attnT, 0, T, Bat)
                    bB = nextbank()
                    proj(bB, sb_, j, hT, 128, T, BhT)
                    sg = sgl[j % 2]
                    rt_ = (r3, r4)[j % 2]
                    Brt_ = [Br3, Bra[1]] if j % 2 == 0 else [Brb[1]]
                    ACT(sg[:, 0:T], ps[:, bB, 0:T], AF.Sigmoid, [PB[bB], Bvec], [Bsgs[j % 2]], bias=vcol(V_BIN + 36 + c), au=False)
                    TTo(rt_[:, 0:T], ps[:, bA, 0:T], sg[:, 0:T], ALU.mult, [PB[bA], Bsgs[j % 2]], Brt_, au=False)
                    TTo(mergedT[:, c, 0:T], rt_[:, 0:T], m1buf[:, j, :], ALU.add, Brt_ + [Bm1s[j]], [Bsq[c]])
            for og in range(2):
                s = loadw(15 + og)
                for j in range(4):
                    c = og * 4 + j
                    b = nextbank()
                    proj(b, s, j, mergedT, 0, T, Bsq)
                    TTo(xt[:, c, :], xt[:, c, :], ps[:, b, 0:T], ALU.add, [Bx[c], PB[b]], [Bx[c]], au=False)
            if dbg and ti == 0:
                DMA(lambda e, xt_=xt: e.dma_start(out=ddbg.rearrange("k p t -> p k t"), in_=xt_[:]), [Bx], [Buf("dbgo")], final=True)

            ckpt(7)
            colstats(xt, 0, T, Bx, st1, Bs1, st2, Bs2, sqb, Bsq, 4)
            for k in range(8):
                STT(h2T[:, k, :], xt[:, k, :], vcol(V_G2 + k), st1[:, 0:T], ALU.mult, ALU.mult, [Bx[k], Bvec, Bs1], [Bh2], au=False)
            barrier()
            for g4 in range(4):
                s = loadw(17 + g4)
                for j in range(4):
                    hc = g4 * 4 + j
                    b = nextbank()
                    for k in range(8):
                        MM(ps[:, b, 0:T], wbuf[s][:, k, j * 128:(j + 1) * 128], h2T[:, k, :], k == 0, k == 7,
                           [Bw[s], Bh2], [PB[b]])
                    if hc % 2 == 0:
                        ACT(qTb[:, hc, :], ps[:, b, 0:T], AF.Copy, [PB[b]], [BqT[hc]])
                    else:
                        CP(qTb[:, hc, :], ps[:, b, 0:T], [PB[b]], [BqT[hc]])
            v16v = v16[:, :, :].rearrange("p (h c) k -> p h c k", c=2)
            i16fv = i16f[:, :, :].rearrange("p (h c) k -> p h c k", c=2)
            B4 = [128, 8, 16, 16]
            for tc in range(2):
                tcs = slice(tc * 128, (tc + 1) * 128)
                for g4 in range(4):
                    for l in range(4):
                        hc = g4 * 4 + l
                        MM(ps[:, 5, l * 128:(l + 1) * 128], qTb[:, hc, tcs], skb[:, hc, :], True, True,
                           [BqT[hc], Bsk], [PB[5]])
                    def L1(step, l):
                        hc = g4 * 4 + l
                        src_ = ps[:, 5, l * 128:(l + 1) * 128]
                        if step == 0:
                            OP("dve", lambda e: e.max(out=v16[:, hc, 0:8], in_=src_), [PB[5]], [Bv16[hc]])
                        elif step == 1:
                            OP("dve", lambda e: e.max_index(out=i16[:, hc, 0:8], in_max=v16[:, hc, 0:8], in_values=src_),
                               [PB[5], Bv16[hc]], [Bi16[hc]])
                        elif step == 2:
                            OP("dve", lambda e: e.match_replace(out=scw[:, l, :], in_to_replace=v16[:, hc, 0:8],
                                                               in_values=src_, imm_value=-1e30),
                               [PB[5], Bv16[hc]], [Bscw[l]], True)
                        elif step == 3:
                            OP("dve", lambda e: e.max(out=v16[:, hc, 8:16], in_=scw[:, l, :]), [Bscw[l]], [Bv16[hc]], True)
                        else:
                            OP("dve", lambda e: e.max_index(out=i16[:, hc, 8:16], in_max=v16[:, hc, 8:16], in_values=scw[:, l, :]),
                               [Bscw[l], Bv16[hc]], [Bi16[hc]], True)
                    for step in (0, 2, 1, 3, 4):
                        for l in range(4):
                            L1(step, l)
                CP(i16f[:], i16[:], [Bi16], [Bi16f], au=False)
                TTo(cand[:], v16v[:, :, 0, :].unsqueeze(3).to_broadcast(B4), v16v[:, :, 1, :].unsqueeze(2).to_broadcast(B4),
                    ALU.add, [Bv16], [Bcand])
                def L2(step, h):
                    src_ = cand[:, h, :, :].rearrange("p a b -> p (a b)")
                    if step == 0:
                        OP("dve", lambda e: e.max(out=best[:, h, 0:8], in_=src_), [Bcand], [Bbest[h]], True)
                    elif step == 1:
                        OP("dve", lambda e: e.max_index(out=posu[:, h, 0:8], in_max=best[:, h, 0:8], in_values=src_),
                           [Bcand, Bbest[h]], [Bpos[h]], True)
                    elif step == 2:
                        OP("dve", lambda e: e.match_replace(out=work2[:, h, :], in_to_replace=best[:, h, 0:8],
                                                           in_values=src_, imm_value=-1e30), [Bcand, Bbest[h]], [Bw2[h]], True)
                    elif step == 3:
                        OP("dve", lambda e: e.max(out=best[:, h, 8:16], in_=work2[:, h, :]), [Bw2[h]], [Bbest[h]], True)
                    else:
                        OP("dve", lambda e: e.max_index(out=posu[:, h, 8:16], in_max=best[:, h, 8:16], in_values=work2[:, h, :]),
                           [Bw2[h], Bbest[h]], [Bpos[h]], True)
                for step in (0, 2, 1, 3, 4):
                    for h in range(8):
                        L2(step, h)
                CP(posf[:], posu[:], [Bpos], [Btk], au=False)
                OP("dve", lambda e: e.tensor_single_scalar(out=k1u[:], in_=posu[:], scalar=4, op=ALU.logical_shift_right),
                   [Bpos], [Btk])
                CP(k1f[:], k1u[:], [Btk], [Btk], au=False)
                STT(k2f[:], k1f[:], -16.0, posf[:], ALU.mult, ALU.add, [Btk], [Btk], au=False)
                TTo(ebuf[:], best[:], best[:, :, 0:1].to_broadcast([128, 8, 16]), ALU.subtract, [Bbest], [Btk], au=False)
                ACT(ebuf[:], ebuf[:], AF.Exp, [Btk], [Btk], au=False)
                OP("dve", lambda e: e.tensor_reduce(out=Zs[:], in_=ebuf[:], axis=AX.X, op=ALU.add), [Btk], [Btk])
                RECIP(Zs[:], Zs[:], [Btk], [Btk], au=False)
                TTo(gate[:], ebuf[:], Zs[:, :].unsqueeze(2).to_broadcast([128, 8, 16]), ALU.mult, [Btk], [Btk], au=False)
                io16 = cmf[:, C_IOTA16:C_IOTA16 + 16].unsqueeze(1).unsqueeze(1).to_broadcast(B4)
                for (kf, cidx, dst) in ((k1f, 0, av), (k2f, 1, bvv)):
                    TTo(E1[:], kf[:].unsqueeze(3).to_broadcast(B4), io16, ALU.is_equal, [Btk, Bcm], [BE1])
                    TTo(E1[:], E1[:], i16fv[:, :, cidx, :].unsqueeze(2).to_broadcast(B4), ALU.mult, [BE1, Bi16f], [BE1])
                    OP("dve", lambda e, dst=dst: e.tensor_reduce(out=dst[:], in_=E1[:], axis=AX.X, op=ALU.add), [BE1], [Btk], True)
                for idx, srcv in enumerate((av, bvv, gate)):
                    OP("pe", lambda e, idx=idx, srcv=srcv: e.transpose(out=ps[:, 5, idx * 128:(idx + 1) * 128],
                                                                     in_=srcv[:].rearrange("p h k -> p (h k)"),
                                                                     identity=cmf[:, C_ID:C_ID + 128]),
                       [Btk, Bcm], [PB[5]])
                CP(abgT[:, tc, :, :].rearrange("p a b -> p (a b)"), ps[:, 5, 0:384], [PB[5]], [Babg], au=False)

            ckpt(8)
            barrier()
            for tb in range(T // 8):
                sl_ = tb % 2
                bk0 = 4 + 2 * (tb % 2)
                for i in range(8):
                    t = tb * 8 + i
                    tc, tl = t // 128, t % 128
                    TS(Pt[:, sl_, i, :], iotab[:], abgT[:, tc, 0, tl:tl + 1], ALU.is_equal, [Bconst, Babg], [BPt[sl_][i]],
                       au=False)
                    TS(Qt[:, sl_, i, :], iotab[:], abgT[:, tc, 1, tl:tl + 1], ALU.is_equal, [Bconst, Babg], [BQt[sl_][i]],
                       au=False)
                    ACT(Qt[:, sl_, i, :], Qt[:, sl_, i, :], AF.Copy, [BQt[sl_][i], Babg], [BQt[sl_][i]],
                        scale=abgT[:, tc, 2, tl:tl + 1], au=False)
                for i in range(8):
                    bank = bk0 + i // 4
                    MM(ps[:, bank, (i % 4) * 128:(i % 4 + 1) * 128], Qt[:, sl_, i, :], Pt[:, sl_, i, :], True, True,
                       [BQt[sl_][i], BPt[sl_][i]], [PB[bank]], au=False)
                for hb in range(2):
                    t0 = tb * 8 + hb * 4
                    ACT(G[:, :, t0:t0 + 4], ps[:, bk0 + hb, :].rearrange("p (t i) -> p i t", t=4), AF.Copy,
                        [PB[bk0 + hb]], [BGm[tb * 2 + hb]])

            ckpt(9)
            PBA = [PB[4], PB[5]]

            def stageA(ec):
                eg, cc, hs = ec // 2, ec % 2, ec % 2
                sl2 = eg % 2
                if cc == 0:
                    DMA(lambda e: e.dma_start(out=UTs[:, sl2, :, :].rearrange("p k c -> p (k c)"), in_=dscr[UV0 + eg]),
                        [Bscr[UV0 + eg]], [BUT[sl2]])
                    DMA(lambda e: e.dma_start(out=Vs[:, sl2, :, :].rearrange("p k c -> p (k c)"), in_=dscr[UV0 + 64 + eg]),
                        [Bscr[UV0 + 64 + eg]], [BVs[sl2]])
                for k in range(8):
                    MM(ps[:, 4 + hs, 0:256], UTs[:, sl2, k, cc * 128:(cc + 1) * 128], h2T[:, k, :],
                       k == 0, k == 7, [BUT[sl2], Bh2], [PBA[hs]], au=False)
                ACT(gl[:, hs, :], ps[:, 4 + hs, 0:256], AF.Gelu, [PBA[hs]], [Bgl[hs]], au=False)
                TTo(GA[:, hs, :], gl[:, hs, :], G[:, ec, :], ALU.mult, [Bgl[hs], BGm], [BGA[hs]])

            def stageV(ec):
                eg, cc, hs = ec // 2, ec % 2, ec % 2
                sl2 = eg % 2
                for dk in range(8):
                    MM(ps[:, dk // 2, (dk % 2) * 256:(dk % 2 + 1) * 256], Vs[:, sl2, cc, dk * 128:(dk + 1) * 128],
                       GA[:, hs, :], (ec == 0 and dk % 2 == 0), ec == 127, [BVs[sl2], BGA[hs]], [PB[dk // 2]],
                       au=False, sgc=True)

            stageA(0)
            for ec in range(128):
                if ec + 1 < 128:
                    stageA(ec + 1)
                stageV(ec)
                if ec == 4 and ti + 1 < NT:
                    prefetch(ti + 1)

            ckpt(10)
            for dk in range(8):
                TTo(xt[:, dk, :], xt[:, dk, :], ps[:, dk // 2, (dk % 2) * 256:(dk % 2 + 1) * 256], ALU.add,
                    [Bx[dk], PB[dk // 2]], [Bx[dk]], au=False)
            barrier()
            colstats(xt, 0, T, Bx, st1, Bs1, st2, Bs2, sqb, Bsq, 4)
            for k in range(8):
                STT(outtmp[:, k, :], xt[:, k, :], vcol(V_GF + k), st1[:, 0:T], ALU.mult, ALU.mult,
                    [Bx[k], Bvec, Bs1], [Bot])
            DMA(lambda e, ti=ti: e.dma_start(out=dout[:, :, ti * T:(ti + 1) * T].rearrange("k p t -> p k t"), in_=outtmp[:, :, :]),
                [Bot], [Bout], True, final=True)

        P.emit()
    return nc


def _prep_shared(inp):
    f = np.float32
    w_in = np.asarray(inp["w_in"], f)[0]
    b_in = np.asarray(inp["b_in"], f)[0]
    blk = lambda base, c: list(range(base + c * 128, base + (c + 1) * 128))
    chunks = []
    for pr in range(2):
        for c in range(4):
            chunks.append(blk(0, pr * 4 + c))
        for c in range(4):
            chunks.append(blk(1024, pr * 4 + c))
    for c in range(8):
        chunks.append(blk(2048, c))
    k0 = list(range(3072, 3136))
    k1 = list(range(3136, 3200))
    chunks.append(k0 + k0)
    chunks.append(k1 + k1)
    chunks.append(list(range(3200, 3328)))
    chunks.append(list(range(3200, 3328)))
    for c in range(8):
        chunks.append(blk(3328, c))
    for c in range(8):
        chunks.append(blk(4352, c))
    assert len(chunks) == 44
    colidx = np.array(sum(chunks, []), dtype=np.int64)
    w_perm = w_in[:, colidx]
    b_perm = b_in[colidx]
    mats = [w_perm[:, g * 512:(g + 1) * 512] for g in range(11)]
    for name in ("w_conv_out", "w_attn_o", "w_out"):
        w = np.asarray(inp[name], f)[0]
        mats += [w[:, 0:512], w[:, 512:1024]]
    wpq = np.asarray(inp["w_peer_q"], f)[0]
    mats += [wpq[:, g * 512:(g + 1) * 512] for g in range(4)]
    assert len(mats) == NMIXG
    wall = np.empty((NPIECE, 128, 2048), f)
    for g, m in enumerate(mats):
        a = m.reshape(8, 128, 512).transpose(1, 0, 2)
        wall[2 * g] = a[:, 0:4, :].reshape(128, 2048)
        wall[2 * g + 1] = a[:, 4:8, :].reshape(128, 2048)
    U = np.asarray(inp["peer_u"], f)[0]
    V = np.asarray(inp["peer_v"], f)[0]
    wall[UV0:UV0 + 64] = U.reshape(64, 256, 8, 128).transpose(0, 3, 2, 1).reshape(64, 128, 2048)
    wall[UV0 + 64:UV0 + 128] = V.reshape(64, 2, 128, 1024).transpose(0, 2, 1, 3).reshape(64, 128, 2048)

    vec = np.zeros((128, NV), f)
    col = lambda v: np.asarray(v, f).reshape(-1, 128).T
    vec[:, V_G1:V_G1 + 8] = col(inp["norm1_g"][0])
    vec[:, V_BIN:V_BIN + 44] = col(b_perm)
    vec[:, V_CB:V_CB + 8] = col(inp["conv_b"][0])
    vec[:, V_LNG:V_LNG + 8] = col(inp["conv_ln_g"][0])
    vec[:, V_LNB:V_LNB + 8] = col(inp["conv_ln_b"][0])
    vec[:, V_G2:V_G2 + 8] = col(inp["norm2_g"][0])
    vec[:, V_GF:V_GF + 8] = col(inp["final_g"])
    p = np.arange(128)
    invf = (np.float32(10000.0) ** (-(np.arange(32, dtype=f) * f(2.0) / f(64)))).astype(f)
    vec[:, V_INVF] = invf[p % 32]
    vec[:, V_SGN] = np.where(p % 64 < 32, -1.0, 1.0)
    cw = np.asarray(inp["conv_w"], f)[0]
    vec[:, V_CW:V_CW + 248] = cw.reshape(31, 8, 128).transpose(2, 1, 0).reshape(128, 248)

    cm = np.zeros((128, NCM), f)
    cm[:, C_ID:C_ID + 128] = np.eye(128, dtype=f)
    cm[p, C_PERM + (p ^ 32)] = 1.0
    cm[:, C_IOTA:C_IOTA + 128] = np.arange(128, dtype=f)[None, :]
    cm[:, C_IOTA16:C_IOTA16 + 16] = np.arange(16, dtype=f)[None, :]
    kk = np.arange(128)[:, None]
    qq = np.arange(128)[None, :]
    NEGM = f(-240000.0)
    m_prev = np.where(kk > qq, f(0), NEGM).astype(f)
    m_cur = np.where(kk <= qq, f(0), NEGM).astype(f)
    cm[:, C_MASK1:C_MASK1 + 128] = m_prev
    cm[:, C_MASK1 + 128:C_MASK1 + 256] = m_cur
    cm[:, C_MASK0 + 128:C_MASK0 + 256] = m_cur
    rows = np.concatenate([b_in[3200:3328], np.asarray(inp["attn_sinks"], f)[0]]).reshape(1, 144).astype(f)
    sk = np.asarray(inp["peer_sub_keys"], f)[0]
    skT = np.ascontiguousarray(sk.transpose(3, 0, 1, 2).reshape(128, 2048))
    return dict(wall=wall, vec=vec, cm=cm, rows=rows, skT=skT), m_prev, NEGM


_NC_CACHE = {}


def kernel(**inputs):
    x = np.asarray(inputs["x"], np.float32)
    pos = np.asarray(inputs["positions"], np.int32)
    B, S, _ = x.shape
    TOK = S // 2
    NT = TOK // TT
    shared, m_prev, NEGM = _prep_shared(inputs)
    in_maps = []
    for core in range(8):
        b, hs = core // 2, core % 2
        s0 = hs * TOK
        xT = np.zeros((8, 128, TOK + HALO), np.float32)
        pp = np.zeros((1, TOK + HALO), np.int32)
        if hs == 0:
            xs = x[b, 0:TOK]
            xT[:, :, HALO:] = xs.T.reshape(8, 128, TOK)
            pp[0, HALO:] = pos[b, 0:TOK]
        else:
            xs = x[b, s0 - HALO:s0 + TOK]
            xT[:] = xs.T.reshape(8, 128, TOK + HALO)
            pp[0] = pos[b, s0 - HALO:s0 + TOK]
        vec = shared["vec"].copy()
        vec[:, V_HV] = 0.0 if hs == 0 else 1.0
        cm = shared["cm"].copy()
        if hs == 0:
            cm[:, C_MASK0:C_MASK0 + 128] = NEGM
        else:
            cm[:, C_MASK0:C_MASK0 + 128] = m_prev
        in_maps.append(dict(xT=xT, pos=pp, wall=shared["wall"], vec=vec, cm=cm, rows=shared["rows"], skT=shared["skT"]))
    if NT not in _NC_CACHE:
        _NC_CACHE[NT] = build_nc(NT)
    nc = _NC_CACHE[NT]
    res = run_bass_kernel_spmd(nc, in_maps, core_ids=list(range(8)))
    out = np.empty((B, S, D), np.float32)
    for core in range(8):
        b, hs = core // 2, core % 2
        oT = np.asarray(res.results[core]["outT"], np.float32)
        out[b, hs * TOK:(hs + 1) * TOK, :] = oT.reshape(1024, TOK).T
    return out
```

```python
import numpy as np
from contextlib import ExitStack
import concourse.bass as bass
import concourse.mybir as mybir
from concourse.bass_utils import run_bass_kernel_spmd

F32 = mybir.dt.float32
BF16 = mybir.dt.bfloat16
I32 = mybir.dt.int32
U32 = mybir.dt.uint32
AF = mybir.ActivationFunctionType
ALU = mybir.AluOpType
AX = mybir.AxisListType


class Buf:
    __slots__ = ("name", "w", "r")

    def __init__(self, name=""):
        self.name = name
        self.w = None
        self.r = []


class BG(list):
    def __init__(self, name, n):
        super().__init__(Buf("%s%d" % (name, i)) for i in range(n))


def _flat(bs):
    out = []
    for b in bs:
        if isinstance(b, list):
            out.extend(_flat(b))
        else:
            out.append(b)
    return out


class _Ins:
    __slots__ = ("eng", "fn", "deps", "dma", "idx", "sig", "cnt", "semi", "final")

    def __init__(self, eng, fn, dma):
        self.eng = eng
        self.fn = fn
        self.deps = set()
        self.dma = dma
        self.sig = False
        self.cnt = 0
        self.semi = 0
        self.final = False


class Prog:
    NDMA_SEM = 12
    ENGS = ("pe", "act", "dve", "pool", "sp")

    def __init__(self, nc, es):
        self.nc = nc
        self.es = es
        self.q = {e: [] for e in self.ENGS}
        self.all = []
        self.dma_engine = "sp"
        self.halted = False

    def _add(self, ins, reads, writes):
        if self.halted:
            return ins
        reads = _flat(reads)
        writes = _flat(writes)
        for b in reads:
            if b.w is not None:
                ins.deps.add(b.w)
        for b in writes:
            if b.w is not None:
                ins.deps.add(b.w)
            for r in b.r:
                ins.deps.add(r)
        ins.deps.discard(ins)
        for b in reads:
            b.r.append(ins)
        for b in writes:
            b.w = ins
            b.r = []
        ins.idx = len(self.all)
        self.all.append(ins)
        self.q[ins.eng].append(ins)
        return ins

    def op(self, eng, fn, reads=(), writes=()):
        return self._add(_Ins(eng, fn, False), reads, writes)

    def dma(self, fn, reads=(), writes=(), final=False, eng=None):
        ins = _Ins(eng or self.dma_engine, fn, True)
        ins.final = final
        return self._add(ins, reads, writes)

    def emit(self):
        nc = self.nc
        for ins in self.all:
            for d in ins.deps:
                if d.eng == "pe" and ins.eng == "pe" and not d.dma and not ins.dma:
                    continue
                d.sig = True
            if ins.final:
                ins.sig = True
        sems = {e: self.es.enter_context(nc.semaphore("s_" + e)) for e in self.ENGS}
        dsems = [self.es.enter_context(nc.semaphore("d%d" % i)) for i in range(self.NDMA_SEM)]
        cnt = {e: 0 for e in self.ENGS}
        ndma = 0
        dma_prev = {}
        last_on_sem = [None] * self.NDMA_SEM
        for ins in self.all:
            if ins.dma:
                ins.semi = ndma % self.NDMA_SEM
                ins.cnt = 16 * (ndma // self.NDMA_SEM + 1)
                dma_prev[ins] = last_on_sem[ins.semi]
                last_on_sem[ins.semi] = ins
                ndma += 1
            elif ins.sig:
                cnt[ins.eng] += 1
                ins.cnt = cnt[ins.eng]
        finals = [i for i in self.all if i.final]
        block = self.es.enter_context(nc.Block())

        def run(engname, e):
            waited = {}

            def wait_for(d):
                if d.dma:
                    key = ("d", d.semi)
                    sem = dsems[d.semi]
                else:
                    key = ("e", d.eng)
                    sem = sems[d.eng]
                if waited.get(key, 0) >= d.cnt:
                    return
                e.wait_ge(sem, d.cnt)
                waited[key] = d.cnt

            for ins in self.q[engname]:
                for d in sorted(ins.deps, key=lambda z: z.idx):
                    if (d.eng == "pe" and engname == "pe" and not d.dma and not ins.dma):
                        continue
                    wait_for(d)
                if ins.dma:
                    p = dma_prev[ins]
                    if p is not None:
                        wait_for(p)
                h = ins.fn(e)
                if ins.dma:
                    h.then_inc(dsems[ins.semi], 16)
                elif ins.sig:
                    h.then_inc(sems[engname], 1)
            if engname == self.dma_engine:
                for f in finals:
                    wait_for(f)

        @block.sync
        def _(e):
            run("sp", e)

        @block.tensor
        def _(e):
            run("pe", e)

        @block.scalar
        def _(e):
            run("act", e)

        @block.vector
        def _(e):
            run("dve", e)

        @block.gpsimd
        def _(e):
            run("pool", e)


D = 1024
KC = 8
TT = 256
HALO = 128
EPSV = 1e-6
NMIXG = 21
NPIECE = 2 * NMIXG + 128
UV0 = 2 * NMIXG
V_G1, V_BIN, V_CB, V_LNG, V_LNB, V_G2, V_GF, V_INVF, V_SGN, V_HV, V_CW = 0, 8, 52, 60, 68, 76, 84, 92, 93, 94, 95
NV = 95 + 248
C_ID, C_PERM, C_IOTA, C_IOTA16, C_MASK0, C_MASK1 = 0, 128, 256, 384, 400, 656
NCM = 912
MAGIC = 12582912.0
CW1 = 6.28125
CW2 = 2.0 * np.pi - 6.28125


def _chunk_val(c):
    return (c // 4) * 8 + (c % 4)


def _chunk_gate(c):
    return (c // 4) * 8 + 4 + (c % 4)


class _Stop(Exception):
    pass


def build_nc(NT, dbg=False, stop=None):
    T = TT
    TOK = NT * T
    TOKH = TOK + HALO
    nc = bass.Bass("TRN2", target_bir_lowering=False)
    dx = nc.dram_tensor("xT", [8, 128, TOKH], F32, kind="ExternalInput").ap()
    dpos = nc.dram_tensor("pos", [1, TOKH], I32, kind="ExternalInput").ap()
    dwall = nc.dram_tensor("wall", [NPIECE, 128, 2048], F32, kind="ExternalInput").ap()
    dvec = nc.dram_tensor("vec", [128, NV], F32, kind="ExternalInput").ap()
    dcm = nc.dram_tensor("cm", [128, NCM], F32, kind="ExternalInput").ap()
    drow = nc.dram_tensor("rows", [1, 144], F32, kind="ExternalInput").ap()
    dsk = nc.dram_tensor("skT", [128, 2048], F32, kind="ExternalInput").ap()
    dout = nc.dram_tensor("outT", [8, 128, TOK], F32, kind="ExternalOutput").ap()
    dscr = nc.dram_tensor("wscr", [NPIECE, 128, 2048], BF16, kind="Internal").ap()
    ddiag = nc.dram_tensor("dgscr", [8, 128, 31 * 128], BF16, kind="Internal").ap()
    if dbg:
        ddbg = nc.dram_tensor("dbg", [8, 128, T], F32, kind="ExternalOutput").ap()

    es = ExitStack()
    with es:
        def sb(name, shape, dt):
            return es.enter_context(nc.sbuf_tensor("sb_" + name, shape, dt))

        P = Prog(nc, es)
        ps = es.enter_context(nc.psum_tensor("ps", [128, 8, 512], F32))
        PB = [Buf("bank%d" % i) for i in range(8)]

        vec = sb("vec", [128, NV], F32)
        cmf = sb("cmf", [128, NCM], F32)
        identb = sb("identb", [128, 128], BF16)
        permb = sb("permb", [128, 128], BF16)
        onesb = sb("onesb", [128, 128], BF16)
        onesmb = sb("onesmb", [128, 128], BF16)
        onesmf = sb("onesmf", [128, 128], F32)
        iotab = sb("iotab", [128, 128], BF16)
        maskb = sb("maskb", [128, 2, 2, 2, 128], BF16)
        esink = sb("esink", [128, 16], F32)
        bvb = sb("bvb", [128, 128], F32)
        skf = sb("skf", [128, 2048], F32)
        skb = sb("skb", [128, 16, 128], BF16)
        xt_a = sb("xt", [128, 8, T], F32)
        xt_b = sb("xt2", [128, 8, T], F32)
        xts = [xt_a, xt_b]
        sqn = sb("sqn", [128, 8, T], BF16)
        rn1 = sb("rn1", [128, T], F32)
        ubuf = sb("ubuf", [128, 2, 8, 32 + T], BF16)
        kbuf = sb("kbuf", [128, 2, 2, 128 + T], BF16)
        vdup = sb("vdup", [128, 2, 3, 2, 128], BF16)
        cosT = sb("cosT", [128, 128 + T], F32)
        sinT = sb("sinT", [128, 128 + T], F32)
        posi = sb("posi", [128, 128 + T], I32)
        r1 = sb("r1", [128, 128 + T], F32)
        r2 = sb("r2", [128, 128 + T], F32)
        r3 = sb("r3", [128, 128 + T], F32)
        st1 = sb("st1", [128, 128 + T], F32)
        st2 = sb("st2", [128, 128 + T], F32)
        st3 = sb("st3", [128, 128 + T], F32)
        sg1 = sb("sg1", [128, 128 + T], F32)
        sg2 = sb("sg2", [128, 128 + T], F32)
        m1buf = sb("m1buf", [128, 4, T], F32)
        qb = sb("qb", [128, 128 + T], BF16)
        qb2 = sb("qb2", [128, 128 + T], BF16)
        r4 = sb("r4", [128, 128 + T], F32)
        eT = sb("eT", [128, 2, 2, 4, 128], BF16)
        den = sb("den", [128, 512], F32)
        rden = sb("rden", [128, 512], F32)
        h2T = sb("h2T", [128, 8, T], BF16)
        gl = sb("gl", [128, 2, T], BF16)
        GA = sb("GA", [128, 2, T], BF16)
        Pt = sb("Pt", [128, 2, 8, 128], BF16)
        Qt = sb("Qt", [128, 2, 8, 128], BF16)
        UTs = sb("UTs", [128, 2, 8, 256], BF16)
        Vs = sb("Vs", [128, 2, 2, 1024], BF16)
        v16 = sb("v16", [128, 16, 16], F32)
        i16 = sb("i16", [128, 16, 16], U32)
        i16f = sb("i16f", [128, 16, 16], F32)
        best = sb("best", [128, 8, 16], F32)
        posu = sb("posu", [128, 8, 16], U32)
        k1u = sb("k1u", [128, 8, 16], U32)
        posf = sb("posf", [128, 8, 16], F32)
        k1f = sb("k1f", [128, 8, 16], F32)
        k2f = sb("k2f", [128, 8, 16], F32)
        ebuf = sb("ebuf", [128, 8, 16], F32)
        gate = sb("gate", [128, 8, 16], F32)
        Zs = sb("Zs", [128, 8], F32)
        av = sb("av", [128, 8, 16], F32)
        bvv = sb("bvv", [128, 8, 16], F32)
        abgT = sb("abgT", [128, 2, 3, 128], F32)
        one1 = sb("one1", [128, 4], F32)
        arena = sb("arena", [128, 32768], BF16)

        def av_(off, nbytes, dt):
            a = arena[:, off // 2:(off + nbytes) // 2]
            return a if dt == BF16 else a.bitcast(dt)

        K = 1024
        wbuf = [av_(0, 8 * K, BF16).rearrange("p (k c) -> p k c", k=8),
                av_(8 * K, 8 * K, BF16).rearrange("p (k c) -> p k c", k=8)]
        diag = [av_(16 * K, 7936, BF16).rearrange("p (j c) -> p j c", j=31),
                av_(24 * K, 7936, BF16).rearrange("p (j c) -> p j c", j=31)]
        hT = av_(32 * K, 6 * K, BF16).rearrange("p (k c) -> p k c", k=8)
        sqb = av_(38 * K, 6 * K, BF16).rearrange("p (k c) -> p k c", k=8)
        ysb = av_(44 * K, 8 * K, F32).rearrange("p (k c) -> p k c", k=8)
        sT = av_(52 * K, 4 * K, BF16).rearrange("p (k c) -> p k c", k=8)
        qrope = av_(56 * K, 4 * K, BF16).rearrange("p (k c) -> p k c", k=8)
        attnT = av_(60 * K, 4 * K, BF16).rearrange("p (k c) -> p k c", k=8)
        mergedT = sqb
        stin = [av_(i * 8 * K, 8 * K, F32) for i in range(4)]
        stout = [av_(32 * K + i * 4 * K, 4 * K, BF16) for i in range(4)]
        dgst = [av_(48 * K, 7936, BF16).rearrange("p (j c) -> p j c", j=31),
                av_(56 * K, 7936, BF16).rearrange("p (j c) -> p j c", j=31)]
        qTb = av_(16 * K, 8 * K, BF16).rearrange("p (k c) -> p k c", k=16)
        cand = av_(24 * K, 8 * K, F32).rearrange("p (h a b) -> p h a b", h=8, a=16)
        work2 = av_(32 * K, 8 * K, F32).rearrange("p (h c) -> p h c", h=8)
        E1 = av_(40 * K, 8 * K, F32).rearrange("p (h a b) -> p h a b", h=8, a=16)
        scw = av_(48 * K, 2 * K, F32).rearrange("p (l c) -> p l c", l=4)
        G = arena[:, :].rearrange("p (i t) -> p i t", i=128)
        outtmp = av_(0, 8 * K, F32).rearrange("p (k c) -> p k c", k=8)

        ATOK = Buf("atok")

        def vcol(i):
            return vec[:, i:i + 1]

        capture = [None]

        def OP(eng, fn, reads, writes, arena_use=False):
            if capture[0] is not None:
                capture[0].append(("op", (eng, fn, list(reads), list(writes), arena_use)))
                return None
            if arena_use:
                reads = list(reads) + [ATOK]
            return P.op(eng, fn, reads, writes)

        def DMA(fn, reads, writes, arena_use=False, final=False, eng=None):
            if capture[0] is not None:
                capture[0].append(("dma", (fn, list(reads), list(writes), arena_use, final, eng)))
                return None
            if arena_use:
                reads = list(reads) + [ATOK]
            return P.dma(fn, reads, writes, final=final, eng=eng)

        def replay(item):
            kind, args = item
            if kind == "op":
                OP(*args)
            else:
                DMA(*args)

        def barrier():
            P.op("pool", lambda e: e.memset(one1[:, 0:1], 0.0), [], [ATOK])

        def MM(out, lhsT, rhs, start, stop, reads, writes, au=True, sgc=False):
            OP("pe", lambda e: e.matmul(out, lhsT=lhsT, rhs=rhs, start=start, stop=stop,
                                        skip_group_check=sgc), reads, writes, au)

        def ACT(out, in_, func, reads, writes, bias=None, scale=None, au=True):
            kw = {}
            if bias is not None:
                kw["bias"] = bias
            if scale is not None:
                kw["scale"] = scale
            OP("act", lambda e: e.activation(out=out, in_=in_, func=func, **kw), reads, writes, au)

        def TTo(out, in0, in1, op, reads, writes, eng="dve", au=True):
            OP(eng, lambda e: e.tensor_tensor(out=out, in0=in0, in1=in1, op=op), reads, writes, au)

        def TS(out, in0, s1, op0, reads, writes, s2=None, op1=None, eng="dve", au=True):
            if op1 is None:
                OP(eng, lambda e: e.tensor_scalar(out=out, in0=in0, scalar1=s1, scalar2=None, op0=op0),
                   reads, writes, au)
            else:
                OP(eng, lambda e: e.tensor_scalar(out=out, in0=in0, scalar1=s1, scalar2=s2, op0=op0, op1=op1),
                   reads, writes, au)

        def STT(out, in0, scalar, in1, op0, op1, reads, writes, au=True):
            OP("dve", lambda e: e.scalar_tensor_tensor(out=out, in0=in0, scalar=scalar, in1=in1,
                                                       op0=op0, op1=op1), reads, writes, au)

        def CP(out, in_, reads, writes, eng="dve", au=True):
            OP(eng, lambda e: e.tensor_copy(out=out, in_=in_), reads, writes, au)

        def RECIP(out, in_, reads, writes, au=True):
            OP("dve", lambda e: e.reciprocal(out=out, in_=in_), reads, writes, au)

        Bvec, Bcm, Bconst, Bsk = Buf("vec"), Buf("cm"), Buf("const"), Buf("sk")
        Bxs = [BG("xta", 8), BG("xtb", 8)]
        Bsqn, Brn1 = BG("sqn", 8), Buf("rn1")
        Bscr = [Buf("scr%d" % i) for i in range(NPIECE)]
        Bst_in = BG("sti", 4)
        Bst_out = BG("sto", 4)
        Bw = [Buf("w0"), Buf("w1")]
        Bdiag = [Buf("dg0"), Buf("dg1")]
        BhT, Bsq, Bys, BsT, Bqr, Bat = BG("hT", 8), BG("sq", 8), BG("ys", 8), BG("sT", 8), BG("qr", 8), BG("at", 8)
        Bub = [BG("ub0_", 8), BG("ub1_", 8)]
        Bkb = [Buf("kb0"), Buf("kb1")]
        Bvd = [[Buf("vd%d%d" % (p, b)) for b in range(3)] for p in range(2)]
        Bcs, Bposi, Br1, Br2, Br3 = Buf("cs"), Buf("posi"), Buf("r1"), Buf("r2"), Buf("r3")
        Bs1, Bs2, Bs3, Bg1, Bg2, Bm1, Bqb = Buf("st1"), Buf("st2"), Buf("st3"), Buf("sg1"), Buf("sg2"), Buf("m1"), Buf("qb")
        BeT = [Buf("eT0"), Buf("eT1")]
        Bden, Brden = Buf("den"), Buf("rden")
        Bh2, Bgl, BGA = Buf("h2T"), [Buf("gl0"), Buf("gl1")], [Buf("GA0"), Buf("GA1")]
        BPt, BQt = [BG("Pt0_", 8), BG("Pt1_", 8)], [BG("Qt0_", 8), BG("Qt1_", 8)]
        BUT, BVs = [Buf("UT0"), Buf("UT1")], [Buf("Vs0"), Buf("Vs1")]
        Bv16, Bi16, Bi16f, Bbest, Bpos, Btk, Babg = BG("v16_", 16), BG("i16_", 16), Buf("i16f"), BG("best", 8), BG("pos", 8), Buf("tk"), Buf("abg")
        BqT, Bcand, Bw2, BE1, Bscw, BGm, Bot = BG("qTb", 16), Buf("cand"), BG("w2_", 8), Buf("E1"), BG("scw", 4), BG("G", 64), Buf("ot")
        Bsgs = BG("sgs", 2)
        Bra, Brb, Bqbs = BG("ra", 2), BG("rb", 2), BG("qbs", 2)
        Bout = Buf("out")

        DMA(lambda e: e.dma_start(out=vec[:], in_=dvec[:, :]), [], [Bvec])
        DMA(lambda e: e.dma_start(out=cmf[:], in_=dcm[:, :]), [], [Bcm])
        DMA(lambda e: e.dma_start(out=skf[:], in_=dsk[:, :]), [], [Bsk])
        DMA(lambda e: e.dma_start(out=bvb[:], in_=drow[0:1, 0:128].partition_broadcast(128)[:, 0, :]), [], [Bconst])
        DMA(lambda e: e.dma_start(out=esink[:], in_=drow[0:1, 128:144].partition_broadcast(128)[:, 0, :]), [], [Bconst])
        CP(identb[:], cmf[:, C_ID:C_ID + 128], [Bcm], [Bconst], au=False)
        CP(permb[:], cmf[:, C_PERM:C_PERM + 128], [Bcm], [Bconst], au=False)
        CP(iotab[:], cmf[:, C_IOTA:C_IOTA + 128], [Bcm], [Bconst], au=False)
        OP("dve", lambda e: e.memset(onesb[:], 1.0), [], [Bconst])
        OP("dve", lambda e: e.memset(onesmb[:], 1.0 / 1024.0), [], [Bconst])
        OP("dve", lambda e: e.memset(onesmf[:], 1.0 / 1024.0), [], [Bconst])
        for var in range(2):
            for kb in range(2):
                for h4 in range(2):
                    c0 = (C_MASK0 if var == 0 else C_MASK1) + kb * 128
                    CP(maskb[:, var, kb, h4, :], cmf[:, c0:c0 + 128], [Bcm], [Bconst], au=False)
        ACT(esink[:], esink[:], AF.Exp, [Bconst], [Bconst], au=False)
        CP(skb[:].rearrange("p a b -> p (a b)"), skf[:], [Bsk], [Bsk], au=False)

        cast_engs = ["act", "dve", "pool"]
        NPR = NPIECE if stop != 1 else 0

        def pl_load(i):
            s = i % 4
            DMA(lambda e: e.dma_start(out=stin[s], in_=dwall[i]), [], [Bst_in[s]], True)

        def pl_cast_store(i):
            s = i % 4
            ce = cast_engs[i % 3]
            if ce == "act":
                ACT(stout[s], stin[s], AF.Copy, [Bst_in[s]], [Bst_out[s]])
            else:
                CP(stout[s], stin[s], [Bst_in[s]], [Bst_out[s]], eng=ce)
            DMA(lambda e: e.dma_start(out=dscr[i], in_=stout[s]), [Bst_out[s]], [Bscr[i]], True, eng="act")

        for i in range(min(3, NPR)):
            pl_load(i)
        for i in range(NPR):
            if i + 3 < NPR:
                pl_load(i + 3)
            pl_cast_store(i)

        Bdg = [BG("dgs0_", 31), BG("dgs1_", 31)]
        Bdgd = BG("dgd", 8)
        for c in range(8 if stop != 1 else 0):
            s = c % 2
            for jt in range(31):
                ACT(dgst[s][:, jt, :], cmf[:, C_ID:C_ID + 128], AF.Copy, [Bcm, Bvec], [Bdg[s][jt]],
                    scale=vcol(V_CW + c * 31 + jt))
            DMA(lambda e, c=c, s=s: e.dma_start(out=ddiag[c], in_=dgst[s][:, :, :].rearrange("p j c -> p (j c)")),
                [Bdg[s]], [Bdgd[c]], True)

        wslot = [0]

        def loadw(g):
            s = wslot[0]
            wslot[0] ^= 1
            DMA(lambda e: e.dma_start(out=wbuf[s].rearrange("p (r k) c -> p r (k c)", r=2),
                                      in_=dscr[2 * g:2 * g + 2].rearrange("r p f -> p r f")),
                [Bscr[2 * g], Bscr[2 * g + 1]], [Bw[s]], True)
            return s

        bankrr = [0]

        def nextbank():
            b = bankrr[0]
            bankrr[0] = (b + 1) % 4
            return b

        def proj(bank, s, j, rhs3, c0, n, rbuf):
            for k in range(8):
                MM(ps[:, bank, 0:n], wbuf[s][:, k, j * 128:(j + 1) * 128], rhs3[:, k, c0:c0 + n],
                   k == 0, k == 7, [Bw[s], rbuf], [PB[bank]])

        def colstats(src3, c0, n, srcbuf, outrr, outbuf, tmp, tmpbuf, sqv, sqbuf, bank, sq_au=True):
            for k in range(8):
                ACT(sqv[:, k, c0:c0 + n], src3[:, k, c0:c0 + n], AF.Square, [srcbuf[k]], [sqbuf[k]], au=sq_au)
            for k in range(8):
                MM(ps[:, bank, 0:n], onesmb[:], sqv[:, k, c0:c0 + n], k == 0, k == 7, [Bconst, sqbuf[k]], [PB[bank]], au=sq_au)
            TS(tmp[:, c0:c0 + n], ps[:, bank, 0:n], EPSV, ALU.add, [PB[bank]], [tmpbuf], au=False)
            ACT(tmp[:, c0:c0 + n], tmp[:, c0:c0 + n], AF.Sqrt, [tmpbuf], [tmpbuf], au=False)
            RECIP(outrr[:, c0:c0 + n], tmp[:, c0:c0 + n], [tmpbuf], [outbuf], au=False)

        def rope_tables(c0, n, colbase):
            DMA(lambda e: e.dma_start(out=posi[:, c0:c0 + n],
                                      in_=dpos[0:1, colbase:colbase + n].partition_broadcast(128)[:, 0, :]),
                [], [Bposi])
            sl = slice(c0, c0 + n)
            CP(r1[:, sl], posi[:, sl], [Bposi], [Br1], au=False)
            TS(r1[:, sl], r1[:, sl], vcol(V_INVF), ALU.mult, [Br1, Bvec], [Br1], au=False)
            for (dst, shift, useSgn) in ((sinT, 0.0, True), (cosT, float(np.pi / 2), False)):
                TS(r2[:, sl], r1[:, sl], shift, ALU.add, [Br1], [Br2], au=False)
                TS(r3[:, sl], r2[:, sl], float(1.0 / (2 * np.pi)), ALU.mult, [Br2], [Br3], s2=MAGIC, op1=ALU.add, au=False)
                TS(r3[:, sl], r3[:, sl], MAGIC, ALU.subtract, [Br3], [Br3], au=False)
                STT(r2[:, sl], r3[:, sl], -CW1, r2[:, sl], ALU.mult, ALU.add, [Br3, Br2], [Br2], au=False)
                STT(r2[:, sl], r3[:, sl], -CW2, r2[:, sl], ALU.mult, ALU.add, [Br3, Br2], [Br2], au=False)
                TS(r2[:, sl], r2[:, sl], 3.1415925, ALU.min, [Br2], [Br2], s2=-3.1415925, op1=ALU.max, au=False)
                if useSgn:
                    ACT(dst[:, sl], r2[:, sl], AF.Sin, [Br2, Bvec], [Bcs], scale=vcol(V_SGN), au=False)
                else:
                    ACT(dst[:, sl], r2[:, sl], AF.Sin, [Br2], [Bcs], au=False)

        def ckpt(i):
            if stop == i:
                P.halted = True

        if stop in (1, 2):
            P.halted = True
        for ti in range(NT):
            par = ti % 2
            c0 = 0 if ti == 0 else 128
            n = 128 + T - c0
            xcol = HALO + ti * T
            barrier()
            xt = xts[par]
            Bx = Bxs[par]

            def prefetch(tn):
                pn = tn % 2
                xc = HALO + tn * T
                cc0 = 0 if tn == 0 else 128
                DMA(lambda e: e.dma_start(out=xts[pn][:], in_=dx[:, :, xc:xc + T].rearrange("k p t -> p k t")),
                    [], [Bxs[pn]])
                rope_tables(cc0, 128 + T - cc0, xc - 128 + cc0)
                colstats(xts[pn], 0, T, Bxs[pn], rn1, Brn1, st2, Bs2, sqn, Bsqn, 6, sq_au=False)

            if ti == 0:
                prefetch(0)
            for k in range(8):
                STT(hT[:, k, 128:128 + T], xt[:, k, :], vcol(V_G1 + k), rn1[:, 0:T], ALU.mult, ALU.mult,
                    [Bx[k], Bvec, Brn1], [BhT[k]])
            if ti == 0:
                DMA(lambda e: e.dma_start(out=ysb[:, :, 0:128], in_=dx[:, :, 0:128].rearrange("k p t -> p k t")),
                    [], [Bys], True)
                colstats(ysb, 0, 128, Bys, st3, Bs3, st2, Bs2, sqb, Bsq, 4)
                for k in range(8):
                    STT(hT[:, k, 0:128], ysb[:, k, 0:128], vcol(V_G1 + k), st3[:, 0:128], ALU.mult, ALU.mult,
                        [Bys[k], Bvec, Bs3], [BhT[k]])
            else:
                for c in range(8):
                    CP(ubuf[:, par, c, 0:32], ubuf[:, 1 - par, c, T:T + 32], [Bub[1 - par][c]], [Bub[par][c]], eng="pool", au=False)
                for g in range(2):
                    CP(kbuf[:, par, g, 0:128], kbuf[:, 1 - par, g, T:T + 128], [Bkb[1 - par]], [Bkb[par]], eng="pool", au=False)
                CP(vdup[:, par, 0, :, :], vdup[:, 1 - par, 2, :, :], [Bvd[1 - par][2]], [Bvd[par][0]], eng="pool", au=False)

            ckpt(3)
            cu0 = 96 if ti == 0 else 128
            nu = 128 + T - cu0
            wsl = {}
            sgl = [sg1, sg2]

            def glu_proj(c):
                pr, j = c // 4, c % 4
                if j == 0:
                    wsl[pr] = (loadw(2 * pr), loadw(2 * pr + 1))
                sv, sgt = wsl[pr]
                bA = nextbank()
                proj(bA, sv, j, hT, cu0, nu, BhT)
                bB = nextbank()
                proj(bB, sgt, j, hT, cu0, nu, BhT)
                sg = sgl[c % 2]
                ACT(sg[:, 0:nu], ps[:, bB, 0:nu], AF.Sigmoid, [PB[bB], Bvec], [Bsgs[c % 2]],
                    bias=vcol(V_BIN + _chunk_gate(c)), au=False)
                STT(ubuf[:, par, c, cu0 - 96:cu0 - 96 + nu], ps[:, bA, 0:nu], vcol(V_BIN + _chunk_val(c)),
                    sg[:, 0:nu], ALU.add, ALU.mult, [PB[bA], Bvec, Bsgs[c % 2]], [Bub[par][c]], au=False)
                if ti == 0:
                    TS(ubuf[:, par, c, 0:32], ubuf[:, par, c, 0:32], vcol(V_HV), ALU.mult,
                       [Bub[par][c], Bvec], [Bub[par][c]], au=False)
                ds_ = c % 2
                DMA(lambda e: e.dma_start(out=diag[ds_][:, :, :].rearrange("p j c -> p (j c)"), in_=ddiag[c]),
                    [Bdgd[c]], [Bdiag[ds_]], True)

            def conv(c):
                ds_ = c % 2
                bC = nextbank()
                for jt in range(31):
                    MM(ps[:, bC, 0:T], diag[ds_][:, jt, :], ubuf[:, par, c, 2 + jt:2 + jt + T],
                       jt == 0, jt == 30, [Bdiag[ds_], Bub[par][c]], [PB[bC]])
                ACT(ysb[:, c, :], ps[:, bC, 0:T], AF.Identity, [PB[bC], Bvec], [Bys[c]], bias=vcol(V_CB + c))
                ACT(sqb[:, c, 0:T], ps[:, bC, 0:T], AF.Square, [PB[bC], Bvec], [Bsq[c]], bias=vcol(V_CB + c))

            glu_proj(0)
            for c in range(8):
                if c + 1 < 8:
                    glu_proj(c + 1)
                conv(c)
            for c in range(8):
                MM(ps[:, 4, 0:T], onesmf[:], ysb[:, c, :], c == 0, c == 7, [Bconst, Bys[c]], [PB[4]])
            for c in range(8):
                MM(ps[:, 5, 0:T], onesmb[:], sqb[:, c, 0:T], c == 0, c == 7, [Bconst, Bsq[c]], [PB[5]])
            CP(st1[:, 0:T], ps[:, 4, 0:T], [PB[4]], [Bs1], au=False)
            TTo(st2[:, 0:T], st1[:, 0:T], st1[:, 0:T], ALU.mult, [Bs1], [Bs2], au=False)
            TTo(st2[:, 0:T], ps[:, 5, 0:T], st2[:, 0:T], ALU.subtract, [PB[5], Bs2], [Bs2], au=False)
            TS(st2[:, 0:T], st2[:, 0:T], EPSV, ALU.add, [Bs2], [Bs2], au=False)
            ACT(st2[:, 0:T], st2[:, 0:T], AF.Sqrt, [Bs2], [Bs2], au=False)
            RECIP(st3[:, 0:T], st2[:, 0:T], [Bs2], [Bs3], au=False)
            STT(st1[:, 0:T], st1[:, 0:T], -1.0, st3[:, 0:T], ALU.mult, ALU.mult, [Bs1, Bs3], [Bs1], au=False)
            for c in range(8):
                ra_, rb_ = (r1, r2) if c % 2 == 0 else (r3, r4)
                Ba_, Bb_ = ([Br1, Bra[0]], [Br2, Brb[0]]) if c % 2 == 0 else ([Br3, Bra[1]], [Brb[1]])
                TTo(ra_[:, 0:T], ysb[:, c, :], st3[:, 0:T], ALU.mult, [Bys[c], Bs3], Ba_)
                TTo(rb_[:, 0:T], ra_[:, 0:T], st1[:, 0:T], ALU.add, Ba_ + [Bs1], Bb_, au=False)
                ACT(sT[:, c, :], rb_[:, 0:T], AF.Silu, Bb_ + [Bvec], [BsT[c]], bias=vcol(V_LNB + c), scale=vcol(V_LNG + c))

            ckpt(4)
            rcnt = [0]

            def rope_chunk(bank, outap, cc0, nn, bcol, outbuf):
                i_ = rcnt[0] % 2
                rcnt[0] += 1
                qb_ = (qb, qb2)[i_]
                ra_, rb_ = ((r1, r2), (r3, r4))[i_]
                Bq_ = [Bqb, Bqbs[0]] if i_ == 0 else [Bqbs[1]]
                Ba_ = [Br1, Bra[0]] if i_ == 0 else [Br3, Bra[1]]
                Bb_ = [Br2, Brb[0]] if i_ == 0 else [Brb[1]]
                ACT(qb_[:, cc0:cc0 + nn], ps[:, bank, 0:nn], AF.Identity, [PB[bank], Bvec], Bq_, bias=vcol(bcol), au=False)
                b2 = nextbank()
                MM(ps[:, b2, 0:nn], permb[:], qb_[:, cc0:cc0 + nn], True, True, [Bconst] + Bq_, [PB[b2]], au=False)
                TTo(ra_[:, cc0:cc0 + nn], qb_[:, cc0:cc0 + nn], cosT[:, cc0:cc0 + nn], ALU.mult, Bq_ + [Bcs], Ba_, au=False)
                TTo(rb_[:, cc0:cc0 + nn], ps[:, b2, 0:nn], sinT[:, cc0:cc0 + nn], ALU.mult, [PB[b2], Bcs], Bb_, au=False)
                TTo(outap, ra_[:, cc0:cc0 + nn], rb_[:, cc0:cc0 + nn], ALU.add, Ba_ + Bb_, [outbuf], au=True)

            for qg in range(2):
                s = loadw(4 + qg)
                for j in range(4):
                    cq = qg * 4 + j
                    b = nextbank()
                    proj(b, s, j, hT, 128, T, BhT)
                    rope_chunk(b, qrope[:, cq, :], 128, T, V_BIN + 16 + cq, Bqr[cq])
            s = loadw(6)
            for g in range(2):
                b = nextbank()
                proj(b, s, g, hT, c0, n, BhT)
                rope_chunk(b, kbuf[:, par, g, c0:c0 + n], c0, n, V_BIN + 24 + g, Bkb[par])
            for blk in range(3):
                if ti > 0 and blk == 0:
                    continue
                b = nextbank()
                for k in range(8):
                    MM(ps[:, b, 0:128], hT[:, k, blk * 128:(blk + 1) * 128], wbuf[s][:, k, 256:384],
                       k == 0, k == 7, [BhT, Bw[s]], [PB[b]])
                for dup in range(2):
                    TTo(vdup[:, par, blk, :, dup * 64:(dup + 1) * 64],
                        ps[:, b, 0:128].rearrange("p (g d) -> p g d", g=2),
                        bvb[:].rearrange("p (g d) -> p g d", g=2), ALU.add, [PB[b], Bconst], [Bvd[par][blk]], au=False)

            ckpt(5)
            iters = [(b, g, hg) for b in range(2) for g in range(2) for hg in range(2)]

            def att_S(i):
                b, g, hg = iters[i]
                sb0 = 6 if i % 2 == 0 else 2
                var = 0 if (ti == 0 and b == 0) else 1
                j0 = (8 * g + 4 * hg) // 2
                for half in range(2):
                    MM(ps[:, sb0 + half, :], identb[:], maskb[:, var, :, :, :].rearrange("p k a q -> p (k a q)"),
                       True, False, [Bconst], [PB[sb0 + half]], au=False, sgc=True)
                for half in range(2):
                    pa = slice(half * 64, (half + 1) * 64)
                    for kb in range(2):
                        for a in range(2):
                            MM(ps[:, sb0 + half, (kb * 2 + a) * 128:(kb * 2 + a + 1) * 128],
                               kbuf[pa, par, g, (b + kb) * 128:(b + kb + 1) * 128],
                               qrope[pa, j0 + a, b * 128:(b + 1) * 128],
                               False, True, [Bkb[par], Bqr[j0 + a]], [PB[sb0 + half]], sgc=True)

            def att_rest(i):
                b, g, hg = iters[i]
                sl_ = i % 2
                sb0 = 6 if i % 2 == 0 else 2
                ob, db = (4, 5) if i % 2 == 0 else (0, 1)
                h0 = 8 * g + 4 * hg
                j0 = h0 // 2
                ACT(eT[:, sl_, :, :, :].rearrange("p k h q -> p (k h q)"),
                    ps[:, sb0:sb0 + 2, :].rearrange("p k c -> p (k c)"), AF.Exp, [PB[sb0], PB[sb0 + 1]], [BeT[sl_]],
                    scale=0.125, au=False)
                eTv = eT[:, sl_, :, :, :].rearrange("p half (kb a) q -> p half kb a q", kb=2)
                for kb in range(2):
                    for half in range(2):
                        MM(ps[:, ob, half * 256:(half + 1) * 256], vdup[:, par, b + kb, g, :],
                           eTv[:, half, kb, :, :].rearrange("p a q -> p (a q)"),
                           (kb == 0 and half == 0), kb == 1, [Bvd[par][b + kb], BeT[sl_]], [PB[ob]], au=False, sgc=True)
                for kb in range(2):
                    for half in range(2):
                        MM(ps[:, db, half * 256:(half + 1) * 256], onesb[:],
                           eTv[:, half, kb, :, :].rearrange("p a q -> p (a q)"),
                           (kb == 0 and half == 0), kb == 1, [Bconst, BeT[sl_]], [PB[db]], au=False, sgc=True)
                TTo(den[:].rearrange("p (half a q) -> p half a q", half=2, a=2),
                    ps[:, db, :].rearrange("p (half a q) -> p half a q", half=2, a=2),
                    esink[:, h0:h0 + 4].rearrange("p (a half) -> p half a", half=2).unsqueeze(3).to_broadcast([128, 2, 2, 128]),
                    ALU.add, [PB[db], Bconst], [Bden], au=False)
                RECIP(rden[:], den[:], [Bden], [Brden], au=False)
                for half in range(2):
                    pa = slice(half * 64, (half + 1) * 64)
                    TTo(attnT[pa, j0:j0 + 2, b * 128:(b + 1) * 128],
                        ps[pa, ob, half * 256:(half + 1) * 256].rearrange("p (a q) -> p a q", a=2),
                        rden[pa, half * 256:(half + 1) * 256].rearrange("p (a q) -> p a q", a=2),
                        ALU.mult, [PB[ob], Brden], [Bat[j0], Bat[j0 + 1]])

            att_S(0)
            for i in range(8):
                if i + 1 < 8:
                    att_S(i + 1)
                att_rest(i)

            ckpt(6)
            Bm1s = BG("m1s", 4)
            for jg in range(2):
                sa = loadw(11 + jg)
                sb_ = loadw(7 + jg)
                for j in range(4):
                    c = jg * 4 + j
                    bA = nextbank()
                    proj(bA, sa, j, sT, 0, T, BsT)
                    bB = nextbank()
                    proj(bB, sb_, j, hT, 128, T, BhT)
                    sg = sgl[j % 2]
                    ACT(sg[:, 0:T], ps[:, bB, 0:T], AF.Sigmoid, [PB[bB], Bvec], [Bsgs[j % 2]], bias=vcol(V_BIN + 28 + c), au=False)
                    TTo(m1buf[:, j, :], ps[:, bA, 0:T], sg[:, 0:T], ALU.mult, [PB[bA], Bsgs[j % 2]], [Bm1s[j]], au=False)
                sa = loadw(13 + jg)
                sb_ = loadw(9 + jg)
                for j in range(4):
                    c = jg * 4 + j
                    bA = nextbank()
                    proj(bA, sa, j, attnT, 0, T, Bat)
                    bB = nextbank()
                    proj(bB, sb_, j, hT, 128, T, BhT)
                    sg = sgl[j % 2]
                    rt_ = (r3, r4)[j % 2]
                    Brt_ = [Br3, Bra[1]] if j % 2 == 0 else [Brb[1]]
                    ACT(sg[:, 0:T], ps[:, bB, 0:T], AF.Sigmoid, [PB[bB], Bvec], [Bsgs[j % 2]], bias=vcol(V_BIN + 36 + c), au=False)
                    TTo(rt_[:, 0:T], ps[:, bA, 0:T], sg[:, 0:T], ALU.mult, [PB[bA], Bsgs[j % 2]], Brt_, au=False)
                    TTo(mergedT[:, c, 0:T], rt_[:, 0:T], m1buf[:, j, :], ALU.add, Brt_ + [Bm1s[j]], [Bsq[c]])
            for og in range(2):
                s = loadw(15 + og)
                for j in range(4):
                    c = og * 4 + j
                    b = nextbank()
                    proj(b, s, j, mergedT, 0, T, Bsq)
                    TTo(xt[:, c, :], xt[:, c, :], ps[:, b, 0:T], ALU.add, [Bx[c], PB[b]], [Bx[c]], au=False)
            if dbg and ti == 0:
                DMA(lambda e, xt_=xt: e.dma_start(out=ddbg.rearrange("k p t -> p k t"), in_=xt_[:]), [Bx], [Buf("dbgo")], final=True)

            ckpt(7)
            colstats(xt, 0, T, Bx, st1, Bs1, st2, Bs2, sqb, Bsq, 4)
            for k in range(8):
                STT(h2T[:, k, :], xt[:, k, :], vcol(V_G2 + k), st1[:, 0:T], ALU.mult, ALU.mult, [Bx[k], Bvec, Bs1], [Bh2], au=False)
            barrier()
            for g4 in range(4):
                s = loadw(17 + g4)
                for j in range(4):
                    hc = g4 * 4 + j
                    b = nextbank()
                    for k in range(8):
                        MM(ps[:, b, 0:T], wbuf[s][:, k, j * 128:(j + 1) * 128], h2T[:, k, :], k == 0, k == 7,
                           [Bw[s], Bh2], [PB[b]])
                    if hc % 2 == 0:
                        ACT(qTb[:, hc, :], ps[:, b, 0:T], AF.Copy, [PB[b]], [BqT[hc]])
                    else:
                        CP(qTb[:, hc, :], ps[:, b, 0:T], [PB[b]], [BqT[hc]])
            v16v = v16[:, :, :].rearrange("p (h c) k -> p h c k", c=2)
            i16fv = i16f[:, :, :].rearrange("p (h c) k -> p h c k", c=2)
            B4 = [128, 8, 16, 16]
            for tc in range(2):
                tcs = slice(tc * 128, (tc + 1) * 128)
                for g4 in range(4):
                    for l in range(4):
                        hc = g4 * 4 + l
                        MM(ps[:, 5, l * 128:(l + 1) * 128], qTb[:, hc, tcs], skb[:, hc, :], True, True,
                           [BqT[hc], Bsk], [PB[5]])
                    def L1(step, l):
                        hc = g4 * 4 + l
                        src_ = ps[:, 5, l * 128:(l + 1) * 128]
                        if step == 0:
                            OP("dve", lambda e: e.max(out=v16[:, hc, 0:8], in_=src_), [PB[5]], [Bv16[hc]])
                        elif step == 1:
                            OP("dve", lambda e: e.max_index(out=i16[:, hc, 0:8], in_max=v16[:, hc, 0:8], in_values=src_),
                               [PB[5], Bv16[hc]], [Bi16[hc]])
                        elif step == 2:
                            OP("dve", lambda e: e.match_replace(out=scw[:, l, :], in_to_replace=v16[:, hc, 0:8],
                                                               in_values=src_, imm_value=-1e30),
                               [PB[5], Bv16[hc]], [Bscw[l]], True)
                        elif step == 3:
                            OP("dve", lambda e: e.max(out=v16[:, hc, 8:16], in_=scw[:, l, :]), [Bscw[l]], [Bv16[hc]], True)
                        else:
                            OP("dve", lambda e: e.max_index(out=i16[:, hc, 8:16], in_max=v16[:, hc, 8:16], in_values=scw[:, l, :]),
                               [Bscw[l], Bv16[hc]], [Bi16[hc]], True)
                    for step in (0, 2, 1, 3, 4):
                        for l in range(4):
                            L1(step, l)
                CP(i16f[:], i16[:], [Bi16], [Bi16f], au=False)
                TTo(cand[:], v16v[:, :, 0, :].unsqueeze(3).to_broadcast(B4), v16v[:, :, 1, :].unsqueeze(2).to_broadcast(B4),
                    ALU.add, [Bv16], [Bcand])
                def L2(step, h):
                    src_ = cand[:, h, :, :].rearrange("p a b -> p (a b)")
                    if step == 0:
                        OP("dve", lambda e: e.max(out=best[:, h, 0:8], in_=src_), [Bcand], [Bbest[h]], True)
                    elif step == 1:
                        OP("dve", lambda e: e.max_index(out=posu[:, h, 0:8], in_max=best[:, h, 0:8], in_values=src_),
                           [Bcand, Bbest[h]], [Bpos[h]], True)
                    elif step == 2:
                        OP("dve", lambda e: e.match_replace(out=work2[:, h, :], in_to_replace=best[:, h, 0:8],
                                                           in_values=src_, imm_value=-1e30), [Bcand, Bbest[h]], [Bw2[h]], True)
                    elif step == 3:
                        OP("dve", lambda e: e.max(out=best[:, h, 8:16], in_=work2[:, h, :]), [Bw2[h]], [Bbest[h]], True)
                    else:
                        OP("dve", lambda e: e.max_index(out=posu[:, h, 8:16], in_max=best[:, h, 8:16], in_values=work2[:, h, :]),
                           [Bw2[h], Bbest[h]], [Bpos[h]], True)
                for step in (0, 2, 1, 3, 4):
                    for h in range(8):
                        L2(step, h)
                CP(posf[:], posu[:], [Bpos], [Btk], au=False)
                OP("dve", lambda e: e.tensor_single_scalar(out=k1u[:], in_=posu[:], scalar=4, op=ALU.logical_shift_right),
                   [Bpos], [Btk])
                CP(k1f[:], k1u[:], [Btk], [Btk], au=False)
                STT(k2f[:], k1f[:], -16.0, posf[:], ALU.mult, ALU.add, [Btk], [Btk], au=False)
                TTo(ebuf[:], best[:], best[:, :, 0:1].to_broadcast([128, 8, 16]), ALU.subtract, [Bbest], [Btk], au=False)
                ACT(ebuf[:], ebuf[:], AF.Exp, [Btk], [Btk], au=False)
                OP("dve", lambda e: e.tensor_reduce(out=Zs[:], in_=ebuf[:], axis=AX.X, op=ALU.add), [Btk], [Btk])
                RECIP(Zs[:], Zs[:], [Btk], [Btk], au=False)
                TTo(gate[:], ebuf[:], Zs[:, :].unsqueeze(2).to_broadcast([128, 8, 16]), ALU.mult, [Btk], [Btk], au=False)
                io16 = cmf[:, C_IOTA16:C_IOTA16 + 16].unsqueeze(1).unsqueeze(1).to_broadcast(B4)
                for (kf, cidx, dst) in ((k1f, 0, av), (k2f, 1, bvv)):
                    TTo(E1[:], kf[:].unsqueeze(3).to_broadcast(B4), io16, ALU.is_equal, [Btk, Bcm], [BE1])
                    TTo(E1[:], E1[:], i16fv[:, :, cidx, :].unsqueeze(2).to_broadcast(B4), ALU.mult, [BE1, Bi16f], [BE1])
                    OP("dve", lambda e, dst=dst: e.tensor_reduce(out=dst[:], in_=E1[:], axis=AX.X, op=ALU.add), [BE1], [Btk], True)
                for idx, srcv in enumerate((av, bvv, gate)):
                    OP("pe", lambda e, idx=idx, srcv=srcv: e.transpose(out=ps[:, 5, idx * 128:(idx + 1) * 128],
                                                                     in_=srcv[:].rearrange("p h k -> p (h k)"),
                                                                     identity=cmf[:, C_ID:C_ID + 128]),
                       [Btk, Bcm], [PB[5]])
                CP(abgT[:, tc, :, :].rearrange("p a b -> p (a b)"), ps[:, 5, 0:384], [PB[5]], [Babg], au=False)

            ckpt(8)
            barrier()
            for tb in range(T // 8):
                sl_ = tb % 2
                bk0 = 4 + 2 * (tb % 2)
                for i in range(8):
                    t = tb * 8 + i
                    tc, tl = t // 128, t % 128
                    TS(Pt[:, sl_, i, :], iotab[:], abgT[:, tc, 0, tl:tl + 1], ALU.is_equal, [Bconst, Babg], [BPt[sl_][i]],
                       s2=abgT[:, tc, 2, tl:tl + 1], op1=ALU.mult, au=False)
                    TS(Qt[:, sl_, i, :], iotab[:], abgT[:, tc, 1, tl:tl + 1], ALU.is_equal, [Bconst, Babg], [BQt[sl_][i]],
                       au=False)
                for i in range(8):
                    bank = bk0 + i // 4
                    MM(ps[:, bank, (i % 4) * 128:(i % 4 + 1) * 128], Qt[:, sl_, i, :], Pt[:, sl_, i, :], True, True,
                       [BQt[sl_][i], BPt[sl_][i]], [PB[bank]], au=False)
                for hb in range(2):
                    t0 = tb * 8 + hb * 4
                    ACT(G[:, :, t0:t0 + 4], ps[:, bk0 + hb, :].rearrange("p (t i) -> p i t", t=4), AF.Copy,
                        [PB[bk0 + hb]], [BGm[tb * 2 + hb]])

            ckpt(9)
            PBA = [PB[4], PB[5]]

            def stageA(ec):
                eg, cc, hs = ec // 2, ec % 2, ec % 2
                sl2 = eg % 2
                if cc == 0:
                    DMA(lambda e: e.dma_start(out=UTs[:, sl2, :, :].rearrange("p k c -> p (k c)"), in_=dscr[UV0 + eg]),
                        [Bscr[UV0 + eg]], [BUT[sl2]])
                    DMA(lambda e: e.dma_start(out=Vs[:, sl2, :, :].rearrange("p k c -> p (k c)"), in_=dscr[UV0 + 64 + eg]),
                        [Bscr[UV0 + 64 + eg]], [BVs[sl2]])
                for k in range(8):
                    MM(ps[:, 4 + hs, 0:256], UTs[:, sl2, k, cc * 128:(cc + 1) * 128], h2T[:, k, :],
                       k == 0, k == 7, [BUT[sl2], Bh2], [PBA[hs]], au=False)
                ACT(gl[:, hs, :], ps[:, 4 + hs, 0:256], AF.Gelu, [PBA[hs]], [Bgl[hs]], au=False)
                TTo(GA[:, hs, :], gl[:, hs, :], G[:, ec, :], ALU.mult, [Bgl[hs], BGm], [BGA[hs]])

            def stageV(ec):
                eg, cc, hs = ec // 2, ec % 2, ec % 2
                sl2 = eg % 2
                for dk in range(8):
                    MM(ps[:, dk // 2, (dk % 2) * 256:(dk % 2 + 1) * 256], Vs[:, sl2, cc, dk * 128:(dk + 1) * 128],
                       GA[:, hs, :], (ec == 0 and dk % 2 == 0), ec == 127, [BVs[sl2], BGA[hs]], [PB[dk // 2]],
                       au=False, sgc=True)

            stageA(0)
            for ec in range(128):
                if ec + 1 < 128:
                    stageA(ec + 1)
                stageV(ec)
                if ec == 2 and ti + 1 < NT:
                    capture[0] = []
                    prefetch(ti + 1)
                    pending = capture[0]
                    capture[0] = None
                if ec >= 2 and ti + 1 < NT and pending:
                    replay(pending.pop(0))
            if ti + 1 < NT:
                while pending:
                    replay(pending.pop(0))

            ckpt(10)
            for dk in range(8):
                TTo(xt[:, dk, :], xt[:, dk, :], ps[:, dk // 2, (dk % 2) * 256:(dk % 2 + 1) * 256], ALU.add,
                    [Bx[dk], PB[dk // 2]], [Bx[dk]], au=False)
            colstats(xt, 0, T, Bx, st1, Bs1, st3, Bs3, sqn, Bsqn, 4, sq_au=False)
            for k in range(8):
                STT(xt[:, k, :], xt[:, k, :], vcol(V_GF + k), st1[:, 0:T], ALU.mult, ALU.mult,
                    [Bx[k], Bvec, Bs1], [Bx[k]], au=False)
            DMA(lambda e, ti=ti, xt_=xt: e.dma_start(out=dout[:, :, ti * T:(ti + 1) * T].rearrange("k p t -> p k t"), in_=xt_[:, :, :]),
                [Bx], [Bout], False, final=True)

        P.emit()
    return nc


def _prep_shared(inp):
    f = np.float32
    w_in = np.asarray(inp["w_in"], f)[0]
    b_in = np.asarray(inp["b_in"], f)[0]
    blk = lambda base, c: list(range(base + c * 128, base + (c + 1) * 128))
    chunks = []
    for pr in range(2):
        for c in range(4):
            chunks.append(blk(0, pr * 4 + c))
        for c in range(4):
            chunks.append(blk(1024, pr * 4 + c))
    for c in range(8):
        chunks.append(blk(2048, c))
    k0 = list(range(3072, 3136))
    k1 = list(range(3136, 3200))
    chunks.append(k0 + k0)
    chunks.append(k1 + k1)
    chunks.append(list(range(3200, 3328)))
    chunks.append(list(range(3200, 3328)))
    for c in range(8):
        chunks.append(blk(3328, c))
    for c in range(8):
        chunks.append(blk(4352, c))
    assert len(chunks) == 44
    colidx = np.array(sum(chunks, []), dtype=np.int64)
    w_perm = w_in[:, colidx]
    b_perm = b_in[colidx]
    mats = [w_perm[:, g * 512:(g + 1) * 512] for g in range(11)]
    for name in ("w_conv_out", "w_attn_o", "w_out"):
        w = np.asarray(inp[name], f)[0]
        mats += [w[:, 0:512], w[:, 512:1024]]
    wpq = np.asarray(inp["w_peer_q"], f)[0]
    mats += [wpq[:, g * 512:(g + 1) * 512] for g in range(4)]
    assert len(mats) == NMIXG
    wall = np.empty((NPIECE, 128, 2048), f)
    for g, m in enumerate(mats):
        a = m.reshape(8, 128, 512).transpose(1, 0, 2)
        wall[2 * g] = a[:, 0:4, :].reshape(128, 2048)
        wall[2 * g + 1] = a[:, 4:8, :].reshape(128, 2048)
    U = np.asarray(inp["peer_u"], f)[0]
    V = np.asarray(inp["peer_v"], f)[0]
    wall[UV0:UV0 + 64] = U.reshape(64, 256, 8, 128).transpose(0, 3, 2, 1).reshape(64, 128, 2048)
    wall[UV0 + 64:UV0 + 128] = V.reshape(64, 2, 128, 1024).transpose(0, 2, 1, 3).reshape(64, 128, 2048)

    vec = np.zeros((128, NV), f)
    col = lambda v: np.asarray(v, f).reshape(-1, 128).T
    vec[:, V_G1:V_G1 + 8] = col(inp["norm1_g"][0])
    vec[:, V_BIN:V_BIN + 44] = col(b_perm)
    vec[:, V_CB:V_CB + 8] = col(inp["conv_b"][0])
    vec[:, V_LNG:V_LNG + 8] = col(inp["conv_ln_g"][0])
    vec[:, V_LNB:V_LNB + 8] = col(inp["conv_ln_b"][0])
    vec[:, V_G2:V_G2 + 8] = col(inp["norm2_g"][0])
    vec[:, V_GF:V_GF + 8] = col(inp["final_g"])
    p = np.arange(128)
    invf = (np.float32(10000.0) ** (-(np.arange(32, dtype=f) * f(2.0) / f(64)))).astype(f)
    vec[:, V_INVF] = invf[p % 32]
    vec[:, V_SGN] = np.where(p % 64 < 32, -1.0, 1.0)
    cw = np.asarray(inp["conv_w"], f)[0]
    vec[:, V_CW:V_CW + 248] = cw.reshape(31, 8, 128).transpose(2, 1, 0).reshape(128, 248)

    cm = np.zeros((128, NCM), f)
    cm[:, C_ID:C_ID + 128] = np.eye(128, dtype=f)
    cm[p, C_PERM + (p ^ 32)] = 1.0
    cm[:, C_IOTA:C_IOTA + 128] = np.arange(128, dtype=f)[None, :]
    cm[:, C_IOTA16:C_IOTA16 + 16] = np.arange(16, dtype=f)[None, :]
    kk = np.arange(128)[:, None]
    qq = np.arange(128)[None, :]
    NEGM = f(-240000.0)
    m_prev = np.where(kk > qq, f(0), NEGM).astype(f)
    m_cur = np.where(kk <= qq, f(0), NEGM).astype(f)
    cm[:, C_MASK1:C_MASK1 + 128] = m_prev
    cm[:, C_MASK1 + 128:C_MASK1 + 256] = m_cur
    cm[:, C_MASK0 + 128:C_MASK0 + 256] = m_cur
    rows = np.concatenate([b_in[3200:3328], np.asarray(inp["attn_sinks"], f)[0]]).reshape(1, 144).astype(f)
    sk = np.asarray(inp["peer_sub_keys"], f)[0]
    skT = np.ascontiguousarray(sk.transpose(3, 0, 1, 2).reshape(128, 2048))
    return dict(wall=wall, vec=vec, cm=cm, rows=rows, skT=skT), m_prev, NEGM


_NC_CACHE = {}


def kernel(**inputs):
    x = np.asarray(inputs["x"], np.float32)
    pos = np.asarray(inputs["positions"], np.int32)
    B, S, _ = x.shape
    TOK = S // 2
    NT = TOK // TT
    shared, m_prev, NEGM = _prep_shared(inputs)
    in_maps = []
    for core in range(8):
        b, hs = core // 2, core % 2
        s0 = hs * TOK
        xT = np.zeros((8, 128, TOK + HALO), np.float32)
        pp = np.zeros((1, TOK + HALO), np.int32)
        if hs == 0:
            xs = x[b, 0:TOK]
            xT[:, :, HALO:] = xs.T.reshape(8, 128, TOK)
            pp[0, HALO:] = pos[b, 0:TOK]
        else:
            xs = x[b, s0 - HALO:s0 + TOK]
            xT[:] = xs.T.reshape(8, 128, TOK + HALO)
            pp[0] = pos[b, s0 - HALO:s0 + TOK]
        vec = shared["vec"].copy()
        vec[:, V_HV] = 0.0 if hs == 0 else 1.0
        cm = shared["cm"].copy()
        if hs == 0:
            cm[:, C_MASK0:C_MASK0 + 128] = NEGM
        else:
            cm[:, C_MASK0:C_MASK0 + 128] = m_prev
        in_maps.append(dict(xT=xT, pos=pp, wall=shared["wall"], vec=vec, cm=cm, rows=shared["rows"], skT=shared["skT"]))
    if NT not in _NC_CACHE:
        _NC_CACHE[NT] = build_nc(NT)
    nc = _NC_CACHE[NT]
    res = run_bass_kernel_spmd(nc, in_maps, core_ids=list(range(8)))
    out = np.empty((B, S, D), np.float32)
    for core in range(8):
        b, hs = core // 2, core % 2
        oT = np.asarray(res.results[core]["outT"], np.float32)
        out[b, hs * TOK:(hs + 1) * TOK, :] = oT.reshape(1024, TOK).T
    return out
```

```python
import numpy as np
from contextlib import ExitStack
import concourse.bass as bass
import concourse.mybir as mybir
from concourse.bass_utils import run_bass_kernel_spmd

F32 = mybir.dt.float32
BF16 = mybir.dt.bfloat16
I32 = mybir.dt.int32
U32 = mybir.dt.uint32
AF = mybir.ActivationFunctionType
ALU = mybir.AluOpType
AX = mybir.AxisListType


class Buf:
    __slots__ = ("name", "w", "r")

    def __init__(self, name=""):
        self.name = name
        self.w = None
        self.r = []


class BG(list):
    def __init__(self, name, n):
        super().__init__(Buf("%s%d" % (name, i)) for i in range(n))


def _flat(bs):
    out = []
    for b in bs:
        if isinstance(b, list):
            out.extend(_flat(b))
        else:
            out.append(b)
    return out


class _Ins:
    __slots__ = ("eng", "fn", "deps", "dma", "idx", "sig", "cnt", "semi", "final")

    def __init__(self, eng, fn, dma):
        self.eng = eng
        self.fn = fn
        self.deps = set()
        self.dma = dma
        self.sig = False
        self.cnt = 0
        self.semi = 0
        self.final = False


class Prog:
    NDMA_SEM = 12
    ENGS = ("pe", "act", "dve", "pool", "sp")

    def __init__(self, nc, es):
        self.nc = nc
        self.es = es
        self.q = {e: [] for e in self.ENGS}
        self.all = []
        self.dma_engine = "sp"
        self.halted = False

    def _add(self, ins, reads, writes):
        if self.halted:
            return ins
        reads = _flat(reads)
        writes = _flat(writes)
        for b in reads:
            if b.w is not None:
                ins.deps.add(b.w)
        for b in writes:
            if b.w is not None:
                ins.deps.add(b.w)
            for r in b.r:
                ins.deps.add(r)
        ins.deps.discard(ins)
        for b in reads:
            b.r.append(ins)
        for b in writes:
            b.w = ins
            b.r = []
        ins.idx = len(self.all)
        self.all.append(ins)
        self.q[ins.eng].append(ins)
        return ins

    def op(self, eng, fn, reads=(), writes=()):
        return self._add(_Ins(eng, fn, False), reads, writes)

    def dma(self, fn, reads=(), writes=(), final=False, eng=None):
        ins = _Ins(eng or self.dma_engine, fn, True)
        ins.final = final
        return self._add(ins, reads, writes)

    def emit(self):
        nc = self.nc
        for ins in self.all:
            for d in ins.deps:
                if d.eng == "pe" and ins.eng == "pe" and not d.dma and not ins.dma:
                    continue
                d.sig = True
            if ins.final:
                ins.sig = True
        sems = {e: self.es.enter_context(nc.semaphore("s_" + e)) for e in self.ENGS}
        dsems = [self.es.enter_context(nc.semaphore("d%d" % i)) for i in range(self.NDMA_SEM)]
        cnt = {e: 0 for e in self.ENGS}
        ndma = 0
        dma_prev = {}
        last_on_sem = [None] * self.NDMA_SEM
        for ins in self.all:
            if ins.dma:
                ins.semi = ndma % self.NDMA_SEM
                ins.cnt = 16 * (ndma // self.NDMA_SEM + 1)
                dma_prev[ins] = last_on_sem[ins.semi]
                last_on_sem[ins.semi] = ins
                ndma += 1
            elif ins.sig:
                cnt[ins.eng] += 1
                ins.cnt = cnt[ins.eng]
        finals = [i for i in self.all if i.final]
        block = self.es.enter_context(nc.Block())

        def run(engname, e):
            waited = {}

            def wait_for(d):
                if d.dma:
                    key = ("d", d.semi)
                    sem = dsems[d.semi]
                else:
                    key = ("e", d.eng)
                    sem = sems[d.eng]
                if waited.get(key, 0) >= d.cnt:
                    return
                e.wait_ge(sem, d.cnt)
                waited[key] = d.cnt

            for ins in self.q[engname]:
                for d in sorted(ins.deps, key=lambda z: z.idx):
                    if (d.eng == "pe" and engname == "pe" and not d.dma and not ins.dma):
                        continue
                    wait_for(d)
                if ins.dma:
                    p = dma_prev[ins]
                    if p is not None:
                        wait_for(p)
                h = ins.fn(e)
                if ins.dma:
                    h.then_inc(dsems[ins.semi], 16)
                elif ins.sig:
                    h.then_inc(sems[engname], 1)
            if engname == self.dma_engine:
                for f in finals:
                    wait_for(f)

        @block.sync
        def _(e):
            run("sp", e)

        @block.tensor
        def _(e):
            run("pe", e)

        @block.scalar
        def _(e):
            run("act", e)

        @block.vector
        def _(e):
            run("dve", e)

        @block.gpsimd
        def _(e):
            run("pool", e)


D = 1024
KC = 8
TT = 256
HALO = 128
EPSV = 1e-6
NMIXG = 21
NPIECE = 2 * NMIXG + 128
UV0 = 2 * NMIXG
V_G1, V_BIN, V_CB, V_LNG, V_LNB, V_G2, V_GF, V_INVF, V_SGN, V_HV, V_CW = 0, 8, 52, 60, 68, 76, 84, 92, 93, 94, 95
NV = 95 + 248
C_ID, C_PERM, C_IOTA, C_IOTA16, C_MASK0, C_MASK1 = 0, 128, 256, 384, 400, 656
NCM = 912
MAGIC = 12582912.0
CW1 = 6.28125
CW2 = 2.0 * np.pi - 6.28125


def _chunk_val(c):
    return (c // 4) * 8 + (c % 4)


def _chunk_gate(c):
    return (c // 4) * 8 + 4 + (c % 4)


class _Stop(Exception):
    pass


def build_nc(NT, dbg=False, stop=None):
    T = TT
    TOK = NT * T
    TOKH = TOK + HALO
    nc = bass.Bass("TRN2", target_bir_lowering=False)
    dx = nc.dram_tensor("xT", [8, 128, TOKH], F32, kind="ExternalInput").ap()
    dpos = nc.dram_tensor("pos", [1, TOKH], I32, kind="ExternalInput").ap()
    dwall = nc.dram_tensor("wall", [NPIECE, 128, 2048], F32, kind="ExternalInput").ap()
    dvec = nc.dram_tensor("vec", [128, NV], F32, kind="ExternalInput").ap()
    dcm = nc.dram_tensor("cm", [128, NCM], F32, kind="ExternalInput").ap()
    drow = nc.dram_tensor("rows", [1, 144], F32, kind="ExternalInput").ap()
    dsk = nc.dram_tensor("skT", [128, 2048], F32, kind="ExternalInput").ap()
    dout = nc.dram_tensor("outT", [8, 128, TOK], F32, kind="ExternalOutput").ap()
    dscr = nc.dram_tensor("wscr", [NPIECE, 128, 2048], BF16, kind="Internal").ap()
    ddiag = nc.dram_tensor("dgscr", [8, 128, 31 * 128], BF16, kind="Internal").ap()
    if dbg:
        ddbg = nc.dram_tensor("dbg", [8, 128, T], F32, kind="ExternalOutput").ap()

    es = ExitStack()
    with es:
        def sb(name, shape, dt):
            return es.enter_context(nc.sbuf_tensor("sb_" + name, shape, dt))

        P = Prog(nc, es)
        ps = es.enter_context(nc.psum_tensor("ps", [128, 8, 512], F32))
        PB = [Buf("bank%d" % i) for i in range(8)]

        vec = sb("vec", [128, NV], F32)
        cmf = sb("cmf", [128, NCM], F32)
        identb = sb("identb", [128, 128], BF16)
        permb = sb("permb", [128, 128], BF16)
        onesb = sb("onesb", [128, 128], BF16)
        onesmb = sb("onesmb", [128, 128], BF16)
        onesmf = sb("onesmf", [128, 128], F32)
        iotab = sb("iotab", [128, 128], BF16)
        maskb = sb("maskb", [128, 2, 2, 2, 128], BF16)
        esink = sb("esink", [128, 16], F32)
        bvb = sb("bvb", [128, 128], F32)
        skf = sb("skf", [128, 2048], F32)
        skb = sb("skb", [128, 16, 128], BF16)
        xt_a = sb("xt", [128, 8, T], F32)
        xt_b = sb("xt2", [128, 8, T], F32)
        xts = [xt_a, xt_b]
        sqn = sb("sqn", [128, 8, T], BF16)
        rn1 = sb("rn1", [128, T], F32)
        ubuf = sb("ubuf", [128, 2, 8, 32 + T], BF16)
        kbuf = sb("kbuf", [128, 2, 2, 128 + T], BF16)
        vdup = sb("vdup", [128, 2, 3, 2, 128], BF16)
        cosT = sb("cosT", [128, 128 + T], F32)
        sinT = sb("sinT", [128, 128 + T], F32)
        posi = sb("posi", [128, 128 + T], I32)
        r1 = sb("r1", [128, 128 + T], F32)
        r2 = sb("r2", [128, 128 + T], F32)
        r3 = sb("r3", [128, 128 + T], F32)
        st1 = sb("st1", [128, 128 + T], F32)
        st2 = sb("st2", [128, 128 + T], F32)
        st3 = sb("st3", [128, 128 + T], F32)
        sg1 = sb("sg1", [128, 128 + T], F32)
        sg2 = sb("sg2", [128, 128 + T], F32)
        m1buf = sb("m1buf", [128, 4, T], F32)
        qb = sb("qb", [128, 128 + T], BF16)
        qb2 = sb("qb2", [128, 128 + T], BF16)
        r4 = sb("r4", [128, 128 + T], F32)
        eT = sb("eT", [128, 2, 2, 4, 128], BF16)
        den = sb("den", [128, 512], F32)
        rden = sb("rden", [128, 512], F32)
        rscr = sb("rscr", [128, 512], F32)
        h2T = sb("h2T", [128, 8, T], BF16)
        gl = sb("gl", [128, 2, T], BF16)
        GA = sb("GA", [128, 2, T], BF16)
        Pt = sb("Pt", [128, 2, 8, 128], BF16)
        Qt = sb("Qt", [128, 2, 8, 128], BF16)
        UTs = sb("UTs", [128, 2, 8, 256], BF16)
        Vs = sb("Vs", [128, 2, 2, 1024], BF16)
        v16 = sb("v16", [128, 16, 16], F32)
        i16 = sb("i16", [128, 16, 16], U32)
        i16f = sb("i16f", [128, 16, 16], F32)
        best = sb("best", [128, 8, 16], F32)
        posu = sb("posu", [128, 8, 16], U32)
        k1u = sb("k1u", [128, 8, 16], U32)
        posf = sb("posf", [128, 8, 16], F32)
        k1f = sb("k1f", [128, 8, 16], F32)
        k2f = sb("k2f", [128, 8, 16], F32)
        ebuf = sb("ebuf", [128, 8, 16], F32)
        gate = sb("gate", [128, 8, 16], F32)
        Zs = sb("Zs", [128, 8], F32)
        av = sb("av", [128, 8, 16], F32)
        bvv = sb("bvv", [128, 8, 16], F32)
        abgT = sb("abgT", [128, 2, 3, 128], F32)
        one1 = sb("one1", [128, 4], F32)
        arena = sb("arena", [128, 32768], BF16)

        def av_(off, nbytes, dt):
            a = arena[:, off // 2:(off + nbytes) // 2]
            return a if dt == BF16 else a.bitcast(dt)

        K = 1024
        wbuf = [av_(0, 8 * K, BF16).rearrange("p (k c) -> p k c", k=8),
                av_(8 * K, 8 * K, BF16).rearrange("p (k c) -> p k c", k=8),
                av_(16 * K, 8 * K, BF16).rearrange("p (k c) -> p k c", k=8),
                av_(24 * K, 8 * K, BF16).rearrange("p (k c) -> p k c", k=8)]
        diag = [av_(16 * K, 7936, BF16).rearrange("p (j c) -> p j c", j=31),
                av_(24 * K, 7936, BF16).rearrange("p (j c) -> p j c", j=31)]
        hT = av_(32 * K, 6 * K, BF16).rearrange("p (k c) -> p k c", k=8)
        sqb = av_(38 * K, 6 * K, BF16).rearrange("p (k c) -> p k c", k=8)
        ysb = av_(44 * K, 8 * K, F32).rearrange("p (k c) -> p k c", k=8)
        sT = av_(52 * K, 4 * K, BF16).rearrange("p (k c) -> p k c", k=8)
        qrope = av_(56 * K, 4 * K, BF16).rearrange("p (k c) -> p k c", k=8)
        attnT = av_(60 * K, 4 * K, BF16).rearrange("p (k c) -> p k c", k=8)
        mergedT = sqb
        stin = [av_(i * 8 * K, 8 * K, F32) for i in range(4)]
        stout = [av_(32 * K + i * 4 * K, 4 * K, BF16) for i in range(4)]
        dgst = [av_(48 * K, 7936, BF16).rearrange("p (j c) -> p j c", j=31),
                av_(56 * K, 7936, BF16).rearrange("p (j c) -> p j c", j=31)]
        qTb = av_(16 * K, 8 * K, BF16).rearrange("p (k c) -> p k c", k=16)
        cand = av_(24 * K, 8 * K, F32).rearrange("p (h a b) -> p h a b", h=8, a=16)
        work2 = av_(32 * K, 8 * K, F32).rearrange("p (h c) -> p h c", h=8)
        E1 = av_(40 * K, 8 * K, F32).rearrange("p (h a b) -> p h a b", h=8, a=16)
        scw = av_(48 * K, 2 * K, F32).rearrange("p (l c) -> p l c", l=4)
        G = arena[:, :].rearrange("p (i t) -> p i t", i=128)
        outtmp = av_(0, 8 * K, F32).rearrange("p (k c) -> p k c", k=8)

        ATOK = Buf("atok")

        def vcol(i):
            return vec[:, i:i + 1]

        capture = [None]

        def OP(eng, fn, reads, writes, arena_use=False):
            if capture[0] is not None:
                capture[0].append(("op", (eng, fn, list(reads), list(writes), arena_use)))
                return None
            if arena_use:
                reads = list(reads) + [ATOK]
            return P.op(eng, fn, reads, writes)

        def DMA(fn, reads, writes, arena_use=False, final=False, eng=None):
            if capture[0] is not None:
                capture[0].append(("dma", (fn, list(reads), list(writes), arena_use, final, eng)))
                return None
            if arena_use:
                reads = list(reads) + [ATOK]
            return P.dma(fn, reads, writes, final=final, eng=eng)

        def replay(item):
            kind, args = item
            if kind == "op":
                OP(*args)
            else:
                DMA(*args)

        def barrier():
            P.op("pool", lambda e: e.memset(one1[:, 0:1], 0.0), [], [ATOK])

        def MM(out, lhsT, rhs, start, stop, reads, writes, au=True, sgc=False):
            OP("pe", lambda e: e.matmul(out, lhsT=lhsT, rhs=rhs, start=start, stop=stop,
                                        skip_group_check=sgc), reads, writes, au)

        def ACT(out, in_, func, reads, writes, bias=None, scale=None, au=True):
            kw = {}
            if bias is not None:
                kw["bias"] = bias
            if scale is not None:
                kw["scale"] = scale
            OP("act", lambda e: e.activation(out=out, in_=in_, func=func, **kw), reads, writes, au)

        def TTo(out, in0, in1, op, reads, writes, eng="dve", au=True):
            OP(eng, lambda e: e.tensor_tensor(out=out, in0=in0, in1=in1, op=op), reads, writes, au)

        def TS(out, in0, s1, op0, reads, writes, s2=None, op1=None, eng="dve", au=True):
            if op1 is None:
                OP(eng, lambda e: e.tensor_scalar(out=out, in0=in0, scalar1=s1, scalar2=None, op0=op0),
                   reads, writes, au)
            else:
                OP(eng, lambda e: e.tensor_scalar(out=out, in0=in0, scalar1=s1, scalar2=s2, op0=op0, op1=op1),
                   reads, writes, au)

        def STT(out, in0, scalar, in1, op0, op1, reads, writes, au=True):
            OP("dve", lambda e: e.scalar_tensor_tensor(out=out, in0=in0, scalar=scalar, in1=in1,
                                                       op0=op0, op1=op1), reads, writes, au)

        def CP(out, in_, reads, writes, eng="dve", au=True):
            OP(eng, lambda e: e.tensor_copy(out=out, in_=in_), reads, writes, au)

        def RECIP(out, in_, reads, writes, au=True):
            OP("dve", lambda e: e.reciprocal(out=out, in_=in_), reads, writes, au)

        Bvec, Bcm, Bconst, Bsk = Buf("vec"), Buf("cm"), Buf("const"), Buf("sk")
        Bxs = [BG("xta", 8), BG("xtb", 8)]
        Bsqn, Brn1 = BG("sqn", 8), Buf("rn1")
        Bscr = [Buf("scr%d" % i) for i in range(NPIECE)]
        Bst_in = BG("sti", 4)
        Bst_out = BG("sto", 4)
        Bw = [Buf("w0"), Buf("w1")]
        Bdiag = [Buf("dg0"), Buf("dg1")]
        Bw = Bw + Bdiag
        BhT, Bsq, Bys, BsT, Bqr, Bat = BG("hT", 8), BG("sq", 8), BG("ys", 8), BG("sT", 8), BG("qr", 8), BG("at", 8)
        Bub = [BG("ub0_", 8), BG("ub1_", 8)]
        Bkb = [Buf("kb0"), Buf("kb1")]
        Bvd = [[Buf("vd%d%d" % (p, b)) for b in range(3)] for p in range(2)]
        Bcs, Bposi, Br1, Br2, Br3 = Buf("cs"), Buf("posi"), Buf("r1"), Buf("r2"), Buf("r3")
        Bs1, Bs2, Bs3, Bg1, Bg2, Bm1, Bqb = Buf("st1"), Buf("st2"), Buf("st3"), Buf("sg1"), Buf("sg2"), Buf("m1"), Buf("qb")
        BeT = [Buf("eT0"), Buf("eT1")]
        Bden, Brden, Brscr = Buf("den"), Buf("rden"), Buf("rscr")
        Bh2, Bgl, BGA = Buf("h2T"), [Buf("gl0"), Buf("gl1")], [Buf("GA0"), Buf("GA1")]
        BPt, BQt = [BG("Pt0_", 8), BG("Pt1_", 8)], [BG("Qt0_", 8), BG("Qt1_", 8)]
        BUT, BVs = [Buf("UT0"), Buf("UT1")], [Buf("Vs0"), Buf("Vs1")]
        Bv16, Bi16, Bi16f, Bbest, Bpos, Btk, Babg = BG("v16_", 16), BG("i16_", 16), Buf("i16f"), BG("best", 8), BG("pos", 8), Buf("tk"), Buf("abg")
        BqT, Bcand, Bw2, BE1, Bscw, BGm, Bot = BG("qTb", 16), Buf("cand"), BG("w2_", 8), Buf("E1"), BG("scw", 4), BG("G", 64), Buf("ot")
        Bsgs = BG("sgs", 2)
        Bra, Brb, Bqbs = BG("ra", 2), BG("rb", 2), BG("qbs", 2)
        Bout = Buf("out")

        DMA(lambda e: e.dma_start(out=vec[:], in_=dvec[:, :]), [], [Bvec])
        DMA(lambda e: e.dma_start(out=cmf[:], in_=dcm[:, :]), [], [Bcm])
        DMA(lambda e: e.dma_start(out=skf[:], in_=dsk[:, :]), [], [Bsk])
        DMA(lambda e: e.dma_start(out=bvb[:], in_=drow[0:1, 0:128].partition_broadcast(128)[:, 0, :]), [], [Bconst])
        DMA(lambda e: e.dma_start(out=esink[:], in_=drow[0:1, 128:144].partition_broadcast(128)[:, 0, :]), [], [Bconst])
        CP(identb[:], cmf[:, C_ID:C_ID + 128], [Bcm], [Bconst], au=False)
        CP(permb[:], cmf[:, C_PERM:C_PERM + 128], [Bcm], [Bconst], au=False)
        CP(iotab[:], cmf[:, C_IOTA:C_IOTA + 128], [Bcm], [Bconst], au=False)
        OP("dve", lambda e: e.memset(onesb[:], 1.0), [], [Bconst])
        OP("dve", lambda e: e.memset(onesmb[:], 1.0 / 1024.0), [], [Bconst])
        OP("dve", lambda e: e.memset(onesmf[:], 1.0 / 1024.0), [], [Bconst])
        for var in range(2):
            for kb in range(2):
                for h4 in range(2):
                    c0 = (C_MASK0 if var == 0 else C_MASK1) + kb * 128
                    CP(maskb[:, var, kb, h4, :], cmf[:, c0:c0 + 128], [Bcm], [Bconst], au=False)
        ACT(esink[:], esink[:], AF.Exp, [Bconst], [Bconst], au=False)
        CP(skb[:].rearrange("p a b -> p (a b)"), skf[:], [Bsk], [Bsk], au=False)

        cast_engs = ["act", "dve", "pool"]
        NPR = NPIECE if stop != 1 else 0

        def pl_load(i):
            s = i % 4
            DMA(lambda e: e.dma_start(out=stin[s], in_=dwall[i]), [], [Bst_in[s]], True)

        def pl_cast_store(i):
            s = i % 4
            ce = cast_engs[i % 3]
            if ce == "act":
                ACT(stout[s], stin[s], AF.Copy, [Bst_in[s]], [Bst_out[s]])
            else:
                CP(stout[s], stin[s], [Bst_in[s]], [Bst_out[s]], eng=ce)
            DMA(lambda e: e.dma_start(out=dscr[i], in_=stout[s]), [Bst_out[s]], [Bscr[i]], True, eng="act")

        for i in range(min(3, NPR)):
            pl_load(i)
        for i in range(NPR):
            if i + 3 < NPR:
                pl_load(i + 3)
            pl_cast_store(i)

        Bdg = [BG("dgs0_", 31), BG("dgs1_", 31)]
        Bdgd = BG("dgd", 8)
        for c in range(8 if stop != 1 else 0):
            s = c % 2
            for jt in range(31):
                ACT(dgst[s][:, jt, :], cmf[:, C_ID:C_ID + 128], AF.Copy, [Bcm, Bvec], [Bdg[s][jt]],
                    scale=vcol(V_CW + c * 31 + jt))
            DMA(lambda e, c=c, s=s: e.dma_start(out=ddiag[c], in_=dgst[s][:, :, :].rearrange("p j c -> p (j c)")),
                [Bdg[s]], [Bdgd[c]], True)

        wslot = [0]
        wmode = [2]

        def loadw(g):
            s = wslot[0] % wmode[0]
            wslot[0] += 1
            DMA(lambda e: e.dma_start(out=wbuf[s].rearrange("p (r k) c -> p r (k c)", r=2),
                                      in_=dscr[2 * g:2 * g + 2].rearrange("r p f -> p r f")),
                [Bscr[2 * g], Bscr[2 * g + 1]], [Bw[s]], True)
            return s

        bankrr = [0]

        def nextbank():
            b = bankrr[0]
            bankrr[0] = (b + 1) % 4
            return b

        def proj(bank, s, j, rhs3, c0, n, rbuf):
            for k in range(8):
                MM(ps[:, bank, 0:n], wbuf[s][:, k, j * 128:(j + 1) * 128], rhs3[:, k, c0:c0 + n],
                   k == 0, k == 7, [Bw[s], rbuf], [PB[bank]])

        def colstats(src3, c0, n, srcbuf, outrr, outbuf, tmp, tmpbuf, sqv, sqbuf, bank, sq_au=True):
            for k in range(8):
                ACT(sqv[:, k, c0:c0 + n], src3[:, k, c0:c0 + n], AF.Square, [srcbuf[k]], [sqbuf[k]], au=sq_au)
            for k in range(8):
                MM(ps[:, bank, 0:n], onesmb[:], sqv[:, k, c0:c0 + n], k == 0, k == 7, [Bconst, sqbuf[k]], [PB[bank]], au=sq_au)
            TS(tmp[:, c0:c0 + n], ps[:, bank, 0:n], EPSV, ALU.add, [PB[bank]], [tmpbuf], au=False)
            ACT(tmp[:, c0:c0 + n], tmp[:, c0:c0 + n], AF.Sqrt, [tmpbuf], [tmpbuf], au=False)
            RECIP(outrr[:, c0:c0 + n], tmp[:, c0:c0 + n], [tmpbuf], [outbuf], au=False)

        def rope_tables(c0, n, colbase):
            DMA(lambda e: e.dma_start(out=posi[:, c0:c0 + n],
                                      in_=dpos[0:1, colbase:colbase + n].partition_broadcast(128)[:, 0, :]),
                [], [Bposi])
            sl = slice(c0, c0 + n)
            CP(r1[:, sl], posi[:, sl], [Bposi], [Br1], au=False)
            TS(r1[:, sl], r1[:, sl], vcol(V_INVF), ALU.mult, [Br1, Bvec], [Br1], au=False)
            for (dst, shift, useSgn) in ((sinT, 0.0, True), (cosT, float(np.pi / 2), False)):
                TS(r2[:, sl], r1[:, sl], shift, ALU.add, [Br1], [Br2], au=False)
                TS(r3[:, sl], r2[:, sl], float(1.0 / (2 * np.pi)), ALU.mult, [Br2], [Br3], s2=MAGIC, op1=ALU.add, au=False)
                TS(r3[:, sl], r3[:, sl], MAGIC, ALU.subtract, [Br3], [Br3], au=False)
                STT(r2[:, sl], r3[:, sl], -CW1, r2[:, sl], ALU.mult, ALU.add, [Br3, Br2], [Br2], au=False)
                STT(r2[:, sl], r3[:, sl], -CW2, r2[:, sl], ALU.mult, ALU.add, [Br3, Br2], [Br2], au=False)
                TS(r2[:, sl], r2[:, sl], 3.1415925, ALU.min, [Br2], [Br2], s2=-3.1415925, op1=ALU.max, au=False)
                if useSgn:
                    ACT(dst[:, sl], r2[:, sl], AF.Sin, [Br2, Bvec], [Bcs], scale=vcol(V_SGN), au=False)
                else:
                    ACT(dst[:, sl], r2[:, sl], AF.Sin, [Br2], [Bcs], au=False)

        def ckpt(i):
            if stop == i:
                P.halted = True

        if stop in (1, 2):
            P.halted = True
        for ti in range(NT):
            par = ti % 2
            c0 = 0 if ti == 0 else 128
            n = 128 + T - c0
            xcol = HALO + ti * T
            barrier()
            xt = xts[par]
            Bx = Bxs[par]

            def prefetch(tn):
                pn = tn % 2
                xc = HALO + tn * T
                cc0 = 0 if tn == 0 else 128
                DMA(lambda e: e.dma_start(out=xts[pn][:], in_=dx[:, :, xc:xc + T].rearrange("k p t -> p k t")),
                    [], [Bxs[pn]])
                rope_tables(cc0, 128 + T - cc0, xc - 128 + cc0)
                colstats(xts[pn], 0, T, Bxs[pn], rn1, Brn1, st2, Bs2, sqn, Bsqn, 6, sq_au=False)

            if ti == 0:
                prefetch(0)
            for k in range(8):
                STT(hT[:, k, 128:128 + T], xt[:, k, :], vcol(V_G1 + k), rn1[:, 0:T], ALU.mult, ALU.mult,
                    [Bx[k], Bvec, Brn1], [BhT[k]])
            if ti == 0:
                DMA(lambda e: e.dma_start(out=ysb[:, :, 0:128], in_=dx[:, :, 0:128].rearrange("k p t -> p k t")),
                    [], [Bys], True)
                colstats(ysb, 0, 128, Bys, st3, Bs3, st2, Bs2, sqb, Bsq, 4)
                for k in range(8):
                    STT(hT[:, k, 0:128], ysb[:, k, 0:128], vcol(V_G1 + k), st3[:, 0:128], ALU.mult, ALU.mult,
                        [Bys[k], Bvec, Bs3], [BhT[k]])
            else:
                for c in range(8):
                    CP(ubuf[:, par, c, 0:32], ubuf[:, 1 - par, c, T:T + 32], [Bub[1 - par][c]], [Bub[par][c]], eng="pool", au=False)
                for g in range(2):
                    CP(kbuf[:, par, g, 0:128], kbuf[:, 1 - par, g, T:T + 128], [Bkb[1 - par]], [Bkb[par]], eng="pool", au=False)
                CP(vdup[:, par, 0, :, :], vdup[:, 1 - par, 2, :, :], [Bvd[1 - par][2]], [Bvd[par][0]], eng="pool", au=False)

            ckpt(3)
            cu0 = 96 if ti == 0 else 128
            nu = 128 + T - cu0
            wsl = {}
            sgl = [sg1, sg2]

            def glu_proj(c):
                pr, j = c // 4, c % 4
                if j == 0:
                    wsl[pr] = (loadw(2 * pr), loadw(2 * pr + 1))
                sv, sgt = wsl[pr]
                bA = nextbank()
                proj(bA, sv, j, hT, cu0, nu, BhT)
                bB = nextbank()
                proj(bB, sgt, j, hT, cu0, nu, BhT)
                sg = sgl[c % 2]
                ACT(sg[:, 0:nu], ps[:, bB, 0:nu], AF.Sigmoid, [PB[bB], Bvec], [Bsgs[c % 2]],
                    bias=vcol(V_BIN + _chunk_gate(c)), au=False)
                STT(ubuf[:, par, c, cu0 - 96:cu0 - 96 + nu], ps[:, bA, 0:nu], vcol(V_BIN + _chunk_val(c)),
                    sg[:, 0:nu], ALU.add, ALU.mult, [PB[bA], Bvec, Bsgs[c % 2]], [Bub[par][c]], au=False)
                if ti == 0:
                    TS(ubuf[:, par, c, 0:32], ubuf[:, par, c, 0:32], vcol(V_HV), ALU.mult,
                       [Bub[par][c], Bvec], [Bub[par][c]], au=False)
                ds_ = c % 2
                DMA(lambda e: e.dma_start(out=diag[ds_][:, :, :].rearrange("p j c -> p (j c)"), in_=ddiag[c]),
                    [Bdgd[c]], [Bdiag[ds_]], True)

            def conv(c):
                ds_ = c % 2
                bC = nextbank()
                for jt in range(31):
                    MM(ps[:, bC, 0:T], diag[ds_][:, jt, :], ubuf[:, par, c, 2 + jt:2 + jt + T],
                       jt == 0, jt == 30, [Bdiag[ds_], Bub[par][c]], [PB[bC]])
                ACT(ysb[:, c, :], ps[:, bC, 0:T], AF.Identity, [PB[bC], Bvec], [Bys[c]], bias=vcol(V_CB + c))
                ACT(sqb[:, c, 0:T], ps[:, bC, 0:T], AF.Square, [PB[bC], Bvec], [Bsq[c]], bias=vcol(V_CB + c))

            glu_proj(0)
            for c in range(8):
                if c + 1 < 8:
                    glu_proj(c + 1)
                conv(c)
            wmode[0] = 4
            for c in range(8):
                MM(ps[:, 4, 0:T], onesmf[:], ysb[:, c, :], c == 0, c == 7, [Bconst, Bys[c]], [PB[4]])
            for c in range(8):
                MM(ps[:, 5, 0:T], onesmb[:], sqb[:, c, 0:T], c == 0, c == 7, [Bconst, Bsq[c]], [PB[5]])
            CP(st1[:, 0:T], ps[:, 4, 0:T], [PB[4]], [Bs1], au=False)
            TTo(st2[:, 0:T], st1[:, 0:T], st1[:, 0:T], ALU.mult, [Bs1], [Bs2], au=False)
            TTo(st2[:, 0:T], ps[:, 5, 0:T], st2[:, 0:T], ALU.subtract, [PB[5], Bs2], [Bs2], au=False)
            TS(st2[:, 0:T], st2[:, 0:T], EPSV, ALU.add, [Bs2], [Bs2], au=False)
            ACT(st2[:, 0:T], st2[:, 0:T], AF.Sqrt, [Bs2], [Bs2], au=False)
            RECIP(st3[:, 0:T], st2[:, 0:T], [Bs2], [Bs3], au=False)
            STT(st1[:, 0:T], st1[:, 0:T], -1.0, st3[:, 0:T], ALU.mult, ALU.mult, [Bs1, Bs3], [Bs1], au=False)
            for c in range(8):
                ra_, rb_ = (r1, r2) if c % 2 == 0 else (r3, r4)
                Ba_, Bb_ = ([Br1, Bra[0]], [Br2, Brb[0]]) if c % 2 == 0 else ([Br3, Bra[1]], [Brb[1]])
                TTo(ra_[:, 0:T], ysb[:, c, :], st3[:, 0:T], ALU.mult, [Bys[c], Bs3], Ba_)
                TTo(rb_[:, 0:T], ra_[:, 0:T], st1[:, 0:T], ALU.add, Ba_ + [Bs1], Bb_, au=False)
                ACT(sT[:, c, :], rb_[:, 0:T], AF.Silu, Bb_ + [Bvec], [BsT[c]], bias=vcol(V_LNB + c), scale=vcol(V_LNG + c))

            ckpt(4)
            rcnt = [0]

            def rope_chunk(bank, outap, cc0, nn, bcol, outbuf):
                i_ = rcnt[0] % 2
                rcnt[0] += 1
                qb_ = (qb, qb2)[i_]
                ra_, rb_ = ((r1, r2), (r3, r4))[i_]
                Bq_ = [Bqb, Bqbs[0]] if i_ == 0 else [Bqbs[1]]
                Ba_ = [Br1, Bra[0]] if i_ == 0 else [Br3, Bra[1]]
                Bb_ = [Br2, Brb[0]] if i_ == 0 else [Brb[1]]
                ACT(qb_[:, cc0:cc0 + nn], ps[:, bank, 0:nn], AF.Identity, [PB[bank], Bvec], Bq_, bias=vcol(bcol), au=False)
                b2 = nextbank()
                MM(ps[:, b2, 0:nn], permb[:], qb_[:, cc0:cc0 + nn], True, True, [Bconst] + Bq_, [PB[b2]], au=False)
                TTo(ra_[:, cc0:cc0 + nn], qb_[:, cc0:cc0 + nn], cosT[:, cc0:cc0 + nn], ALU.mult, Bq_ + [Bcs], Ba_, au=False)
                TTo(rb_[:, cc0:cc0 + nn], ps[:, b2, 0:nn], sinT[:, cc0:cc0 + nn], ALU.mult, [PB[b2], Bcs], Bb_, au=False)
                TTo(outap, ra_[:, cc0:cc0 + nn], rb_[:, cc0:cc0 + nn], ALU.add, Ba_ + Bb_, [outbuf], au=True)

            for qg in range(2):
                s = loadw(4 + qg)
                for j in range(4):
                    cq = qg * 4 + j
                    b = nextbank()
                    proj(b, s, j, hT, 128, T, BhT)
                    rope_chunk(b, qrope[:, cq, :], 128, T, V_BIN + 16 + cq, Bqr[cq])
            s = loadw(6)
            for g in range(2):
                b = nextbank()
                proj(b, s, g, hT, c0, n, BhT)
                rope_chunk(b, kbuf[:, par, g, c0:c0 + n], c0, n, V_BIN + 24 + g, Bkb[par])
            for blk in range(3):
                if ti > 0 and blk == 0:
                    continue
                b = nextbank()
                for k in range(8):
                    MM(ps[:, b, 0:128], hT[:, k, blk * 128:(blk + 1) * 128], wbuf[s][:, k, 256:384],
                       k == 0, k == 7, [BhT, Bw[s]], [PB[b]])
                for dup in range(2):
                    TTo(vdup[:, par, blk, :, dup * 64:(dup + 1) * 64],
                        ps[:, b, 0:128].rearrange("p (g d) -> p g d", g=2),
                        bvb[:].rearrange("p (g d) -> p g d", g=2), ALU.add, [PB[b], Bconst], [Bvd[par][blk]], au=False)

            ckpt(5)
            iters = [(b, g, hg) for b in range(2) for g in range(2) for hg in range(2)]

            def att_S(i):
                b, g, hg = iters[i]
                sb0 = 6 if i % 2 == 0 else 2
                var = 0 if (ti == 0 and b == 0) else 1
                j0 = (8 * g + 4 * hg) // 2
                for half in range(2):
                    MM(ps[:, sb0 + half, :], identb[:], maskb[:, var, :, :, :].rearrange("p k a q -> p (k a q)"),
                       True, False, [Bconst], [PB[sb0 + half]], au=False, sgc=True)
                for half in range(2):
                    pa = slice(half * 64, (half + 1) * 64)
                    for kb in range(2):
                        for a in range(2):
                            MM(ps[:, sb0 + half, (kb * 2 + a) * 128:(kb * 2 + a + 1) * 128],
                               kbuf[pa, par, g, (b + kb) * 128:(b + kb + 1) * 128],
                               qrope[pa, j0 + a, b * 128:(b + 1) * 128],
                               False, True, [Bkb[par], Bqr[j0 + a]], [PB[sb0 + half]], sgc=True)

            def att_rest(i):
                b, g, hg = iters[i]
                sl_ = i % 2
                sb0 = 6 if i % 2 == 0 else 2
                ob, db = (4, 5) if i % 2 == 0 else (0, 1)
                h0 = 8 * g + 4 * hg
                j0 = h0 // 2
                ACT(eT[:, sl_, :, :, :].rearrange("p k h q -> p (k h q)"),
                    ps[:, sb0:sb0 + 2, :].rearrange("p k c -> p (k c)"), AF.Exp, [PB[sb0], PB[sb0 + 1]], [BeT[sl_]],
                    scale=0.125, au=False)
                eTv = eT[:, sl_, :, :, :].rearrange("p half (kb a) q -> p half kb a q", kb=2)
                for kb in range(2):
                    for half in range(2):
                        MM(ps[:, ob, half * 256:(half + 1) * 256], vdup[:, par, b + kb, g, :],
                           eTv[:, half, kb, :, :].rearrange("p a q -> p (a q)"),
                           (kb == 0 and half == 0), kb == 1, [Bvd[par][b + kb], BeT[sl_]], [PB[ob]], au=False, sgc=True)
                for kb in range(2):
                    for half in range(2):
                        MM(ps[:, db, half * 256:(half + 1) * 256], onesb[:],
                           eTv[:, half, kb, :, :].rearrange("p a q -> p (a q)"),
                           (kb == 0 and half == 0), kb == 1, [Bconst, BeT[sl_]], [PB[db]], au=False, sgc=True)
                TTo(den[:].rearrange("p (half a q) -> p half a q", half=2, a=2),
                    ps[:, db, :].rearrange("p (half a q) -> p half a q", half=2, a=2),
                    esink[:, h0:h0 + 4].rearrange("p (a half) -> p half a", half=2).unsqueeze(3).to_broadcast([128, 2, 2, 128]),
                    ALU.add, [PB[db], Bconst], [Bden], au=False)
                RECIP(rden[:], den[:], [Bden], [Brden], au=False)
                for half in range(2):
                    pa = slice(half * 64, (half + 1) * 64)
                    TTo(attnT[pa, j0:j0 + 2, b * 128:(b + 1) * 128],
                        ps[pa, ob, half * 256:(half + 1) * 256].rearrange("p (a q) -> p a q", a=2),
                        rden[pa, half * 256:(half + 1) * 256].rearrange("p (a q) -> p a q", a=2),
                        ALU.mult, [PB[ob], Brden], [Bat[j0], Bat[j0 + 1]])

            att_S(0)
            for i in range(8):
                if i + 1 < 8:
                    att_S(i + 1)
                att_rest(i)

            ckpt(6)
            Bm1s = BG("m1s", 4)
            for jg in range(2):
                sa = loadw(11 + jg)
                sb_ = loadw(7 + jg)
                for j in range(4):
                    c = jg * 4 + j
                    bA = nextbank()
                    proj(bA, sa, j, sT, 0, T, BsT)
                    bB = nextbank()
                    proj(bB, sb_, j, hT, 128, T, BhT)
                    sg = sgl[j % 2]
                    ACT(sg[:, 0:T], ps[:, bB, 0:T], AF.Sigmoid, [PB[bB], Bvec], [Bsgs[j % 2]], bias=vcol(V_BIN + 28 + c), au=False)
                    TTo(m1buf[:, j, :], ps[:, bA, 0:T], sg[:, 0:T], ALU.mult, [PB[bA], Bsgs[j % 2]], [Bm1s[j]], au=False)
                sa = loadw(13 + jg)
                sb_ = loadw(9 + jg)
                for j in range(4):
                    c = jg * 4 + j
                    bA = nextbank()
                    proj(bA, sa, j, attnT, 0, T, Bat)
                    bB = nextbank()
                    proj(bB, sb_, j, hT, 128, T, BhT)
                    sg = sgl[j % 2]
                    rt_ = (r3, r4)[j % 2]
                    Brt_ = [Br3, Bra[1]] if j % 2 == 0 else [Brb[1]]
                    ACT(sg[:, 0:T], ps[:, bB, 0:T], AF.Sigmoid, [PB[bB], Bvec], [Bsgs[j % 2]], bias=vcol(V_BIN + 36 + c), au=False)
                    TTo(rt_[:, 0:T], ps[:, bA, 0:T], sg[:, 0:T], ALU.mult, [PB[bA], Bsgs[j % 2]], Brt_, au=False)
                    TTo(mergedT[:, c, 0:T], rt_[:, 0:T], m1buf[:, j, :], ALU.add, Brt_ + [Bm1s[j]], [Bsq[c]])
            for og in range(2):
                s = loadw(15 + og)
                for j in range(4):
                    c = og * 4 + j
                    b = nextbank()
                    proj(b, s, j, mergedT, 0, T, Bsq)
                    TTo(xt[:, c, :], xt[:, c, :], ps[:, b, 0:T], ALU.add, [Bx[c], PB[b]], [Bx[c]], au=False)
            if dbg and ti == 0:
                DMA(lambda e, xt_=xt: e.dma_start(out=ddbg.rearrange("k p t -> p k t"), in_=xt_[:]), [Bx], [Buf("dbgo")], final=True)

            ckpt(7)
            colstats(xt, 0, T, Bx, st1, Bs1, st2, Bs2, sqb, Bsq, 4)
            for k in range(8):
                STT(h2T[:, k, :], xt[:, k, :], vcol(V_G2 + k), st1[:, 0:T], ALU.mult, ALU.mult, [Bx[k], Bvec, Bs1], [Bh2], au=False)
            wmode[0] = 2
            barrier()
            for g4 in range(4):
                s = loadw(17 + g4)
                for j in range(4):
                    hc = g4 * 4 + j
                    b = nextbank()
                    for k in range(8):
                        MM(ps[:, b, 0:T], wbuf[s][:, k, j * 128:(j + 1) * 128], h2T[:, k, :], k == 0, k == 7,
                           [Bw[s], Bh2], [PB[b]])
                    if hc % 2 == 0:
                        ACT(qTb[:, hc, :], ps[:, b, 0:T], AF.Copy, [PB[b]], [BqT[hc]])
                    else:
                        CP(qTb[:, hc, :], ps[:, b, 0:T], [PB[b]], [BqT[hc]])
            v16v = v16[:, :, :].rearrange("p (h c) k -> p h c k", c=2)
            i16fv = i16f[:, :, :].rearrange("p (h c) k -> p h c k", c=2)
            B4 = [128, 8, 16, 16]
            for tc in range(2):
                tcs = slice(tc * 128, (tc + 1) * 128)
                for g4 in range(4):
                    for l in range(4):
                        hc = g4 * 4 + l
                        MM(ps[:, 5, l * 128:(l + 1) * 128], qTb[:, hc, tcs], skb[:, hc, :], True, True,
                           [BqT[hc], Bsk], [PB[5]])
                    def L1(step, l):
                        hc = g4 * 4 + l
                        src_ = ps[:, 5, l * 128:(l + 1) * 128]
                        if step == 0:
                            OP("dve", lambda e: e.max(out=v16[:, hc, 0:8], in_=src_), [PB[5]], [Bv16[hc]])
                        elif step == 1:
                            OP("dve", lambda e: e.max_index(out=i16[:, hc, 0:8], in_max=v16[:, hc, 0:8], in_values=src_),
                               [PB[5], Bv16[hc]], [Bi16[hc]])
                        elif step == 2:
                            OP("dve", lambda e: e.match_replace(out=scw[:, l, :], in_to_replace=v16[:, hc, 0:8],
                                                               in_values=src_, imm_value=-1e30),
                               [PB[5], Bv16[hc]], [Bscw[l]], True)
                        elif step == 3:
                            OP("dve", lambda e: e.max(out=v16[:, hc, 8:16], in_=scw[:, l, :]), [Bscw[l]], [Bv16[hc]], True)
                        else:
                            OP("dve", lambda e: e.max_index(out=i16[:, hc, 8:16], in_max=v16[:, hc, 8:16], in_values=scw[:, l, :]),
                               [Bscw[l], Bv16[hc]], [Bi16[hc]], True)
                    for step in (0, 2, 1, 3, 4):
                        for l in range(4):
                            L1(step, l)
                CP(i16f[:], i16[:], [Bi16], [Bi16f], au=False)
                TTo(cand[:], v16v[:, :, 0, :].unsqueeze(3).to_broadcast(B4), v16v[:, :, 1, :].unsqueeze(2).to_broadcast(B4),
                    ALU.add, [Bv16], [Bcand])
                def L2(step, h):
                    src_ = cand[:, h, :, :].rearrange("p a b -> p (a b)")
                    if step == 0:
                        OP("dve", lambda e: e.max(out=best[:, h, 0:8], in_=src_), [Bcand], [Bbest[h]], True)
                    elif step == 1:
                        OP("dve", lambda e: e.max_index(out=posu[:, h, 0:8], in_max=best[:, h, 0:8], in_values=src_),
                           [Bcand, Bbest[h]], [Bpos[h]], True)
                    elif step == 2:
                        OP("dve", lambda e: e.match_replace(out=work2[:, h, :], in_to_replace=best[:, h, 0:8],
                                                           in_values=src_, imm_value=-1e30), [Bcand, Bbest[h]], [Bw2[h]], True)
                    elif step == 3:
                        OP("dve", lambda e: e.max(out=best[:, h, 8:16], in_=work2[:, h, :]), [Bw2[h]], [Bbest[h]], True)
                    else:
                        OP("dve", lambda e: e.max_index(out=posu[:, h, 8:16], in_max=best[:, h, 8:16], in_values=work2[:, h, :]),
                           [Bw2[h], Bbest[h]], [Bpos[h]], True)
                for step in (0, 2, 1, 3, 4):
                    for h in range(8):
                        L2(step, h)
                CP(posf[:], posu[:], [Bpos], [Btk], au=False)
                OP("dve", lambda e: e.tensor_single_scalar(out=k1u[:], in_=posu[:], scalar=4, op=ALU.logical_shift_right),
                   [Bpos], [Btk])
                CP(k1f[:], k1u[:], [Btk], [Btk], au=False)
                STT(k2f[:], k1f[:], -16.0, posf[:], ALU.mult, ALU.add, [Btk], [Btk], au=False)
                TTo(ebuf[:], best[:], best[:, :, 0:1].to_broadcast([128, 8, 16]), ALU.subtract, [Bbest], [Btk], au=False)
                ACT(ebuf[:], ebuf[:], AF.Exp, [Btk], [Btk], au=False)
                OP("dve", lambda e: e.tensor_reduce(out=Zs[:], in_=ebuf[:], axis=AX.X, op=ALU.add), [Btk], [Btk])
                RECIP(Zs[:], Zs[:], [Btk], [Btk], au=False)
                TTo(gate[:], ebuf[:], Zs[:, :].unsqueeze(2).to_broadcast([128, 8, 16]), ALU.mult, [Btk], [Btk], au=False)
                io16 = cmf[:, C_IOTA16:C_IOTA16 + 16].unsqueeze(1).unsqueeze(1).to_broadcast(B4)
                for (kf, cidx, dst) in ((k1f, 0, av), (k2f, 1, bvv)):
                    TTo(E1[:], kf[:].unsqueeze(3).to_broadcast(B4), io16, ALU.is_equal, [Btk, Bcm], [BE1])
                    TTo(E1[:], E1[:], i16fv[:, :, cidx, :].unsqueeze(2).to_broadcast(B4), ALU.mult, [BE1, Bi16f], [BE1])
                    OP("dve", lambda e, dst=dst: e.tensor_reduce(out=dst[:], in_=E1[:], axis=AX.X, op=ALU.add), [BE1], [Btk], True)
                for idx, srcv in enumerate((av, bvv, gate)):
                    OP("pe", lambda e, idx=idx, srcv=srcv: e.transpose(out=ps[:, 5, idx * 128:(idx + 1) * 128],
                                                                     in_=srcv[:].rearrange("p h k -> p (h k)"),
                                                                     identity=cmf[:, C_ID:C_ID + 128]),
                       [Btk, Bcm], [PB[5]])
                CP(abgT[:, tc, :, :].rearrange("p a b -> p (a b)"), ps[:, 5, 0:384], [PB[5]], [Babg], au=False)

            ckpt(8)
            barrier()
            for tb in range(T // 8):
                sl_ = tb % 2
                bk0 = 4 + 2 * (tb % 2)
                for i in range(8):
                    t = tb * 8 + i
                    tc, tl = t // 128, t % 128
                    TS(Pt[:, sl_, i, :], iotab[:], abgT[:, tc, 0, tl:tl + 1], ALU.is_equal, [Bconst, Babg], [BPt[sl_][i]],
                       s2=abgT[:, tc, 2, tl:tl + 1], op1=ALU.mult, au=False)
                    TS(Qt[:, sl_, i, :], iotab[:], abgT[:, tc, 1, tl:tl + 1], ALU.is_equal, [Bconst, Babg], [BQt[sl_][i]],
                       au=False)
                for i in range(8):
                    bank = bk0 + i // 4
                    MM(ps[:, bank, (i % 4) * 128:(i % 4 + 1) * 128], Qt[:, sl_, i, :], Pt[:, sl_, i, :], True, True,
                       [BQt[sl_][i], BPt[sl_][i]], [PB[bank]], au=False)
                for hb in range(2):
                    t0 = tb * 8 + hb * 4
                    ACT(G[:, :, t0:t0 + 4], ps[:, bk0 + hb, :].rearrange("p (t i) -> p i t", t=4), AF.Copy,
                        [PB[bk0 + hb]], [BGm[tb * 2 + hb]])

            ckpt(9)
            PBA = [PB[4], PB[5]]

            def stageA(ec):
                eg, cc, hs = ec // 2, ec % 2, ec % 2
                sl2 = eg % 2
                if cc == 0:
                    DMA(lambda e: e.dma_start(out=UTs[:, sl2, :, :].rearrange("p k c -> p (k c)"), in_=dscr[UV0 + eg]),
                        [Bscr[UV0 + eg]], [BUT[sl2]])
                    DMA(lambda e: e.dma_start(out=Vs[:, sl2, :, :].rearrange("p k c -> p (k c)"), in_=dscr[UV0 + 64 + eg]),
                        [Bscr[UV0 + 64 + eg]], [BVs[sl2]])
                for k in range(8):
                    MM(ps[:, 4 + hs, 0:256], UTs[:, sl2, k, cc * 128:(cc + 1) * 128], h2T[:, k, :],
                       k == 0, k == 7, [BUT[sl2], Bh2], [PBA[hs]], au=False)
                ACT(gl[:, hs, :], ps[:, 4 + hs, 0:256], AF.Gelu, [PBA[hs]], [Bgl[hs]], au=False)
                TTo(GA[:, hs, :], gl[:, hs, :], G[:, ec, :], ALU.mult, [Bgl[hs], BGm], [BGA[hs]])

            def stageV(ec):
                eg, cc, hs = ec // 2, ec % 2, ec % 2
                sl2 = eg % 2
                for dk in range(8):
                    MM(ps[:, dk // 2, (dk % 2) * 256:(dk % 2 + 1) * 256], Vs[:, sl2, cc, dk * 128:(dk + 1) * 128],
                       GA[:, hs, :], (ec == 0 and dk % 2 == 0), ec == 127, [BVs[sl2], BGA[hs]], [PB[dk // 2]],
                       au=False, sgc=True)

            stageA(0)
            for ec in range(128):
                if ec + 1 < 128:
                    stageA(ec + 1)
                stageV(ec)
                if ec == 2 and ti + 1 < NT:
                    capture[0] = []
                    prefetch(ti + 1)
                    pending = capture[0]
                    capture[0] = None
                if ec >= 2 and ti + 1 < NT and pending:
                    replay(pending.pop(0))
            if ti + 1 < NT:
                while pending:
                    replay(pending.pop(0))

            ckpt(10)
            for dk in range(8):
                TTo(xt[:, dk, :], xt[:, dk, :], ps[:, dk // 2, (dk % 2) * 256:(dk % 2 + 1) * 256], ALU.add,
                    [Bx[dk], PB[dk // 2]], [Bx[dk]], au=False)
            colstats(xt, 0, T, Bx, st1, Bs1, st3, Bs3, sqn, Bsqn, 4, sq_au=False)
            for k in range(8):
                STT(xt[:, k, :], xt[:, k, :], vcol(V_GF + k), st1[:, 0:T], ALU.mult, ALU.mult,
                    [Bx[k], Bvec, Bs1], [Bx[k]], au=False)
            DMA(lambda e, ti=ti, xt_=xt: e.dma_start(out=dout[:, :, ti * T:(ti + 1) * T].rearrange("k p t -> p k t"), in_=xt_[:, :, :]),
                [Bx], [Bout], False, final=True)

        P.emit()
    return nc


def _prep_shared(inp):
    f = np.float32
    w_in = np.asarray(inp["w_in"], f)[0]
    b_in = np.asarray(inp["b_in"], f)[0]
    blk = lambda base, c: list(range(base + c * 128, base + (c + 1) * 128))
    chunks = []
    for pr in range(2):
        for c in range(4):
            chunks.append(blk(0, pr * 4 + c))
        for c in range(4):
            chunks.append(blk(1024, pr * 4 + c))
    for c in range(8):
        chunks.append(blk(2048, c))
    k0 = list(range(3072, 3136))
    k1 = list(range(3136, 3200))
    chunks.append(k0 + k0)
    chunks.append(k1 + k1)
    chunks.append(list(range(3200, 3328)))
    chunks.append(list(range(3200, 3328)))
    for c in range(8):
        chunks.append(blk(3328, c))
    for c in range(8):
        chunks.append(blk(4352, c))
    assert len(chunks) == 44
    colidx = np.array(sum(chunks, []), dtype=np.int64)
    w_perm = w_in[:, colidx]
    b_perm = b_in[colidx]
    mats = [w_perm[:, g * 512:(g + 1) * 512] for g in range(11)]
    for name in ("w_conv_out", "w_attn_o", "w_out"):
        w = np.asarray(inp[name], f)[0]
        mats += [w[:, 0:512], w[:, 512:1024]]
    wpq = np.asarray(inp["w_peer_q"], f)[0]
    mats += [wpq[:, g * 512:(g + 1) * 512] for g in range(4)]
    assert len(mats) == NMIXG
    wall = np.empty((NPIECE, 128, 2048), f)
    for g, m in enumerate(mats):
        a = m.reshape(8, 128, 512).transpose(1, 0, 2)
        wall[2 * g] = a[:, 0:4, :].reshape(128, 2048)
        wall[2 * g + 1] = a[:, 4:8, :].reshape(128, 2048)
    U = np.asarray(inp["peer_u"], f)[0]
    V = np.asarray(inp["peer_v"], f)[0]
    wall[UV0:UV0 + 64] = U.reshape(64, 256, 8, 128).transpose(0, 3, 2, 1).reshape(64, 128, 2048)
    wall[UV0 + 64:UV0 + 128] = V.reshape(64, 2, 128, 1024).transpose(0, 2, 1, 3).reshape(64, 128, 2048)

    vec = np.zeros((128, NV), f)
    col = lambda v: np.asarray(v, f).reshape(-1, 128).T
    vec[:, V_G1:V_G1 + 8] = col(inp["norm1_g"][0])
    vec[:, V_BIN:V_BIN + 44] = col(b_perm)
    vec[:, V_CB:V_CB + 8] = col(inp["conv_b"][0])
    vec[:, V_LNG:V_LNG + 8] = col(inp["conv_ln_g"][0])
    vec[:, V_LNB:V_LNB + 8] = col(inp["conv_ln_b"][0])
    vec[:, V_G2:V_G2 + 8] = col(inp["norm2_g"][0])
    vec[:, V_GF:V_GF + 8] = col(inp["final_g"])
    p = np.arange(128)
    invf = (np.float32(10000.0) ** (-(np.arange(32, dtype=f) * f(2.0) / f(64)))).astype(f)
    vec[:, V_INVF] = invf[p % 32]
    vec[:, V_SGN] = np.where(p % 64 < 32, -1.0, 1.0)
    cw = np.asarray(inp["conv_w"], f)[0]
    vec[:, V_CW:V_CW + 248] = cw.reshape(31, 8, 128).transpose(2, 1, 0).reshape(128, 248)

    cm = np.zeros((128, NCM), f)
    cm[:, C_ID:C_ID + 128] = np.eye(128, dtype=f)
    cm[p, C_PERM + (p ^ 32)] = 1.0
    cm[:, C_IOTA:C_IOTA + 128] = np.arange(128, dtype=f)[None, :]
    cm[:, C_IOTA16:C_IOTA16 + 16] = np.arange(16, dtype=f)[None, :]
    kk = np.arange(128)[:, None]
    qq = np.arange(128)[None, :]
    NEGM = f(-240000.0)
    m_prev = np.where(kk > qq, f(0), NEGM).astype(f)
    m_cur = np.where(kk <= qq, f(0), NEGM).astype(f)
    cm[:, C_MASK1:C_MASK1 + 128] = m_prev
    cm[:, C_MASK1 + 128:C_MASK1 + 256] = m_cur
    cm[:, C_MASK0 + 128:C_MASK0 + 256] = m_cur
    rows = np.concatenate([b_in[3200:3328], np.asarray(inp["attn_sinks"], f)[0]]).reshape(1, 144).astype(f)
    sk = np.asarray(inp["peer_sub_keys"], f)[0]
    skT = np.ascontiguousarray(sk.transpose(3, 0, 1, 2).reshape(128, 2048))
    return dict(wall=wall, vec=vec, cm=cm, rows=rows, skT=skT), m_prev, NEGM


_NC_CACHE = {}


def kernel(**inputs):
    x = np.asarray(inputs["x"], np.float32)
    pos = np.asarray(inputs["positions"], np.int32)
    B, S, _ = x.shape
    TOK = S // 2
    NT = TOK // TT
    shared, m_prev, NEGM = _prep_shared(inputs)
    in_maps = []
    for core in range(8):
        b, hs = core // 2, core % 2
        s0 = hs * TOK
        xT = np.zeros((8, 128, TOK + HALO), np.float32)
        pp = np.zeros((1, TOK + HALO), np.int32)
        if hs == 0:
            xs = x[b, 0:TOK]
            xT[:, :, HALO:] = xs.T.reshape(8, 128, TOK)
            pp[0, HALO:] = pos[b, 0:TOK]
        else:
            xs = x[b, s0 - HALO:s0 + TOK]
            xT[:] = xs.T.reshape(8, 128, TOK + HALO)
            pp[0] = pos[b, s0 - HALO:s0 + TOK]
        vec = shared["vec"].copy()
        vec[:, V_HV] = 0.0 if hs == 0 else 1.0
        cm = shared["cm"].copy()
        if hs == 0:
            cm[:, C_MASK0:C_MASK0 + 128] = NEGM
        else:
            cm[:, C_MASK0:C_MASK0 + 128] = m_prev
        in_maps.append(dict(xT=xT, pos=pp, wall=shared["wall"], vec=vec, cm=cm, rows=shared["rows"], skT=shared["skT"]))
    if NT not in _NC_CACHE:
        _NC_CACHE[NT] = build_nc(NT)
    nc = _NC_CACHE[NT]
    res = run_bass_kernel_spmd(nc, in_maps, core_ids=list(range(8)))
    out = np.empty((B, S, D), np.float32)
    for core in range(8):
        b, hs = core // 2, core % 2
        oT = np.asarray(res.results[core]["outT"], np.float32)
        out[b, hs * TOK:(hs + 1) * TOK, :] = oT.reshape(1024, TOK).T
    return out
```

```python
import numpy as np
from contextlib import ExitStack
import concourse.bass as bass
import concourse.mybir as mybir
from concourse.bass_utils import run_bass_kernel_spmd

F32 = mybir.dt.float32
BF16 = mybir.dt.bfloat16
I32 = mybir.dt.int32
U32 = mybir.dt.uint32
AF = mybir.ActivationFunctionType
ALU = mybir.AluOpType
AX = mybir.AxisListType


class Buf:
    __slots__ = ("name", "w", "r")

    def __init__(self, name=""):
        self.name = name
        self.w = None
        self.r = []


class BG(list):
    def __init__(self, name, n):
        super().__init__(Buf("%s%d" % (name, i)) for i in range(n))


def _flat(bs):
    out = []
    for b in bs:
        if isinstance(b, list):
            out.extend(_flat(b))
        else:
            out.append(b)
    return out


class _Ins:
    __slots__ = ("eng", "fn", "deps", "dma", "idx", "sig", "cnt", "semi", "final")

    def __init__(self, eng, fn, dma):
        self.eng = eng
        self.fn = fn
        self.deps = set()
        self.dma = dma
        self.sig = False
        self.cnt = 0
        self.semi = 0
        self.final = False


class Prog:
    NDMA_SEM = 12
    ENGS = ("pe", "act", "dve", "pool", "sp")

    def __init__(self, nc, es):
        self.nc = nc
        self.es = es
        self.q = {e: [] for e in self.ENGS}
        self.all = []
        self.dma_engine = "sp"
        self.halted = False

    def _add(self, ins, reads, writes):
        if self.halted:
            return ins
        reads = _flat(reads)
        writes = _flat(writes)
        for b in reads:
            if b.w is not None:
                ins.deps.add(b.w)
        for b in writes:
            if b.w is not None:
                ins.deps.add(b.w)
            for r in b.r:
                ins.deps.add(r)
        ins.deps.discard(ins)
        for b in reads:
            b.r.append(ins)
        for b in writes:
            b.w = ins
            b.r = []
        ins.idx = len(self.all)
        self.all.append(ins)
        self.q[ins.eng].append(ins)
        return ins

    def op(self, eng, fn, reads=(), writes=()):
        return self._add(_Ins(eng, fn, False), reads, writes)

    def dma(self, fn, reads=(), writes=(), final=False, eng=None):
        ins = _Ins(eng or self.dma_engine, fn, True)
        ins.final = final
        return self._add(ins, reads, writes)

    def emit(self):
        nc = self.nc
        for ins in self.all:
            for d in ins.deps:
                if d.eng == "pe" and ins.eng == "pe" and not d.dma and not ins.dma:
                    continue
                d.sig = True
            if ins.final:
                ins.sig = True
        sems = {e: self.es.enter_context(nc.semaphore("s_" + e)) for e in self.ENGS}
        dsems = [self.es.enter_context(nc.semaphore("d%d" % i)) for i in range(self.NDMA_SEM)]
        cnt = {e: 0 for e in self.ENGS}
        ndma = 0
        dma_prev = {}
        last_on_sem = [None] * self.NDMA_SEM
        for ins in self.all:
            if ins.dma:
                ins.semi = ndma % self.NDMA_SEM
                ins.cnt = 16 * (ndma // self.NDMA_SEM + 1)
                dma_prev[ins] = last_on_sem[ins.semi]
                last_on_sem[ins.semi] = ins
                ndma += 1
            elif ins.sig:
                cnt[ins.eng] += 1
                ins.cnt = cnt[ins.eng]
        finals = [i for i in self.all if i.final]
        block = self.es.enter_context(nc.Block())

        def run(engname, e):
            waited = {}

            def wait_for(d):
                if d.dma:
                    key = ("d", d.semi)
                    sem = dsems[d.semi]
                else:
                    key = ("e", d.eng)
                    sem = sems[d.eng]
                if waited.get(key, 0) >= d.cnt:
                    return
                e.wait_ge(sem, d.cnt)
                waited[key] = d.cnt

            for ins in self.q[engname]:
                for d in sorted(ins.deps, key=lambda z: z.idx):
                    if (d.eng == "pe" and engname == "pe" and not d.dma and not ins.dma):
                        continue
                    wait_for(d)
                if ins.dma:
                    p = dma_prev[ins]
                    if p is not None:
                        wait_for(p)
                h = ins.fn(e)
                if ins.dma:
                    h.then_inc(dsems[ins.semi], 16)
                elif ins.sig:
                    h.then_inc(sems[engname], 1)
            if engname == self.dma_engine:
                for f in finals:
                    wait_for(f)

        @block.sync
        def _(e):
            run("sp", e)

        @block.tensor
        def _(e):
            run("pe", e)

        @block.scalar
        def _(e):
            run("act", e)

        @block.vector
        def _(e):
            run("dve", e)

        @block.gpsimd
        def _(e):
            run("pool", e)


D = 1024
KC = 8
TT = 256
HALO = 128
EPSV = 1e-6
NMIXG = 21
NPIECE = 2 * NMIXG + 128
UV0 = 2 * NMIXG
V_G1, V_BIN, V_CB, V_LNG, V_LNB, V_G2, V_GF, V_INVF, V_SGN, V_HV, V_CW = 0, 8, 52, 60, 68, 76, 84, 92, 93, 94, 95
NV = 95 + 248
C_ID, C_PERM, C_IOTA, C_IOTA16, C_MASK0, C_MASK1 = 0, 128, 256, 384, 400, 656
NCM = 912
MAGIC = 12582912.0
CW1 = 6.28125
CW2 = 2.0 * np.pi - 6.28125


def _chunk_val(c):
    return (c // 4) * 8 + (c % 4)


def _chunk_gate(c):
    return (c // 4) * 8 + 4 + (c % 4)


class _Stop(Exception):
    pass


def build_nc(NT, dbg=False, stop=None):
    T = TT
    TOK = NT * T
    TOKH = TOK + HALO
    nc = bass.Bass("TRN2", target_bir_lowering=False)
    dx = nc.dram_tensor("xT", [8, 128, TOKH], F32, kind="ExternalInput").ap()
    dpos = nc.dram_tensor("pos", [1, TOKH], I32, kind="ExternalInput").ap()
    dwall = nc.dram_tensor("wall", [NPIECE, 128, 2048], F32, kind="ExternalInput").ap()
    dvec = nc.dram_tensor("vec", [128, NV], F32, kind="ExternalInput").ap()
    dcm = nc.dram_tensor("cm", [128, NCM], F32, kind="ExternalInput").ap()
    drow = nc.dram_tensor("rows", [1, 144], F32, kind="ExternalInput").ap()
    dsk = nc.dram_tensor("skT", [128, 2048], F32, kind="ExternalInput").ap()
    dout = nc.dram_tensor("outT", [8, 128, TOK], F32, kind="ExternalOutput").ap()
    dscr = nc.dram_tensor("wscr", [NPIECE, 128, 2048], BF16, kind="Internal").ap()
    ddiag = nc.dram_tensor("dgscr", [8, 128, 31 * 128], BF16, kind="Internal").ap()
    if dbg:
        ddbg = nc.dram_tensor("dbg", [8, 128, T], F32, kind="ExternalOutput").ap()

    es = ExitStack()
    with es:
        def sb(name, shape, dt):
            return es.enter_context(nc.sbuf_tensor("sb_" + name, shape, dt))

        P = Prog(nc, es)
        ps = es.enter_context(nc.psum_tensor("ps", [128, 8, 512], F32))
        PB = [Buf("bank%d" % i) for i in range(8)]

        vec = sb("vec", [128, NV], F32)
        cmf = sb("cmf", [128, NCM], F32)
        identb = sb("identb", [128, 128], BF16)
        permb = sb("permb", [128, 128], BF16)
        onesb = sb("onesb", [128, 128], BF16)
        onesmb = sb("onesmb", [128, 128], BF16)
        onesmf = sb("onesmf", [128, 128], F32)
        iotab = sb("iotab", [128, 128], BF16)
        maskb = sb("maskb", [128, 2, 2, 2, 128], BF16)
        esink = sb("esink", [128, 16], F32)
        bvb = sb("bvb", [128, 128], F32)
        skf = sb("skf", [128, 2048], F32)
        skb = sb("skb", [128, 16, 128], BF16)
        xt_a = sb("xt", [128, 8, T], F32)
        xt_b = sb("xt2", [128, 8, T], F32)
        xts = [xt_a, xt_b]
        sqn = sb("sqn", [128, 8, T], BF16)
        rn1 = sb("rn1", [128, T], F32)
        ubuf = sb("ubuf", [128, 2, 8, 32 + T], BF16)
        kbuf = sb("kbuf", [128, 2, 2, 128 + T], BF16)
        vdup = sb("vdup", [128, 2, 3, 2, 128], BF16)
        cosT = sb("cosT", [128, 128 + T], F32)
        sinT = sb("sinT", [128, 128 + T], F32)
        posi = sb("posi", [128, 128 + T], I32)
        r1 = sb("r1", [128, 128 + T], F32)
        r2 = sb("r2", [128, 128 + T], F32)
        r3 = sb("r3", [128, 128 + T], F32)
        st1 = sb("st1", [128, 128 + T], F32)
        st2 = sb("st2", [128, 128 + T], F32)
        st3 = sb("st3", [128, 128 + T], F32)
        sg1 = sb("sg1", [128, 128 + T], F32)
        sg2 = sb("sg2", [128, 128 + T], F32)
        m1buf = sb("m1buf", [128, 4, T], F32)
        qb = sb("qb", [128, 128 + T], BF16)
        qb2 = sb("qb2", [128, 128 + T], BF16)
        r4 = sb("r4", [128, 128 + T], F32)
        eT = sb("eT", [128, 2, 2, 4, 128], BF16)
        den = sb("den", [128, 512], F32)
        rden = sb("rden", [128, 512], F32)
        rscr = sb("rscr", [128, 512], F32)
        h2T = sb("h2T", [128, 8, T], BF16)
        gl = sb("gl", [128, 2, T], BF16)
        GA = sb("GA", [128, 2, T], BF16)
        Pt = sb("Pt", [128, 2, 8, 128], BF16)
        Qt = sb("Qt", [128, 2, 8, 128], BF16)
        UTs = sb("UTs", [128, 2, 8, 256], BF16)
        Vs = sb("Vs", [128, 2, 2, 1024], BF16)
        v16 = sb("v16", [128, 16, 16], F32)
        i16 = sb("i16", [128, 16, 16], U32)
        i16f = sb("i16f", [128, 16, 16], F32)
        best = sb("best", [128, 8, 16], F32)
        posu = sb("posu", [128, 8, 16], U32)
        k1u = sb("k1u", [128, 8, 16], U32)
        posf = sb("posf", [128, 8, 16], F32)
        k1f = sb("k1f", [128, 8, 16], F32)
        k2f = sb("k2f", [128, 8, 16], F32)
        ebuf = sb("ebuf", [128, 8, 16], F32)
        gate = sb("gate", [128, 8, 16], F32)
        Zs = sb("Zs", [128, 8], F32)
        av = sb("av", [128, 8, 16], F32)
        bvv = sb("bvv", [128, 8, 16], F32)
        abgT = sb("abgT", [128, 2, 3, 128], F32)
        one1 = sb("one1", [128, 4], F32)
        arena = sb("arena", [128, 32768], BF16)

        def av_(off, nbytes, dt):
            a = arena[:, off // 2:(off + nbytes) // 2]
            return a if dt == BF16 else a.bitcast(dt)

        K = 1024
        wbuf = [av_(0, 8 * K, BF16).rearrange("p (k c) -> p k c", k=8),
                av_(8 * K, 8 * K, BF16).rearrange("p (k c) -> p k c", k=8),
                av_(16 * K, 8 * K, BF16).rearrange("p (k c) -> p k c", k=8),
                av_(24 * K, 8 * K, BF16).rearrange("p (k c) -> p k c", k=8)]
        diag = [av_(16 * K, 7936, BF16).rearrange("p (j c) -> p j c", j=31),
                av_(24 * K, 7936, BF16).rearrange("p (j c) -> p j c", j=31)]
        hT = av_(32 * K, 6 * K, BF16).rearrange("p (k c) -> p k c", k=8)
        sqb = av_(38 * K, 6 * K, BF16).rearrange("p (k c) -> p k c", k=8)
        ysb = av_(44 * K, 8 * K, F32).rearrange("p (k c) -> p k c", k=8)
        sT = av_(52 * K, 4 * K, BF16).rearrange("p (k c) -> p k c", k=8)
        qrope = av_(56 * K, 4 * K, BF16).rearrange("p (k c) -> p k c", k=8)
        attnT = av_(60 * K, 4 * K, BF16).rearrange("p (k c) -> p k c", k=8)
        mergedT = sqb
        stin = [av_(i * 8 * K, 8 * K, F32) for i in range(4)]
        stout = [av_(32 * K + i * 4 * K, 4 * K, BF16) for i in range(4)]
        dgst = [av_(48 * K, 7936, BF16).rearrange("p (j c) -> p j c", j=31),
                av_(56 * K, 7936, BF16).rearrange("p (j c) -> p j c", j=31)]
        qTb = av_(16 * K, 8 * K, BF16).rearrange("p (k c) -> p k c", k=16)
        cand = av_(24 * K, 8 * K, F32).rearrange("p (h a b) -> p h a b", h=8, a=16)
        work2 = av_(32 * K, 8 * K, F32).rearrange("p (h c) -> p h c", h=8)
        E1 = av_(40 * K, 8 * K, F32).rearrange("p (h a b) -> p h a b", h=8, a=16)
        scw = av_(48 * K, 2 * K, F32).rearrange("p (l c) -> p l c", l=4)
        G = arena[:, :].rearrange("p (i t) -> p i t", i=128)
        outtmp = av_(0, 8 * K, F32).rearrange("p (k c) -> p k c", k=8)

        ATOK = Buf("atok")

        def vcol(i):
            return vec[:, i:i + 1]

        capture = [None]

        def OP(eng, fn, reads, writes, arena_use=False):
            if capture[0] is not None:
                capture[0].append(("op", (eng, fn, list(reads), list(writes), arena_use)))
                return None
            if arena_use:
                reads = list(reads) + [ATOK]
            return P.op(eng, fn, reads, writes)

        def DMA(fn, reads, writes, arena_use=False, final=False, eng=None):
            if capture[0] is not None:
                capture[0].append(("dma", (fn, list(reads), list(writes), arena_use, final, eng)))
                return None
            if arena_use:
                reads = list(reads) + [ATOK]
            return P.dma(fn, reads, writes, final=final, eng=eng)

        def replay(item):
            kind, args = item
            if kind == "op":
                OP(*args)
            else:
                DMA(*args)

        def barrier():
            P.op("pool", lambda e: e.memset(one1[:, 0:1], 0.0), [], [ATOK])

        def MM(out, lhsT, rhs, start, stop, reads, writes, au=True, sgc=False):
            OP("pe", lambda e: e.matmul(out, lhsT=lhsT, rhs=rhs, start=start, stop=stop,
                                        skip_group_check=sgc), reads, writes, au)

        def ACT(out, in_, func, reads, writes, bias=None, scale=None, au=True):
            kw = {}
            if bias is not None:
                kw["bias"] = bias
            if scale is not None:
                kw["scale"] = scale
            OP("act", lambda e: e.activation(out=out, in_=in_, func=func, **kw), reads, writes, au)

        def TTo(out, in0, in1, op, reads, writes, eng="dve", au=True):
            OP(eng, lambda e: e.tensor_tensor(out=out, in0=in0, in1=in1, op=op), reads, writes, au)

        def TS(out, in0, s1, op0, reads, writes, s2=None, op1=None, eng="dve", au=True):
            if op1 is None:
                OP(eng, lambda e: e.tensor_scalar(out=out, in0=in0, scalar1=s1, scalar2=None, op0=op0),
                   reads, writes, au)
            else:
                OP(eng, lambda e: e.tensor_scalar(out=out, in0=in0, scalar1=s1, scalar2=s2, op0=op0, op1=op1),
                   reads, writes, au)

        def STT(out, in0, scalar, in1, op0, op1, reads, writes, au=True):
            OP("dve", lambda e: e.scalar_tensor_tensor(out=out, in0=in0, scalar=scalar, in1=in1,
                                                       op0=op0, op1=op1), reads, writes, au)

        def CP(out, in_, reads, writes, eng="dve", au=True):
            OP(eng, lambda e: e.tensor_copy(out=out, in_=in_), reads, writes, au)

        def RECIP(out, in_, reads, writes, au=True):
            OP("dve", lambda e: e.reciprocal(out=out, in_=in_), reads, writes, au)

        Bvec, Bcm, Bconst, Bsk = Buf("vec"), Buf("cm"), Buf("const"), Buf("sk")
        Bxs = [BG("xta", 8), BG("xtb", 8)]
        Bsqn, Brn1 = BG("sqn", 8), Buf("rn1")
        Bscr = [Buf("scr%d" % i) for i in range(NPIECE)]
        Bst_in = BG("sti", 4)
        Bst_out = BG("sto", 4)
        Bw = [Buf("w0"), Buf("w1")]
        Bdiag = [Buf("dg0"), Buf("dg1")]
        Bw = Bw + Bdiag
        BhT, Bsq, Bys, BsT, Bqr, Bat = BG("hT", 8), BG("sq", 8), BG("ys", 8), BG("sT", 8), BG("qr", 8), BG("at", 8)
        Bub = [BG("ub0_", 8), BG("ub1_", 8)]
        Bkb = [Buf("kb0"), Buf("kb1")]
        Bvd = [[Buf("vd%d%d" % (p, b)) for b in range(3)] for p in range(2)]
        Bcs, Bposi, Br1, Br2, Br3 = Buf("cs"), Buf("posi"), Buf("r1"), Buf("r2"), Buf("r3")
        Bs1, Bs2, Bs3, Bg1, Bg2, Bm1, Bqb = Buf("st1"), Buf("st2"), Buf("st3"), Buf("sg1"), Buf("sg2"), Buf("m1"), Buf("qb")
        BeT = [Buf("eT0"), Buf("eT1")]
        Bden, Brden, Brscr = Buf("den"), Buf("rden"), Buf("rscr")
        Bh2, Bgl, BGA = Buf("h2T"), [Buf("gl0"), Buf("gl1")], [Buf("GA0"), Buf("GA1")]
        BPt, BQt = [BG("Pt0_", 8), BG("Pt1_", 8)], [BG("Qt0_", 8), BG("Qt1_", 8)]
        BUT, BVs = [Buf("UT0"), Buf("UT1")], [Buf("Vs0"), Buf("Vs1")]
        Bv16, Bi16, Bi16f, Bbest, Bpos, Btk, Babg = BG("v16_", 16), BG("i16_", 16), Buf("i16f"), BG("best", 8), BG("pos", 8), Buf("tk"), Buf("abg")
        BqT, Bcand, Bw2, BE1, Bscw, BGm, Bot = BG("qTb", 16), Buf("cand"), BG("w2_", 8), Buf("E1"), BG("scw", 4), BG("G", 64), Buf("ot")
        Bsgs = BG("sgs", 2)
        Bra, Brb, Bqbs = BG("ra", 2), BG("rb", 2), BG("qbs", 2)
        Bout = Buf("out")

        DMA(lambda e: e.dma_start(out=vec[:], in_=dvec[:, :]), [], [Bvec])
        DMA(lambda e: e.dma_start(out=cmf[:], in_=dcm[:, :]), [], [Bcm])
        DMA(lambda e: e.dma_start(out=skf[:], in_=dsk[:, :]), [], [Bsk])
        DMA(lambda e: e.dma_start(out=bvb[:], in_=drow[0:1, 0:128].partition_broadcast(128)[:, 0, :]), [], [Bconst])
        DMA(lambda e: e.dma_start(out=esink[:], in_=drow[0:1, 128:144].partition_broadcast(128)[:, 0, :]), [], [Bconst])
        CP(identb[:], cmf[:, C_ID:C_ID + 128], [Bcm], [Bconst], au=False)
        CP(permb[:], cmf[:, C_PERM:C_PERM + 128], [Bcm], [Bconst], au=False)
        CP(iotab[:], cmf[:, C_IOTA:C_IOTA + 128], [Bcm], [Bconst], au=False)
        OP("dve", lambda e: e.memset(onesb[:], 1.0), [], [Bconst])
        OP("dve", lambda e: e.memset(onesmb[:], 1.0 / 1024.0), [], [Bconst])
        OP("dve", lambda e: e.memset(onesmf[:], 1.0 / 1024.0), [], [Bconst])
        for var in range(2):
            for kb in range(2):
                for h4 in range(2):
                    c0 = (C_MASK0 if var == 0 else C_MASK1) + kb * 128
                    CP(maskb[:, var, kb, h4, :], cmf[:, c0:c0 + 128], [Bcm], [Bconst], au=False)
        ACT(esink[:], esink[:], AF.Exp, [Bconst], [Bconst], au=False)
        CP(skb[:].rearrange("p a b -> p (a b)"), skf[:], [Bsk], [Bsk], au=False)

        cast_engs = ["act", "dve", "pool"]
        NPR = NPIECE if stop != 1 else 0

        def pl_load(i):
            s = i % 4
            DMA(lambda e: e.dma_start(out=stin[s], in_=dwall[i]), [], [Bst_in[s]], True)

        def pl_cast_store(i):
            s = i % 4
            ce = cast_engs[i % 3]
            if ce == "act":
                ACT(stout[s], stin[s], AF.Copy, [Bst_in[s]], [Bst_out[s]])
            else:
                CP(stout[s], stin[s], [Bst_in[s]], [Bst_out[s]], eng=ce)
            DMA(lambda e: e.dma_start(out=dscr[i], in_=stout[s]), [Bst_out[s]], [Bscr[i]], True, eng="act")

        for i in range(min(3, NPR)):
            pl_load(i)
        for i in range(NPR):
            if i + 3 < NPR:
                pl_load(i + 3)
            pl_cast_store(i)

        Bdg = [BG("dgs0_", 31), BG("dgs1_", 31)]
        Bdgd = BG("dgd", 8)
        for c in range(8 if stop != 1 else 0):
            s = c % 2
            for jt in range(31):
                ACT(dgst[s][:, jt, :], cmf[:, C_ID:C_ID + 128], AF.Copy, [Bcm, Bvec], [Bdg[s][jt]],
                    scale=vcol(V_CW + c * 31 + jt))
            DMA(lambda e, c=c, s=s: e.dma_start(out=ddiag[c], in_=dgst[s][:, :, :].rearrange("p j c -> p (j c)")),
                [Bdg[s]], [Bdgd[c]], True)

        wslot = [0]
        wmode = [2]

        def loadw(g):
            s = wslot[0] % wmode[0]
            wslot[0] += 1
            DMA(lambda e: e.dma_start(out=wbuf[s].rearrange("p (r k) c -> p r (k c)", r=2),
                                      in_=dscr[2 * g:2 * g + 2].rearrange("r p f -> p r f")),
                [Bscr[2 * g], Bscr[2 * g + 1]], [Bw[s]], True)
            return s

        bankrr = [0]

        def nextbank():
            b = bankrr[0]
            bankrr[0] = (b + 1) % 4
            return b

        def proj(bank, s, j, rhs3, c0, n, rbuf):
            for k in range(8):
                MM(ps[:, bank, 0:n], wbuf[s][:, k, j * 128:(j + 1) * 128], rhs3[:, k, c0:c0 + n],
                   k == 0, k == 7, [Bw[s], rbuf], [PB[bank]])

        def colstats(src3, c0, n, srcbuf, outrr, outbuf, tmp, tmpbuf, sqv, sqbuf, bank, sq_au=True):
            for k in range(8):
                ACT(sqv[:, k, c0:c0 + n], src3[:, k, c0:c0 + n], AF.Square, [srcbuf[k]], [sqbuf[k]], au=sq_au)
            for k in range(8):
                MM(ps[:, bank, 0:n], onesmb[:], sqv[:, k, c0:c0 + n], k == 0, k == 7, [Bconst, sqbuf[k]], [PB[bank]], au=sq_au)
            TS(tmp[:, c0:c0 + n], ps[:, bank, 0:n], EPSV, ALU.add, [PB[bank]], [tmpbuf], au=False)
            ACT(tmp[:, c0:c0 + n], tmp[:, c0:c0 + n], AF.Sqrt, [tmpbuf], [tmpbuf], au=False)
            RECIP(outrr[:, c0:c0 + n], tmp[:, c0:c0 + n], [tmpbuf], [outbuf], au=False)

        def rope_tables(c0, n, colbase):
            DMA(lambda e: e.dma_start(out=posi[:, c0:c0 + n],
                                      in_=dpos[0:1, colbase:colbase + n].partition_broadcast(128)[:, 0, :]),
                [], [Bposi])
            sl = slice(c0, c0 + n)
            CP(r1[:, sl], posi[:, sl], [Bposi], [Br1], au=False)
            TS(r1[:, sl], r1[:, sl], vcol(V_INVF), ALU.mult, [Br1, Bvec], [Br1], au=False)
            for (dst, shift, useSgn) in ((sinT, 0.0, True), (cosT, float(np.pi / 2), False)):
                TS(r2[:, sl], r1[:, sl], shift, ALU.add, [Br1], [Br2], au=False)
                TS(r3[:, sl], r2[:, sl], float(1.0 / (2 * np.pi)), ALU.mult, [Br2], [Br3], s2=MAGIC, op1=ALU.add, au=False)
                TS(r3[:, sl], r3[:, sl], MAGIC, ALU.subtract, [Br3], [Br3], au=False)
                STT(r2[:, sl], r3[:, sl], -CW1, r2[:, sl], ALU.mult, ALU.add, [Br3, Br2], [Br2], au=False)
                STT(r2[:, sl], r3[:, sl], -CW2, r2[:, sl], ALU.mult, ALU.add, [Br3, Br2], [Br2], au=False)
                TS(r2[:, sl], r2[:, sl], 3.1415925, ALU.min, [Br2], [Br2], s2=-3.1415925, op1=ALU.max, au=False)
                if useSgn:
                    ACT(dst[:, sl], r2[:, sl], AF.Sin, [Br2, Bvec], [Bcs], scale=vcol(V_SGN), au=False)
                else:
                    ACT(dst[:, sl], r2[:, sl], AF.Sin, [Br2], [Bcs], au=False)

        pending_final = [None]

        def ckpt(i):
            if stop == i:
                P.halted = True

        if stop in (1, 2):
            P.halted = True
        for ti in range(NT):
            par = ti % 2
            c0 = 0 if ti == 0 else 128
            n = 128 + T - c0
            xcol = HALO + ti * T
            barrier()
            xt = xts[par]
            Bx = Bxs[par]

            def prefetch(tn):
                pn = tn % 2
                xc = HALO + tn * T
                cc0 = 0 if tn == 0 else 128
                DMA(lambda e: e.dma_start(out=xts[pn][:], in_=dx[:, :, xc:xc + T].rearrange("k p t -> p k t")),
                    [], [Bxs[pn]])
                rope_tables(cc0, 128 + T - cc0, xc - 128 + cc0)
                colstats(xts[pn], 0, T, Bxs[pn], rn1, Brn1, st2, Bs2, sqn, Bsqn, 6, sq_au=False)

            if ti == 0:
                prefetch(0)
            for k in range(8):
                STT(hT[:, k, 128:128 + T], xt[:, k, :], vcol(V_G1 + k), rn1[:, 0:T], ALU.mult, ALU.mult,
                    [Bx[k], Bvec, Brn1], [BhT[k]])
            if ti == 0:
                DMA(lambda e: e.dma_start(out=ysb[:, :, 0:128], in_=dx[:, :, 0:128].rearrange("k p t -> p k t")),
                    [], [Bys], True)
                colstats(ysb, 0, 128, Bys, st3, Bs3, st2, Bs2, sqb, Bsq, 4)
                for k in range(8):
                    STT(hT[:, k, 0:128], ysb[:, k, 0:128], vcol(V_G1 + k), st3[:, 0:128], ALU.mult, ALU.mult,
                        [Bys[k], Bvec, Bs3], [BhT[k]])
            else:
                for c in range(8):
                    CP(ubuf[:, par, c, 0:32], ubuf[:, 1 - par, c, T:T + 32], [Bub[1 - par][c]], [Bub[par][c]], eng="pool", au=False)
                for g in range(2):
                    CP(kbuf[:, par, g, 0:128], kbuf[:, 1 - par, g, T:T + 128], [Bkb[1 - par]], [Bkb[par]], eng="pool", au=False)
                CP(vdup[:, par, 0, :, :], vdup[:, 1 - par, 2, :, :], [Bvd[1 - par][2]], [Bvd[par][0]], eng="pool", au=False)

            ckpt(3)
            cu0 = 96 if ti == 0 else 128
            nu = 128 + T - cu0
            wsl = {}
            sgl = [sg1, sg2]

            def glu_proj(c):
                pr, j = c // 4, c % 4
                if j == 0:
                    wsl[pr] = (loadw(2 * pr), loadw(2 * pr + 1))
                sv, sgt = wsl[pr]
                bA = nextbank()
                proj(bA, sv, j, hT, cu0, nu, BhT)
                bB = nextbank()
                proj(bB, sgt, j, hT, cu0, nu, BhT)
                sg = sgl[c % 2]
                ACT(sg[:, 0:nu], ps[:, bB, 0:nu], AF.Sigmoid, [PB[bB], Bvec], [Bsgs[c % 2]],
                    bias=vcol(V_BIN + _chunk_gate(c)), au=False)
                STT(ubuf[:, par, c, cu0 - 96:cu0 - 96 + nu], ps[:, bA, 0:nu], vcol(V_BIN + _chunk_val(c)),
                    sg[:, 0:nu], ALU.add, ALU.mult, [PB[bA], Bvec, Bsgs[c % 2]], [Bub[par][c]], au=False)
                if ti == 0:
                    TS(ubuf[:, par, c, 0:32], ubuf[:, par, c, 0:32], vcol(V_HV), ALU.mult,
                       [Bub[par][c], Bvec], [Bub[par][c]], au=False)
                ds_ = c % 2
                DMA(lambda e: e.dma_start(out=diag[ds_][:, :, :].rearrange("p j c -> p (j c)"), in_=ddiag[c]),
                    [Bdgd[c]], [Bdiag[ds_]], True)

            def conv(c):
                ds_ = c % 2
                bC = nextbank()
                for jt in range(31):
                    MM(ps[:, bC, 0:T], diag[ds_][:, jt, :], ubuf[:, par, c, 2 + jt:2 + jt + T],
                       jt == 0, jt == 30, [Bdiag[ds_], Bub[par][c]], [PB[bC]])
                ACT(ysb[:, c, :], ps[:, bC, 0:T], AF.Identity, [PB[bC], Bvec], [Bys[c]], bias=vcol(V_CB + c))
                ACT(sqb[:, c, 0:T], ps[:, bC, 0:T], AF.Square, [PB[bC], Bvec], [Bsq[c]], bias=vcol(V_CB + c))

            glu_proj(0)
            if pending_final[0] is not None:
                pending_final[0]()
                pending_final[0] = None
            for c in range(8):
                if c + 1 < 8:
                    glu_proj(c + 1)
                conv(c)
            wmode[0] = 4
            for c in range(8):
                MM(ps[:, 4, 0:T], onesmf[:], ysb[:, c, :], c == 0, c == 7, [Bconst, Bys[c]], [PB[4]])
            for c in range(8):
                MM(ps[:, 5, 0:T], onesmb[:], sqb[:, c, 0:T], c == 0, c == 7, [Bconst, Bsq[c]], [PB[5]])
            CP(st1[:, 0:T], ps[:, 4, 0:T], [PB[4]], [Bs1], au=False)
            TTo(st2[:, 0:T], st1[:, 0:T], st1[:, 0:T], ALU.mult, [Bs1], [Bs2], au=False)
            TTo(st2[:, 0:T], ps[:, 5, 0:T], st2[:, 0:T], ALU.subtract, [PB[5], Bs2], [Bs2], au=False)
            TS(st2[:, 0:T], st2[:, 0:T], EPSV, ALU.add, [Bs2], [Bs2], au=False)
            ACT(st2[:, 0:T], st2[:, 0:T], AF.Sqrt, [Bs2], [Bs2], au=False)
            RECIP(st3[:, 0:T], st2[:, 0:T], [Bs2], [Bs3], au=False)
            STT(st1[:, 0:T], st1[:, 0:T], -1.0, st3[:, 0:T], ALU.mult, ALU.mult, [Bs1, Bs3], [Bs1], au=False)
            for c in range(8):
                ra_, rb_ = (r1, r2) if c % 2 == 0 else (r3, r4)
                Ba_, Bb_ = ([Br1, Bra[0]], [Br2, Brb[0]]) if c % 2 == 0 else ([Br3, Bra[1]], [Brb[1]])
                TTo(ra_[:, 0:T], ysb[:, c, :], st3[:, 0:T], ALU.mult, [Bys[c], Bs3], Ba_)
                TTo(rb_[:, 0:T], ra_[:, 0:T], st1[:, 0:T], ALU.add, Ba_ + [Bs1], Bb_, au=False)
                ACT(sT[:, c, :], rb_[:, 0:T], AF.Silu, Bb_ + [Bvec], [BsT[c]], bias=vcol(V_LNB + c), scale=vcol(V_LNG + c))

            ckpt(4)
            rcnt = [0]

            def rope_chunk(bank, outap, cc0, nn, bcol, outbuf):
                i_ = rcnt[0] % 2
                rcnt[0] += 1
                qb_ = (qb, qb2)[i_]
                ra_, rb_ = ((r1, r2), (r3, r4))[i_]
                Bq_ = [Bqb, Bqbs[0]] if i_ == 0 else [Bqbs[1]]
                Ba_ = [Br1, Bra[0]] if i_ == 0 else [Br3, Bra[1]]
                Bb_ = [Br2, Brb[0]] if i_ == 0 else [Brb[1]]
                ACT(qb_[:, cc0:cc0 + nn], ps[:, bank, 0:nn], AF.Identity, [PB[bank], Bvec], Bq_, bias=vcol(bcol), au=False)
                b2 = nextbank()
                MM(ps[:, b2, 0:nn], permb[:], qb_[:, cc0:cc0 + nn], True, True, [Bconst] + Bq_, [PB[b2]], au=False)
                TTo(ra_[:, cc0:cc0 + nn], qb_[:, cc0:cc0 + nn], cosT[:, cc0:cc0 + nn], ALU.mult, Bq_ + [Bcs], Ba_, au=False)
                TTo(rb_[:, cc0:cc0 + nn], ps[:, b2, 0:nn], sinT[:, cc0:cc0 + nn], ALU.mult, [PB[b2], Bcs], Bb_, au=False)
                TTo(outap, ra_[:, cc0:cc0 + nn], rb_[:, cc0:cc0 + nn], ALU.add, Ba_ + Bb_, [outbuf], au=True)

            for qg in range(2):
                s = loadw(4 + qg)
                for j in range(4):
                    cq = qg * 4 + j
                    b = nextbank()
                    proj(b, s, j, hT, 128, T, BhT)
                    rope_chunk(b, qrope[:, cq, :], 128, T, V_BIN + 16 + cq, Bqr[cq])
            s = loadw(6)
            for g in range(2):
                b = nextbank()
                proj(b, s, g, hT, c0, n, BhT)
                rope_chunk(b, kbuf[:, par, g, c0:c0 + n], c0, n, V_BIN + 24 + g, Bkb[par])
            for blk in range(3):
                if ti > 0 and blk == 0:
                    continue
                b = nextbank()
                for k in range(8):
                    MM(ps[:, b, 0:128], hT[:, k, blk * 128:(blk + 1) * 128], wbuf[s][:, k, 256:384],
                       k == 0, k == 7, [BhT, Bw[s]], [PB[b]])
                for dup in range(2):
                    TTo(vdup[:, par, blk, :, dup * 64:(dup + 1) * 64],
                        ps[:, b, 0:128].rearrange("p (g d) -> p g d", g=2),
                        bvb[:].rearrange("p (g d) -> p g d", g=2), ALU.add, [PB[b], Bconst], [Bvd[par][blk]], au=False)

            ckpt(5)
            iters = [(b, g, hg) for b in range(2) for g in range(2) for hg in range(2)]

            def att_S(i):
                b, g, hg = iters[i]
                sb0 = 6 if i % 2 == 0 else 2
                var = 0 if (ti == 0 and b == 0) else 1
                j0 = (8 * g + 4 * hg) // 2
                for half in range(2):
                    MM(ps[:, sb0 + half, :], identb[:], maskb[:, var, :, :, :].rearrange("p k a q -> p (k a q)"),
                       True, False, [Bconst], [PB[sb0 + half]], au=False, sgc=True)
                for half in range(2):
                    pa = slice(half * 64, (half + 1) * 64)
                    for kb in range(2):
                        for a in range(2):
                            MM(ps[:, sb0 + half, (kb * 2 + a) * 128:(kb * 2 + a + 1) * 128],
                               kbuf[pa, par, g, (b + kb) * 128:(b + kb + 1) * 128],
                               qrope[pa, j0 + a, b * 128:(b + 1) * 128],
                               False, True, [Bkb[par], Bqr[j0 + a]], [PB[sb0 + half]], sgc=True)

            def att_rest(i):
                b, g, hg = iters[i]
                sl_ = i % 2
                sb0 = 6 if i % 2 == 0 else 2
                ob, db = (4, 5) if i % 2 == 0 else (0, 1)
                h0 = 8 * g + 4 * hg
                j0 = h0 // 2
                ACT(eT[:, sl_, :, :, :].rearrange("p k h q -> p (k h q)"),
                    ps[:, sb0:sb0 + 2, :].rearrange("p k c -> p (k c)"), AF.Exp, [PB[sb0], PB[sb0 + 1]], [BeT[sl_]],
                    scale=0.125, au=False)
                eTv = eT[:, sl_, :, :, :].rearrange("p half (kb a) q -> p half kb a q", kb=2)
                for kb in range(2):
                    for half in range(2):
                        MM(ps[:, ob, half * 256:(half + 1) * 256], vdup[:, par, b + kb, g, :],
                           eTv[:, half, kb, :, :].rearrange("p a q -> p (a q)"),
                           (kb == 0 and half == 0), kb == 1, [Bvd[par][b + kb], BeT[sl_]], [PB[ob]], au=False, sgc=True)
                for kb in range(2):
                    for half in range(2):
                        MM(ps[:, db, half * 256:(half + 1) * 256], onesb[:],
                           eTv[:, half, kb, :, :].rearrange("p a q -> p (a q)"),
                           (kb == 0 and half == 0), kb == 1, [Bconst, BeT[sl_]], [PB[db]], au=False, sgc=True)
                TTo(den[:].rearrange("p (half a q) -> p half a q", half=2, a=2),
                    ps[:, db, :].rearrange("p (half a q) -> p half a q", half=2, a=2),
                    esink[:, h0:h0 + 4].rearrange("p (a half) -> p half a", half=2).unsqueeze(3).to_broadcast([128, 2, 2, 128]),
                    ALU.add, [PB[db], Bconst], [Bden], au=False)
                RECIP(rden[:], den[:], [Bden], [Brden], au=False)
                for half in range(2):
                    pa = slice(half * 64, (half + 1) * 64)
                    TTo(attnT[pa, j0:j0 + 2, b * 128:(b + 1) * 128],
                        ps[pa, ob, half * 256:(half + 1) * 256].rearrange("p (a q) -> p a q", a=2),
                        rden[pa, half * 256:(half + 1) * 256].rearrange("p (a q) -> p a q", a=2),
                        ALU.mult, [PB[ob], Brden], [Bat[j0], Bat[j0 + 1]])

            att_S(0)
            for i in range(8):
                if i + 1 < 8:
                    att_S(i + 1)
                att_rest(i)

            ckpt(6)
            Bm1s = BG("m1s", 4)
            for jg in range(2):
                sa = loadw(11 + jg)
                sb_ = loadw(7 + jg)
                for j in range(4):
                    c = jg * 4 + j
                    bA = nextbank()
                    proj(bA, sa, j, sT, 0, T, BsT)
                    bB = nextbank()
                    proj(bB, sb_, j, hT, 128, T, BhT)
                    sg = sgl[j % 2]
                    ACT(sg[:, 0:T], ps[:, bB, 0:T], AF.Sigmoid, [PB[bB], Bvec], [Bsgs[j % 2]], bias=vcol(V_BIN + 28 + c), au=False)
                    TTo(m1buf[:, j, :], ps[:, bA, 0:T], sg[:, 0:T], ALU.mult, [PB[bA], Bsgs[j % 2]], [Bm1s[j]], au=False)
                sa = loadw(13 + jg)
                sb_ = loadw(9 + jg)
                for j in range(4):
                    c = jg * 4 + j
                    bA = nextbank()
                    proj(bA, sa, j, attnT, 0, T, Bat)
                    bB = nextbank()
                    proj(bB, sb_, j, hT, 128, T, BhT)
                    sg = sgl[j % 2]
                    rt_ = (r3, r4)[j % 2]
                    Brt_ = [Br3, Bra[1]] if j % 2 == 0 else [Brb[1]]
                    ACT(sg[:, 0:T], ps[:, bB, 0:T], AF.Sigmoid, [PB[bB], Bvec], [Bsgs[j % 2]], bias=vcol(V_BIN + 36 + c), au=False)
                    TTo(rt_[:, 0:T], ps[:, bA, 0:T], sg[:, 0:T], ALU.mult, [PB[bA], Bsgs[j % 2]], Brt_, au=False)
                    TTo(mergedT[:, c, 0:T], rt_[:, 0:T], m1buf[:, j, :], ALU.add, Brt_ + [Bm1s[j]], [Bsq[c]])
            for og in range(2):
                s = loadw(15 + og)
                for j in range(4):
                    c = og * 4 + j
                    b = nextbank()
                    proj(b, s, j, mergedT, 0, T, Bsq)
                    TTo(xt[:, c, :], xt[:, c, :], ps[:, b, 0:T], ALU.add, [Bx[c], PB[b]], [Bx[c]], au=False)
            if dbg and ti == 0:
                DMA(lambda e, xt_=xt: e.dma_start(out=ddbg.rearrange("k p t -> p k t"), in_=xt_[:]), [Bx], [Buf("dbgo")], final=True)

            ckpt(7)
            colstats(xt, 0, T, Bx, st1, Bs1, st2, Bs2, sqb, Bsq, 4)
            for k in range(8):
                STT(h2T[:, k, :], xt[:, k, :], vcol(V_G2 + k), st1[:, 0:T], ALU.mult, ALU.mult, [Bx[k], Bvec, Bs1], [Bh2], au=False)
            wmode[0] = 2
            barrier()
            for g4 in range(4):
                s = loadw(17 + g4)
                for j in range(4):
                    hc = g4 * 4 + j
                    b = nextbank()
                    for k in range(8):
                        MM(ps[:, b, 0:T], wbuf[s][:, k, j * 128:(j + 1) * 128], h2T[:, k, :], k == 0, k == 7,
                           [Bw[s], Bh2], [PB[b]])
                    if hc % 2 == 0:
                        ACT(qTb[:, hc, :], ps[:, b, 0:T], AF.Copy, [PB[b]], [BqT[hc]])
                    else:
                        CP(qTb[:, hc, :], ps[:, b, 0:T], [PB[b]], [BqT[hc]])
            v16v = v16[:, :, :].rearrange("p (h c) k -> p h c k", c=2)
            i16fv = i16f[:, :, :].rearrange("p (h c) k -> p h c k", c=2)
            B4 = [128, 8, 16, 16]
            for tc in range(2):
                tcs = slice(tc * 128, (tc + 1) * 128)
                for g4 in range(4):
                    for l in range(4):
                        hc = g4 * 4 + l
                        MM(ps[:, 5, l * 128:(l + 1) * 128], qTb[:, hc, tcs], skb[:, hc, :], True, True,
                           [BqT[hc], Bsk], [PB[5]])
                    def L1(step, l):
                        hc = g4 * 4 + l
                        src_ = ps[:, 5, l * 128:(l + 1) * 128]
                        if step == 0:
                            OP("dve", lambda e: e.max(out=v16[:, hc, 0:8], in_=src_), [PB[5]], [Bv16[hc]])
                        elif step == 1:
                            OP("dve", lambda e: e.max_index(out=i16[:, hc, 0:8], in_max=v16[:, hc, 0:8], in_values=src_),
                               [PB[5], Bv16[hc]], [Bi16[hc]])
                        elif step == 2:
                            OP("dve", lambda e: e.match_replace(out=scw[:, l, :], in_to_replace=v16[:, hc, 0:8],
                                                               in_values=src_, imm_value=-1e30),
                               [PB[5], Bv16[hc]], [Bscw[l]], True)
                        elif step == 3:
                            OP("dve", lambda e: e.max(out=v16[:, hc, 8:16], in_=scw[:, l, :]), [Bscw[l]], [Bv16[hc]], True)
                        else:
                            OP("dve", lambda e: e.max_index(out=i16[:, hc, 8:16], in_max=v16[:, hc, 8:16], in_values=scw[:, l, :]),
                               [Bscw[l], Bv16[hc]], [Bi16[hc]], True)
                    for step in (0, 2, 1, 3, 4):
                        for l in range(4):
                            L1(step, l)
                CP(i16f[:], i16[:], [Bi16], [Bi16f], au=False)
                TTo(cand[:], v16v[:, :, 0, :].unsqueeze(3).to_broadcast(B4), v16v[:, :, 1, :].unsqueeze(2).to_broadcast(B4),
                    ALU.add, [Bv16], [Bcand])
                def L2(step, h):
                    src_ = cand[:, h, :, :].rearrange("p a b -> p (a b)")
                    if step == 0:
                        OP("dve", lambda e: e.max(out=best[:, h, 0:8], in_=src_), [Bcand], [Bbest[h]], True)
                    elif step == 1:
                        OP("dve", lambda e: e.max_index(out=posu[:, h, 0:8], in_max=best[:, h, 0:8], in_values=src_),
                           [Bcand, Bbest[h]], [Bpos[h]], True)
                    elif step == 2:
                        OP("dve", lambda e: e.match_replace(out=work2[:, h, :], in_to_replace=best[:, h, 0:8],
                                                           in_values=src_, imm_value=-1e30), [Bcand, Bbest[h]], [Bw2[h]], True)
                    elif step == 3:
                        OP("dve", lambda e: e.max(out=best[:, h, 8:16], in_=work2[:, h, :]), [Bw2[h]], [Bbest[h]], True)
                    else:
                        OP("dve", lambda e: e.max_index(out=posu[:, h, 8:16], in_max=best[:, h, 8:16], in_values=work2[:, h, :]),
                           [Bw2[h], Bbest[h]], [Bpos[h]], True)
                for step in (0, 2, 1, 3, 4):
                    for h in range(8):
                        L2(step, h)
                CP(posf[:], posu[:], [Bpos], [Btk], au=False)
                OP("dve", lambda e: e.tensor_single_scalar(out=k1u[:], in_=posu[:], scalar=4, op=ALU.logical_shift_right),
                   [Bpos], [Btk])
                CP(k1f[:], k1u[:], [Btk], [Btk], au=False)
                STT(k2f[:], k1f[:], -16.0, posf[:], ALU.mult, ALU.add, [Btk], [Btk], au=False)
                TTo(ebuf[:], best[:], best[:, :, 0:1].to_broadcast([128, 8, 16]), ALU.subtract, [Bbest], [Btk], au=False)
                ACT(ebuf[:], ebuf[:], AF.Exp, [Btk], [Btk], au=False)
                OP("dve", lambda e: e.tensor_reduce(out=Zs[:], in_=ebuf[:], axis=AX.X, op=ALU.add), [Btk], [Btk])
                RECIP(Zs[:], Zs[:], [Btk], [Btk], au=False)
                TTo(gate[:], ebuf[:], Zs[:, :].unsqueeze(2).to_broadcast([128, 8, 16]), ALU.mult, [Btk], [Btk], au=False)
                io16 = cmf[:, C_IOTA16:C_IOTA16 + 16].unsqueeze(1).unsqueeze(1).to_broadcast(B4)
                for (kf, cidx, dst) in ((k1f, 0, av), (k2f, 1, bvv)):
                    TTo(E1[:], kf[:].unsqueeze(3).to_broadcast(B4), io16, ALU.is_equal, [Btk, Bcm], [BE1])
                    TTo(E1[:], E1[:], i16fv[:, :, cidx, :].unsqueeze(2).to_broadcast(B4), ALU.mult, [BE1, Bi16f], [BE1])
                    OP("dve", lambda e, dst=dst: e.tensor_reduce(out=dst[:], in_=E1[:], axis=AX.X, op=ALU.add), [BE1], [Btk], True)
                for idx, srcv in enumerate((av, bvv, gate)):
                    OP("pe", lambda e, idx=idx, srcv=srcv: e.transpose(out=ps[:, 5, idx * 128:(idx + 1) * 128],
                                                                     in_=srcv[:].rearrange("p h k -> p (h k)"),
                                                                     identity=cmf[:, C_ID:C_ID + 128]),
                       [Btk, Bcm], [PB[5]])
                CP(abgT[:, tc, :, :].rearrange("p a b -> p (a b)"), ps[:, 5, 0:384], [PB[5]], [Babg], au=False)

            ckpt(8)
            barrier()
            for tb in range(T // 8):
                sl_ = tb % 2
                bk0 = 4 + 2 * (tb % 2)
                for i in range(8):
                    t = tb * 8 + i
                    tc, tl = t // 128, t % 128
                    TS(Pt[:, sl_, i, :], iotab[:], abgT[:, tc, 0, tl:tl + 1], ALU.is_equal, [Bconst, Babg], [BPt[sl_][i]],
                       s2=abgT[:, tc, 2, tl:tl + 1], op1=ALU.mult, au=False)
                    TS(Qt[:, sl_, i, :], iotab[:], abgT[:, tc, 1, tl:tl + 1], ALU.is_equal, [Bconst, Babg], [BQt[sl_][i]],
                       au=False)
                for i in range(8):
                    bank = bk0 + i // 4
                    MM(ps[:, bank, (i % 4) * 128:(i % 4 + 1) * 128], Qt[:, sl_, i, :], Pt[:, sl_, i, :], True, True,
                       [BQt[sl_][i], BPt[sl_][i]], [PB[bank]], au=False)
                for hb in range(2):
                    t0 = tb * 8 + hb * 4
                    ACT(G[:, :, t0:t0 + 4], ps[:, bk0 + hb, :].rearrange("p (t i) -> p i t", t=4), AF.Copy,
                        [PB[bk0 + hb]], [BGm[tb * 2 + hb]])

            ckpt(9)
            PBA = [PB[4], PB[5]]

            def stageA(ec):
                eg, cc, hs = ec // 2, ec % 2, ec % 2
                sl2 = eg % 2
                if cc == 0:
                    DMA(lambda e: e.dma_start(out=UTs[:, sl2, :, :].rearrange("p k c -> p (k c)"), in_=dscr[UV0 + eg]),
                        [Bscr[UV0 + eg]], [BUT[sl2]])
                    DMA(lambda e: e.dma_start(out=Vs[:, sl2, :, :].rearrange("p k c -> p (k c)"), in_=dscr[UV0 + 64 + eg]),
                        [Bscr[UV0 + 64 + eg]], [BVs[sl2]])
                for k in range(8):
                    MM(ps[:, 4 + hs, 0:256], UTs[:, sl2, k, cc * 128:(cc + 1) * 128], h2T[:, k, :],
                       k == 0, k == 7, [BUT[sl2], Bh2], [PBA[hs]], au=False)
                ACT(gl[:, hs, :], ps[:, 4 + hs, 0:256], AF.Gelu, [PBA[hs]], [Bgl[hs]], au=False)
                TTo(GA[:, hs, :], gl[:, hs, :], G[:, ec, :], ALU.mult, [Bgl[hs], BGm], [BGA[hs]])

            def stageV(ec):
                eg, cc, hs = ec // 2, ec % 2, ec % 2
                sl2 = eg % 2
                for dk in range(8):
                    MM(ps[:, dk // 2, (dk % 2) * 256:(dk % 2 + 1) * 256], Vs[:, sl2, cc, dk * 128:(dk + 1) * 128],
                       GA[:, hs, :], (ec == 0 and dk % 2 == 0), ec == 127, [BVs[sl2], BGA[hs]], [PB[dk // 2]],
                       au=False, sgc=True)

            stageA(0)
            for ec in range(128):
                if ec + 1 < 128:
                    stageA(ec + 1)
                stageV(ec)
                if ec == 2 and ti + 1 < NT:
                    capture[0] = []
                    prefetch(ti + 1)
                    pending = capture[0]
                    capture[0] = None
                if ec >= 2 and ti + 1 < NT and pending:
                    replay(pending.pop(0))
            if ti + 1 < NT:
                while pending:
                    replay(pending.pop(0))

            ckpt(10)
            for dk in range(8):
                TTo(xt[:, dk, :], xt[:, dk, :], ps[:, dk // 2, (dk % 2) * 256:(dk % 2 + 1) * 256], ALU.add,
                    [Bx[dk], PB[dk // 2]], [Bx[dk]], au=False)
            def make_final(ti_, xt_, Bx_):
                def fin():
                    colstats(xt_, 0, T, Bx_, st1, Bs1, st3, Bs3, sqn, Bsqn, 4, sq_au=False)
                    for k in range(8):
                        STT(xt_[:, k, :], xt_[:, k, :], vcol(V_GF + k), st1[:, 0:T], ALU.mult, ALU.mult,
                            [Bx_[k], Bvec, Bs1], [Bx_[k]], au=False)
                    DMA(lambda e: e.dma_start(out=dout[:, :, ti_ * T:(ti_ + 1) * T].rearrange("k p t -> p k t"), in_=xt_[:, :, :]),
                        [Bx_], [Bout], False, final=True)
                return fin

            pending_final[0] = make_final(ti, xt, Bx)
            if ti == NT - 1:
                pending_final[0]()
                pending_final[0] = None

        P.emit()
    return nc


def _prep_shared(inp):
    f = np.float32
    w_in = np.asarray(inp["w_in"], f)[0]
    b_in = np.asarray(inp["b_in"], f)[0]
    blk = lambda base, c: list(range(base + c * 128, base + (c + 1) * 128))
    chunks = []
    for pr in range(2):
        for c in range(4):
            chunks.append(blk(0, pr * 4 + c))
        for c in range(4):
            chunks.append(blk(1024, pr * 4 + c))
    for c in range(8):
        chunks.append(blk(2048, c))
    k0 = list(range(3072, 3136))
    k1 = list(range(3136, 3200))
    chunks.append(k0 + k0)
    chunks.append(k1 + k1)
    chunks.append(list(range(3200, 3328)))
    chunks.append(list(range(3200, 3328)))
    for c in range(8):
        chunks.append(blk(3328, c))
    for c in range(8):
        chunks.append(blk(4352, c))
    assert len(chunks) == 44
    colidx = np.array(sum(chunks, []), dtype=np.int64)
    w_perm = w_in[:, colidx]
    b_perm = b_in[colidx]
    mats = [w_perm[:, g * 512:(g + 1) * 512] for g in range(11)]
    for name in ("w_conv_out", "w_attn_o", "w_out"):
        w = np.asarray(inp[name], f)[0]
        mats += [w[:, 0:512], w[:, 512:1024]]
    wpq = np.asarray(inp["w_peer_q"], f)[0]
    mats += [wpq[:, g * 512:(g + 1) * 512] for g in range(4)]
    assert len(mats) == NMIXG
    wall = np.empty((NPIECE, 128, 2048), f)
    for g, m in enumerate(mats):
        a = m.reshape(8, 128, 512).transpose(1, 0, 2)
        wall[2 * g] = a[:, 0:4, :].reshape(128, 2048)
        wall[2 * g + 1] = a[:, 4:8, :].reshape(128, 2048)
    U = np.asarray(inp["peer_u"], f)[0]
    V = np.asarray(inp["peer_v"], f)[0]
    wall[UV0:UV0 + 64] = U.reshape(64, 256, 8, 128).transpose(0, 3, 2, 1).reshape(64, 128, 2048)
    wall[UV0 + 64:UV0 + 128] = V.reshape(64, 2, 128, 1024).transpose(0, 2, 1, 3).reshape(64, 128, 2048)

    vec = np.zeros((128, NV), f)
    col = lambda v: np.asarray(v, f).reshape(-1, 128).T
    vec[:, V_G1:V_G1 + 8] = col(inp["norm1_g"][0])
    vec[:, V_BIN:V_BIN + 44] = col(b_perm)
    vec[:, V_CB:V_CB + 8] = col(inp["conv_b"][0])
    vec[:, V_LNG:V_LNG + 8] = col(inp["conv_ln_g"][0])
    vec[:, V_LNB:V_LNB + 8] = col(inp["conv_ln_b"][0])
    vec[:, V_G2:V_G2 + 8] = col(inp["norm2_g"][0])
    vec[:, V_GF:V_GF + 8] = col(inp["final_g"])
    p = np.arange(128)
    invf = (np.float32(10000.0) ** (-(np.arange(32, dtype=f) * f(2.0) / f(64)))).astype(f)
    vec[:, V_INVF] = invf[p % 32]
    vec[:, V_SGN] = np.where(p % 64 < 32, -1.0, 1.0)
    cw = np.asarray(inp["conv_w"], f)[0]
    vec[:, V_CW:V_CW + 248] = cw.reshape(31, 8, 128).transpose(2, 1, 0).reshape(128, 248)

    cm = np.zeros((128, NCM), f)
    cm[:, C_ID:C_ID + 128] = np.eye(128, dtype=f)
    cm[p, C_PERM + (p ^ 32)] = 1.0
    cm[:, C_IOTA:C_IOTA + 128] = np.arange(128, dtype=f)[None, :]
    cm[:, C_IOTA16:C_IOTA16 + 16] = np.arange(16, dtype=f)[None, :]
    kk = np.arange(128)[:, None]
    qq = np.arange(128)[None, :]
    NEGM = f(-240000.0)
    m_prev = np.where(kk > qq, f(0), NEGM).astype(f)
    m_cur = np.where(kk <= qq, f(0), NEGM).astype(f)
    cm[:, C_MASK1:C_MASK1 + 128] = m_prev
    cm[:, C_MASK1 + 128:C_MASK1 + 256] = m_cur
    cm[:, C_MASK0 + 128:C_MASK0 + 256] = m_cur
    rows = np.concatenate([b_in[3200:3328], np.asarray(inp["attn_sinks"], f)[0]]).reshape(1, 144).astype(f)
    sk = np.asarray(inp["peer_sub_keys"], f)[0]
    skT = np.ascontiguousarray(sk.transpose(3, 0, 1, 2).reshape(128, 2048))
    return dict(wall=wall, vec=vec, cm=cm, rows=rows, skT=skT), m_prev, NEGM


_NC_CACHE = {}


def kernel(**inputs):
    x = np.asarray(inputs["x"], np.float32)
    pos = np.asarray(inputs["positions"], np.int32)
    B, S, _ = x.shape
    TOK = S // 2
    NT = TOK // TT
    shared, m_prev, NEGM = _prep_shared(inputs)
    in_maps = []
    for core in range(8):
        b, hs = core // 2, core % 2
        s0 = hs * TOK
        xT = np.zeros((8, 128, TOK + HALO), np.float32)
        pp = np.zeros((1, TOK + HALO), np.int32)
        if hs == 0:
            xs = x[b, 0:TOK]
            xT[:, :, HALO:] = xs.T.reshape(8, 128, TOK)
            pp[0, HALO:] = pos[b, 0:TOK]
        else:
            xs = x[b, s0 - HALO:s0 + TOK]
            xT[:] = xs.T.reshape(8, 128, TOK + HALO)
            pp[0] = pos[b, s0 - HALO:s0 + TOK]
        vec = shared["vec"].copy()
        vec[:, V_HV] = 0.0 if hs == 0 else 1.0
        cm = shared["cm"].copy()
        if hs == 0:
            cm[:, C_MASK0:C_MASK0 + 128] = NEGM
        else:
            cm[:, C_MASK0:C_MASK0 + 128] = m_prev
        in_maps.append(dict(xT=xT, pos=pp, wall=shared["wall"], vec=vec, cm=cm, rows=shared["rows"], skT=shared["skT"]))
    if NT not in _NC_CACHE:
        _NC_CACHE[NT] = build_nc(NT)
    nc = _NC_CACHE[NT]
    res = run_bass_kernel_spmd(nc, in_maps, core_ids=list(range(8)))
    out = np.empty((B, S, D), np.float32)
    for core in range(8):
        b, hs = core // 2, core % 2
        oT = np.asarray(res.results[core]["outT"], np.float32)
        out[b, hs * TOK:(hs + 1) * TOK, :] = oT.reshape(1024, TOK).T
    return out
```

```python
import numpy as np
from contextlib import ExitStack
import concourse.bass as bass
import concourse.mybir as mybir
from concourse.bass_utils import run_bass_kernel_spmd

F32 = mybir.dt.float32
BF16 = mybir.dt.bfloat16
I32 = mybir.dt.int32
U32 = mybir.dt.uint32
AF = mybir.ActivationFunctionType
ALU = mybir.AluOpType
AX = mybir.AxisListType


class Buf:
    __slots__ = ("name", "w", "r")

    def __init__(self, name=""):
        self.name = name
        self.w = None
        self.r = []


class BG(list):
    def __init__(self, name, n):
        super().__init__(Buf("%s%d" % (name, i)) for i in range(n))


def _flat(bs):
    out = []
    for b in bs:
        if isinstance(b, list):
            out.extend(_flat(b))
        else:
            out.append(b)
    return out


class _Ins:
    __slots__ = ("eng", "fn", "deps", "dma", "idx", "sig", "cnt", "semi", "final")

    def __init__(self, eng, fn, dma):
        self.eng = eng
        self.fn = fn
        self.deps = set()
        self.dma = dma
        self.sig = False
        self.cnt = 0
        self.semi = 0
        self.final = False


class Prog:
    NDMA_SEM = 12
    ENGS = ("pe", "act", "dve", "pool", "sp")

    def __init__(self, nc, es):
        self.nc = nc
        self.es = es
        self.q = {e: [] for e in self.ENGS}
        self.all = []
        self.dma_engine = "sp"
        self.halted = False

    def _add(self, ins, reads, writes):
        if self.halted:
            return ins
        reads = _flat(reads)
        writes = _flat(writes)
        for b in reads:
            if b.w is not None:
                ins.deps.add(b.w)
        for b in writes:
            if b.w is not None:
                ins.deps.add(b.w)
            for r in b.r:
                ins.deps.add(r)
        ins.deps.discard(ins)
        for b in reads:
            b.r.append(ins)
        for b in writes:
            b.w = ins
            b.r = []
        ins.idx = len(self.all)
        self.all.append(ins)
        self.q[ins.eng].append(ins)
        return ins

    def op(self, eng, fn, reads=(), writes=()):
        return self._add(_Ins(eng, fn, False), reads, writes)

    def dma(self, fn, reads=(), writes=(), final=False, eng=None):
        ins = _Ins(eng or self.dma_engine, fn, True)
        ins.final = final
        return self._add(ins, reads, writes)

    def emit(self):
        nc = self.nc
        for ins in self.all:
            for d in ins.deps:
                if d.eng == "pe" and ins.eng == "pe" and not d.dma and not ins.dma:
                    continue
                d.sig = True
            if ins.final:
                ins.sig = True
        sems = {e: self.es.enter_context(nc.semaphore("s_" + e)) for e in self.ENGS}
        dsems = [self.es.enter_context(nc.semaphore("d%d" % i)) for i in range(self.NDMA_SEM)]
        cnt = {e: 0 for e in self.ENGS}
        ndma = 0
        dma_prev = {}
        last_on_sem = [None] * self.NDMA_SEM
        for ins in self.all:
            if ins.dma:
                ins.semi = ndma % self.NDMA_SEM
                ins.cnt = 16 * (ndma // self.NDMA_SEM + 1)
                dma_prev[ins] = last_on_sem[ins.semi]
                last_on_sem[ins.semi] = ins
                ndma += 1
            elif ins.sig:
                cnt[ins.eng] += 1
                ins.cnt = cnt[ins.eng]
        finals = [i for i in self.all if i.final]
        block = self.es.enter_context(nc.Block())

        def run(engname, e):
            waited = {}

            def wait_for(d):
                if d.dma:
                    key = ("d", d.semi)
                    sem = dsems[d.semi]
                else:
                    key = ("e", d.eng)
                    sem = sems[d.eng]
                if waited.get(key, 0) >= d.cnt:
                    return
                e.wait_ge(sem, d.cnt)
                waited[key] = d.cnt

            for ins in self.q[engname]:
                for d in sorted(ins.deps, key=lambda z: z.idx):
                    if (d.eng == "pe" and engname == "pe" and not d.dma and not ins.dma):
                        continue
                    wait_for(d)
                if ins.dma:
                    p = dma_prev[ins]
                    if p is not None:
                        wait_for(p)
                h = ins.fn(e)
                if ins.dma:
                    h.then_inc(dsems[ins.semi], 16)
                elif ins.sig:
                    h.then_inc(sems[engname], 1)
            if engname == self.dma_engine:
                for f in finals:
                    wait_for(f)

        @block.sync
        def _(e):
            run("sp", e)

        @block.tensor
        def _(e):
            run("pe", e)

        @block.scalar
        def _(e):
            run("act", e)

        @block.vector
        def _(e):
            run("dve", e)

        @block.gpsimd
        def _(e):
            run("pool", e)


D = 1024
KC = 8
TT = 256
HALO = 128
EPSV = 1e-6
NMIXG = 21
NPIECE = 2 * NMIXG + 128
UV0 = 2 * NMIXG
V_G1, V_BIN, V_CB, V_LNG, V_LNB, V_G2, V_GF, V_INVF, V_SGN, V_HV, V_CW = 0, 8, 52, 60, 68, 76, 84, 92, 93, 94, 95
NV = 95 + 248
C_ID, C_PERM, C_IOTA, C_IOTA16, C_MASK0, C_MASK1 = 0, 128, 256, 384, 400, 656
NCM = 912
MAGIC = 12582912.0
CW1 = 6.28125
CW2 = 2.0 * np.pi - 6.28125


def _chunk_val(c):
    return (c // 4) * 8 + (c % 4)


def _chunk_gate(c):
    return (c // 4) * 8 + 4 + (c % 4)


class _Stop(Exception):
    pass


def build_nc(NT, dbg=False, stop=None):
    T = TT
    TOK = NT * T
    TOKH = TOK + HALO
    nc = bass.Bass("TRN2", target_bir_lowering=False)
    dx = nc.dram_tensor("xT", [8, 128, TOKH], F32, kind="ExternalInput").ap()
    dpos = nc.dram_tensor("pos", [1, TOKH], I32, kind="ExternalInput").ap()
    dwall = nc.dram_tensor("wall", [NPIECE, 128, 2048], F32, kind="ExternalInput").ap()
    dvec = nc.dram_tensor("vec", [128, NV], F32, kind="ExternalInput").ap()
    dcm = nc.dram_tensor("cm", [128, NCM], F32, kind="ExternalInput").ap()
    drow = nc.dram_tensor("rows", [1, 144], F32, kind="ExternalInput").ap()
    dsk = nc.dram_tensor("skT", [128, 2048], F32, kind="ExternalInput").ap()
    dout = nc.dram_tensor("outT", [8, 128, TOK], F32, kind="ExternalOutput").ap()
    dscr = nc.dram_tensor("wscr", [NPIECE, 128, 2048], BF16, kind="Internal").ap()
    ddiag = nc.dram_tensor("dgscr", [8, 128, 31 * 128], BF16, kind="Internal").ap()
    if dbg:
        ddbg = nc.dram_tensor("dbg", [8, 128, T], F32, kind="ExternalOutput").ap()

    es = ExitStack()
    with es:
        def sb(name, shape, dt):
            return es.enter_context(nc.sbuf_tensor("sb_" + name, shape, dt))

        P = Prog(nc, es)
        ps = es.enter_context(nc.psum_tensor("ps", [128, 8, 512], F32))
        PB = [Buf("bank%d" % i) for i in range(8)]

        vec = sb("vec", [128, NV], F32)
        cmf = sb("cmf", [128, NCM], F32)
        identb = sb("identb", [128, 128], BF16)
        permb = sb("permb", [128, 128], BF16)
        onesb = sb("onesb", [128, 128], BF16)
        onesmb = sb("onesmb", [128, 128], BF16)
        onesmf = sb("onesmf", [128, 128], F32)
        iotab = sb("iotab", [128, 128], BF16)
        maskb = sb("maskb", [128, 2, 2, 2, 128], BF16)
        esink = sb("esink", [128, 16], F32)
        bvb = sb("bvb", [128, 128], F32)
        skf = sb("skf", [128, 2048], F32)
        skb = sb("skb", [128, 16, 128], BF16)
        xt_a = sb("xt", [128, 8, T], F32)
        xt_b = sb("xt2", [128, 8, T], F32)
        xts = [xt_a, xt_b]
        sqn = sb("sqn", [128, 8, T], BF16)
        rn1 = sb("rn1", [128, T], F32)
        ubuf = sb("ubuf", [128, 2, 8, 32 + T], BF16)
        kbuf = sb("kbuf", [128, 2, 2, 128 + T], BF16)
        vdup = sb("vdup", [128, 2, 3, 2, 128], BF16)
        cosT = sb("cosT", [128, 128 + T], F32)
        sinT = sb("sinT", [128, 128 + T], F32)
        posi = sb("posi", [128, 128 + T], I32)
        r1 = sb("r1", [128, 128 + T], F32)
        r2 = sb("r2", [128, 128 + T], F32)
        r3 = sb("r3", [128, 128 + T], F32)
        st1 = sb("st1", [128, 128 + T], F32)
        st2 = sb("st2", [128, 128 + T], F32)
        st3 = sb("st3", [128, 128 + T], F32)
        sg1 = sb("sg1", [128, 128 + T], F32)
        sg2 = sb("sg2", [128, 128 + T], F32)
        m1buf = sb("m1buf", [128, 4, T], F32)
        qb = sb("qb", [128, 128 + T], BF16)
        qb2 = sb("qb2", [128, 128 + T], BF16)
        r4 = sb("r4", [128, 128 + T], F32)
        eT = sb("eT", [128, 2, 2, 4, 128], BF16)
        den = sb("den", [128, 512], F32)
        rden = sb("rden", [128, 512], F32)
        rscr = sb("rscr", [128, 512], F32)
        h2T = sb("h2T", [128, 8, T], BF16)
        gl = sb("gl", [128, 2, T], BF16)
        GA = sb("GA", [128, 2, T], BF16)
        Pt = sb("Pt", [128, 2, 8, 128], BF16)
        Qt = sb("Qt", [128, 2, 8, 128], BF16)
        UTs = sb("UTs", [128, 3, 8, 256], BF16)
        Vs = sb("Vs", [128, 3, 2, 1024], BF16)
        v16 = sb("v16", [128, 16, 16], F32)
        i16 = sb("i16", [128, 16, 16], U32)
        i16f = sb("i16f", [128, 16, 16], F32)
        best = sb("best", [128, 8, 16], F32)
        posu = sb("posu", [128, 8, 16], U32)
        k1u = sb("k1u", [128, 8, 16], U32)
        posf = sb("posf", [128, 8, 16], F32)
        k1f = sb("k1f", [128, 8, 16], F32)
        k2f = sb("k2f", [128, 8, 16], F32)
        ebuf = sb("ebuf", [128, 8, 16], F32)
        gate = sb("gate", [128, 8, 16], F32)
        Zs = sb("Zs", [128, 8], F32)
        av = sb("av", [128, 8, 16], F32)
        bvv = sb("bvv", [128, 8, 16], F32)
        abgT = sb("abgT", [128, 2, 3, 128], F32)
        one1 = sb("one1", [128, 4], F32)
        arena = sb("arena", [128, 32768], BF16)

        def av_(off, nbytes, dt):
            a = arena[:, off // 2:(off + nbytes) // 2]
            return a if dt == BF16 else a.bitcast(dt)

        K = 1024
        wbuf = [av_(0, 8 * K, BF16).rearrange("p (k c) -> p k c", k=8),
                av_(8 * K, 8 * K, BF16).rearrange("p (k c) -> p k c", k=8),
                av_(16 * K, 8 * K, BF16).rearrange("p (k c) -> p k c", k=8),
                av_(24 * K, 8 * K, BF16).rearrange("p (k c) -> p k c", k=8)]
        diag = [av_(16 * K, 7936, BF16).rearrange("p (j c) -> p j c", j=31),
                av_(24 * K, 7936, BF16).rearrange("p (j c) -> p j c", j=31)]
        hT = av_(32 * K, 6 * K, BF16).rearrange("p (k c) -> p k c", k=8)
        sqb = av_(38 * K, 6 * K, BF16).rearrange("p (k c) -> p k c", k=8)
        ysb = av_(44 * K, 8 * K, F32).rearrange("p (k c) -> p k c", k=8)
        sT = av_(52 * K, 4 * K, BF16).rearrange("p (k c) -> p k c", k=8)
        qrope = av_(56 * K, 4 * K, BF16).rearrange("p (k c) -> p k c", k=8)
        attnT = av_(60 * K, 4 * K, BF16).rearrange("p (k c) -> p k c", k=8)
        mergedT = sqb
        stin = [av_(i * 8 * K, 8 * K, F32) for i in range(4)]
        stout = [av_(32 * K + i * 4 * K, 4 * K, BF16) for i in range(4)]
        dgst = [av_(48 * K, 7936, BF16).rearrange("p (j c) -> p j c", j=31),
                av_(56 * K, 7936, BF16).rearrange("p (j c) -> p j c", j=31)]
        qTb = av_(16 * K, 8 * K, BF16).rearrange("p (k c) -> p k c", k=16)
        cand = av_(24 * K, 8 * K, F32).rearrange("p (h a b) -> p h a b", h=8, a=16)
        work2 = av_(32 * K, 8 * K, F32).rearrange("p (h c) -> p h c", h=8)
        E1 = av_(40 * K, 8 * K, F32).rearrange("p (h a b) -> p h a b", h=8, a=16)
        scw = av_(48 * K, 2 * K, F32).rearrange("p (l c) -> p l c", l=4)
        G = arena[:, :].rearrange("p (i t) -> p i t", i=128)
        outtmp = av_(0, 8 * K, F32).rearrange("p (k c) -> p k c", k=8)

        ATOK = Buf("atok")

        def vcol(i):
            return vec[:, i:i + 1]

        capture = [None]

        def OP(eng, fn, reads, writes, arena_use=False):
            if capture[0] is not None:
                capture[0].append(("op", (eng, fn, list(reads), list(writes), arena_use)))
                return None
            if arena_use:
                reads = list(reads) + [ATOK]
            return P.op(eng, fn, reads, writes)

        def DMA(fn, reads, writes, arena_use=False, final=False, eng=None):
            if capture[0] is not None:
                capture[0].append(("dma", (fn, list(reads), list(writes), arena_use, final, eng)))
                return None
            if arena_use:
                reads = list(reads) + [ATOK]
            return P.dma(fn, reads, writes, final=final, eng=eng)

        def replay(item):
            kind, args = item
            if kind == "op":
                OP(*args)
            else:
                DMA(*args)

        def barrier():
            P.op("pool", lambda e: e.memset(one1[:, 0:1], 0.0), [], [ATOK])

        def MM(out, lhsT, rhs, start, stop, reads, writes, au=True, sgc=False):
            OP("pe", lambda e: e.matmul(out, lhsT=lhsT, rhs=rhs, start=start, stop=stop,
                                        skip_group_check=sgc), reads, writes, au)

        def ACT(out, in_, func, reads, writes, bias=None, scale=None, au=True):
            kw = {}
            if bias is not None:
                kw["bias"] = bias
            if scale is not None:
                kw["scale"] = scale
            OP("act", lambda e: e.activation(out=out, in_=in_, func=func, **kw), reads, writes, au)

        def TTo(out, in0, in1, op, reads, writes, eng="dve", au=True):
            OP(eng, lambda e: e.tensor_tensor(out=out, in0=in0, in1=in1, op=op), reads, writes, au)

        def TS(out, in0, s1, op0, reads, writes, s2=None, op1=None, eng="dve", au=True):
            if op1 is None:
                OP(eng, lambda e: e.tensor_scalar(out=out, in0=in0, scalar1=s1, scalar2=None, op0=op0),
                   reads, writes, au)
            else:
                OP(eng, lambda e: e.tensor_scalar(out=out, in0=in0, scalar1=s1, scalar2=s2, op0=op0, op1=op1),
                   reads, writes, au)

        def STT(out, in0, scalar, in1, op0, op1, reads, writes, au=True):
            OP("dve", lambda e: e.scalar_tensor_tensor(out=out, in0=in0, scalar=scalar, in1=in1,
                                                       op0=op0, op1=op1), reads, writes, au)

        def CP(out, in_, reads, writes, eng="dve", au=True):
            OP(eng, lambda e: e.tensor_copy(out=out, in_=in_), reads, writes, au)

        def RECIP(out, in_, reads, writes, au=True):
            OP("dve", lambda e: e.reciprocal(out=out, in_=in_), reads, writes, au)

        Bvec, Bcm, Bconst, Bsk = Buf("vec"), Buf("cm"), Buf("const"), Buf("sk")
        Bxs = [BG("xta", 8), BG("xtb", 8)]
        Bsqn, Brn1 = BG("sqn", 8), Buf("rn1")
        Bscr = [Buf("scr%d" % i) for i in range(NPIECE)]
        Bst_in = BG("sti", 4)
        Bst_out = BG("sto", 4)
        Bw = [Buf("w0"), Buf("w1")]
        Bdiag = [Buf("dg0"), Buf("dg1")]
        Bw = Bw + Bdiag
        BhT, Bsq, Bys, BsT, Bqr, Bat = BG("hT", 8), BG("sq", 8), BG("ys", 8), BG("sT", 8), BG("qr", 8), BG("at", 8)
        Bub = [BG("ub0_", 8), BG("ub1_", 8)]
        Bkb = [Buf("kb0"), Buf("kb1")]
        Bvd = [[Buf("vd%d%d" % (p, b)) for b in range(3)] for p in range(2)]
        Bcs, Bposi, Br1, Br2, Br3 = Buf("cs"), Buf("posi"), Buf("r1"), Buf("r2"), Buf("r3")
        Bs1, Bs2, Bs3, Bg1, Bg2, Bm1, Bqb = Buf("st1"), Buf("st2"), Buf("st3"), Buf("sg1"), Buf("sg2"), Buf("m1"), Buf("qb")
        BeT = [Buf("eT0"), Buf("eT1")]
        Bden, Brden, Brscr = Buf("den"), Buf("rden"), Buf("rscr")
        Bh2, Bgl, BGA = Buf("h2T"), [Buf("gl0"), Buf("gl1")], [Buf("GA0"), Buf("GA1")]
        BPt, BQt = [BG("Pt0_", 8), BG("Pt1_", 8)], [BG("Qt0_", 8), BG("Qt1_", 8)]
        BUT, BVs = BG("UT", 3), BG("Vs", 3)
        Bv16, Bi16, Bi16f, Bbest, Bpos, Btk, Babg = BG("v16_", 16), BG("i16_", 16), Buf("i16f"), BG("best", 8), BG("pos", 8), Buf("tk"), Buf("abg")
        BqT, Bcand, Bw2, BE1, Bscw, BGm, Bot = BG("qTb", 16), Buf("cand"), BG("w2_", 8), Buf("E1"), BG("scw", 4), BG("G", 64), Buf("ot")
        Bsgs = BG("sgs", 2)
        Bra, Brb, Bqbs = BG("ra", 2), BG("rb", 2), BG("qbs", 2)
        Bout = Buf("out")

        DMA(lambda e: e.dma_start(out=vec[:], in_=dvec[:, :]), [], [Bvec])
        DMA(lambda e: e.dma_start(out=cmf[:], in_=dcm[:, :]), [], [Bcm])
        DMA(lambda e: e.dma_start(out=skf[:], in_=dsk[:, :]), [], [Bsk])
        DMA(lambda e: e.dma_start(out=bvb[:], in_=drow[0:1, 0:128].partition_broadcast(128)[:, 0, :]), [], [Bconst])
        DMA(lambda e: e.dma_start(out=esink[:], in_=drow[0:1, 128:144].partition_broadcast(128)[:, 0, :]), [], [Bconst])
        CP(identb[:], cmf[:, C_ID:C_ID + 128], [Bcm], [Bconst], au=False)
        CP(permb[:], cmf[:, C_PERM:C_PERM + 128], [Bcm], [Bconst], au=False)
        CP(iotab[:], cmf[:, C_IOTA:C_IOTA + 128], [Bcm], [Bconst], au=False)
        OP("dve", lambda e: e.memset(onesb[:], 1.0), [], [Bconst])
        OP("dve", lambda e: e.memset(onesmb[:], 1.0 / 1024.0), [], [Bconst])
        OP("dve", lambda e: e.memset(onesmf[:], 1.0 / 1024.0), [], [Bconst])
        for var in range(2):
            for kb in range(2):
                for h4 in range(2):
                    c0 = (C_MASK0 if var == 0 else C_MASK1) + kb * 128
                    CP(maskb[:, var, kb, h4, :], cmf[:, c0:c0 + 128], [Bcm], [Bconst], au=False)
        ACT(esink[:], esink[:], AF.Exp, [Bconst], [Bconst], au=False)
        CP(skb[:].rearrange("p a b -> p (a b)"), skf[:], [Bsk], [Bsk], au=False)

        cast_engs = ["act", "dve", "pool"]
        NPR = NPIECE if stop != 1 else 0

        def pl_load(i):
            s = i % 4
            DMA(lambda e: e.dma_start(out=stin[s], in_=dwall[i]), [], [Bst_in[s]], True)

        def pl_cast_store(i):
            s = i % 4
            ce = cast_engs[i % 3]
            if ce == "act":
                ACT(stout[s], stin[s], AF.Copy, [Bst_in[s]], [Bst_out[s]])
            else:
                CP(stout[s], stin[s], [Bst_in[s]], [Bst_out[s]], eng=ce)
            DMA(lambda e: e.dma_start(out=dscr[i], in_=stout[s]), [Bst_out[s]], [Bscr[i]], True, eng="act")

        for i in range(min(3, NPR)):
            pl_load(i)
        for i in range(NPR):
            if i + 3 < NPR:
                pl_load(i + 3)
            pl_cast_store(i)

        Bdg = [BG("dgs0_", 31), BG("dgs1_", 31)]
        Bdgd = BG("dgd", 8)
        for c in range(8 if stop != 1 else 0):
            s = c % 2
            for jt in range(31):
                ACT(dgst[s][:, jt, :], cmf[:, C_ID:C_ID + 128], AF.Copy, [Bcm, Bvec], [Bdg[s][jt]],
                    scale=vcol(V_CW + c * 31 + jt))
            DMA(lambda e, c=c, s=s: e.dma_start(out=ddiag[c], in_=dgst[s][:, :, :].rearrange("p j c -> p (j c)")),
                [Bdg[s]], [Bdgd[c]], True)

        wslot = [0]
        wmode = [2]

        def loadw(g):
            s = wslot[0] % wmode[0]
            wslot[0] += 1
            DMA(lambda e: e.dma_start(out=wbuf[s].rearrange("p (r k) c -> p r (k c)", r=2),
                                      in_=dscr[2 * g:2 * g + 2].rearrange("r p f -> p r f")),
                [Bscr[2 * g], Bscr[2 * g + 1]], [Bw[s]], True)
            return s

        bankrr = [0]

        def nextbank():
            b = bankrr[0]
            bankrr[0] = (b + 1) % 4
            return b

        def proj(bank, s, j, rhs3, c0, n, rbuf):
            for k in range(8):
                MM(ps[:, bank, 0:n], wbuf[s][:, k, j * 128:(j + 1) * 128], rhs3[:, k, c0:c0 + n],
                   k == 0, k == 7, [Bw[s], rbuf], [PB[bank]])

        def colstats(src3, c0, n, srcbuf, outrr, outbuf, tmp, tmpbuf, sqv, sqbuf, bank, sq_au=True):
            for k in range(8):
                ACT(sqv[:, k, c0:c0 + n], src3[:, k, c0:c0 + n], AF.Square, [srcbuf[k]], [sqbuf[k]], au=sq_au)
            for k in range(8):
                MM(ps[:, bank, 0:n], onesmb[:], sqv[:, k, c0:c0 + n], k == 0, k == 7, [Bconst, sqbuf[k]], [PB[bank]], au=sq_au)
            TS(tmp[:, c0:c0 + n], ps[:, bank, 0:n], EPSV, ALU.add, [PB[bank]], [tmpbuf], au=False)
            ACT(tmp[:, c0:c0 + n], tmp[:, c0:c0 + n], AF.Sqrt, [tmpbuf], [tmpbuf], au=False)
            RECIP(outrr[:, c0:c0 + n], tmp[:, c0:c0 + n], [tmpbuf], [outbuf], au=False)

        def rope_tables(c0, n, colbase):
            DMA(lambda e: e.dma_start(out=posi[:, c0:c0 + n],
                                      in_=dpos[0:1, colbase:colbase + n].partition_broadcast(128)[:, 0, :]),
                [], [Bposi])
            sl = slice(c0, c0 + n)
            CP(r1[:, sl], posi[:, sl], [Bposi], [Br1], au=False)
            TS(r1[:, sl], r1[:, sl], vcol(V_INVF), ALU.mult, [Br1, Bvec], [Br1], au=False)
            for (dst, shift, useSgn) in ((sinT, 0.0, True), (cosT, float(np.pi / 2), False)):
                TS(r2[:, sl], r1[:, sl], shift, ALU.add, [Br1], [Br2], au=False)
                TS(r3[:, sl], r2[:, sl], float(1.0 / (2 * np.pi)), ALU.mult, [Br2], [Br3], s2=MAGIC, op1=ALU.add, au=False)
                TS(r3[:, sl], r3[:, sl], MAGIC, ALU.subtract, [Br3], [Br3], au=False)
                STT(r2[:, sl], r3[:, sl], -CW1, r2[:, sl], ALU.mult, ALU.add, [Br3, Br2], [Br2], au=False)
                STT(r2[:, sl], r3[:, sl], -CW2, r2[:, sl], ALU.mult, ALU.add, [Br3, Br2], [Br2], au=False)
                TS(r2[:, sl], r2[:, sl], 3.1415925, ALU.min, [Br2], [Br2], s2=-3.1415925, op1=ALU.max, au=False)
                if useSgn:
                    ACT(dst[:, sl], r2[:, sl], AF.Sin, [Br2, Bvec], [Bcs], scale=vcol(V_SGN), au=False)
                else:
                    ACT(dst[:, sl], r2[:, sl], AF.Sin, [Br2], [Bcs], au=False)

        pending_final = [None]

        def ckpt(i):
            if stop == i:
                P.halted = True

        if stop in (1, 2):
            P.halted = True
        for ti in range(NT):
            par = ti % 2
            c0 = 0 if ti == 0 else 128
            n = 128 + T - c0
            xcol = HALO + ti * T
            barrier()
            xt = xts[par]
            Bx = Bxs[par]

            def prefetch(tn):
                pn = tn % 2
                xc = HALO + tn * T
                cc0 = 0 if tn == 0 else 128
                DMA(lambda e: e.dma_start(out=xts[pn][:], in_=dx[:, :, xc:xc + T].rearrange("k p t -> p k t")),
                    [], [Bxs[pn]])
                rope_tables(cc0, 128 + T - cc0, xc - 128 + cc0)
                colstats(xts[pn], 0, T, Bxs[pn], rn1, Brn1, st2, Bs2, sqn, Bsqn, 6, sq_au=False)

            if ti == 0:
                prefetch(0)
            for k in range(8):
                STT(hT[:, k, 128:128 + T], xt[:, k, :], vcol(V_G1 + k), rn1[:, 0:T], ALU.mult, ALU.mult,
                    [Bx[k], Bvec, Brn1], [BhT[k]])
            if ti == 0:
                DMA(lambda e: e.dma_start(out=ysb[:, :, 0:128], in_=dx[:, :, 0:128].rearrange("k p t -> p k t")),
                    [], [Bys], True)
                colstats(ysb, 0, 128, Bys, st3, Bs3, st2, Bs2, sqb, Bsq, 4)
                for k in range(8):
                    STT(hT[:, k, 0:128], ysb[:, k, 0:128], vcol(V_G1 + k), st3[:, 0:128], ALU.mult, ALU.mult,
                        [Bys[k], Bvec, Bs3], [BhT[k]])
            else:
                for c in range(8):
                    CP(ubuf[:, par, c, 0:32], ubuf[:, 1 - par, c, T:T + 32], [Bub[1 - par][c]], [Bub[par][c]], eng="pool", au=False)
                for g in range(2):
                    CP(kbuf[:, par, g, 0:128], kbuf[:, 1 - par, g, T:T + 128], [Bkb[1 - par]], [Bkb[par]], eng="pool", au=False)
                CP(vdup[:, par, 0, :, :], vdup[:, 1 - par, 2, :, :], [Bvd[1 - par][2]], [Bvd[par][0]], eng="pool", au=False)

            ckpt(3)
            cu0 = 96 if ti == 0 else 128
            nu = 128 + T - cu0
            wsl = {}
            sgl = [sg1, sg2]

            def glu_proj(c):
                pr, j = c // 4, c % 4
                if j == 0:
                    wsl[pr] = (loadw(2 * pr), loadw(2 * pr + 1))
                sv, sgt = wsl[pr]
                bA = nextbank()
                proj(bA, sv, j, hT, cu0, nu, BhT)
                bB = nextbank()
                proj(bB, sgt, j, hT, cu0, nu, BhT)
                sg = sgl[c % 2]
                ACT(sg[:, 0:nu], ps[:, bB, 0:nu], AF.Sigmoid, [PB[bB], Bvec], [Bsgs[c % 2]],
                    bias=vcol(V_BIN + _chunk_gate(c)), au=False)
                STT(ubuf[:, par, c, cu0 - 96:cu0 - 96 + nu], ps[:, bA, 0:nu], vcol(V_BIN + _chunk_val(c)),
                    sg[:, 0:nu], ALU.add, ALU.mult, [PB[bA], Bvec, Bsgs[c % 2]], [Bub[par][c]], au=False)
                if ti == 0:
                    TS(ubuf[:, par, c, 0:32], ubuf[:, par, c, 0:32], vcol(V_HV), ALU.mult,
                       [Bub[par][c], Bvec], [Bub[par][c]], au=False)
                ds_ = c % 2
                DMA(lambda e: e.dma_start(out=diag[ds_][:, :, :].rearrange("p j c -> p (j c)"), in_=ddiag[c]),
                    [Bdgd[c]], [Bdiag[ds_]], True)

            def conv(c):
                ds_ = c % 2
                bC = nextbank()
                for jt in range(31):
                    MM(ps[:, bC, 0:T], diag[ds_][:, jt, :], ubuf[:, par, c, 2 + jt:2 + jt + T],
                       jt == 0, jt == 30, [Bdiag[ds_], Bub[par][c]], [PB[bC]])
                ACT(ysb[:, c, :], ps[:, bC, 0:T], AF.Identity, [PB[bC], Bvec], [Bys[c]], bias=vcol(V_CB + c))
                ACT(sqb[:, c, 0:T], ps[:, bC, 0:T], AF.Square, [PB[bC], Bvec], [Bsq[c]], bias=vcol(V_CB + c))

            glu_proj(0)
            if pending_final[0] is not None:
                pending_final[0]()
                pending_final[0] = None
            for c in range(8):
                if c + 1 < 8:
                    glu_proj(c + 1)
                conv(c)
            wmode[0] = 4
            for c in range(8):
                MM(ps[:, 4, 0:T], onesmf[:], ysb[:, c, :], c == 0, c == 7, [Bconst, Bys[c]], [PB[4]])
            for c in range(8):
                MM(ps[:, 5, 0:T], onesmb[:], sqb[:, c, 0:T], c == 0, c == 7, [Bconst, Bsq[c]], [PB[5]])
            CP(st1[:, 0:T], ps[:, 4, 0:T], [PB[4]], [Bs1], au=False)
            TTo(st2[:, 0:T], st1[:, 0:T], st1[:, 0:T], ALU.mult, [Bs1], [Bs2], au=False)
            TTo(st2[:, 0:T], ps[:, 5, 0:T], st2[:, 0:T], ALU.subtract, [PB[5], Bs2], [Bs2], au=False)
            TS(st2[:, 0:T], st2[:, 0:T], EPSV, ALU.add, [Bs2], [Bs2], au=False)
            ACT(st2[:, 0:T], st2[:, 0:T], AF.Sqrt, [Bs2], [Bs2], au=False)
            RECIP(st3[:, 0:T], st2[:, 0:T], [Bs2], [Bs3], au=False)
            STT(st1[:, 0:T], st1[:, 0:T], -1.0, st3[:, 0:T], ALU.mult, ALU.mult, [Bs1, Bs3], [Bs1], au=False)
            for c in range(8):
                ra_, rb_ = (r1, r2) if c % 2 == 0 else (r3, r4)
                Ba_, Bb_ = ([Br1, Bra[0]], [Br2, Brb[0]]) if c % 2 == 0 else ([Br3, Bra[1]], [Brb[1]])
                TTo(ra_[:, 0:T], ysb[:, c, :], st3[:, 0:T], ALU.mult, [Bys[c], Bs3], Ba_)
                TTo(rb_[:, 0:T], ra_[:, 0:T], st1[:, 0:T], ALU.add, Ba_ + [Bs1], Bb_, au=False)
                ACT(sT[:, c, :], rb_[:, 0:T], AF.Silu, Bb_ + [Bvec], [BsT[c]], bias=vcol(V_LNB + c), scale=vcol(V_LNG + c))

            ckpt(4)
            rcnt = [0]

            def rope_chunk(bank, outap, cc0, nn, bcol, outbuf):
                i_ = rcnt[0] % 2
                rcnt[0] += 1
                qb_ = (qb, qb2)[i_]
                ra_, rb_ = ((r1, r2), (r3, r4))[i_]
                Bq_ = [Bqb, Bqbs[0]] if i_ == 0 else [Bqbs[1]]
                Ba_ = [Br1, Bra[0]] if i_ == 0 else [Br3, Bra[1]]
                Bb_ = [Br2, Brb[0]] if i_ == 0 else [Brb[1]]
                ACT(qb_[:, cc0:cc0 + nn], ps[:, bank, 0:nn], AF.Identity, [PB[bank], Bvec], Bq_, bias=vcol(bcol), au=False)
                b2 = nextbank()
                MM(ps[:, b2, 0:nn], permb[:], qb_[:, cc0:cc0 + nn], True, True, [Bconst] + Bq_, [PB[b2]], au=False)
                TTo(ra_[:, cc0:cc0 + nn], qb_[:, cc0:cc0 + nn], cosT[:, cc0:cc0 + nn], ALU.mult, Bq_ + [Bcs], Ba_, au=False)
                TTo(rb_[:, cc0:cc0 + nn], ps[:, b2, 0:nn], sinT[:, cc0:cc0 + nn], ALU.mult, [PB[b2], Bcs], Bb_, au=False)
                TTo(outap, ra_[:, cc0:cc0 + nn], rb_[:, cc0:cc0 + nn], ALU.add, Ba_ + Bb_, [outbuf], au=True)

            for qg in range(2):
                s = loadw(4 + qg)
                for j in range(4):
                    cq = qg * 4 + j
                    b = nextbank()
                    proj(b, s, j, hT, 128, T, BhT)
                    rope_chunk(b, qrope[:, cq, :], 128, T, V_BIN + 16 + cq, Bqr[cq])
            s = loadw(6)
            for g in range(2):
                b = nextbank()
                proj(b, s, g, hT, c0, n, BhT)
                rope_chunk(b, kbuf[:, par, g, c0:c0 + n], c0, n, V_BIN + 24 + g, Bkb[par])
            for blk in range(3):
                if ti > 0 and blk == 0:
                    continue
                b = nextbank()
                for k in range(8):
                    MM(ps[:, b, 0:128], hT[:, k, blk * 128:(blk + 1) * 128], wbuf[s][:, k, 256:384],
                       k == 0, k == 7, [BhT, Bw[s]], [PB[b]])
                for dup in range(2):
                    TTo(vdup[:, par, blk, :, dup * 64:(dup + 1) * 64],
                        ps[:, b, 0:128].rearrange("p (g d) -> p g d", g=2),
                        bvb[:].rearrange("p (g d) -> p g d", g=2), ALU.add, [PB[b], Bconst], [Bvd[par][blk]], au=False)

            ckpt(5)
            iters = [(b, g, hg) for b in range(2) for g in range(2) for hg in range(2)]

            def att_S(i):
                b, g, hg = iters[i]
                sb0 = 6 if i % 2 == 0 else 2
                var = 0 if (ti == 0 and b == 0) else 1
                j0 = (8 * g + 4 * hg) // 2
                for half in range(2):
                    MM(ps[:, sb0 + half, :], identb[:], maskb[:, var, :, :, :].rearrange("p k a q -> p (k a q)"),
                       True, False, [Bconst], [PB[sb0 + half]], au=False, sgc=True)
                for half in range(2):
                    pa = slice(half * 64, (half + 1) * 64)
                    for kb in range(2):
                        for a in range(2):
                            MM(ps[:, sb0 + half, (kb * 2 + a) * 128:(kb * 2 + a + 1) * 128],
                               kbuf[pa, par, g, (b + kb) * 128:(b + kb + 1) * 128],
                               qrope[pa, j0 + a, b * 128:(b + 1) * 128],
                               False, True, [Bkb[par], Bqr[j0 + a]], [PB[sb0 + half]], sgc=True)

            def att_rest(i):
                b, g, hg = iters[i]
                sl_ = i % 2
                sb0 = 6 if i % 2 == 0 else 2
                ob, db = (4, 5) if i % 2 == 0 else (0, 1)
                h0 = 8 * g + 4 * hg
                j0 = h0 // 2
                ACT(eT[:, sl_, :, :, :].rearrange("p k h q -> p (k h q)"),
                    ps[:, sb0:sb0 + 2, :].rearrange("p k c -> p (k c)"), AF.Exp, [PB[sb0], PB[sb0 + 1]], [BeT[sl_]],
                    scale=0.125, au=False)
                eTv = eT[:, sl_, :, :, :].rearrange("p half (kb a) q -> p half kb a q", kb=2)
                for kb in range(2):
                    for half in range(2):
                        MM(ps[:, ob, half * 256:(half + 1) * 256], vdup[:, par, b + kb, g, :],
                           eTv[:, half, kb, :, :].rearrange("p a q -> p (a q)"),
                           (kb == 0 and half == 0), kb == 1, [Bvd[par][b + kb], BeT[sl_]], [PB[ob]], au=False, sgc=True)
                for kb in range(2):
                    for half in range(2):
                        MM(ps[:, db, half * 256:(half + 1) * 256], onesb[:],
                           eTv[:, half, kb, :, :].rearrange("p a q -> p (a q)"),
                           (kb == 0 and half == 0), kb == 1, [Bconst, BeT[sl_]], [PB[db]], au=False, sgc=True)
                TTo(den[:].rearrange("p (half a q) -> p half a q", half=2, a=2),
                    ps[:, db, :].rearrange("p (half a q) -> p half a q", half=2, a=2),
                    esink[:, h0:h0 + 4].rearrange("p (a half) -> p half a", half=2).unsqueeze(3).to_broadcast([128, 2, 2, 128]),
                    ALU.add, [PB[db], Bconst], [Bden], au=False)
                RECIP(rden[:], den[:], [Bden], [Brden], au=False)
                for half in range(2):
                    pa = slice(half * 64, (half + 1) * 64)
                    TTo(attnT[pa, j0:j0 + 2, b * 128:(b + 1) * 128],
                        ps[pa, ob, half * 256:(half + 1) * 256].rearrange("p (a q) -> p a q", a=2),
                        rden[pa, half * 256:(half + 1) * 256].rearrange("p (a q) -> p a q", a=2),
                        ALU.mult, [PB[ob], Brden], [Bat[j0], Bat[j0 + 1]])

            att_S(0)
            for i in range(8):
                if i + 1 < 8:
                    att_S(i + 1)
                att_rest(i)

            ckpt(6)
            Bm1s = BG("m1s", 4)
            for jg in range(2):
                sa = loadw(11 + jg)
                sb_ = loadw(7 + jg)
                for j in range(4):
                    c = jg * 4 + j
                    bA = nextbank()
                    proj(bA, sa, j, sT, 0, T, BsT)
                    bB = nextbank()
                    proj(bB, sb_, j, hT, 128, T, BhT)
                    sg = sgl[j % 2]
                    ACT(sg[:, 0:T], ps[:, bB, 0:T], AF.Sigmoid, [PB[bB], Bvec], [Bsgs[j % 2]], bias=vcol(V_BIN + 28 + c), au=False)
                    TTo(m1buf[:, j, :], ps[:, bA, 0:T], sg[:, 0:T], ALU.mult, [PB[bA], Bsgs[j % 2]], [Bm1s[j]], au=False)
                sa = loadw(13 + jg)
                sb_ = loadw(9 + jg)
                for j in range(4):
                    c = jg * 4 + j
                    bA = nextbank()
                    proj(bA, sa, j, attnT, 0, T, Bat)
                    bB = nextbank()
                    proj(bB, sb_, j, hT, 128, T, BhT)
                    sg = sgl[j % 2]
                    rt_ = (r3, r4)[j % 2]
                    Brt_ = [Br3, Bra[1]] if j % 2 == 0 else [Brb[1]]
                    ACT(sg[:, 0:T], ps[:, bB, 0:T], AF.Sigmoid, [PB[bB], Bvec], [Bsgs[j % 2]], bias=vcol(V_BIN + 36 + c), au=False)
                    TTo(rt_[:, 0:T], ps[:, bA, 0:T], sg[:, 0:T], ALU.mult, [PB[bA], Bsgs[j % 2]], Brt_, au=False)
                    TTo(mergedT[:, c, 0:T], rt_[:, 0:T], m1buf[:, j, :], ALU.add, Brt_ + [Bm1s[j]], [Bsq[c]])
            for og in range(2):
                s = loadw(15 + og)
                for j in range(4):
                    c = og * 4 + j
                    b = nextbank()
                    proj(b, s, j, mergedT, 0, T, Bsq)
                    TTo(xt[:, c, :], xt[:, c, :], ps[:, b, 0:T], ALU.add, [Bx[c], PB[b]], [Bx[c]], au=False)
            if dbg and ti == 0:
                DMA(lambda e, xt_=xt: e.dma_start(out=ddbg.rearrange("k p t -> p k t"), in_=xt_[:]), [Bx], [Buf("dbgo")], final=True)

            ckpt(7)
            colstats(xt, 0, T, Bx, st1, Bs1, st2, Bs2, sqb, Bsq, 4)
            for k in range(8):
                STT(h2T[:, k, :], xt[:, k, :], vcol(V_G2 + k), st1[:, 0:T], ALU.mult, ALU.mult, [Bx[k], Bvec, Bs1], [Bh2], au=False)
            wmode[0] = 2
            barrier()
            for g4 in range(4):
                s = loadw(17 + g4)
                for j in range(4):
                    hc = g4 * 4 + j
                    b = nextbank()
                    for k in range(8):
                        MM(ps[:, b, 0:T], wbuf[s][:, k, j * 128:(j + 1) * 128], h2T[:, k, :], k == 0, k == 7,
                           [Bw[s], Bh2], [PB[b]])
                    if hc % 2 == 0:
                        ACT(qTb[:, hc, :], ps[:, b, 0:T], AF.Copy, [PB[b]], [BqT[hc]])
                    else:
                        CP(qTb[:, hc, :], ps[:, b, 0:T], [PB[b]], [BqT[hc]])
            v16v = v16[:, :, :].rearrange("p (h c) k -> p h c k", c=2)
            i16fv = i16f[:, :, :].rearrange("p (h c) k -> p h c k", c=2)
            B4 = [128, 8, 16, 16]
            for tc in range(2):
                tcs = slice(tc * 128, (tc + 1) * 128)
                for g4 in range(4):
                    for l in range(4):
                        hc = g4 * 4 + l
                        MM(ps[:, 5, l * 128:(l + 1) * 128], qTb[:, hc, tcs], skb[:, hc, :], True, True,
                           [BqT[hc], Bsk], [PB[5]])
                    def L1(step, l):
                        hc = g4 * 4 + l
                        src_ = ps[:, 5, l * 128:(l + 1) * 128]
                        if step == 0:
                            OP("dve", lambda e: e.max(out=v16[:, hc, 0:8], in_=src_), [PB[5]], [Bv16[hc]])
                        elif step == 1:
                            OP("dve", lambda e: e.max_index(out=i16[:, hc, 0:8], in_max=v16[:, hc, 0:8], in_values=src_),
                               [PB[5], Bv16[hc]], [Bi16[hc]])
                        elif step == 2:
                            OP("dve", lambda e: e.match_replace(out=scw[:, l, :], in_to_replace=v16[:, hc, 0:8],
                                                               in_values=src_, imm_value=-1e30),
                               [PB[5], Bv16[hc]], [Bscw[l]], True)
                        elif step == 3:
                            OP("dve", lambda e: e.max(out=v16[:, hc, 8:16], in_=scw[:, l, :]), [Bscw[l]], [Bv16[hc]], True)
                        else:
                            OP("dve", lambda e: e.max_index(out=i16[:, hc, 8:16], in_max=v16[:, hc, 8:16], in_values=scw[:, l, :]),
                               [Bscw[l], Bv16[hc]], [Bi16[hc]], True)
                    for step in (0, 2, 1, 3, 4):
                        for l in range(4):
                            L1(step, l)
                CP(i16f[:], i16[:], [Bi16], [Bi16f], au=False)
                TTo(cand[:], v16v[:, :, 0, :].unsqueeze(3).to_broadcast(B4), v16v[:, :, 1, :].unsqueeze(2).to_broadcast(B4),
                    ALU.add, [Bv16], [Bcand])
                def L2(step, h):
                    src_ = cand[:, h, :, :].rearrange("p a b -> p (a b)")
                    if step == 0:
                        OP("dve", lambda e: e.max(out=best[:, h, 0:8], in_=src_), [Bcand], [Bbest[h]], True)
                    elif step == 1:
                        OP("dve", lambda e: e.max_index(out=posu[:, h, 0:8], in_max=best[:, h, 0:8], in_values=src_),
                           [Bcand, Bbest[h]], [Bpos[h]], True)
                    elif step == 2:
                        OP("dve", lambda e: e.match_replace(out=work2[:, h, :], in_to_replace=best[:, h, 0:8],
                                                           in_values=src_, imm_value=-1e30), [Bcand, Bbest[h]], [Bw2[h]], True)
                    elif step == 3:
                        OP("dve", lambda e: e.max(out=best[:, h, 8:16], in_=work2[:, h, :]), [Bw2[h]], [Bbest[h]], True)
                    else:
                        OP("dve", lambda e: e.max_index(out=posu[:, h, 8:16], in_max=best[:, h, 8:16], in_values=work2[:, h, :]),
                           [Bw2[h], Bbest[h]], [Bpos[h]], True)
                for step in (0, 2, 1, 3, 4):
                    for h in range(8):
                        L2(step, h)
                CP(posf[:], posu[:], [Bpos], [Btk], au=False)
                OP("dve", lambda e: e.tensor_single_scalar(out=k1u[:], in_=posu[:], scalar=4, op=ALU.logical_shift_right),
                   [Bpos], [Btk])
                CP(k1f[:], k1u[:], [Btk], [Btk], au=False)
                STT(k2f[:], k1f[:], -16.0, posf[:], ALU.mult, ALU.add, [Btk], [Btk], au=False)
                TTo(ebuf[:], best[:], best[:, :, 0:1].to_broadcast([128, 8, 16]), ALU.subtract, [Bbest], [Btk], au=False)
                ACT(ebuf[:], ebuf[:], AF.Exp, [Btk], [Btk], au=False)
                OP("dve", lambda e: e.tensor_reduce(out=Zs[:], in_=ebuf[:], axis=AX.X, op=ALU.add), [Btk], [Btk])
                RECIP(Zs[:], Zs[:], [Btk], [Btk], au=False)
                TTo(gate[:], ebuf[:], Zs[:, :].unsqueeze(2).to_broadcast([128, 8, 16]), ALU.mult, [Btk], [Btk], au=False)
                io16 = cmf[:, C_IOTA16:C_IOTA16 + 16].unsqueeze(1).unsqueeze(1).to_broadcast(B4)
                for (kf, cidx, dst) in ((k1f, 0, av), (k2f, 1, bvv)):
                    TTo(E1[:], kf[:].unsqueeze(3).to_broadcast(B4), io16, ALU.is_equal, [Btk, Bcm], [BE1])
                    TTo(E1[:], E1[:], i16fv[:, :, cidx, :].unsqueeze(2).to_broadcast(B4), ALU.mult, [BE1, Bi16f], [BE1])
                    OP("dve", lambda e, dst=dst: e.tensor_reduce(out=dst[:], in_=E1[:], axis=AX.X, op=ALU.add), [BE1], [Btk], True)
                for idx, srcv in enumerate((av, bvv, gate)):
                    OP("pe", lambda e, idx=idx, srcv=srcv: e.transpose(out=ps[:, 5, idx * 128:(idx + 1) * 128],
                                                                     in_=srcv[:].rearrange("p h k -> p (h k)"),
                                                                     identity=cmf[:, C_ID:C_ID + 128]),
                       [Btk, Bcm], [PB[5]])
                CP(abgT[:, tc, :, :].rearrange("p a b -> p (a b)"), ps[:, 5, 0:384], [PB[5]], [Babg], au=False)

            ckpt(8)
            barrier()
            for tb in range(T // 8):
                sl_ = tb % 2
                bk0 = 4 + 2 * (tb % 2)
                for i in range(8):
                    t = tb * 8 + i
                    tc, tl = t // 128, t % 128
                    TS(Pt[:, sl_, i, :], iotab[:], abgT[:, tc, 0, tl:tl + 1], ALU.is_equal, [Bconst, Babg], [BPt[sl_][i]],
                       s2=abgT[:, tc, 2, tl:tl + 1], op1=ALU.mult, au=False)
                    TS(Qt[:, sl_, i, :], iotab[:], abgT[:, tc, 1, tl:tl + 1], ALU.is_equal, [Bconst, Babg], [BQt[sl_][i]],
                       au=False)
                for i in range(8):
                    bank = bk0 + i // 4
                    MM(ps[:, bank, (i % 4) * 128:(i % 4 + 1) * 128], Qt[:, sl_, i, :], Pt[:, sl_, i, :], True, True,
                       [BQt[sl_][i], BPt[sl_][i]], [PB[bank]], au=False)
                for hb in range(2):
                    t0 = tb * 8 + hb * 4
                    ACT(G[:, :, t0:t0 + 4], ps[:, bk0 + hb, :].rearrange("p (t i) -> p i t", t=4), AF.Copy,
                        [PB[bk0 + hb]], [BGm[tb * 2 + hb]])

            ckpt(9)
            PBA = [PB[4], PB[5]]

            def stageA(ec):
                eg, cc, hs = ec // 2, ec % 2, ec % 2
                sl2 = eg % 3
                if cc == 0:
                    DMA(lambda e: e.dma_start(out=UTs[:, sl2, :, :].rearrange("p k c -> p (k c)"), in_=dscr[UV0 + eg]),
                        [Bscr[UV0 + eg]], [BUT[sl2]])
                    DMA(lambda e: e.dma_start(out=Vs[:, sl2, :, :].rearrange("p k c -> p (k c)"), in_=dscr[UV0 + 64 + eg]),
                        [Bscr[UV0 + 64 + eg]], [BVs[sl2]])
                for k in range(8):
                    MM(ps[:, 4 + hs, 0:256], UTs[:, sl2, k, cc * 128:(cc + 1) * 128], h2T[:, k, :],
                       k == 0, k == 7, [BUT[sl2], Bh2], [PBA[hs]], au=False)
                ACT(gl[:, hs, :], ps[:, 4 + hs, 0:256], AF.Gelu, [PBA[hs]], [Bgl[hs]], au=False)
                TTo(GA[:, hs, :], gl[:, hs, :], G[:, ec, :], ALU.mult, [Bgl[hs], BGm], [BGA[hs]])

            def stageV(ec):
                eg, cc, hs = ec // 2, ec % 2, ec % 2
                sl2 = eg % 3
                for dk in range(8):
                    MM(ps[:, dk // 2, (dk % 2) * 256:(dk % 2 + 1) * 256], Vs[:, sl2, cc, dk * 128:(dk + 1) * 128],
                       GA[:, hs, :], (ec == 0 and dk % 2 == 0), ec == 127, [BVs[sl2], BGA[hs]], [PB[dk // 2]],
                       au=False, sgc=True)

            stageA(0)
            for ec in range(128):
                if ec + 1 < 128:
                    stageA(ec + 1)
                stageV(ec)
                if ec == 2 and ti + 1 < NT:
                    capture[0] = []
                    prefetch(ti + 1)
                    pending = capture[0]
                    capture[0] = None
                if ec >= 2 and ti + 1 < NT and pending:
                    replay(pending.pop(0))
            if ti + 1 < NT:
                while pending:
                    replay(pending.pop(0))

            ckpt(10)
            for dk in range(8):
                TTo(xt[:, dk, :], xt[:, dk, :], ps[:, dk // 2, (dk % 2) * 256:(dk % 2 + 1) * 256], ALU.add,
                    [Bx[dk], PB[dk // 2]], [Bx[dk]], au=False)
            def make_final(ti_, xt_, Bx_):
                def fin():
                    colstats(xt_, 0, T, Bx_, st1, Bs1, st3, Bs3, sqn, Bsqn, 4, sq_au=False)
                    for k in range(8):
                        STT(xt_[:, k, :], xt_[:, k, :], vcol(V_GF + k), st1[:, 0:T], ALU.mult, ALU.mult,
                            [Bx_[k], Bvec, Bs1], [Bx_[k]], au=False)
                    DMA(lambda e: e.dma_start(out=dout[:, :, ti_ * T:(ti_ + 1) * T].rearrange("k p t -> p k t"), in_=xt_[:, :, :]),
                        [Bx_], [Bout], False, final=True)
                return fin

            pending_final[0] = make_final(ti, xt, Bx)
            if ti == NT - 1:
                pending_final[0]()
                pending_final[0] = None

        P.emit()
    return nc


def _prep_shared(inp):
    f = np.float32
    w_in = np.asarray(inp["w_in"], f)[0]
    b_in = np.asarray(inp["b_in"], f)[0]
    blk = lambda base, c: list(range(base + c * 128, base + (c + 1) * 128))
    chunks = []
    for pr in range(2):
        for c in range(4):
            chunks.append(blk(0, pr * 4 + c))
        for c in range(4):
            chunks.append(blk(1024, pr * 4 + c))
    for c in range(8):
        chunks.append(blk(2048, c))
    k0 = list(range(3072, 3136))
    k1 = list(range(3136, 3200))
    chunks.append(k0 + k0)
    chunks.append(k1 + k1)
    chunks.append(list(range(3200, 3328)))
    chunks.append(list(range(3200, 3328)))
    for c in range(8):
        chunks.append(blk(3328, c))
    for c in range(8):
        chunks.append(blk(4352, c))
    assert len(chunks) == 44
    colidx = np.array(sum(chunks, []), dtype=np.int64)
    w_perm = w_in[:, colidx]
    b_perm = b_in[colidx]
    mats = [w_perm[:, g * 512:(g + 1) * 512] for g in range(11)]
    for name in ("w_conv_out", "w_attn_o", "w_out"):
        w = np.asarray(inp[name], f)[0]
        mats += [w[:, 0:512], w[:, 512:1024]]
    wpq = np.asarray(inp["w_peer_q"], f)[0]
    mats += [wpq[:, g * 512:(g + 1) * 512] for g in range(4)]
    assert len(mats) == NMIXG
    wall = np.empty((NPIECE, 128, 2048), f)
    for g, m in enumerate(mats):
        a = m.reshape(8, 128, 512).transpose(1, 0, 2)
        wall[2 * g] = a[:, 0:4, :].reshape(128, 2048)
        wall[2 * g + 1] = a[:, 4:8, :].reshape(128, 2048)
    U = np.asarray(inp["peer_u"], f)[0]
    V = np.asarray(inp["peer_v"], f)[0]
    wall[UV0:UV0 + 64] = U.reshape(64, 256, 8, 128).transpose(0, 3, 2, 1).reshape(64, 128, 2048)
    wall[UV0 + 64:UV0 + 128] = V.reshape(64, 2, 128, 1024).transpose(0, 2, 1, 3).reshape(64, 128, 2048)

    vec = np.zeros((128, NV), f)
    col = lambda v: np.asarray(v, f).reshape(-1, 128).T
    vec[:, V_G1:V_G1 + 8] = col(inp["norm1_g"][0])
    vec[:, V_BIN:V_BIN + 44] = col(b_perm)
    vec[:, V_CB:V_CB + 8] = col(inp["conv_b"][0])
    vec[:, V_LNG:V_LNG + 8] = col(inp["conv_ln_g"][0])
    vec[:, V_LNB:V_LNB + 8] = col(inp["conv_ln_b"][0])
    vec[:, V_G2:V_G2 + 8] = col(inp["norm2_g"][0])
    vec[:, V_GF:V_GF + 8] = col(inp["final_g"])
    p = np.arange(128)
    invf = (np.float32(10000.0) ** (-(np.arange(32, dtype=f) * f(2.0) / f(64)))).astype(f)
    vec[:, V_INVF] = invf[p % 32]
    vec[:, V_SGN] = np.where(p % 64 < 32, -1.0, 1.0)
    cw = np.asarray(inp["conv_w"], f)[0]
    vec[:, V_CW:V_CW + 248] = cw.reshape(31, 8, 128).transpose(2, 1, 0).reshape(128, 248)

    cm = np.zeros((128, NCM), f)
    cm[:, C_ID:C_ID + 128] = np.eye(128, dtype=f)
    cm[p, C_PERM + (p ^ 32)] = 1.0
    cm[:, C_IOTA:C_IOTA + 128] = np.arange(128, dtype=f)[None, :]
    cm[:, C_IOTA16:C_IOTA16 + 16] = np.arange(16, dtype=f)[None, :]
    kk = np.arange(128)[:, None]
    qq = np.arange(128)[None, :]
    NEGM = f(-240000.0)
    m_prev = np.where(kk > qq, f(0), NEGM).astype(f)
    m_cur = np.where(kk <= qq, f(0), NEGM).astype(f)
    cm[:, C_MASK1:C_MASK1 + 128] = m_prev
    cm[:, C_MASK1 + 128:C_MASK1 + 256] = m_cur
    cm[:, C_MASK0 + 128:C_MASK0 + 256] = m_cur
    rows = np.concatenate([b_in[3200:3328], np.asarray(inp["attn_sinks"], f)[0]]).reshape(1, 144).astype(f)
    sk = np.asarray(inp["peer_sub_keys"], f)[0]
    skT = np.ascontiguousarray(sk.transpose(3, 0, 1, 2).reshape(128, 2048))
    return dict(wall=wall, vec=vec, cm=cm, rows=rows, skT=skT), m_prev, NEGM


_NC_CACHE = {}


def kernel(**inputs):
    x = np.asarray(inputs["x"], np.float32)
    pos = np.asarray(inputs["positions"], np.int32)
    B, S, _ = x.shape
    TOK = S // 2
    NT = TOK // TT
    shared, m_prev, NEGM = _prep_shared(inputs)
    in_maps = []
    for core in range(8):
        b, hs = core // 2, core % 2
        s0 = hs * TOK
        xT = np.zeros((8, 128, TOK + HALO), np.float32)
        pp = np.zeros((1, TOK + HALO), np.int32)
        if hs == 0:
            xs = x[b, 0:TOK]
            xT[:, :, HALO:] = xs.T.reshape(8, 128, TOK)
            pp[0, HALO:] = pos[b, 0:TOK]
        else:
            xs = x[b, s0 - HALO:s0 + TOK]
            xT[:] = xs.T.reshape(8, 128, TOK + HALO)
            pp[0] = pos[b, s0 - HALO:s0 + TOK]
        vec = shared["vec"].copy()
        vec[:, V_HV] = 0.0 if hs == 0 else 1.0
        cm = shared["cm"].copy()
        if hs == 0:
            cm[:, C_MASK0:C_MASK0 + 128] = NEGM
        else:
            cm[:, C_MASK0:C_MASK0 + 128] = m_prev
        in_maps.append(dict(xT=xT, pos=pp, wall=shared["wall"], vec=vec, cm=cm, rows=shared["rows"], skT=shared["skT"]))
    if NT not in _NC_CACHE:
        _NC_CACHE[NT] = build_nc(NT)
    nc = _NC_CACHE[NT]
    res = run_bass_kernel_spmd(nc, in_maps, core_ids=list(range(8)))
    out = np.empty((B, S, D), np.float32)
    for core in range(8):
        b, hs = core // 2, core % 2
        oT = np.asarray(res.results[core]["outT"], np.float32)
        out[b, hs * TOK:(hs + 1) * TOK, :] = oT.reshape(1024, TOK).T
    return out
```

```python
import numpy as np
from contextlib import ExitStack
import concourse.bass as bass
import concourse.mybir as mybir
from concourse.bass_utils import run_bass_kernel_spmd

F32 = mybir.dt.float32
BF16 = mybir.dt.bfloat16
I32 = mybir.dt.int32
U32 = mybir.dt.uint32
AF = mybir.ActivationFunctionType
ALU = mybir.AluOpType
AX = mybir.AxisListType


class Buf:
    __slots__ = ("name", "w", "r")

    def __init__(self, name=""):
        self.name = name
        self.w = None
        self.r = []


class BG(list):
    def __init__(self, name, n):
        super().__init__(Buf("%s%d" % (name, i)) for i in range(n))


def _flat(bs):
    out = []
    for b in bs:
        if isinstance(b, list):
            out.extend(_flat(b))
        else:
            out.append(b)
    return out


class _Ins:
    __slots__ = ("eng", "fn", "deps", "dma", "idx", "sig", "cnt", "semi", "final")

    def __init__(self, eng, fn, dma):
        self.eng = eng
        self.fn = fn
        self.deps = set()
        self.dma = dma
        self.sig = False
        self.cnt = 0
        self.semi = 0
        self.final = False


class Prog:
    NDMA_SEM = 12
    ENGS = ("pe", "act", "dve", "pool", "sp")

    def __init__(self, nc, es):
        self.nc = nc
        self.es = es
        self.q = {e: [] for e in self.ENGS}
        self.all = []
        self.dma_engine = "sp"
        self.halted = False

    def _add(self, ins, reads, writes):
        if self.halted:
            return ins
        reads = _flat(reads)
        writes = _flat(writes)
        for b in reads:
            if b.w is not None:
                ins.deps.add(b.w)
        for b in writes:
            if b.w is not None:
                ins.deps.add(b.w)
            for r in b.r:
                ins.deps.add(r)
        ins.deps.discard(ins)
        for b in reads:
            b.r.append(ins)
        for b in writes:
            b.w = ins
            b.r = []
        ins.idx = len(self.all)
        self.all.append(ins)
        self.q[ins.eng].append(ins)
        return ins

    def op(self, eng, fn, reads=(), writes=()):
        return self._add(_Ins(eng, fn, False), reads, writes)

    def dma(self, fn, reads=(), writes=(), final=False, eng=None):
        ins = _Ins(eng or self.dma_engine, fn, True)
        ins.final = final
        return self._add(ins, reads, writes)

    def emit(self):
        nc = self.nc
        for ins in self.all:
            for d in ins.deps:
                if d.eng == "pe" and ins.eng == "pe" and not d.dma and not ins.dma:
                    continue
                d.sig = True
            if ins.final:
                ins.sig = True
        sems = {e: self.es.enter_context(nc.semaphore("s_" + e)) for e in self.ENGS}
        dsems = [self.es.enter_context(nc.semaphore("d%d" % i)) for i in range(self.NDMA_SEM)]
        cnt = {e: 0 for e in self.ENGS}
        ndma = 0
        dma_prev = {}
        last_on_sem = [None] * self.NDMA_SEM
        for ins in self.all:
            if ins.dma:
                ins.semi = ndma % self.NDMA_SEM
                ins.cnt = 16 * (ndma // self.NDMA_SEM + 1)
                dma_prev[ins] = last_on_sem[ins.semi]
                last_on_sem[ins.semi] = ins
                ndma += 1
            elif ins.sig:
                cnt[ins.eng] += 1
                ins.cnt = cnt[ins.eng]
        finals = [i for i in self.all if i.final]
        block = self.es.enter_context(nc.Block())

        def run(engname, e):
            waited = {}

            def wait_for(d):
                if d.dma:
                    key = ("d", d.semi)
                    sem = dsems[d.semi]
                else:
                    key = ("e", d.eng)
                    sem = sems[d.eng]
                if waited.get(key, 0) >= d.cnt:
                    return
                e.wait_ge(sem, d.cnt)
                waited[key] = d.cnt

            for ins in self.q[engname]:
                for d in sorted(ins.deps, key=lambda z: z.idx):
                    if (d.eng == "pe" and engname == "pe" and not d.dma and not ins.dma):
                        continue
                    wait_for(d)
                if ins.dma:
                    p = dma_prev[ins]
                    if p is not None:
                        wait_for(p)
                h = ins.fn(e)
                if ins.dma:
                    h.then_inc(dsems[ins.semi], 16)
                elif ins.sig:
                    h.then_inc(sems[engname], 1)
            if engname == self.dma_engine:
                for f in finals:
                    wait_for(f)

        @block.sync
        def _(e):
            run("sp", e)

        @block.tensor
        def _(e):
            run("pe", e)

        @block.scalar
        def _(e):
            run("act", e)

        @block.vector
        def _(e):
            run("dve", e)

        @block.gpsimd
        def _(e):
            run("pool", e)


D = 1024
KC = 8
TT = 256
HALO = 128
EPSV = 1e-6
NMIXG = 21
NPIECE = 2 * NMIXG + 128
UV0 = 2 * NMIXG
V_G1, V_BIN, V_CB, V_LNG, V_LNB, V_G2, V_GF, V_INVF, V_SGN, V_HV, V_CW = 0, 8, 52, 60, 68, 76, 84, 92, 93, 94, 95
NV = 95 + 248
C_ID, C_PERM, C_IOTA, C_IOTA16, C_MASK0, C_MASK1 = 0, 128, 256, 384, 400, 656
NCM = 912
MAGIC = 12582912.0
CW1 = 6.28125
CW2 = 2.0 * np.pi - 6.28125


def _chunk_val(c):
    return (c // 4) * 8 + (c % 4)


def _chunk_gate(c):
    return (c // 4) * 8 + 4 + (c % 4)


class _Stop(Exception):
    pass


def build_nc(NT, dbg=False, stop=None):
    T = TT
    TOK = NT * T
    TOKH = TOK + HALO
    nc = bass.Bass("TRN2", target_bir_lowering=False)
    dx = nc.dram_tensor("xT", [8, 128, TOKH], F32, kind="ExternalInput").ap()
    dpos = nc.dram_tensor("pos", [1, TOKH], I32, kind="ExternalInput").ap()
    dwall = nc.dram_tensor("wall", [NPIECE, 128, 2048], F32, kind="ExternalInput").ap()
    dvec = nc.dram_tensor("vec", [128, NV], F32, kind="ExternalInput").ap()
    dcm = nc.dram_tensor("cm", [128, NCM], F32, kind="ExternalInput").ap()
    drow = nc.dram_tensor("rows", [1, 144], F32, kind="ExternalInput").ap()
    dsk = nc.dram_tensor("skT", [128, 2048], F32, kind="ExternalInput").ap()
    dout = nc.dram_tensor("outT", [8, 128, TOK], F32, kind="ExternalOutput").ap()
    dscr = nc.dram_tensor("wscr", [NPIECE, 128, 2048], BF16, kind="Internal").ap()
    ddiag = nc.dram_tensor("dgscr", [8, 128, 31 * 128], BF16, kind="Internal").ap()
    if dbg:
        ddbg = nc.dram_tensor("dbg", [8, 128, T], F32, kind="ExternalOutput").ap()

    es = ExitStack()
    with es:
        def sb(name, shape, dt):
            return es.enter_context(nc.sbuf_tensor("sb_" + name, shape, dt))

        P = Prog(nc, es)
        ps = es.enter_context(nc.psum_tensor("ps", [128, 8, 512], F32))
        PB = [Buf("bank%d" % i) for i in range(8)]

        vec = sb("vec", [128, NV], F32)
        cmf = sb("cmf", [128, NCM], F32)
        identb = sb("identb", [128, 128], BF16)
        permb = sb("permb", [128, 128], BF16)
        onesb = sb("onesb", [128, 128], BF16)
        onesmb = sb("onesmb", [128, 128], BF16)
        onesmf = sb("onesmf", [128, 128], F32)
        iotab = sb("iotab", [128, 128], BF16)
        maskb = sb("maskb", [128, 2, 2, 2, 128], BF16)
        esink = sb("esink", [128, 16], F32)
        bvb = sb("bvb", [128, 128], F32)
        skb = sb("skb", [128, 16, 128], BF16)
        xt_a = sb("xt", [128, 8, T], F32)
        xt_b = sb("xt2", [128, 8, T], F32)
        xts = [xt_a, xt_b]
        skf = xt_b[:, :, :].rearrange("p k t -> p (k t)")
        sqn = sb("sqn", [128, 8, T], BF16)
        rn1 = sb("rn1", [128, T], F32)
        ubuf = sb("ubuf", [128, 2, 8, 32 + T], BF16)
        kbuf = sb("kbuf", [128, 2, 2, 128 + T], BF16)
        vdup = sb("vdup", [128, 2, 3, 2, 128], BF16)
        cosT = sb("cosT", [128, 128 + T], F32)
        sinT = sb("sinT", [128, 128 + T], F32)
        posi = sb("posi", [128, 128 + T], I32)
        r1 = sb("r1", [128, 128 + T], F32)
        r2 = sb("r2", [128, 128 + T], F32)
        r3 = sb("r3", [128, 128 + T], F32)
        st1 = sb("st1", [128, 128 + T], F32)
        st2 = sb("st2", [128, 128 + T], F32)
        st3 = sb("st3", [128, 128 + T], F32)
        sg1 = sb("sg1", [128, 128 + T], F32)
        sg2 = sb("sg2", [128, 128 + T], F32)
        m1buf = sb("m1buf", [128, 4, T], F32)
        qb = sb("qb", [128, 128 + T], BF16)
        qb2 = sb("qb2", [128, 128 + T], BF16)
        r4 = sb("r4", [128, 128 + T], F32)
        eT = sb("eT", [128, 2, 2, 4, 128], BF16)
        den = sb("den", [128, 512], F32)
        rden = sb("rden", [128, 512], F32)
        h2T = sb("h2T", [128, 8, T], BF16)
        gl = sb("gl", [128, 2, T], BF16)
        GA = sb("GA", [128, 2, T], BF16)
        Pt = sb("Pt", [128, 2, 8, 128], BF16)
        Qt = sb("Qt", [128, 2, 8, 128], BF16)
        UTs = sb("UTs", [128, 4, 8, 256], BF16)
        Vs = sb("Vs", [128, 4, 2, 1024], BF16)
        v16 = sb("v16", [128, 16, 16], F32)
        i16 = sb("i16", [128, 16, 16], U32)
        i16f = sb("i16f", [128, 16, 16], F32)
        best = sb("best", [128, 8, 16], F32)
        posu = sb("posu", [128, 8, 16], U32)
        k1u = sb("k1u", [128, 8, 16], U32)
        posf = sb("posf", [128, 8, 16], F32)
        k1f = sb("k1f", [128, 8, 16], F32)
        k2f = sb("k2f", [128, 8, 16], F32)
        ebuf = sb("ebuf", [128, 8, 16], F32)
        gate = sb("gate", [128, 8, 16], F32)
        Zs = sb("Zs", [128, 8], F32)
        av = sb("av", [128, 8, 16], F32)
        bvv = sb("bvv", [128, 8, 16], F32)
        abgT = sb("abgT", [128, 2, 3, 128], F32)
        one1 = sb("one1", [128, 4], F32)
        arena = sb("arena", [128, 32768], BF16)

        def av_(off, nbytes, dt):
            a = arena[:, off // 2:(off + nbytes) // 2]
            return a if dt == BF16 else a.bitcast(dt)

        K = 1024
        wbuf = [av_(0, 8 * K, BF16).rearrange("p (k c) -> p k c", k=8),
                av_(8 * K, 8 * K, BF16).rearrange("p (k c) -> p k c", k=8),
                av_(16 * K, 8 * K, BF16).rearrange("p (k c) -> p k c", k=8),
                av_(24 * K, 8 * K, BF16).rearrange("p (k c) -> p k c", k=8)]
        diag = [av_(16 * K, 7936, BF16).rearrange("p (j c) -> p j c", j=31),
                av_(24 * K, 7936, BF16).rearrange("p (j c) -> p j c", j=31)]
        hT = av_(32 * K, 6 * K, BF16).rearrange("p (k c) -> p k c", k=8)
        sqb = av_(38 * K, 6 * K, BF16).rearrange("p (k c) -> p k c", k=8)
        ysb = av_(44 * K, 8 * K, F32).rearrange("p (k c) -> p k c", k=8)
        sT = av_(52 * K, 4 * K, BF16).rearrange("p (k c) -> p k c", k=8)
        qrope = av_(56 * K, 4 * K, BF16).rearrange("p (k c) -> p k c", k=8)
        attnT = av_(60 * K, 4 * K, BF16).rearrange("p (k c) -> p k c", k=8)
        mergedT = sqb
        stin = [av_(i * 8 * K, 8 * K, F32) for i in range(4)]
        stout = [av_(32 * K + i * 4 * K, 4 * K, BF16) for i in range(4)]
        dgst = [av_(48 * K, 7936, BF16).rearrange("p (j c) -> p j c", j=31),
                av_(56 * K, 7936, BF16).rearrange("p (j c) -> p j c", j=31)]
        qTb = av_(16 * K, 8 * K, BF16).rearrange("p (k c) -> p k c", k=16)
        cand = av_(24 * K, 8 * K, F32).rearrange("p (h a b) -> p h a b", h=8, a=16)
        work2 = av_(32 * K, 8 * K, F32).rearrange("p (h c) -> p h c", h=8)
        E1 = av_(40 * K, 8 * K, F32).rearrange("p (h a b) -> p h a b", h=8, a=16)
        scw = av_(48 * K, 2 * K, F32).rearrange("p (l c) -> p l c", l=4)
        G = arena[:, :].rearrange("p (i t) -> p i t", i=128)
        outtmp = av_(0, 8 * K, F32).rearrange("p (k c) -> p k c", k=8)

        ATOK = Buf("atok")

        def vcol(i):
            return vec[:, i:i + 1]

        capture = [None]

        def OP(eng, fn, reads, writes, arena_use=False):
            if capture[0] is not None:
                capture[0].append(("op", (eng, fn, list(reads), list(writes), arena_use)))
                return None
            if arena_use:
                reads = list(reads) + [ATOK]
            return P.op(eng, fn, reads, writes)

        def DMA(fn, reads, writes, arena_use=False, final=False, eng=None):
            if capture[0] is not None:
                capture[0].append(("dma", (fn, list(reads), list(writes), arena_use, final, eng)))
                return None
            if arena_use:
                reads = list(reads) + [ATOK]
            return P.dma(fn, reads, writes, final=final, eng=eng)

        def replay(item):
            kind, args = item
            if kind == "op":
                OP(*args)
            else:
                DMA(*args)

        def barrier():
            P.op("pool", lambda e: e.memset(one1[:, 0:1], 0.0), [], [ATOK])

        def MM(out, lhsT, rhs, start, stop, reads, writes, au=True, sgc=False):
            OP("pe", lambda e: e.matmul(out, lhsT=lhsT, rhs=rhs, start=start, stop=stop,
                                        skip_group_check=sgc), reads, writes, au)

        def ACT(out, in_, func, reads, writes, bias=None, scale=None, au=True):
            kw = {}
            if bias is not None:
                kw["bias"] = bias
            if scale is not None:
                kw["scale"] = scale
            OP("act", lambda e: e.activation(out=out, in_=in_, func=func, **kw), reads, writes, au)

        def TTo(out, in0, in1, op, reads, writes, eng="dve", au=True):
            OP(eng, lambda e: e.tensor_tensor(out=out, in0=in0, in1=in1, op=op), reads, writes, au)

        def TS(out, in0, s1, op0, reads, writes, s2=None, op1=None, eng="dve", au=True):
            if op1 is None:
                OP(eng, lambda e: e.tensor_scalar(out=out, in0=in0, scalar1=s1, scalar2=None, op0=op0),
                   reads, writes, au)
            else:
                OP(eng, lambda e: e.tensor_scalar(out=out, in0=in0, scalar1=s1, scalar2=s2, op0=op0, op1=op1),
                   reads, writes, au)

        def STT(out, in0, scalar, in1, op0, op1, reads, writes, au=True):
            OP("dve", lambda e: e.scalar_tensor_tensor(out=out, in0=in0, scalar=scalar, in1=in1,
                                                       op0=op0, op1=op1), reads, writes, au)

        def CP(out, in_, reads, writes, eng="dve", au=True):
            OP(eng, lambda e: e.tensor_copy(out=out, in_=in_), reads, writes, au)

        def RECIP(out, in_, reads, writes, au=True):
            OP("dve", lambda e: e.reciprocal(out=out, in_=in_), reads, writes, au)

        Bvec, Bcm, Bconst, Bsk = Buf("vec"), Buf("cm"), Buf("const"), Buf("sk")
        Bxs = [BG("xta", 8), BG("xtb", 8)]
        Bsqn, Brn1 = BG("sqn", 8), Buf("rn1")
        Bscr = [Buf("scr%d" % i) for i in range(NPIECE)]
        Bst_in = BG("sti", 4)
        Bst_out = BG("sto", 4)
        Bw = [Buf("w0"), Buf("w1")]
        Bdiag = [Buf("dg0"), Buf("dg1")]
        Bw = Bw + Bdiag
        BhT, Bsq, Bys, BsT, Bqr, Bat = BG("hT", 8), BG("sq", 8), BG("ys", 8), BG("sT", 8), BG("qr", 8), BG("at", 8)
        Bub = [BG("ub0_", 8), BG("ub1_", 8)]
        Bkb = [Buf("kb0"), Buf("kb1")]
        Bvd = [[Buf("vd%d%d" % (p, b)) for b in range(3)] for p in range(2)]
        Bcs, Bposi, Br1, Br2, Br3 = Buf("cs"), Buf("posi"), Buf("r1"), Buf("r2"), Buf("r3")
        Bs1, Bs2, Bs3, Bg1, Bg2, Bm1, Bqb = Buf("st1"), Buf("st2"), Buf("st3"), Buf("sg1"), Buf("sg2"), Buf("m1"), Buf("qb")
        BeT = [Buf("eT0"), Buf("eT1")]
        Bden, Brden, Brscr = Buf("den"), Buf("rden"), Buf("rscr")
        Bh2, Bgl, BGA = Buf("h2T"), [Buf("gl0"), Buf("gl1")], [Buf("GA0"), Buf("GA1")]
        BPt, BQt = [BG("Pt0_", 8), BG("Pt1_", 8)], [BG("Qt0_", 8), BG("Qt1_", 8)]
        BUT, BVs = BG("UT", 4), BG("Vs", 4)
        Bv16, Bi16, Bi16f, Bbest, Bpos, Btk, Babg = BG("v16_", 16), BG("i16_", 16), Buf("i16f"), BG("best", 8), BG("pos", 8), Buf("tk"), Buf("abg")
        BqT, Bcand, Bw2, BE1, Bscw, BGm, Bot = BG("qTb", 16), Buf("cand"), BG("w2_", 8), Buf("E1"), BG("scw", 4), BG("G", 64), Buf("ot")
        Bsgs = BG("sgs", 2)
        Bra, Brb, Bqbs = BG("ra", 2), BG("rb", 2), BG("qbs", 2)
        Bout = Buf("out")

        DMA(lambda e: e.dma_start(out=vec[:], in_=dvec[:, :]), [], [Bvec])
        DMA(lambda e: e.dma_start(out=cmf[:], in_=dcm[:, :]), [], [Bcm])
        DMA(lambda e: e.dma_start(out=skf, in_=dsk[:, :]), [], [Bsk, Bxs[1]])
        DMA(lambda e: e.dma_start(out=bvb[:], in_=drow[0:1, 0:128].partition_broadcast(128)[:, 0, :]), [], [Bconst])
        DMA(lambda e: e.dma_start(out=esink[:], in_=drow[0:1, 128:144].partition_broadcast(128)[:, 0, :]), [], [Bconst])
        CP(identb[:], cmf[:, C_ID:C_ID + 128], [Bcm], [Bconst], au=False)
        CP(permb[:], cmf[:, C_PERM:C_PERM + 128], [Bcm], [Bconst], au=False)
        CP(iotab[:], cmf[:, C_IOTA:C_IOTA + 128], [Bcm], [Bconst], au=False)
        OP("dve", lambda e: e.memset(onesb[:], 1.0), [], [Bconst])
        OP("dve", lambda e: e.memset(onesmb[:], 1.0 / 1024.0), [], [Bconst])
        OP("dve", lambda e: e.memset(onesmf[:], 1.0 / 1024.0), [], [Bconst])
        for var in range(2):
            for kb in range(2):
                for h4 in range(2):
                    c0 = (C_MASK0 if var == 0 else C_MASK1) + kb * 128
                    CP(maskb[:, var, kb, h4, :], cmf[:, c0:c0 + 128], [Bcm], [Bconst], au=False)
        ACT(esink[:], esink[:], AF.Exp, [Bconst], [Bconst], au=False)
        CP(skb[:].rearrange("p a b -> p (a b)"), skf, [Bsk, Bxs[1]], [Bsk], au=False)

        cast_engs = ["act", "dve", "pool"]
        NPR = NPIECE if stop != 1 else 0

        def pl_load(i):
            s = i % 4
            DMA(lambda e: e.dma_start(out=stin[s], in_=dwall[i]), [], [Bst_in[s]], True)

        def pl_cast_store(i):
            s = i % 4
            ce = cast_engs[i % 3]
            if ce == "act":
                ACT(stout[s], stin[s], AF.Copy, [Bst_in[s]], [Bst_out[s]])
            else:
                CP(stout[s], stin[s], [Bst_in[s]], [Bst_out[s]], eng=ce)
            DMA(lambda e: e.dma_start(out=dscr[i], in_=stout[s]), [Bst_out[s]], [Bscr[i]], True, eng="act")

        for i in range(min(3, NPR)):
            pl_load(i)
        for i in range(NPR):
            if i + 3 < NPR:
                pl_load(i + 3)
            pl_cast_store(i)

        Bdg = [BG("dgs0_", 31), BG("dgs1_", 31)]
        Bdgd = BG("dgd", 8)
        for c in range(8 if stop != 1 else 0):
            s = c % 2
            for jt in range(31):
                ACT(dgst[s][:, jt, :], cmf[:, C_ID:C_ID + 128], AF.Copy, [Bcm, Bvec], [Bdg[s][jt]],
                    scale=vcol(V_CW + c * 31 + jt))
            DMA(lambda e, c=c, s=s: e.dma_start(out=ddiag[c], in_=dgst[s][:, :, :].rearrange("p j c -> p (j c)")),
                [Bdg[s]], [Bdgd[c]], True)

        wslot = [0]
        wmode = [2]

        def loadw(g):
            s = wslot[0] % wmode[0]
            wslot[0] += 1
            DMA(lambda e: e.dma_start(out=wbuf[s].rearrange("p (r k) c -> p r (k c)", r=2),
                                      in_=dscr[2 * g:2 * g + 2].rearrange("r p f -> p r f")),
                [Bscr[2 * g], Bscr[2 * g + 1]], [Bw[s]], True)
            return s

        bankrr = [0]

        def nextbank():
            b = bankrr[0]
            bankrr[0] = (b + 1) % 4
            return b

        def proj(bank, s, j, rhs3, c0, n, rbuf):
            for k in range(8):
                MM(ps[:, bank, 0:n], wbuf[s][:, k, j * 128:(j + 1) * 128], rhs3[:, k, c0:c0 + n],
                   k == 0, k == 7, [Bw[s], rbuf], [PB[bank]])

        def colstats(src3, c0, n, srcbuf, outrr, outbuf, tmp, tmpbuf, sqv, sqbuf, bank, sq_au=True):
            for k in range(8):
                ACT(sqv[:, k, c0:c0 + n], src3[:, k, c0:c0 + n], AF.Square, [srcbuf[k]], [sqbuf[k]], au=sq_au)
            for k in range(8):
                MM(ps[:, bank, 0:n], onesmb[:], sqv[:, k, c0:c0 + n], k == 0, k == 7, [Bconst, sqbuf[k]], [PB[bank]], au=sq_au)
            TS(tmp[:, c0:c0 + n], ps[:, bank, 0:n], EPSV, ALU.add, [PB[bank]], [tmpbuf], au=False)
            ACT(tmp[:, c0:c0 + n], tmp[:, c0:c0 + n], AF.Sqrt, [tmpbuf], [tmpbuf], au=False)
            RECIP(outrr[:, c0:c0 + n], tmp[:, c0:c0 + n], [tmpbuf], [outbuf], au=False)

        def rope_tables(c0, n, colbase):
            DMA(lambda e: e.dma_start(out=posi[:, c0:c0 + n],
                                      in_=dpos[0:1, colbase:colbase + n].partition_broadcast(128)[:, 0, :]),
                [], [Bposi])
            sl = slice(c0, c0 + n)
            CP(r1[:, sl], posi[:, sl], [Bposi], [Br1], au=False)
            TS(r1[:, sl], r1[:, sl], vcol(V_INVF), ALU.mult, [Br1, Bvec], [Br1], au=False)
            for (dst, shift, useSgn) in ((sinT, 0.0, True), (cosT, float(np.pi / 2), False)):
                TS(r2[:, sl], r1[:, sl], shift, ALU.add, [Br1], [Br2], au=False)
                TS(r3[:, sl], r2[:, sl], float(1.0 / (2 * np.pi)), ALU.mult, [Br2], [Br3], s2=MAGIC, op1=ALU.add, au=False)
                TS(r3[:, sl], r3[:, sl], MAGIC, ALU.subtract, [Br3], [Br3], au=False)
                STT(r2[:, sl], r3[:, sl], -CW1, r2[:, sl], ALU.mult, ALU.add, [Br3, Br2], [Br2], au=False)
                STT(r2[:, sl], r3[:, sl], -CW2, r2[:, sl], ALU.mult, ALU.add, [Br3, Br2], [Br2], au=False)
                TS(r2[:, sl], r2[:, sl], 3.1415925, ALU.min, [Br2], [Br2], s2=-3.1415925, op1=ALU.max, au=False)
                if useSgn:
                    ACT(dst[:, sl], r2[:, sl], AF.Sin, [Br2, Bvec], [Bcs], scale=vcol(V_SGN), au=False)
                else:
                    ACT(dst[:, sl], r2[:, sl], AF.Sin, [Br2], [Bcs], au=False)

        pending_final = [None]

        def ckpt(i):
            if stop == i:
                P.halted = True

        if stop in (1, 2):
            P.halted = True
        for ti in range(NT):
            par = ti % 2
            c0 = 0 if ti == 0 else 128
            n = 128 + T - c0
            xcol = HALO + ti * T
            barrier()
            xt = xts[par]
            Bx = Bxs[par]

            def prefetch(tn):
                pn = tn % 2
                xc = HALO + tn * T
                cc0 = 0 if tn == 0 else 128
                DMA(lambda e: e.dma_start(out=xts[pn][:], in_=dx[:, :, xc:xc + T].rearrange("k p t -> p k t")),
                    [], [Bxs[pn]])
                rope_tables(cc0, 128 + T - cc0, xc - 128 + cc0)
                colstats(xts[pn], 0, T, Bxs[pn], rn1, Brn1, st2, Bs2, sqn, Bsqn, 6, sq_au=False)

            if ti == 0:
                prefetch(0)
            for k in range(8):
                STT(hT[:, k, 128:128 + T], xt[:, k, :], vcol(V_G1 + k), rn1[:, 0:T], ALU.mult, ALU.mult,
                    [Bx[k], Bvec, Brn1], [BhT[k]])
            if ti == 0:
                DMA(lambda e: e.dma_start(out=ysb[:, :, 0:128], in_=dx[:, :, 0:128].rearrange("k p t -> p k t")),
                    [], [Bys], True)
                colstats(ysb, 0, 128, Bys, st3, Bs3, st2, Bs2, sqb, Bsq, 4)
                for k in range(8):
                    STT(hT[:, k, 0:128], ysb[:, k, 0:128], vcol(V_G1 + k), st3[:, 0:128], ALU.mult, ALU.mult,
                        [Bys[k], Bvec, Bs3], [BhT[k]])
            else:
                for c in range(8):
                    CP(ubuf[:, par, c, 0:32], ubuf[:, 1 - par, c, T:T + 32], [Bub[1 - par][c]], [Bub[par][c]], eng="pool", au=False)
                for g in range(2):
                    CP(kbuf[:, par, g, 0:128], kbuf[:, 1 - par, g, T:T + 128], [Bkb[1 - par]], [Bkb[par]], eng="pool", au=False)
                CP(vdup[:, par, 0, :, :], vdup[:, 1 - par, 2, :, :], [Bvd[1 - par][2]], [Bvd[par][0]], eng="pool", au=False)

            ckpt(3)
            cu0 = 96 if ti == 0 else 128
            nu = 128 + T - cu0
            wsl = {}
            sgl = [sg1, sg2]

            def glu_proj(c):
                pr, j = c // 4, c % 4
                if j == 0:
                    wsl[pr] = (loadw(2 * pr), loadw(2 * pr + 1))
                sv, sgt = wsl[pr]
                bA = nextbank()
                proj(bA, sv, j, hT, cu0, nu, BhT)
                bB = nextbank()
                proj(bB, sgt, j, hT, cu0, nu, BhT)
                sg = sgl[c % 2]
                ACT(sg[:, 0:nu], ps[:, bB, 0:nu], AF.Sigmoid, [PB[bB], Bvec], [Bsgs[c % 2]],
                    bias=vcol(V_BIN + _chunk_gate(c)), au=False)
                STT(ubuf[:, par, c, cu0 - 96:cu0 - 96 + nu], ps[:, bA, 0:nu], vcol(V_BIN + _chunk_val(c)),
                    sg[:, 0:nu], ALU.add, ALU.mult, [PB[bA], Bvec, Bsgs[c % 2]], [Bub[par][c]], au=False)
                if ti == 0:
                    TS(ubuf[:, par, c, 0:32], ubuf[:, par, c, 0:32], vcol(V_HV), ALU.mult,
                       [Bub[par][c], Bvec], [Bub[par][c]], au=False)
                ds_ = c % 2
                DMA(lambda e: e.dma_start(out=diag[ds_][:, :, :].rearrange("p j c -> p (j c)"), in_=ddiag[c]),
                    [Bdgd[c]], [Bdiag[ds_]], True)

            def conv(c):
                ds_ = c % 2
                bC = nextbank()
                for jt in range(31):
                    MM(ps[:, bC, 0:T], diag[ds_][:, jt, :], ubuf[:, par, c, 2 + jt:2 + jt + T],
                       jt == 0, jt == 30, [Bdiag[ds_], Bub[par][c]], [PB[bC]])
                ACT(ysb[:, c, :], ps[:, bC, 0:T], AF.Identity, [PB[bC], Bvec], [Bys[c]], bias=vcol(V_CB + c))
                ACT(sqb[:, c, 0:T], ps[:, bC, 0:T], AF.Square, [PB[bC], Bvec], [Bsq[c]], bias=vcol(V_CB + c))

            glu_proj(0)
            if pending_final[0] is not None:
                pending_final[0]()
                pending_final[0] = None
            for c in range(8):
                if c + 1 < 8:
                    glu_proj(c + 1)
                conv(c)
            wmode[0] = 4
            for c in range(8):
                MM(ps[:, 4, 0:T], onesmf[:], ysb[:, c, :], c == 0, c == 7, [Bconst, Bys[c]], [PB[4]])
            for c in range(8):
                MM(ps[:, 5, 0:T], onesmb[:], sqb[:, c, 0:T], c == 0, c == 7, [Bconst, Bsq[c]], [PB[5]])
            CP(st1[:, 0:T], ps[:, 4, 0:T], [PB[4]], [Bs1], au=False)
            TTo(st2[:, 0:T], st1[:, 0:T], st1[:, 0:T], ALU.mult, [Bs1], [Bs2], au=False)
            TTo(st2[:, 0:T], ps[:, 5, 0:T], st2[:, 0:T], ALU.subtract, [PB[5], Bs2], [Bs2], au=False)
            TS(st2[:, 0:T], st2[:, 0:T], EPSV, ALU.add, [Bs2], [Bs2], au=False)
            ACT(st2[:, 0:T], st2[:, 0:T], AF.Sqrt, [Bs2], [Bs2], au=False)
            RECIP(st3[:, 0:T], st2[:, 0:T], [Bs2], [Bs3], au=False)
            STT(st1[:, 0:T], st1[:, 0:T], -1.0, st3[:, 0:T], ALU.mult, ALU.mult, [Bs1, Bs3], [Bs1], au=False)
            for c in range(8):
                ra_, rb_ = (r1, r2) if c % 2 == 0 else (r3, r4)
                Ba_, Bb_ = ([Br1, Bra[0]], [Br2, Brb[0]]) if c % 2 == 0 else ([Br3, Bra[1]], [Brb[1]])
                TTo(ra_[:, 0:T], ysb[:, c, :], st3[:, 0:T], ALU.mult, [Bys[c], Bs3], Ba_)
                TTo(rb_[:, 0:T], ra_[:, 0:T], st1[:, 0:T], ALU.add, Ba_ + [Bs1], Bb_, au=False)
                ACT(sT[:, c, :], rb_[:, 0:T], AF.Silu, Bb_ + [Bvec], [BsT[c]], bias=vcol(V_LNB + c), scale=vcol(V_LNG + c))

            ckpt(4)
            rcnt = [0]

            def rope_chunk(bank, outap, cc0, nn, bcol, outbuf):
                i_ = rcnt[0] % 2
                rcnt[0] += 1
                qb_ = (qb, qb2)[i_]
                ra_, rb_ = ((r1, r2), (r3, r4))[i_]
                Bq_ = [Bqb, Bqbs[0]] if i_ == 0 else [Bqbs[1]]
                Ba_ = [Br1, Bra[0]] if i_ == 0 else [Br3, Bra[1]]
                Bb_ = [Br2, Brb[0]] if i_ == 0 else [Brb[1]]
                ACT(qb_[:, cc0:cc0 + nn], ps[:, bank, 0:nn], AF.Identity, [PB[bank], Bvec], Bq_, bias=vcol(bcol), au=False)
                b2 = nextbank()
                MM(ps[:, b2, 0:nn], permb[:], qb_[:, cc0:cc0 + nn], True, True, [Bconst] + Bq_, [PB[b2]], au=False)
                TTo(ra_[:, cc0:cc0 + nn], qb_[:, cc0:cc0 + nn], cosT[:, cc0:cc0 + nn], ALU.mult, Bq_ + [Bcs], Ba_, au=False)
                TTo(rb_[:, cc0:cc0 + nn], ps[:, b2, 0:nn], sinT[:, cc0:cc0 + nn], ALU.mult, [PB[b2], Bcs], Bb_, au=False)
                TTo(outap, ra_[:, cc0:cc0 + nn], rb_[:, cc0:cc0 + nn], ALU.add, Ba_ + Bb_, [outbuf], au=True)

            for qg in range(2):
                s = loadw(4 + qg)
                for j in range(4):
                    cq = qg * 4 + j
                    b = nextbank()
                    proj(b, s, j, hT, 128, T, BhT)
                    rope_chunk(b, qrope[:, cq, :], 128, T, V_BIN + 16 + cq, Bqr[cq])
            s = loadw(6)
            for g in range(2):
                b = nextbank()
                proj(b, s, g, hT, c0, n, BhT)
                rope_chunk(b, kbuf[:, par, g, c0:c0 + n], c0, n, V_BIN + 24 + g, Bkb[par])
            for blk in range(3):
                if ti > 0 and blk == 0:
                    continue
                b = nextbank()
                for k in range(8):
                    MM(ps[:, b, 0:128], hT[:, k, blk * 128:(blk + 1) * 128], wbuf[s][:, k, 256:384],
                       k == 0, k == 7, [BhT, Bw[s]], [PB[b]])
                for dup in range(2):
                    TTo(vdup[:, par, blk, :, dup * 64:(dup + 1) * 64],
                        ps[:, b, 0:128].rearrange("p (g d) -> p g d", g=2),
                        bvb[:].rearrange("p (g d) -> p g d", g=2), ALU.add, [PB[b], Bconst], [Bvd[par][blk]], au=False)

            ckpt(5)
            iters = [(b, g, hg) for b in range(2) for g in range(2) for hg in range(2)]

            def att_S(i):
                b, g, hg = iters[i]
                sb0 = 6 if i % 2 == 0 else 2
                var = 0 if (ti == 0 and b == 0) else 1
                j0 = (8 * g + 4 * hg) // 2
                for half in range(2):
                    MM(ps[:, sb0 + half, :], identb[:], maskb[:, var, :, :, :].rearrange("p k a q -> p (k a q)"),
                       True, False, [Bconst], [PB[sb0 + half]], au=False, sgc=True)
                for half in range(2):
                    pa = slice(half * 64, (half + 1) * 64)
                    for kb in range(2):
                        for a in range(2):
                            MM(ps[:, sb0 + half, (kb * 2 + a) * 128:(kb * 2 + a + 1) * 128],
                               kbuf[pa, par, g, (b + kb) * 128:(b + kb + 1) * 128],
                               qrope[pa, j0 + a, b * 128:(b + 1) * 128],
                               False, True, [Bkb[par], Bqr[j0 + a]], [PB[sb0 + half]], sgc=True)

            def att_rest(i):
                b, g, hg = iters[i]
                sl_ = i % 2
                sb0 = 6 if i % 2 == 0 else 2
                ob, db = (4, 5) if i % 2 == 0 else (0, 1)
                h0 = 8 * g + 4 * hg
                j0 = h0 // 2
                ACT(eT[:, sl_, :, :, :].rearrange("p k h q -> p (k h q)"),
                    ps[:, sb0:sb0 + 2, :].rearrange("p k c -> p (k c)"), AF.Exp, [PB[sb0], PB[sb0 + 1]], [BeT[sl_]],
                    scale=0.125, au=False)
                eTv = eT[:, sl_, :, :, :].rearrange("p half (kb a) q -> p half kb a q", kb=2)
                for kb in range(2):
                    for half in range(2):
                        MM(ps[:, ob, half * 256:(half + 1) * 256], vdup[:, par, b + kb, g, :],
                           eTv[:, half, kb, :, :].rearrange("p a q -> p (a q)"),
                           (kb == 0 and half == 0), kb == 1, [Bvd[par][b + kb], BeT[sl_]], [PB[ob]], au=False, sgc=True)
                for kb in range(2):
                    for half in range(2):
                        MM(ps[:, db, half * 256:(half + 1) * 256], onesb[:],
                           eTv[:, half, kb, :, :].rearrange("p a q -> p (a q)"),
                           (kb == 0 and half == 0), kb == 1, [Bconst, BeT[sl_]], [PB[db]], au=False, sgc=True)
                TTo(den[:].rearrange("p (half a q) -> p half a q", half=2, a=2),
                    ps[:, db, :].rearrange("p (half a q) -> p half a q", half=2, a=2),
                    esink[:, h0:h0 + 4].rearrange("p (a half) -> p half a", half=2).unsqueeze(3).to_broadcast([128, 2, 2, 128]),
                    ALU.add, [PB[db], Bconst], [Bden], au=False)
                RECIP(rden[:], den[:], [Bden], [Brden], au=False)
                for half in range(2):
                    pa = slice(half * 64, (half + 1) * 64)
                    TTo(attnT[pa, j0:j0 + 2, b * 128:(b + 1) * 128],
                        ps[pa, ob, half * 256:(half + 1) * 256].rearrange("p (a q) -> p a q", a=2),
                        rden[pa, half * 256:(half + 1) * 256].rearrange("p (a q) -> p a q", a=2),
                        ALU.mult, [PB[ob], Brden], [Bat[j0], Bat[j0 + 1]])

            att_S(0)
            for i in range(8):
                if i + 1 < 8:
                    att_S(i + 1)
                att_rest(i)

            ckpt(6)
            Bm1s = BG("m1s", 4)
            for jg in range(2):
                sa = loadw(11 + jg)
                sb_ = loadw(7 + jg)
                for j in range(4):
                    c = jg * 4 + j
                    bA = nextbank()
                    proj(bA, sa, j, sT, 0, T, BsT)
                    bB = nextbank()
                    proj(bB, sb_, j, hT, 128, T, BhT)
                    sg = sgl[j % 2]
                    ACT(sg[:, 0:T], ps[:, bB, 0:T], AF.Sigmoid, [PB[bB], Bvec], [Bsgs[j % 2]], bias=vcol(V_BIN + 28 + c), au=False)
                    TTo(m1buf[:, j, :], ps[:, bA, 0:T], sg[:, 0:T], ALU.mult, [PB[bA], Bsgs[j % 2]], [Bm1s[j]], au=False)
                sa = loadw(13 + jg)
                sb_ = loadw(9 + jg)
                for j in range(4):
                    c = jg * 4 + j
                    bA = nextbank()
                    proj(bA, sa, j, attnT, 0, T, Bat)
                    bB = nextbank()
                    proj(bB, sb_, j, hT, 128, T, BhT)
                    sg = sgl[j % 2]
                    rt_ = (r3, r4)[j % 2]
                    Brt_ = [Br3, Bra[1]] if j % 2 == 0 else [Brb[1]]
                    ACT(sg[:, 0:T], ps[:, bB, 0:T], AF.Sigmoid, [PB[bB], Bvec], [Bsgs[j % 2]], bias=vcol(V_BIN + 36 + c), au=False)
                    TTo(rt_[:, 0:T], ps[:, bA, 0:T], sg[:, 0:T], ALU.mult, [PB[bA], Bsgs[j % 2]], Brt_, au=False)
                    TTo(mergedT[:, c, 0:T], rt_[:, 0:T], m1buf[:, j, :], ALU.add, Brt_ + [Bm1s[j]], [Bsq[c]])
            for og in range(2):
                s = loadw(15 + og)
                for j in range(4):
                    c = og * 4 + j
                    b = nextbank()
                    proj(b, s, j, mergedT, 0, T, Bsq)
                    TTo(xt[:, c, :], xt[:, c, :], ps[:, b, 0:T], ALU.add, [Bx[c], PB[b]], [Bx[c]], au=False)
            if dbg and ti == 0:
                DMA(lambda e, xt_=xt: e.dma_start(out=ddbg.rearrange("k p t -> p k t"), in_=xt_[:]), [Bx], [Buf("dbgo")], final=True)

            ckpt(7)
            colstats(xt, 0, T, Bx, st1, Bs1, st2, Bs2, sqb, Bsq, 4)
            for k in range(8):
                STT(h2T[:, k, :], xt[:, k, :], vcol(V_G2 + k), st1[:, 0:T], ALU.mult, ALU.mult, [Bx[k], Bvec, Bs1], [Bh2], au=False)
            wmode[0] = 2
            barrier()
            for g4 in range(4):
                s = loadw(17 + g4)
                for j in range(4):
                    hc = g4 * 4 + j
                    b = nextbank()
                    for k in range(8):
                        MM(ps[:, b, 0:T], wbuf[s][:, k, j * 128:(j + 1) * 128], h2T[:, k, :], k == 0, k == 7,
                           [Bw[s], Bh2], [PB[b]])
                    if hc % 2 == 0:
                        ACT(qTb[:, hc, :], ps[:, b, 0:T], AF.Copy, [PB[b]], [BqT[hc]])
                    else:
                        CP(qTb[:, hc, :], ps[:, b, 0:T], [PB[b]], [BqT[hc]])
            v16v = v16[:, :, :].rearrange("p (h c) k -> p h c k", c=2)
            i16fv = i16f[:, :, :].rearrange("p (h c) k -> p h c k", c=2)
            B4 = [128, 8, 16, 16]
            for tc in range(2):
                tcs = slice(tc * 128, (tc + 1) * 128)
                for g4 in range(4):
                    for l in range(4):
                        hc = g4 * 4 + l
                        MM(ps[:, 5, l * 128:(l + 1) * 128], qTb[:, hc, tcs], skb[:, hc, :], True, True,
                           [BqT[hc], Bsk], [PB[5]])
                    def L1(step, l):
                        hc = g4 * 4 + l
                        src_ = ps[:, 5, l * 128:(l + 1) * 128]
                        if step == 0:
                            OP("dve", lambda e: e.max(out=v16[:, hc, 0:8], in_=src_), [PB[5]], [Bv16[hc]])
                        elif step == 1:
                            OP("dve", lambda e: e.max_index(out=i16[:, hc, 0:8], in_max=v16[:, hc, 0:8], in_values=src_),
                               [PB[5], Bv16[hc]], [Bi16[hc]])
                        elif step == 2:
                            OP("dve", lambda e: e.match_replace(out=scw[:, l, :], in_to_replace=v16[:, hc, 0:8],
                                                               in_values=src_, imm_value=-1e30),
                               [PB[5], Bv16[hc]], [Bscw[l]], True)
                        elif step == 3:
                            OP("dve", lambda e: e.max(out=v16[:, hc, 8:16], in_=scw[:, l, :]), [Bscw[l]], [Bv16[hc]], True)
                        else:
                            OP("dve", lambda e: e.max_index(out=i16[:, hc, 8:16], in_max=v16[:, hc, 8:16], in_values=scw[:, l, :]),
                               [Bscw[l], Bv16[hc]], [Bi16[hc]], True)
                    for step in (0, 2, 1, 3, 4):
                        for l in range(4):
                            L1(step, l)
                CP(i16f[:], i16[:], [Bi16], [Bi16f], au=False)
                TTo(cand[:], v16v[:, :, 0, :].unsqueeze(3).to_broadcast(B4), v16v[:, :, 1, :].unsqueeze(2).to_broadcast(B4),
                    ALU.add, [Bv16], [Bcand])
                def L2(step, h):
                    src_ = cand[:, h, :, :].rearrange("p a b -> p (a b)")
                    if step == 0:
                        OP("dve", lambda e: e.max(out=best[:, h, 0:8], in_=src_), [Bcand], [Bbest[h]], True)
                    elif step == 1:
                        OP("dve", lambda e: e.max_index(out=posu[:, h, 0:8], in_max=best[:, h, 0:8], in_values=src_),
                           [Bcand, Bbest[h]], [Bpos[h]], True)
                    elif step == 2:
                        OP("dve", lambda e: e.match_replace(out=work2[:, h, :], in_to_replace=best[:, h, 0:8],
                                                           in_values=src_, imm_value=-1e30), [Bcand, Bbest[h]], [Bw2[h]], True)
                    elif step == 3:
                        OP("dve", lambda e: e.max(out=best[:, h, 8:16], in_=work2[:, h, :]), [Bw2[h]], [Bbest[h]], True)
                    else:
                        OP("dve", lambda e: e.max_index(out=posu[:, h, 8:16], in_max=best[:, h, 8:16], in_values=work2[:, h, :]),
                           [Bw2[h], Bbest[h]], [Bpos[h]], True)
                for step in (0, 2, 1, 3, 4):
                    for h in range(8):
                        L2(step, h)
                CP(posf[:], posu[:], [Bpos], [Btk], au=False)
                OP("dve", lambda e: e.tensor_single_scalar(out=k1u[:], in_=posu[:], scalar=4, op=ALU.logical_shift_right),
                   [Bpos], [Btk])
                CP(k1f[:], k1u[:], [Btk], [Btk], au=False)
                STT(k2f[:], k1f[:], -16.0, posf[:], ALU.mult, ALU.add, [Btk], [Btk], au=False)
                TTo(ebuf[:], best[:], best[:, :, 0:1].to_broadcast([128, 8, 16]), ALU.subtract, [Bbest], [Btk], au=False)
                ACT(ebuf[:], ebuf[:], AF.Exp, [Btk], [Btk], au=False)
                OP("dve", lambda e: e.tensor_reduce(out=Zs[:], in_=ebuf[:], axis=AX.X, op=ALU.add), [Btk], [Btk])
                RECIP(Zs[:], Zs[:], [Btk], [Btk], au=False)
                TTo(gate[:], ebuf[:], Zs[:, :].unsqueeze(2).to_broadcast([128, 8, 16]), ALU.mult, [Btk], [Btk], au=False)
                io16 = cmf[:, C_IOTA16:C_IOTA16 + 16].unsqueeze(1).unsqueeze(1).to_broadcast(B4)
                for (kf, cidx, dst) in ((k1f, 0, av), (k2f, 1, bvv)):
                    TTo(E1[:], kf[:].unsqueeze(3).to_broadcast(B4), io16, ALU.is_equal, [Btk, Bcm], [BE1])
                    TTo(E1[:], E1[:], i16fv[:, :, cidx, :].unsqueeze(2).to_broadcast(B4), ALU.mult, [BE1, Bi16f], [BE1])
                    OP("dve", lambda e, dst=dst: e.tensor_reduce(out=dst[:], in_=E1[:], axis=AX.X, op=ALU.add), [BE1], [Btk], True)
                for idx, srcv in enumerate((av, bvv, gate)):
                    OP("pe", lambda e, idx=idx, srcv=srcv: e.transpose(out=ps[:, 5, idx * 128:(idx + 1) * 128],
                                                                     in_=srcv[:].rearrange("p h k -> p (h k)"),
                                                                     identity=cmf[:, C_ID:C_ID + 128]),
                       [Btk, Bcm], [PB[5]])
                CP(abgT[:, tc, :, :].rearrange("p a b -> p (a b)"), ps[:, 5, 0:384], [PB[5]], [Babg], au=False)

            ckpt(8)
            barrier()
            for tb in range(T // 8):
                sl_ = tb % 2
                bk0 = 4 + 2 * (tb % 2)
                for i in range(8):
                    t = tb * 8 + i
                    tc, tl = t // 128, t % 128
                    TS(Pt[:, sl_, i, :], iotab[:], abgT[:, tc, 0, tl:tl + 1], ALU.is_equal, [Bconst, Babg], [BPt[sl_][i]],
                       s2=abgT[:, tc, 2, tl:tl + 1], op1=ALU.mult, au=False)
                    TS(Qt[:, sl_, i, :], iotab[:], abgT[:, tc, 1, tl:tl + 1], ALU.is_equal, [Bconst, Babg], [BQt[sl_][i]],
                       au=False)
                for i in range(8):
                    bank = bk0 + i // 4
                    MM(ps[:, bank, (i % 4) * 128:(i % 4 + 1) * 128], Qt[:, sl_, i, :], Pt[:, sl_, i, :], True, True,
                       [BQt[sl_][i], BPt[sl_][i]], [PB[bank]], au=False)
                for hb in range(2):
                    t0 = tb * 8 + hb * 4
                    ACT(G[:, :, t0:t0 + 4], ps[:, bk0 + hb, :].rearrange("p (t i) -> p i t", t=4), AF.Copy,
                        [PB[bk0 + hb]], [BGm[tb * 2 + hb]])

            ckpt(9)
            PBA = [PB[4], PB[5]]

            def stageA(ec):
                eg, cc, hs = ec // 2, ec % 2, ec % 2
                sl2 = eg % 4
                if cc == 0:
                    DMA(lambda e: e.dma_start(out=UTs[:, sl2, :, :].rearrange("p k c -> p (k c)"), in_=dscr[UV0 + eg]),
                        [Bscr[UV0 + eg]], [BUT[sl2]])
                    DMA(lambda e: e.dma_start(out=Vs[:, sl2, :, :].rearrange("p k c -> p (k c)"), in_=dscr[UV0 + 64 + eg]),
                        [Bscr[UV0 + 64 + eg]], [BVs[sl2]])
                for k in range(8):
                    MM(ps[:, 4 + hs, 0:256], UTs[:, sl2, k, cc * 128:(cc + 1) * 128], h2T[:, k, :],
                       k == 0, k == 7, [BUT[sl2], Bh2], [PBA[hs]], au=False)
                ACT(gl[:, hs, :], ps[:, 4 + hs, 0:256], AF.Gelu, [PBA[hs]], [Bgl[hs]], au=False)
                TTo(GA[:, hs, :], gl[:, hs, :], G[:, ec, :], ALU.mult, [Bgl[hs], BGm], [BGA[hs]])

            def stageV(ec):
                eg, cc, hs = ec // 2, ec % 2, ec % 2
                sl2 = eg % 4
                for dk in range(8):
                    MM(ps[:, dk // 2, (dk % 2) * 256:(dk % 2 + 1) * 256], Vs[:, sl2, cc, dk * 128:(dk + 1) * 128],
                       GA[:, hs, :], (ec == 0 and dk % 2 == 0), ec == 127, [BVs[sl2], BGA[hs]], [PB[dk // 2]],
                       au=False, sgc=True)

            stageA(0)
            for ec in range(128):
                if ec + 1 < 128:
                    stageA(ec + 1)
                stageV(ec)
                if ec == 2 and ti + 1 < NT:
                    capture[0] = []
                    prefetch(ti + 1)
                    pending = capture[0]
                    capture[0] = None
                if ec >= 2 and ti + 1 < NT and pending:
                    replay(pending.pop(0))
            if ti + 1 < NT:
                while pending:
                    replay(pending.pop(0))

            ckpt(10)
            for dk in range(8):
                TTo(xt[:, dk, :], xt[:, dk, :], ps[:, dk // 2, (dk % 2) * 256:(dk % 2 + 1) * 256], ALU.add,
                    [Bx[dk], PB[dk // 2]], [Bx[dk]], au=False)
            def make_final(ti_, xt_, Bx_):
                def fin():
                    colstats(xt_, 0, T, Bx_, st1, Bs1, st3, Bs3, sqn, Bsqn, 4, sq_au=False)
                    for k in range(8):
                        STT(xt_[:, k, :], xt_[:, k, :], vcol(V_GF + k), st1[:, 0:T], ALU.mult, ALU.mult,
                            [Bx_[k], Bvec, Bs1], [Bx_[k]], au=False)
                    DMA(lambda e: e.dma_start(out=dout[:, :, ti_ * T:(ti_ + 1) * T].rearrange("k p t -> p k t"), in_=xt_[:, :, :]),
                        [Bx_], [Bout], False, final=True)
                return fin

            pending_final[0] = make_final(ti, xt, Bx)
            if ti == NT - 1:
                pending_final[0]()
                pending_final[0] = None

        P.emit()
    return nc


def _prep_shared(inp):
    f = np.float32
    w_in = np.asarray(inp["w_in"], f)[0]
    b_in = np.asarray(inp["b_in"], f)[0]
    blk = lambda base, c: list(range(base + c * 128, base + (c + 1) * 128))
    chunks = []
    for pr in range(2):
        for c in range(4):
            chunks.append(blk(0, pr * 4 + c))
        for c in range(4):
            chunks.append(blk(1024, pr * 4 + c))
    for c in range(8):
        chunks.append(blk(2048, c))
    k0 = list(range(3072, 3136))
    k1 = list(range(3136, 3200))
    chunks.append(k0 + k0)
    chunks.append(k1 + k1)
    chunks.append(list(range(3200, 3328)))
    chunks.append(list(range(3200, 3328)))
    for c in range(8):
        chunks.append(blk(3328, c))
    for c in range(8):
        chunks.append(blk(4352, c))
    assert len(chunks) == 44
    colidx = np.array(sum(chunks, []), dtype=np.int64)
    w_perm = w_in[:, colidx]
    b_perm = b_in[colidx]
    mats = [w_perm[:, g * 512:(g + 1) * 512] for g in range(11)]
    for name in ("w_conv_out", "w_attn_o", "w_out"):
        w = np.asarray(inp[name], f)[0]
        mats += [w[:, 0:512], w[:, 512:1024]]
    wpq = np.asarray(inp["w_peer_q"], f)[0]
    mats += [wpq[:, g * 512:(g + 1) * 512] for g in range(4)]
    assert len(mats) == NMIXG
    wall = np.empty((NPIECE, 128, 2048), f)
    for g, m in enumerate(mats):
        a = m.reshape(8, 128, 512).transpose(1, 0, 2)
        wall[2 * g] = a[:, 0:4, :].reshape(128, 2048)
        wall[2 * g + 1] = a[:, 4:8, :].reshape(128, 2048)
    U = np.asarray(inp["peer_u"], f)[0]
    V = np.asarray(inp["peer_v"], f)[0]
    wall[UV0:UV0 + 64] = U.reshape(64, 256, 8, 128).transpose(0, 3, 2, 1).reshape(64, 128, 2048)
    wall[UV0 + 64:UV0 + 128] = V.reshape(64, 2, 128, 1024).transpose(0, 2, 1, 3).reshape(64, 128, 2048)

    vec = np.zeros((128, NV), f)
    col = lambda v: np.asarray(v, f).reshape(-1, 128).T
    vec[:, V_G1:V_G1 + 8] = col(inp["norm1_g"][0])
    vec[:, V_BIN:V_BIN + 44] = col(b_perm)
    vec[:, V_CB:V_CB + 8] = col(inp["conv_b"][0])
    vec[:, V_LNG:V_LNG + 8] = col(inp["conv_ln_g"][0])
    vec[:, V_LNB:V_LNB + 8] = col(inp["conv_ln_b"][0])
    vec[:, V_G2:V_G2 + 8] = col(inp["norm2_g"][0])
    vec[:, V_GF:V_GF + 8] = col(inp["final_g"])
    p = np.arange(128)
    invf = (np.float32(10000.0) ** (-(np.arange(32, dtype=f) * f(2.0) / f(64)))).astype(f)
    vec[:, V_INVF] = invf[p % 32]
    vec[:, V_SGN] = np.where(p % 64 < 32, -1.0, 1.0)
    cw = np.asarray(inp["conv_w"], f)[0]
    vec[:, V_CW:V_CW + 248] = cw.reshape(31, 8, 128).transpose(2, 1, 0).reshape(128, 248)

    cm = np.zeros((128, NCM), f)
    cm[:, C_ID:C_ID + 128] = np.eye(128, dtype=f)
    cm[p, C_PERM + (p ^ 32)] = 1.0
    cm[:, C_IOTA:C_IOTA + 128] = np.arange(128, dtype=f)[None, :]
    cm[:, C_IOTA16:C_IOTA16 + 16] = np.arange(16, dtype=f)[None, :]
    kk = np.arange(128)[:, None]
    qq = np.arange(128)[None, :]
    NEGM = f(-240000.0)
    m_prev = np.where(kk > qq, f(0), NEGM).astype(f)
    m_cur = np.where(kk <= qq, f(0), NEGM).astype(f)
    cm[:, C_MASK1:C_MASK1 + 128] = m_prev
    cm[:, C_MASK1 + 128:C_MASK1 + 256] = m_cur
    cm[:, C_MASK0 + 128:C_MASK0 + 256] = m_cur
    rows = np.concatenate([b_in[3200:3328], np.asarray(inp["attn_sinks"], f)[0]]).reshape(1, 144).astype(f)
    sk = np.asarray(inp["peer_sub_keys"], f)[0]
    skT = np.ascontiguousarray(sk.transpose(3, 0, 1, 2).reshape(128, 2048))
    return dict(wall=wall, vec=vec, cm=cm, rows=rows, skT=skT), m_prev, NEGM


_NC_CACHE = {}


def kernel(**inputs):
    x = np.asarray(inputs["x"], np.float32)
    pos = np.asarray(inputs["positions"], np.int32)
    B, S, _ = x.shape
    TOK = S // 2
    NT = TOK // TT
    shared, m_prev, NEGM = _prep_shared(inputs)
    in_maps = []
    for core in range(8):
        b, hs = core // 2, core % 2
        s0 = hs * TOK
        xT = np.zeros((8, 128, TOK + HALO), np.float32)
        pp = np.zeros((1, TOK + HALO), np.int32)
        if hs == 0:
            xs = x[b, 0:TOK]
            xT[:, :, HALO:] = xs.T.reshape(8, 128, TOK)
            pp[0, HALO:] = pos[b, 0:TOK]
        else:
            xs = x[b, s0 - HALO:s0 + TOK]
            xT[:] = xs.T.reshape(8, 128, TOK + HALO)
            pp[0] = pos[b, s0 - HALO:s0 + TOK]
        vec = shared["vec"].copy()
        vec[:, V_HV] = 0.0 if hs == 0 else 1.0
        cm = shared["cm"].copy()
        if hs == 0:
            cm[:, C_MASK0:C_MASK0 + 128] = NEGM
        else:
            cm[:, C_MASK0:C_MASK0 + 128] = m_prev
        in_maps.append(dict(xT=xT, pos=pp, wall=shared["wall"], vec=vec, cm=cm, rows=shared["rows"], skT=shared["skT"]))
    if NT not in _NC_CACHE:
        _NC_CACHE[NT] = build_nc(NT)
    nc = _NC_CACHE[NT]
    res = run_bass_kernel_spmd(nc, in_maps, core_ids=list(range(8)))
    out = np.empty((B, S, D), np.float32)
    for core in range(8):
        b, hs = core // 2, core % 2
        oT = np.asarray(res.results[core]["outT"], np.float32)
        out[b, hs * TOK:(hs + 1) * TOK, :] = oT.reshape(1024, TOK).T
    return out
```

```python
import numpy as np
from contextlib import ExitStack
import concourse.bass as bass
import concourse.mybir as mybir
from concourse.bass_utils import run_bass_kernel_spmd

F32 = mybir.dt.float32
BF16 = mybir.dt.bfloat16
I32 = mybir.dt.int32
U32 = mybir.dt.uint32
AF = mybir.ActivationFunctionType
ALU = mybir.AluOpType
AX = mybir.AxisListType


class Buf:
    __slots__ = ("name", "w", "r")

    def __init__(self, name=""):
        self.name = name
        self.w = None
        self.r = []


class BG(list):
    def __init__(self, name, n):
        super().__init__(Buf("%s%d" % (name, i)) for i in range(n))


def _flat(bs):
    out = []
    for b in bs:
        if isinstance(b, list):
            out.extend(_flat(b))
        else:
            out.append(b)
    return out


class _Ins:
    __slots__ = ("eng", "fn", "deps", "dma", "idx", "sig", "cnt", "semi", "final")

    def __init__(self, eng, fn, dma):
        self.eng = eng
        self.fn = fn
        self.deps = set()
        self.dma = dma
        self.sig = False
        self.cnt = 0
        self.semi = 0
        self.final = False


class Prog:
    NDMA_SEM = 12
    ENGS = ("pe", "act", "dve", "pool", "sp")

    def __init__(self, nc, es):
        self.nc = nc
        self.es = es
        self.q = {e: [] for e in self.ENGS}
        self.all = []
        self.dma_engine = "sp"
        self.halted = False

    def _add(self, ins, reads, writes):
        if self.halted:
            return ins
        reads = _flat(reads)
        writes = _flat(writes)
        for b in reads:
            if b.w is not None:
                ins.deps.add(b.w)
        for b in writes:
            if b.w is not None:
                ins.deps.add(b.w)
            for r in b.r:
                ins.deps.add(r)
        ins.deps.discard(ins)
        for b in reads:
            b.r.append(ins)
        for b in writes:
            b.w = ins
            b.r = []
        ins.idx = len(self.all)
        self.all.append(ins)
        self.q[ins.eng].append(ins)
        return ins

    def op(self, eng, fn, reads=(), writes=()):
        return self._add(_Ins(eng, fn, False), reads, writes)

    def dma(self, fn, reads=(), writes=(), final=False, eng=None):
        ins = _Ins(eng or self.dma_engine, fn, True)
        ins.final = final
        return self._add(ins, reads, writes)

    def emit(self):
        nc = self.nc
        for ins in self.all:
            for d in ins.deps:
                if d.eng == "pe" and ins.eng == "pe" and not d.dma and not ins.dma:
                    continue
                d.sig = True
            if ins.final:
                ins.sig = True
        sems = {e: self.es.enter_context(nc.semaphore("s_" + e)) for e in self.ENGS}
        dsems = [self.es.enter_context(nc.semaphore("d%d" % i)) for i in range(self.NDMA_SEM)]
        cnt = {e: 0 for e in self.ENGS}
        ndma = 0
        dma_prev = {}
        last_on_sem = [None] * self.NDMA_SEM
        for ins in self.all:
            if ins.dma:
                ins.semi = ndma % self.NDMA_SEM
                ins.cnt = 16 * (ndma // self.NDMA_SEM + 1)
                dma_prev[ins] = last_on_sem[ins.semi]
                last_on_sem[ins.semi] = ins
                ndma += 1
            elif ins.sig:
                cnt[ins.eng] += 1
                ins.cnt = cnt[ins.eng]
        finals = [i for i in self.all if i.final]
        block = self.es.enter_context(nc.Block())

        def run(engname, e):
            waited = {}

            def wait_for(d):
                if d.dma:
                    key = ("d", d.semi)
                    sem = dsems[d.semi]
                else:
                    key = ("e", d.eng)
                    sem = sems[d.eng]
                if waited.get(key, 0) >= d.cnt:
                    return
                e.wait_ge(sem, d.cnt)
                waited[key] = d.cnt

            for ins in self.q[engname]:
                for d in sorted(ins.deps, key=lambda z: z.idx):
                    if (d.eng == "pe" and engname == "pe" and not d.dma and not ins.dma):
                        continue
                    wait_for(d)
                if ins.dma:
                    p = dma_prev[ins]
                    if p is not None:
                        wait_for(p)
                h = ins.fn(e)
                if ins.dma:
                    h.then_inc(dsems[ins.semi], 16)
                elif ins.sig:
                    h.then_inc(sems[engname], 1)
            if engname == self.dma_engine:
                for f in finals:
                    wait_for(f)

        @block.sync
        def _(e):
            run("sp", e)

        @block.tensor
        def _(e):
            run("pe", e)

        @block.scalar
        def _(e):
            run("act", e)

        @block.vector
        def _(e):
            run("dve", e)

        @block.gpsimd
        def _(e):
            run("pool", e)


D = 1024
KC = 8
TT = 256
HALO = 128
EPSV = 1e-6
NMIXG = 21
NPIECE = 2 * NMIXG + 128
UV0 = 2 * NMIXG
V_G1, V_BIN, V_CB, V_LNG, V_LNB, V_G2, V_GF, V_INVF, V_SGN, V_HV, V_CW = 0, 8, 52, 60, 68, 76, 84, 92, 93, 94, 95
NV = 95 + 248
C_ID, C_PERM, C_IOTA, C_IOTA16, C_MASK0, C_MASK1 = 0, 128, 256, 384, 400, 656
NCM = 912
MAGIC = 12582912.0
CW1 = 6.28125
CW2 = 2.0 * np.pi - 6.28125


def _chunk_val(c):
    return (c // 4) * 8 + (c % 4)


def _chunk_gate(c):
    return (c // 4) * 8 + 4 + (c % 4)


class _Stop(Exception):
    pass


def build_nc(NT, dbg=False, stop=None):
    T = TT
    TOK = NT * T
    TOKH = TOK + HALO
    nc = bass.Bass("TRN2", target_bir_lowering=False)
    dx = nc.dram_tensor("xT", [8, 128, TOKH], F32, kind="ExternalInput").ap()
    dpos = nc.dram_tensor("pos", [1, TOKH], I32, kind="ExternalInput").ap()
    dwall = nc.dram_tensor("wall", [NPIECE, 128, 2048], F32, kind="ExternalInput").ap()
    dvec = nc.dram_tensor("vec", [128, NV], F32, kind="ExternalInput").ap()
    dcm = nc.dram_tensor("cm", [128, NCM], F32, kind="ExternalInput").ap()
    drow = nc.dram_tensor("rows", [1, 144], F32, kind="ExternalInput").ap()
    dsk = nc.dram_tensor("skT", [128, 2048], F32, kind="ExternalInput").ap()
    dout = nc.dram_tensor("outT", [8, 128, TOK], F32, kind="ExternalOutput").ap()
    dscr = nc.dram_tensor("wscr", [NPIECE, 128, 2048], BF16, kind="Internal").ap()
    ddiag = nc.dram_tensor("dgscr", [8, 128, 31 * 128], BF16, kind="Internal").ap()
    if dbg:
        ddbg = nc.dram_tensor("dbg", [8, 128, T], F32, kind="ExternalOutput").ap()

    es = ExitStack()
    with es:
        def sb(name, shape, dt):
            return es.enter_context(nc.sbuf_tensor("sb_" + name, shape, dt))

        P = Prog(nc, es)
        ps = es.enter_context(nc.psum_tensor("ps", [128, 8, 512], F32))
        PB = [Buf("bank%d" % i) for i in range(8)]

        vec = sb("vec", [128, NV], F32)
        cmf = sb("cmf", [128, NCM], F32)
        identb = sb("identb", [128, 128], BF16)
        permb = sb("permb", [128, 128], BF16)
        onesb = sb("onesb", [128, 128], BF16)
        onesmb = sb("onesmb", [128, 128], BF16)
        onesmf = sb("onesmf", [128, 128], F32)
        iotab = sb("iotab", [128, 128], BF16)
        maskb = sb("maskb", [128, 2, 2, 2, 128], BF16)
        esink = sb("esink", [128, 16], F32)
        bvb = sb("bvb", [128, 128], F32)
        skf = sb("skf", [128, 2048], F32)
        skb = sb("skb", [128, 16, 128], BF16)
        xt_a = sb("xt", [128, 8, T], F32)
        xt_b = sb("xt2", [128, 8, T], F32)
        xts = [xt_a, xt_b]
        sqn = sb("sqn", [128, 8, T], BF16)
        rn1 = sb("rn1", [128, T], F32)
        ubuf = sb("ubuf", [128, 2, 8, 32 + T], BF16)
        kbuf = sb("kbuf", [128, 2, 2, 128 + T], BF16)
        vdup = sb("vdup", [128, 2, 3, 2, 128], BF16)
        cosT = sb("cosT", [128, 128 + T], F32)
        sinT = sb("sinT", [128, 128 + T], F32)
        posi = sb("posi", [128, 128 + T], I32)
        r1 = sb("r1", [128, 128 + T], F32)
        r2 = sb("r2", [128, 128 + T], F32)
        r3 = sb("r3", [128, 128 + T], F32)
        st1 = sb("st1", [128, 128 + T], F32)
        st2 = sb("st2", [128, 128 + T], F32)
        st3 = sb("st3", [128, 128 + T], F32)
        sg1 = sb("sg1", [128, 128 + T], F32)
        sg2 = sb("sg2", [128, 128 + T], F32)
        m1buf = sb("m1buf", [128, 4, T], F32)
        qb = sb("qb", [128, 128 + T], BF16)
        qb2 = sb("qb2", [128, 128 + T], BF16)
        r4 = sb("r4", [128, 128 + T], F32)
        eT = sb("eT", [128, 2, 2, 4, 128], BF16)
        den = sb("den", [128, 512], F32)
        rden = sb("rden", [128, 512], F32)
        rscr = sb("rscr", [128, 512], F32)
        h2T = sb("h2T", [128, 8, T], BF16)
        gl = sb("gl", [128, 2, T], BF16)
        GA = sb("GA", [128, 2, T], BF16)
        Pt = sb("Pt", [128, 2, 8, 128], BF16)
        Qt = sb("Qt", [128, 2, 8, 128], BF16)
        UTs = sb("UTs", [128, 3, 8, 256], BF16)
        Vs = sb("Vs", [128, 3, 2, 1024], BF16)
        v16 = sb("v16", [128, 16, 16], F32)
        i16 = sb("i16", [128, 16, 16], U32)
        i16f = sb("i16f", [128, 16, 16], F32)
        best = sb("best", [128, 8, 16], F32)
        posu = sb("posu", [128, 8, 16], U32)
        k1u = sb("k1u", [128, 8, 16], U32)
        posf = sb("posf", [128, 8, 16], F32)
        k1f = sb("k1f", [128, 8, 16], F32)
        k2f = sb("k2f", [128, 8, 16], F32)
        ebuf = sb("ebuf", [128, 8, 16], F32)
        gate = sb("gate", [128, 8, 16], F32)
        Zs = sb("Zs", [128, 8], F32)
        av = sb("av", [128, 8, 16], F32)
        bvv = sb("bvv", [128, 8, 16], F32)
        abgT = sb("abgT", [128, 2, 3, 128], F32)
        one1 = sb("one1", [128, 4], F32)
        arena = sb("arena", [128, 32768], BF16)

        def av_(off, nbytes, dt):
            a = arena[:, off // 2:(off + nbytes) // 2]
            return a if dt == BF16 else a.bitcast(dt)

        K = 1024
        wbuf = [av_(0, 8 * K, BF16).rearrange("p (k c) -> p k c", k=8),
                av_(8 * K, 8 * K, BF16).rearrange("p (k c) -> p k c", k=8),
                av_(16 * K, 8 * K, BF16).rearrange("p (k c) -> p k c", k=8),
                av_(24 * K, 8 * K, BF16).rearrange("p (k c) -> p k c", k=8)]
        diag = [av_(16 * K, 7936, BF16).rearrange("p (j c) -> p j c", j=31),
                av_(24 * K, 7936, BF16).rearrange("p (j c) -> p j c", j=31)]
        hT = av_(32 * K, 6 * K, BF16).rearrange("p (k c) -> p k c", k=8)
        sqb = av_(38 * K, 6 * K, BF16).rearrange("p (k c) -> p k c", k=8)
        ysb = av_(44 * K, 8 * K, F32).rearrange("p (k c) -> p k c", k=8)
        sT = av_(52 * K, 4 * K, BF16).rearrange("p (k c) -> p k c", k=8)
        qrope = av_(56 * K, 4 * K, BF16).rearrange("p (k c) -> p k c", k=8)
        attnT = av_(60 * K, 4 * K, BF16).rearrange("p (k c) -> p k c", k=8)
        mergedT = sqb
        stin = [av_(i * 8 * K, 8 * K, F32) for i in range(4)]
        stout = [av_(32 * K + i * 4 * K, 4 * K, BF16) for i in range(4)]
        dgst = [av_(48 * K, 7936, BF16).rearrange("p (j c) -> p j c", j=31),
                av_(56 * K, 7936, BF16).rearrange("p (j c) -> p j c", j=31)]
        qTb = av_(16 * K, 8 * K, BF16).rearrange("p (k c) -> p k c", k=16)
        cand = av_(24 * K, 8 * K, F32).rearrange("p (h a b) -> p h a b", h=8, a=16)
        work2 = av_(32 * K, 8 * K, F32).rearrange("p (h c) -> p h c", h=8)
        E1 = av_(40 * K, 8 * K, F32).rearrange("p (h a b) -> p h a b", h=8, a=16)
        scw = av_(48 * K, 2 * K, F32).rearrange("p (l c) -> p l c", l=4)
        G = arena[:, :].rearrange("p (i t) -> p i t", i=128)
        outtmp = av_(0, 8 * K, F32).rearrange("p (k c) -> p k c", k=8)

        ATOK = Buf("atok")

        def vcol(i):
            return vec[:, i:i + 1]

        capture = [None]

        def OP(eng, fn, reads, writes, arena_use=False):
            if capture[0] is not None:
                capture[0].append(("op", (eng, fn, list(reads), list(writes), arena_use)))
                return None
            if arena_use:
                reads = list(reads) + [ATOK]
            return P.op(eng, fn, reads, writes)

        def DMA(fn, reads, writes, arena_use=False, final=False, eng=None):
            if capture[0] is not None:
                capture[0].append(("dma", (fn, list(reads), list(writes), arena_use, final, eng)))
                return None
            if arena_use:
                reads = list(reads) + [ATOK]
            return P.dma(fn, reads, writes, final=final, eng=eng)

        def replay(item):
            kind, args = item
            if kind == "op":
                OP(*args)
            else:
                DMA(*args)

        def barrier():
            P.op("pool", lambda e: e.memset(one1[:, 0:1], 0.0), [], [ATOK])

        def MM(out, lhsT, rhs, start, stop, reads, writes, au=True, sgc=False):
            OP("pe", lambda e: e.matmul(out, lhsT=lhsT, rhs=rhs, start=start, stop=stop,
                                        skip_group_check=sgc), reads, writes, au)

        def ACT(out, in_, func, reads, writes, bias=None, scale=None, au=True):
            kw = {}
            if bias is not None:
                kw["bias"] = bias
            if scale is not None:
                kw["scale"] = scale
            OP("act", lambda e: e.activation(out=out, in_=in_, func=func, **kw), reads, writes, au)

        def TTo(out, in0, in1, op, reads, writes, eng="dve", au=True):
            OP(eng, lambda e: e.tensor_tensor(out=out, in0=in0, in1=in1, op=op), reads, writes, au)

        def TS(out, in0, s1, op0, reads, writes, s2=None, op1=None, eng="dve", au=True):
            if op1 is None:
                OP(eng, lambda e: e.tensor_scalar(out=out, in0=in0, scalar1=s1, scalar2=None, op0=op0),
                   reads, writes, au)
            else:
                OP(eng, lambda e: e.tensor_scalar(out=out, in0=in0, scalar1=s1, scalar2=s2, op0=op0, op1=op1),
                   reads, writes, au)

        def STT(out, in0, scalar, in1, op0, op1, reads, writes, au=True):
            OP("dve", lambda e: e.scalar_tensor_tensor(out=out, in0=in0, scalar=scalar, in1=in1,
                                                       op0=op0, op1=op1), reads, writes, au)

        def CP(out, in_, reads, writes, eng="dve", au=True):
            OP(eng, lambda e: e.tensor_copy(out=out, in_=in_), reads, writes, au)

        def RECIP(out, in_, reads, writes, au=True):
            OP("dve", lambda e: e.reciprocal(out=out, in_=in_), reads, writes, au)

        Bvec, Bcm, Bconst, Bsk = Buf("vec"), Buf("cm"), Buf("const"), Buf("sk")
        Bxs = [BG("xta", 8), BG("xtb", 8)]
        Bsqn, Brn1 = BG("sqn", 8), Buf("rn1")
        Bscr = [Buf("scr%d" % i) for i in range(NPIECE)]
        Bst_in = BG("sti", 4)
        Bst_out = BG("sto", 4)
        Bw = [Buf("w0"), Buf("w1")]
        Bdiag = [Buf("dg0"), Buf("dg1")]
        Bw = Bw + Bdiag
        BhT, Bsq, Bys, BsT, Bqr, Bat = BG("hT", 8), BG("sq", 8), BG("ys", 8), BG("sT", 8), BG("qr", 8), BG("at", 8)
        Bub = [BG("ub0_", 8), BG("ub1_", 8)]
        Bkb = [Buf("kb0"), Buf("kb1")]
        Bvd = [[Buf("vd%d%d" % (p, b)) for b in range(3)] for p in range(2)]
        Bcs, Bposi, Br1, Br2, Br3 = Buf("cs"), Buf("posi"), Buf("r1"), Buf("r2"), Buf("r3")
        Bs1, Bs2, Bs3, Bg1, Bg2, Bm1, Bqb = Buf("st1"), Buf("st2"), Buf("st3"), Buf("sg1"), Buf("sg2"), Buf("m1"), Buf("qb")
        BeT = [Buf("eT0"), Buf("eT1")]
        Bden, Brden, Brscr = Buf("den"), Buf("rden"), Buf("rscr")
        Bh2, Bgl, BGA = Buf("h2T"), [Buf("gl0"), Buf("gl1")], [Buf("GA0"), Buf("GA1")]
        BPt, BQt = [BG("Pt0_", 8), BG("Pt1_", 8)], [BG("Qt0_", 8), BG("Qt1_", 8)]
        BUT, BVs = BG("UT", 3), BG("Vs", 3)
        Bv16, Bi16, Bi16f, Bbest, Bpos, Btk, Babg = BG("v16_", 16), BG("i16_", 16), Buf("i16f"), BG("best", 8), BG("pos", 8), Buf("tk"), Buf("abg")
        BqT, Bcand, Bw2, BE1, Bscw, BGm, Bot = BG("qTb", 16), Buf("cand"), BG("w2_", 8), Buf("E1"), BG("scw", 4), BG("G", 64), Buf("ot")
        Bsgs = BG("sgs", 2)
        Bra, Brb, Bqbs = BG("ra", 2), BG("rb", 2), BG("qbs", 2)
        Bout = Buf("out")

        DMA(lambda e: e.dma_start(out=vec[:], in_=dvec[:, :]), [], [Bvec])
        DMA(lambda e: e.dma_start(out=cmf[:], in_=dcm[:, :]), [], [Bcm])
        DMA(lambda e: e.dma_start(out=skf[:], in_=dsk[:, :]), [], [Bsk])
        DMA(lambda e: e.dma_start(out=bvb[:], in_=drow[0:1, 0:128].partition_broadcast(128)[:, 0, :]), [], [Bconst])
        DMA(lambda e: e.dma_start(out=esink[:], in_=drow[0:1, 128:144].partition_broadcast(128)[:, 0, :]), [], [Bconst])
        CP(identb[:], cmf[:, C_ID:C_ID + 128], [Bcm], [Bconst], au=False)
        CP(permb[:], cmf[:, C_PERM:C_PERM + 128], [Bcm], [Bconst], au=False)
        CP(iotab[:], cmf[:, C_IOTA:C_IOTA + 128], [Bcm], [Bconst], au=False)
        OP("dve", lambda e: e.memset(onesb[:], 1.0), [], [Bconst])
        OP("dve", lambda e: e.memset(onesmb[:], 1.0 / 1024.0), [], [Bconst])
        OP("dve", lambda e: e.memset(onesmf[:], 1.0 / 1024.0), [], [Bconst])
        for var in range(2):
            for kb in range(2):
                for h4 in range(2):
                    c0 = (C_MASK0 if var == 0 else C_MASK1) + kb * 128
                    CP(maskb[:, var, kb, h4, :], cmf[:, c0:c0 + 128], [Bcm], [Bconst], au=False)
        ACT(esink[:], esink[:], AF.Exp, [Bconst], [Bconst], au=False)
        CP(skb[:].rearrange("p a b -> p (a b)"), skf[:], [Bsk], [Bsk], au=False)

        cast_engs = ["act", "dve", "pool"]
        NPR = NPIECE if stop != 1 else 0

        def pl_load(i):
            s = i % 4
            DMA(lambda e: e.dma_start(out=stin[s], in_=dwall[i]), [], [Bst_in[s]], True)

        def pl_cast_store(i):
            s = i % 4
            ce = cast_engs[i % 3]
            if ce == "act":
                ACT(stout[s], stin[s], AF.Copy, [Bst_in[s]], [Bst_out[s]])
            else:
                CP(stout[s], stin[s], [Bst_in[s]], [Bst_out[s]], eng=ce)
            DMA(lambda e: e.dma_start(out=dscr[i], in_=stout[s]), [Bst_out[s]], [Bscr[i]], True, eng="act")

        for i in range(min(3, NPR)):
            pl_load(i)
        for i in range(NPR):
            if i + 3 < NPR:
                pl_load(i + 3)
            pl_cast_store(i)

        Bdg = [BG("dgs0_", 31), BG("dgs1_", 31)]
        Bdgd = BG("dgd", 8)
        for c in range(8 if stop != 1 else 0):
            s = c % 2
            for jt in range(31):
                ACT(dgst[s][:, jt, :], cmf[:, C_ID:C_ID + 128], AF.Copy, [Bcm, Bvec], [Bdg[s][jt]],
                    scale=vcol(V_CW + c * 31 + jt))
            DMA(lambda e, c=c, s=s: e.dma_start(out=ddiag[c], in_=dgst[s][:, :, :].rearrange("p j c -> p (j c)")),
                [Bdg[s]], [Bdgd[c]], True)

        wslot = [0]
        wmode = [2]

        def loadw(g):
            s = wslot[0] % wmode[0]
            wslot[0] += 1
            DMA(lambda e: e.dma_start(out=wbuf[s].rearrange("p (r k) c -> p r (k c)", r=2),
                                      in_=dscr[2 * g:2 * g + 2].rearrange("r p f -> p r f")),
                [Bscr[2 * g], Bscr[2 * g + 1]], [Bw[s]], True)
            return s

        bankrr = [0]
        bankmode = [4]

        def nextbank():
            b = bankrr[0] % bankmode[0]
            bankrr[0] = (b + 1) % bankmode[0]
            return b

        def proj(bank, s, j, rhs3, c0, n, rbuf):
            for k in range(8):
                MM(ps[:, bank, 0:n], wbuf[s][:, k, j * 128:(j + 1) * 128], rhs3[:, k, c0:c0 + n],
                   k == 0, k == 7, [Bw[s], rbuf], [PB[bank]])

        def colstats(src3, c0, n, srcbuf, outrr, outbuf, tmp, tmpbuf, sqv, sqbuf, bank, sq_au=True):
            for k in range(8):
                ACT(sqv[:, k, c0:c0 + n], src3[:, k, c0:c0 + n], AF.Square, [srcbuf[k]], [sqbuf[k]], au=sq_au)
            for k in range(8):
                MM(ps[:, bank, 0:n], onesmb[:], sqv[:, k, c0:c0 + n], k == 0, k == 7, [Bconst, sqbuf[k]], [PB[bank]], au=sq_au)
            TS(tmp[:, c0:c0 + n], ps[:, bank, 0:n], EPSV, ALU.add, [PB[bank]], [tmpbuf], au=False)
            ACT(tmp[:, c0:c0 + n], tmp[:, c0:c0 + n], AF.Sqrt, [tmpbuf], [tmpbuf], au=False)
            RECIP(outrr[:, c0:c0 + n], tmp[:, c0:c0 + n], [tmpbuf], [outbuf], au=False)

        def rope_tables(c0, n, colbase):
            DMA(lambda e: e.dma_start(out=posi[:, c0:c0 + n],
                                      in_=dpos[0:1, colbase:colbase + n].partition_broadcast(128)[:, 0, :]),
                [], [Bposi])
            sl = slice(c0, c0 + n)
            CP(r1[:, sl], posi[:, sl], [Bposi], [Br1], au=False)
            TS(r1[:, sl], r1[:, sl], vcol(V_INVF), ALU.mult, [Br1, Bvec], [Br1], au=False)
            for (dst, shift, useSgn) in ((sinT, 0.0, True), (cosT, float(np.pi / 2), False)):
                TS(r2[:, sl], r1[:, sl], shift, ALU.add, [Br1], [Br2], au=False)
                TS(r3[:, sl], r2[:, sl], float(1.0 / (2 * np.pi)), ALU.mult, [Br2], [Br3], s2=MAGIC, op1=ALU.add, au=False)
                TS(r3[:, sl], r3[:, sl], MAGIC, ALU.subtract, [Br3], [Br3], au=False)
                STT(r2[:, sl], r3[:, sl], -CW1, r2[:, sl], ALU.mult, ALU.add, [Br3, Br2], [Br2], au=False)
                STT(r2[:, sl], r3[:, sl], -CW2, r2[:, sl], ALU.mult, ALU.add, [Br3, Br2], [Br2], au=False)
                TS(r2[:, sl], r2[:, sl], 3.1415925, ALU.min, [Br2], [Br2], s2=-3.1415925, op1=ALU.max, au=False)
                if useSgn:
                    ACT(dst[:, sl], r2[:, sl], AF.Sin, [Br2, Bvec], [Bcs], scale=vcol(V_SGN), au=False)
                else:
                    ACT(dst[:, sl], r2[:, sl], AF.Sin, [Br2], [Bcs], au=False)

        pending_final = [None]

        def ckpt(i):
            if stop == i:
                P.halted = True

        if stop in (1, 2):
            P.halted = True
        for ti in range(NT):
            par = ti % 2
            c0 = 0 if ti == 0 else 128
            n = 128 + T - c0
            xcol = HALO + ti * T
            barrier()
            xt = xts[par]
            Bx = Bxs[par]

            def prefetch(tn):
                pn = tn % 2
                xc = HALO + tn * T
                cc0 = 0 if tn == 0 else 128
                DMA(lambda e: e.dma_start(out=xts[pn][:], in_=dx[:, :, xc:xc + T].rearrange("k p t -> p k t")),
                    [], [Bxs[pn]])
                rope_tables(cc0, 128 + T - cc0, xc - 128 + cc0)
                colstats(xts[pn], 0, T, Bxs[pn], rn1, Brn1, st2, Bs2, sqn, Bsqn, 6, sq_au=False)

            if ti == 0:
                prefetch(0)
            for k in range(8):
                STT(hT[:, k, 128:128 + T], xt[:, k, :], vcol(V_G1 + k), rn1[:, 0:T], ALU.mult, ALU.mult,
                    [Bx[k], Bvec, Brn1], [BhT[k]])
            if ti == 0:
                DMA(lambda e: e.dma_start(out=ysb[:, :, 0:128], in_=dx[:, :, 0:128].rearrange("k p t -> p k t")),
                    [], [Bys], True)
                colstats(ysb, 0, 128, Bys, st3, Bs3, st2, Bs2, sqb, Bsq, 4)
                for k in range(8):
                    STT(hT[:, k, 0:128], ysb[:, k, 0:128], vcol(V_G1 + k), st3[:, 0:128], ALU.mult, ALU.mult,
                        [Bys[k], Bvec, Bs3], [BhT[k]])
            else:
                for c in range(8):
                    CP(ubuf[:, par, c, 0:32], ubuf[:, 1 - par, c, T:T + 32], [Bub[1 - par][c]], [Bub[par][c]], eng="pool", au=False)
                for g in range(2):
                    CP(kbuf[:, par, g, 0:128], kbuf[:, 1 - par, g, T:T + 128], [Bkb[1 - par]], [Bkb[par]], eng="pool", au=False)
                CP(vdup[:, par, 0, :, :], vdup[:, 1 - par, 2, :, :], [Bvd[1 - par][2]], [Bvd[par][0]], eng="pool", au=False)

            ckpt(3)
            cu0 = 96 if ti == 0 else 128
            nu = 128 + T - cu0
            wsl = {}
            sgl = [sg1, sg2]

            def glu_proj(c):
                pr, j = c // 4, c % 4
                if j == 0:
                    wsl[pr] = (loadw(2 * pr), loadw(2 * pr + 1))
                sv, sgt = wsl[pr]
                bA = nextbank()
                proj(bA, sv, j, hT, cu0, nu, BhT)
                bB = nextbank()
                proj(bB, sgt, j, hT, cu0, nu, BhT)
                sg = sgl[c % 2]
                ACT(sg[:, 0:nu], ps[:, bB, 0:nu], AF.Sigmoid, [PB[bB], Bvec], [Bsgs[c % 2]],
                    bias=vcol(V_BIN + _chunk_gate(c)), au=False)
                STT(ubuf[:, par, c, cu0 - 96:cu0 - 96 + nu], ps[:, bA, 0:nu], vcol(V_BIN + _chunk_val(c)),
                    sg[:, 0:nu], ALU.add, ALU.mult, [PB[bA], Bvec, Bsgs[c % 2]], [Bub[par][c]], au=False)
                if ti == 0:
                    TS(ubuf[:, par, c, 0:32], ubuf[:, par, c, 0:32], vcol(V_HV), ALU.mult,
                       [Bub[par][c], Bvec], [Bub[par][c]], au=False)
                ds_ = c % 2
                DMA(lambda e: e.dma_start(out=diag[ds_][:, :, :].rearrange("p j c -> p (j c)"), in_=ddiag[c]),
                    [Bdgd[c]], [Bdiag[ds_]], True)

            def conv(c):
                ds_ = c % 2
                bC = nextbank()
                for jt in range(31):
                    MM(ps[:, bC, 0:T], diag[ds_][:, jt, :], ubuf[:, par, c, 2 + jt:2 + jt + T],
                       jt == 0, jt == 30, [Bdiag[ds_], Bub[par][c]], [PB[bC]])
                ACT(ysb[:, c, :], ps[:, bC, 0:T], AF.Identity, [PB[bC], Bvec], [Bys[c]], bias=vcol(V_CB + c))
                ACT(sqb[:, c, 0:T], ps[:, bC, 0:T], AF.Square, [PB[bC], Bvec], [Bsq[c]], bias=vcol(V_CB + c))

            glu_proj(0)
            if pending_final[0] is not None:
                pending_final[0]()
                pending_final[0] = None
            for c in range(8):
                if c + 1 < 8:
                    glu_proj(c + 1)
                conv(c)
            wmode[0] = 4
            for c in range(8):
                MM(ps[:, 4, 0:T], onesmf[:], ysb[:, c, :], c == 0, c == 7, [Bconst, Bys[c]], [PB[4]])
            for c in range(8):
                MM(ps[:, 5, 0:T], onesmb[:], sqb[:, c, 0:T], c == 0, c == 7, [Bconst, Bsq[c]], [PB[5]])
            CP(st1[:, 0:T], ps[:, 4, 0:T], [PB[4]], [Bs1], au=False)
            TTo(st2[:, 0:T], st1[:, 0:T], st1[:, 0:T], ALU.mult, [Bs1], [Bs2], au=False)
            TTo(st2[:, 0:T], ps[:, 5, 0:T], st2[:, 0:T], ALU.subtract, [PB[5], Bs2], [Bs2], au=False)
            TS(st2[:, 0:T], st2[:, 0:T], EPSV, ALU.add, [Bs2], [Bs2], au=False)
            ACT(st2[:, 0:T], st2[:, 0:T], AF.Sqrt, [Bs2], [Bs2], au=False)
            RECIP(st3[:, 0:T], st2[:, 0:T], [Bs2], [Bs3], au=False)
            STT(st1[:, 0:T], st1[:, 0:T], -1.0, st3[:, 0:T], ALU.mult, ALU.mult, [Bs1, Bs3], [Bs1], au=False)
            for c in range(8):
                ra_, rb_ = (r1, r2) if c % 2 == 0 else (r3, r4)
                Ba_, Bb_ = ([Br1, Bra[0]], [Br2, Brb[0]]) if c % 2 == 0 else ([Br3, Bra[1]], [Brb[1]])
                TTo(ra_[:, 0:T], ysb[:, c, :], st3[:, 0:T], ALU.mult, [Bys[c], Bs3], Ba_)
                TTo(rb_[:, 0:T], ra_[:, 0:T], st1[:, 0:T], ALU.add, Ba_ + [Bs1], Bb_, au=False)
                ACT(sT[:, c, :], rb_[:, 0:T], AF.Silu, Bb_ + [Bvec], [BsT[c]], bias=vcol(V_LNB + c), scale=vcol(V_LNG + c))

            ckpt(4)
            rcnt = [0]

            def rope_chunk(bank, outap, cc0, nn, bcol, outbuf):
                i_ = rcnt[0] % 2
                rcnt[0] += 1
                qb_ = (qb, qb2)[i_]
                ra_, rb_ = ((r1, r2), (r3, r4))[i_]
                Bq_ = [Bqb, Bqbs[0]] if i_ == 0 else [Bqbs[1]]
                Ba_ = [Br1, Bra[0]] if i_ == 0 else [Br3, Bra[1]]
                Bb_ = [Br2, Brb[0]] if i_ == 0 else [Brb[1]]
                ACT(qb_[:, cc0:cc0 + nn], ps[:, bank, 0:nn], AF.Identity, [PB[bank], Bvec], Bq_, bias=vcol(bcol), au=False)
                b2 = nextbank()
                MM(ps[:, b2, 0:nn], permb[:], qb_[:, cc0:cc0 + nn], True, True, [Bconst] + Bq_, [PB[b2]], au=False)
                TTo(ra_[:, cc0:cc0 + nn], qb_[:, cc0:cc0 + nn], cosT[:, cc0:cc0 + nn], ALU.mult, Bq_ + [Bcs], Ba_, au=False)
                TTo(rb_[:, cc0:cc0 + nn], ps[:, b2, 0:nn], sinT[:, cc0:cc0 + nn], ALU.mult, [PB[b2], Bcs], Bb_, au=False)
                TTo(outap, ra_[:, cc0:cc0 + nn], rb_[:, cc0:cc0 + nn], ALU.add, Ba_ + Bb_, [outbuf], au=True)

            for qg in range(2):
                s = loadw(4 + qg)
                for j in range(4):
                    cq = qg * 4 + j
                    b = nextbank()
                    proj(b, s, j, hT, 128, T, BhT)
                    rope_chunk(b, qrope[:, cq, :], 128, T, V_BIN + 16 + cq, Bqr[cq])
            s = loadw(6)
            for g in range(2):
                b = nextbank()
                proj(b, s, g, hT, c0, n, BhT)
                rope_chunk(b, kbuf[:, par, g, c0:c0 + n], c0, n, V_BIN + 24 + g, Bkb[par])
            for blk in range(3):
                if ti > 0 and blk == 0:
                    continue
                b = nextbank()
                for k in range(8):
                    MM(ps[:, b, 0:128], hT[:, k, blk * 128:(blk + 1) * 128], wbuf[s][:, k, 256:384],
                       k == 0, k == 7, [BhT, Bw[s]], [PB[b]])
                for dup in range(2):
                    TTo(vdup[:, par, blk, :, dup * 64:(dup + 1) * 64],
                        ps[:, b, 0:128].rearrange("p (g d) -> p g d", g=2),
                        bvb[:].rearrange("p (g d) -> p g d", g=2), ALU.add, [PB[b], Bconst], [Bvd[par][blk]], au=False)

            ckpt(5)
            iters = [(b, g, hg) for b in range(2) for g in range(2) for hg in range(2)]

            def att_S(i):
                b, g, hg = iters[i]
                sb0 = 6 if i % 2 == 0 else 2
                var = 0 if (ti == 0 and b == 0) else 1
                j0 = (8 * g + 4 * hg) // 2
                for half in range(2):
                    MM(ps[:, sb0 + half, :], identb[:], maskb[:, var, :, :, :].rearrange("p k a q -> p (k a q)"),
                       True, False, [Bconst], [PB[sb0 + half]], au=False, sgc=True)
                for half in range(2):
                    pa = slice(half * 64, (half + 1) * 64)
                    for kb in range(2):
                        for a in range(2):
                            MM(ps[:, sb0 + half, (kb * 2 + a) * 128:(kb * 2 + a + 1) * 128],
                               kbuf[pa, par, g, (b + kb) * 128:(b + kb + 1) * 128],
                               qrope[pa, j0 + a, b * 128:(b + 1) * 128],
                               False, True, [Bkb[par], Bqr[j0 + a]], [PB[sb0 + half]], sgc=True)

            def att_rest(i):
                b, g, hg = iters[i]
                sl_ = i % 2
                sb0 = 6 if i % 2 == 0 else 2
                ob, db = (4, 5) if i % 2 == 0 else (0, 1)
                h0 = 8 * g + 4 * hg
                j0 = h0 // 2
                ACT(eT[:, sl_, :, :, :].rearrange("p k h q -> p (k h q)"),
                    ps[:, sb0:sb0 + 2, :].rearrange("p k c -> p (k c)"), AF.Exp, [PB[sb0], PB[sb0 + 1]], [BeT[sl_]],
                    scale=0.125, au=False)
                eTv = eT[:, sl_, :, :, :].rearrange("p half (kb a) q -> p half kb a q", kb=2)
                for kb in range(2):
                    for half in range(2):
                        MM(ps[:, ob, half * 256:(half + 1) * 256], vdup[:, par, b + kb, g, :],
                           eTv[:, half, kb, :, :].rearrange("p a q -> p (a q)"),
                           (kb == 0 and half == 0), kb == 1, [Bvd[par][b + kb], BeT[sl_]], [PB[ob]], au=False, sgc=True)
                for kb in range(2):
                    for half in range(2):
                        MM(ps[:, db, half * 256:(half + 1) * 256], onesb[:],
                           eTv[:, half, kb, :, :].rearrange("p a q -> p (a q)"),
                           (kb == 0 and half == 0), kb == 1, [Bconst, BeT[sl_]], [PB[db]], au=False, sgc=True)
                TTo(den[:].rearrange("p (half a q) -> p half a q", half=2, a=2),
                    ps[:, db, :].rearrange("p (half a q) -> p half a q", half=2, a=2),
                    esink[:, h0:h0 + 4].rearrange("p (a half) -> p half a", half=2).unsqueeze(3).to_broadcast([128, 2, 2, 128]),
                    ALU.add, [PB[db], Bconst], [Bden], au=False)
                RECIP(rden[:], den[:], [Bden], [Brden], au=False)
                for half in range(2):
                    pa = slice(half * 64, (half + 1) * 64)
                    TTo(attnT[pa, j0:j0 + 2, b * 128:(b + 1) * 128],
                        ps[pa, ob, half * 256:(half + 1) * 256].rearrange("p (a q) -> p a q", a=2),
                        rden[pa, half * 256:(half + 1) * 256].rearrange("p (a q) -> p a q", a=2),
                        ALU.mult, [PB[ob], Brden], [Bat[j0], Bat[j0 + 1]])

            att_S(0)
            for i in range(8):
                if i + 1 < 8:
                    att_S(i + 1)
                att_rest(i)

            ckpt(6)
            bankmode[0] = 8
            Bm1s = BG("m1s", 4)
            for jg in range(2):
                sa = loadw(11 + jg)
                sb_ = loadw(7 + jg)
                for j in range(4):
                    c = jg * 4 + j
                    bA = nextbank()
                    proj(bA, sa, j, sT, 0, T, BsT)
                    bB = nextbank()
                    proj(bB, sb_, j, hT, 128, T, BhT)
                    sg = sgl[j % 2]
                    ACT(sg[:, 0:T], ps[:, bB, 0:T], AF.Sigmoid, [PB[bB], Bvec], [Bsgs[j % 2]], bias=vcol(V_BIN + 28 + c), au=False)
                    TTo(m1buf[:, j, :], ps[:, bA, 0:T], sg[:, 0:T], ALU.mult, [PB[bA], Bsgs[j % 2]], [Bm1s[j]], au=False)
                sa = loadw(13 + jg)
                sb_ = loadw(9 + jg)
                for j in range(4):
                    c = jg * 4 + j
                    bA = nextbank()
                    proj(bA, sa, j, attnT, 0, T, Bat)
                    bB = nextbank()
                    proj(bB, sb_, j, hT, 128, T, BhT)
                    sg = sgl[j % 2]
                    rt_ = (r3, r4)[j % 2]
                    Brt_ = [Br3, Bra[1]] if j % 2 == 0 else [Brb[1]]
                    ACT(sg[:, 0:T], ps[:, bB, 0:T], AF.Sigmoid, [PB[bB], Bvec], [Bsgs[j % 2]], bias=vcol(V_BIN + 36 + c), au=False)
                    TTo(rt_[:, 0:T], ps[:, bA, 0:T], sg[:, 0:T], ALU.mult, [PB[bA], Bsgs[j % 2]], Brt_, au=False)
                    TTo(mergedT[:, c, 0:T], rt_[:, 0:T], m1buf[:, j, :], ALU.add, Brt_ + [Bm1s[j]], [Bsq[c]])
            for og in range(2):
                s = loadw(15 + og)
                for j in range(4):
                    c = og * 4 + j
                    b = nextbank()
                    proj(b, s, j, mergedT, 0, T, Bsq)
                    TTo(xt[:, c, :], xt[:, c, :], ps[:, b, 0:T], ALU.add, [Bx[c], PB[b]], [Bx[c]], au=False)
            if dbg and ti == 0:
                DMA(lambda e, xt_=xt: e.dma_start(out=ddbg.rearrange("k p t -> p k t"), in_=xt_[:]), [Bx], [Buf("dbgo")], final=True)

            ckpt(7)
            bankmode[0] = 4
            colstats(xt, 0, T, Bx, st1, Bs1, st2, Bs2, sqb, Bsq, 4)
            for k in range(8):
                STT(h2T[:, k, :], xt[:, k, :], vcol(V_G2 + k), st1[:, 0:T], ALU.mult, ALU.mult, [Bx[k], Bvec, Bs1], [Bh2], au=False)
            wmode[0] = 2
            barrier()
            for g4 in range(4):
                s = loadw(17 + g4)
                for j in range(4):
                    hc = g4 * 4 + j
                    b = nextbank()
                    for k in range(8):
                        MM(ps[:, b, 0:T], wbuf[s][:, k, j * 128:(j + 1) * 128], h2T[:, k, :], k == 0, k == 7,
                           [Bw[s], Bh2], [PB[b]])
                    if hc % 2 == 0:
                        ACT(qTb[:, hc, :], ps[:, b, 0:T], AF.Copy, [PB[b]], [BqT[hc]])
                    else:
                        CP(qTb[:, hc, :], ps[:, b, 0:T], [PB[b]], [BqT[hc]])
            v16v = v16[:, :, :].rearrange("p (h c) k -> p h c k", c=2)
            i16fv = i16f[:, :, :].rearrange("p (h c) k -> p h c k", c=2)
            B4 = [128, 8, 16, 16]
            for tc in range(2):
                tcs = slice(tc * 128, (tc + 1) * 128)
                for g4 in range(4):
                    for l in range(4):
                        hc = g4 * 4 + l
                        MM(ps[:, 5, l * 128:(l + 1) * 128], qTb[:, hc, tcs], skb[:, hc, :], True, True,
                           [BqT[hc], Bsk], [PB[5]])
                    def L1(step, l):
                        hc = g4 * 4 + l
                        src_ = ps[:, 5, l * 128:(l + 1) * 128]
                        if step == 0:
                            OP("dve", lambda e: e.max(out=v16[:, hc, 0:8], in_=src_), [PB[5]], [Bv16[hc]])
                        elif step == 1:
                            OP("dve", lambda e: e.max_index(out=i16[:, hc, 0:8], in_max=v16[:, hc, 0:8], in_values=src_),
                               [PB[5], Bv16[hc]], [Bi16[hc]])
                        elif step == 2:
                            OP("dve", lambda e: e.match_replace(out=scw[:, l, :], in_to_replace=v16[:, hc, 0:8],
                                                               in_values=src_, imm_value=-1e30),
                               [PB[5], Bv16[hc]], [Bscw[l]], True)
                        elif step == 3:
                            OP("dve", lambda e: e.max(out=v16[:, hc, 8:16], in_=scw[:, l, :]), [Bscw[l]], [Bv16[hc]], True)
                        else:
                            OP("dve", lambda e: e.max_index(out=i16[:, hc, 8:16], in_max=v16[:, hc, 8:16], in_values=scw[:, l, :]),
                               [Bscw[l], Bv16[hc]], [Bi16[hc]], True)
                    for step in (0, 2, 1, 3, 4):
                        for l in range(4):
                            L1(step, l)
                CP(i16f[:], i16[:], [Bi16], [Bi16f], au=False)
                TTo(cand[:], v16v[:, :, 0, :].unsqueeze(3).to_broadcast(B4), v16v[:, :, 1, :].unsqueeze(2).to_broadcast(B4),
                    ALU.add, [Bv16], [Bcand])
                def L2(step, h):
                    src_ = cand[:, h, :, :].rearrange("p a b -> p (a b)")
                    if step == 0:
                        OP("dve", lambda e: e.max(out=best[:, h, 0:8], in_=src_), [Bcand], [Bbest[h]], True)
                    elif step == 1:
                        OP("dve", lambda e: e.max_index(out=posu[:, h, 0:8], in_max=best[:, h, 0:8], in_values=src_),
                           [Bcand, Bbest[h]], [Bpos[h]], True)
                    elif step == 2:
                        OP("dve", lambda e: e.match_replace(out=work2[:, h, :], in_to_replace=best[:, h, 0:8],
                                                           in_values=src_, imm_value=-1e30), [Bcand, Bbest[h]], [Bw2[h]], True)
                    elif step == 3:
                        OP("dve", lambda e: e.max(out=best[:, h, 8:16], in_=work2[:, h, :]), [Bw2[h]], [Bbest[h]], True)
                    else:
                        OP("dve", lambda e: e.max_index(out=posu[:, h, 8:16], in_max=best[:, h, 8:16], in_values=work2[:, h, :]),
                           [Bw2[h], Bbest[h]], [Bpos[h]], True)
                for step in (0, 2, 1, 3, 4):
                    for h in range(8):
                        L2(step, h)
                CP(posf[:], posu[:], [Bpos], [Btk], au=False)
                OP("dve", lambda e: e.tensor_single_scalar(out=k1u[:], in_=posu[:], scalar=4, op=ALU.logical_shift_right),
                   [Bpos], [Btk])
                CP(k1f[:], k1u[:], [Btk], [Btk], au=False)
                STT(k2f[:], k1f[:], -16.0, posf[:], ALU.mult, ALU.add, [Btk], [Btk], au=False)
                TTo(ebuf[:], best[:], best[:, :, 0:1].to_broadcast([128, 8, 16]), ALU.subtract, [Bbest], [Btk], au=False)
                ACT(ebuf[:], ebuf[:], AF.Exp, [Btk], [Btk], au=False)
                OP("dve", lambda e: e.tensor_reduce(out=Zs[:], in_=ebuf[:], axis=AX.X, op=ALU.add), [Btk], [Btk])
                RECIP(Zs[:], Zs[:], [Btk], [Btk], au=False)
                TTo(gate[:], ebuf[:], Zs[:, :].unsqueeze(2).to_broadcast([128, 8, 16]), ALU.mult, [Btk], [Btk], au=False)
                io16 = cmf[:, C_IOTA16:C_IOTA16 + 16].unsqueeze(1).unsqueeze(1).to_broadcast(B4)
                for (kf, cidx, dst) in ((k1f, 0, av), (k2f, 1, bvv)):
                    TTo(E1[:], kf[:].unsqueeze(3).to_broadcast(B4), io16, ALU.is_equal, [Btk, Bcm], [BE1])
                    TTo(E1[:], E1[:], i16fv[:, :, cidx, :].unsqueeze(2).to_broadcast(B4), ALU.mult, [BE1, Bi16f], [BE1])
                    OP("dve", lambda e, dst=dst: e.tensor_reduce(out=dst[:], in_=E1[:], axis=AX.X, op=ALU.add), [BE1], [Btk], True)
                for idx, srcv in enumerate((av, bvv, gate)):
                    OP("pe", lambda e, idx=idx, srcv=srcv: e.transpose(out=ps[:, 5, idx * 128:(idx + 1) * 128],
                                                                     in_=srcv[:].rearrange("p h k -> p (h k)"),
                                                                     identity=cmf[:, C_ID:C_ID + 128]),
                       [Btk, Bcm], [PB[5]])
                CP(abgT[:, tc, :, :].rearrange("p a b -> p (a b)"), ps[:, 5, 0:384], [PB[5]], [Babg], au=False)

            ckpt(8)
            barrier()
            for tb in range(T // 8):
                sl_ = tb % 2
                bk0 = 4 + 2 * (tb % 2)
                for i in range(8):
                    t = tb * 8 + i
                    tc, tl = t // 128, t % 128
                    TS(Pt[:, sl_, i, :], iotab[:], abgT[:, tc, 0, tl:tl + 1], ALU.is_equal, [Bconst, Babg], [BPt[sl_][i]],
                       s2=abgT[:, tc, 2, tl:tl + 1], op1=ALU.mult, au=False)
                    TS(Qt[:, sl_, i, :], iotab[:], abgT[:, tc, 1, tl:tl + 1], ALU.is_equal, [Bconst, Babg], [BQt[sl_][i]],
                       au=False)
                for i in range(8):
                    bank = bk0 + i // 4
                    MM(ps[:, bank, (i % 4) * 128:(i % 4 + 1) * 128], Qt[:, sl_, i, :], Pt[:, sl_, i, :], True, True,
                       [BQt[sl_][i], BPt[sl_][i]], [PB[bank]], au=False)
                for hb in range(2):
                    t0 = tb * 8 + hb * 4
                    ACT(G[:, :, t0:t0 + 4], ps[:, bk0 + hb, :].rearrange("p (t i) -> p i t", t=4), AF.Copy,
                        [PB[bk0 + hb]], [BGm[tb * 2 + hb]])

            ckpt(9)
            PBA = [PB[4], PB[5]]

            def stageA(ec):
                eg, cc, hs = ec // 2, ec % 2, ec % 2
                sl2 = eg % 3
                if cc == 0:
                    DMA(lambda e: e.dma_start(out=UTs[:, sl2, :, :].rearrange("p k c -> p (k c)"), in_=dscr[UV0 + eg]),
                        [Bscr[UV0 + eg]], [BUT[sl2]])
                    DMA(lambda e: e.dma_start(out=Vs[:, sl2, :, :].rearrange("p k c -> p (k c)"), in_=dscr[UV0 + 64 + eg]),
                        [Bscr[UV0 + 64 + eg]], [BVs[sl2]])
                for k in range(8):
                    MM(ps[:, 4 + hs, 0:256], UTs[:, sl2, k, cc * 128:(cc + 1) * 128], h2T[:, k, :],
                       k == 0, k == 7, [BUT[sl2], Bh2], [PBA[hs]], au=False)
                ACT(gl[:, hs, :], ps[:, 4 + hs, 0:256], AF.Gelu, [PBA[hs]], [Bgl[hs]], au=False)
                TTo(GA[:, hs, :], gl[:, hs, :], G[:, ec, :], ALU.mult, [Bgl[hs], BGm], [BGA[hs]])

            def stageV(ec):
                eg, cc, hs = ec // 2, ec % 2, ec % 2
                sl2 = eg % 3
                for dk in range(8):
                    MM(ps[:, dk // 2, (dk % 2) * 256:(dk % 2 + 1) * 256], Vs[:, sl2, cc, dk * 128:(dk + 1) * 128],
                       GA[:, hs, :], (ec == 0 and dk % 2 == 0), ec == 127, [BVs[sl2], BGA[hs]], [PB[dk // 2]],
                       au=False, sgc=True)

            stageA(0)
            for ec in range(128):
                if ec + 1 < 128:
                    stageA(ec + 1)
                stageV(ec)
                if ec == 2 and ti + 1 < NT:
                    capture[0] = []
                    prefetch(ti + 1)
                    pending = capture[0]
                    capture[0] = None
                if ec >= 2 and ti + 1 < NT and pending:
                    replay(pending.pop(0))
            if ti + 1 < NT:
                while pending:
                    replay(pending.pop(0))

            ckpt(10)
            for dk in range(8):
                TTo(xt[:, dk, :], xt[:, dk, :], ps[:, dk // 2, (dk % 2) * 256:(dk % 2 + 1) * 256], ALU.add,
                    [Bx[dk], PB[dk // 2]], [Bx[dk]], au=False)
            def make_final(ti_, xt_, Bx_):
                def fin():
                    colstats(xt_, 0, T, Bx_, st1, Bs1, st3, Bs3, sqn, Bsqn, 4, sq_au=False)
                    for k in range(8):
                        STT(xt_[:, k, :], xt_[:, k, :], vcol(V_GF + k), st1[:, 0:T], ALU.mult, ALU.mult,
                            [Bx_[k], Bvec, Bs1], [Bx_[k]], au=False)
                    DMA(lambda e: e.dma_start(out=dout[:, :, ti_ * T:(ti_ + 1) * T].rearrange("k p t -> p k t"), in_=xt_[:, :, :]),
                        [Bx_], [Bout], False, final=True)
                return fin

            pending_final[0] = make_final(ti, xt, Bx)
            if ti == NT - 1:
                pending_final[0]()
                pending_final[0] = None

        P.emit()
    return nc


def _prep_shared(inp):
    f = np.float32
    w_in = np.asarray(inp["w_in"], f)[0]
    b_in = np.asarray(inp["b_in"], f)[0]
    blk = lambda base, c: list(range(base + c * 128, base + (c + 1) * 128))
    chunks = []
    for pr in range(2):
        for c in range(4):
            chunks.append(blk(0, pr * 4 + c))
        for c in range(4):
            chunks.append(blk(1024, pr * 4 + c))
    for c in range(8):
        chunks.append(blk(2048, c))
    k0 = list(range(3072, 3136))
    k1 = list(range(3136, 3200))
    chunks.append(k0 + k0)
    chunks.append(k1 + k1)
    chunks.append(list(range(3200, 3328)))
    chunks.append(list(range(3200, 3328)))
    for c in range(8):
        chunks.append(blk(3328, c))
    for c in range(8):
        chunks.append(blk(4352, c))
    assert len(chunks) == 44
    colidx = np.array(sum(chunks, []), dtype=np.int64)
    w_perm = w_in[:, colidx]
    b_perm = b_in[colidx]
    mats = [w_perm[:, g * 512:(g + 1) * 512] for g in range(11)]
    for name in ("w_conv_out", "w_attn_o", "w_out"):
        w = np.asarray(inp[name], f)[0]
        mats += [w[:, 0:512], w[:, 512:1024]]
    wpq = np.asarray(inp["w_peer_q"], f)[0]
    mats += [wpq[:, g * 512:(g + 1) * 512] for g in range(4)]
    assert len(mats) == NMIXG
    wall = np.empty((NPIECE, 128, 2048), f)
    for g, m in enumerate(mats):
        a = m.reshape(8, 128, 512).transpose(1, 0, 2)
        wall[2 * g] = a[:, 0:4, :].reshape(128, 2048)
        wall[2 * g + 1] = a[:, 4:8, :].reshape(128, 2048)
    U = np.asarray(inp["peer_u"], f)[0]
    V = np.asarray(inp["peer_v"], f)[0]
    wall[UV0:UV0 + 64] = U.reshape(64, 256, 8, 128).transpose(0, 3, 2, 1).reshape(64, 128, 2048)
    wall[UV0 + 64:UV0 + 128] = V.reshape(64, 2, 128, 1024).transpose(0, 2, 1, 3).reshape(64, 128, 2048)

    vec = np.zeros((128, NV), f)
    col = lambda v: np.asarray(v, f).reshape(-1, 128).T
    vec[:, V_G1:V_G1 + 8] = col(inp["norm1_g"][0])
    vec[:, V_BIN:V_BIN + 44] = col(b_perm)
    vec[:, V_CB:V_CB + 8] = col(inp["conv_b"][0])
    vec[:, V_LNG:V_LNG + 8] = col(inp["conv_ln_g"][0])
    vec[:, V_LNB:V_LNB + 8] = col(inp["conv_ln_b"][0])
    vec[:, V_G2:V_G2 + 8] = col(inp["norm2_g"][0])
    vec[:, V_GF:V_GF + 8] = col(inp["final_g"])
    p = np.arange(128)
    invf = (np.float32(10000.0) ** (-(np.arange(32, dtype=f) * f(2.0) / f(64)))).astype(f)
    vec[:, V_INVF] = invf[p % 32]
    vec[:, V_SGN] = np.where(p % 64 < 32, -1.0, 1.0)
    cw = np.asarray(inp["conv_w"], f)[0]
    vec[:, V_CW:V_CW + 248] = cw.reshape(31, 8, 128).transpose(2, 1, 0).reshape(128, 248)

    cm = np.zeros((128, NCM), f)
    cm[:, C_ID:C_ID + 128] = np.eye(128, dtype=f)
    cm[p, C_PERM + (p ^ 32)] = 1.0
    cm[:, C_IOTA:C_IOTA + 128] = np.arange(128, dtype=f)[None, :]
    cm[:, C_IOTA16:C_IOTA16 + 16] = np.arange(16, dtype=f)[None, :]
    kk = np.arange(128)[:, None]
    qq = np.arange(128)[None, :]
    NEGM = f(-240000.0)
    m_prev = np.where(kk > qq, f(0), NEGM).astype(f)
    m_cur = np.where(kk <= qq, f(0), NEGM).astype(f)
    cm[:, C_MASK1:C_MASK1 + 128] = m_prev
    cm[:, C_MASK1 + 128:C_MASK1 + 256] = m_cur
    cm[:, C_MASK0 + 128:C_MASK0 + 256] = m_cur
    rows = np.concatenate([b_in[3200:3328], np.asarray(inp["attn_sinks"], f)[0]]).reshape(1, 144).astype(f)
    sk = np.asarray(inp["peer_sub_keys"], f)[0]
    skT = np.ascontiguousarray(sk.transpose(3, 0, 1, 2).reshape(128, 2048))
    return dict(wall=wall, vec=vec, cm=cm, rows=rows, skT=skT), m_prev, NEGM


_NC_CACHE = {}


def kernel(**inputs):
    x = np.asarray(inputs["x"], np.float32)
    pos = np.asarray(inputs["positions"], np.int32)
    B, S, _ = x.shape
    TOK = S // 2
    NT = TOK // TT
    shared, m_prev, NEGM = _prep_shared(inputs)
    in_maps = []
    for core in range(8):
        b, hs = core // 2, core % 2
        s0 = hs * TOK
        xT = np.zeros((8, 128, TOK + HALO), np.float32)
        pp = np.zeros((1, TOK + HALO), np.int32)
        if hs == 0:
            xs = x[b, 0:TOK]
            xT[:, :, HALO:] = xs.T.reshape(8, 128, TOK)
            pp[0, HALO:] = pos[b, 0:TOK]
        else:
            xs = x[b, s0 - HALO:s0 + TOK]
            xT[:] = xs.T.reshape(8, 128, TOK + HALO)
            pp[0] = pos[b, s0 - HALO:s0 + TOK]
        vec = shared["vec"].copy()
        vec[:, V_HV] = 0.0 if hs == 0 else 1.0
        cm = shared["cm"].copy()
        if hs == 0:
            cm[:, C_MASK0:C_MASK0 + 128] = NEGM
        else:
            cm[:, C_MASK0:C_MASK0 + 128] = m_prev
        in_maps.append(dict(xT=xT, pos=pp, wall=shared["wall"], vec=vec, cm=cm, rows=shared["rows"], skT=shared["skT"]))
    if NT not in _NC_CACHE:
        _NC_CACHE[NT] = build_nc(NT)
    nc = _NC_CACHE[NT]
    res = run_bass_kernel_spmd(nc, in_maps, core_ids=list(range(8)))
    out = np.empty((B, S, D), np.float32)
    for core in range(8):
        b, hs = core // 2, core % 2
        oT = np.asarray(res.results[core]["outT"], np.float32)
        out[b, hs * TOK:(hs + 1) * TOK, :] = oT.reshape(1024, TOK).T
    return out
```
